# Optimizing a Trainium2 kernel written in Bass

```python
import math
import jax, jax.numpy as jnp
from jax import lax
import numpy as np

D_MODEL = 1024
BATCH = 16
SEQ = 4096
DEPTH = 2

PLE_DIM = 256
EPS = 1e-6
ROPE_THETA = 500000.0
RET_THETA = 10000.0
BLOCK_Q = 128
CHUNK = 128
N_BRANCH = 4

LRU_WIDTH = D_MODEL
LRU_BLOCKS = 16
LRU_BLOCK = LRU_WIDTH // LRU_BLOCKS
CONV_WIDTH = 4
LRU_C = 8.0
DIFF_HEADS = 8
DIFF_DH = D_MODEL // (2 * DIFF_HEADS)
DIFF_ROT = DIFF_DH // 4
MLA_HEADS = 8
MLA_NOPE = 128
MLA_ROPE = 64
MLA_V = 128
MLA_Q_LORA = 3 * D_MODEL // 8
MLA_KV_LORA = D_MODEL // 4
RET_HEADS = 8
RET_DK = 64
RET_DV = 128
D_FF = 4 * D_MODEL

IN_SIZES = (LRU_WIDTH, LRU_WIDTH,
            DIFF_HEADS * 2 * DIFF_DH, DIFF_HEADS * 2 * DIFF_DH, DIFF_HEADS * 2 * DIFF_DH,
            MLA_Q_LORA, MLA_KV_LORA, MLA_ROPE,
            RET_HEADS * RET_DK, RET_HEADS * RET_DK, RET_HEADS * RET_DV, RET_HEADS * RET_DV,
            N_BRANCH * D_MODEL)
IN_OFFSETS = tuple(sum(IN_SIZES[:j]) for j in range(1, len(IN_SIZES)))
N_IN = sum(IN_SIZES)

kernel_name = 'hybrid_gated_bidir_encoder'


def _rms_norm(x, g):
    x32 = x.astype(jnp.float32)
    y = x32 * lax.rsqrt(jnp.mean(x32 * x32, axis=-1, keepdims=True) + EPS)
    return (y * g.astype(jnp.float32)).astype(x.dtype)


def _head_layer_norm(x, g):
    x32 = x.astype(jnp.float32)
    xc = x32 - jnp.mean(x32, axis=-1, keepdims=True)
    var = jnp.mean(xc * xc, axis=-1, keepdims=True)
    return (xc * lax.rsqrt(var + EPS) * g.astype(jnp.float32)).astype(x.dtype)


def _rope_tables(seq, rot_dim, theta, dtype):
    pos = jnp.arange(seq, dtype=jnp.float32)
    inv_freq = theta ** (-jnp.arange(0, rot_dim, 2, dtype=jnp.float32) / rot_dim)
    ang = pos[:, None] * inv_freq[None, :]
    return jnp.cos(ang).astype(dtype), jnp.sin(ang).astype(dtype)


def _apply_rope(x, cos, sin):
    x1, x2 = jnp.split(x, 2, axis=-1)
    return jnp.concatenate([x1 * cos - x2 * sin, x2 * cos + x1 * sin], axis=-1)


def _partial_rope(x, cos, sin, rot):
    return jnp.concatenate([_apply_rope(x[..., :rot], cos, sin), x[..., rot:]], axis=-1)


def _blocked_queries(fn, qs):
    def to_blocks(t):
        b, h, s, d = t.shape
        return t.reshape(b, h, s // BLOCK_Q, BLOCK_Q, d).transpose(2, 0, 1, 3, 4)
    out = lax.map(fn, tuple(to_blocks(t) for t in qs))
    nb, b, h, qb, e = out.shape
    return out.transpose(1, 2, 0, 3, 4).reshape(b, h, nb * qb, e)


def _linear_recurrence_combine(left, right):
    a_l, b_l = left
    a_r, b_r = right
    return a_l * a_r, a_r * b_l + b_r


def _rglru_mixer(xa, ga, conv_w, conv_b, wa, ba, wx, bx, lam):
    b, s, w = xa.shape
    left = CONV_WIDTH // 2
    xp = jnp.pad(xa, ((0, 0), (left, CONV_WIDTH - 1 - left), (0, 0)))
    xc = conv_b
    for t in range(CONV_WIDTH):
        xc = xc + xp[:, t:t + s] * conv_w[t]
    xblk = xc.reshape(b, s, LRU_BLOCKS, LRU_BLOCK)
    h_sum = jnp.zeros_like(xc)
    for d in range(2):
        r = jax.nn.sigmoid(jnp.einsum('bsnc,ncd->bsnd', xblk, wa[d]).reshape(b, s, w) + ba[d])
        i = jax.nn.sigmoid(jnp.einsum('bsnc,ncd->bsnd', xblk, wx[d]).reshape(b, s, w) + bx[d])
        log_a = -LRU_C * r * jax.nn.softplus(-lam[d])
        a = jnp.exp(log_a)
        u = jnp.sqrt(-jnp.expm1(2.0 * log_a)) * (i * xc)
        _, h = lax.associative_scan(_linear_recurrence_combine, (a, u), axis=1, reverse=(d == 1))
        h_sum = h_sum + h
    return h_sum * jax.nn.gelu(ga)


def _diff_attention(q, k, v, q_g, k_g, lam_p, sub_g, lam_init, cos, sin):
    b, s, _ = q.shape
    q = _rms_norm(q.reshape(b, s, DIFF_HEADS, 2, DIFF_DH), q_g).transpose(0, 2, 3, 1, 4)
    k = _rms_norm(k.reshape(b, s, DIFF_HEADS, 2, DIFF_DH), k_g).transpose(0, 2, 3, 1, 4)
    v = v.reshape(b, s, DIFF_HEADS, 2 * DIFF_DH).transpose(0, 2, 1, 3)
    q = _partial_rope(q, cos, sin, DIFF_ROT)
    k = _partial_rope(k, cos, sin, DIFF_ROT)
    q1, q2 = q[:, :, 0], q[:, :, 1]
    k1, k2 = k[:, :, 0], k[:, :, 1]
    lp = lam_p.astype(jnp.float32)
    lam = jnp.exp(jnp.sum(lp[0] * lp[1])) - jnp.exp(jnp.sum(lp[2] * lp[3])) + lam_init
    scale = DIFF_DH ** -0.5

    def block(qb):
        q1b, q2b = qb
        a1 = jax.nn.softmax((jnp.einsum('bhqd,bhkd->bhqk', q1b, k1) * scale).astype(jnp.float32), axis=-1)
        a2 = jax.nn.softmax((jnp.einsum('bhqd,bhkd->bhqk', q2b, k2) * scale).astype(jnp.float32), axis=-1)
        return jnp.einsum('bhqk,bhke->bhqe', (a1 - lam * a2).astype(v.dtype), v)

    o = _blocked_queries(block, (q1, q2))
    o = _rms_norm(o, sub_g) * (1.0 - lam_init)
    return o.transpose(0, 2, 1, 3).reshape(b, s, DIFF_HEADS * 2 * DIFF_DH)


def _mla(cq, ckv, kpe, qa_g, wuq, kva_g, wukv, q_g, k_g, cos, sin):
    b, s, _ = cq.shape
    q = (_rms_norm(cq, qa_g) @ wuq).reshape(b, s, MLA_HEADS, MLA_NOPE + MLA_ROPE)
    kv = (_rms_norm(ckv, kva_g) @ wukv).reshape(b, s, MLA_HEADS, MLA_NOPE + MLA_V)
    k_nope, v = kv[..., :MLA_NOPE], kv[..., MLA_NOPE:]
    k = jnp.concatenate([k_nope, jnp.broadcast_to(kpe[:, :, None, :], (b, s, MLA_HEADS, MLA_ROPE))], axis=-1)
    q = _rms_norm(q, q_g).transpose(0, 2, 1, 3)
    k = _rms_norm(k, k_g).transpose(0, 2, 1, 3)
    v = v.transpose(0, 2, 1, 3)
    q = jnp.concatenate([q[..., :MLA_NOPE], _apply_rope(q[..., MLA_NOPE:], cos, sin)], axis=-1)
    k = jnp.concatenate([k[..., :MLA_NOPE], _apply_rope(k[..., MLA_NOPE:], cos, sin)], axis=-1)
    scale = (MLA_NOPE + MLA_ROPE) ** -0.5

    def block(qb):
        (q_b,) = qb
        a = jax.nn.softmax((jnp.einsum('bhqd,bhkd->bhqk', q_b, k) * scale).astype(jnp.float32), axis=-1)
        return jnp.einsum('bhqk,bhke->bhqe', a.astype(v.dtype), v)

    o = _blocked_queries(block, (q,))
    return o.transpose(0, 2, 1, 3).reshape(b, s, MLA_HEADS * MLA_V)


def _retention(q, k, v, g, gn_g, cos, sin):
    b, s, _ = q.shape
    nc = s // CHUNK
    q = _apply_rope(q.reshape(b, s, RET_HEADS, RET_DK).transpose(0, 2, 1, 3), cos, sin)
    k = _apply_rope(k.reshape(b, s, RET_HEADS, RET_DK).transpose(0, 2, 1, 3), cos, sin) * (RET_DK ** -0.5)
    v = v.reshape(b, s, RET_HEADS, RET_DV).transpose(0, 2, 1, 3)
    qc = q.reshape(b, RET_HEADS, nc, CHUNK, RET_DK)
    kc = k.reshape(b, RET_HEADS, nc, CHUNK, RET_DK)
    vc = v.reshape(b, RET_HEADS, nc, CHUNK, RET_DV)
    log_g = jnp.log1p(-jnp.exp2(-5.0 - jnp.arange(RET_HEADS, dtype=jnp.float32)))
    idx = jnp.arange(CHUNK, dtype=jnp.float32)
    dt = q.dtype
    d_intra = jnp.exp(log_g[:, None, None] * jnp.abs(idx[:, None] - idx[None, :])).astype(dt)
    d_tail = jnp.exp(log_g[:, None] * (CHUNK - 1.0 - idx)[None, :]).astype(dt)
    d_head = jnp.exp(log_g[:, None] * idx[None, :]).astype(dt)
    d_qf = jnp.exp(log_g[:, None] * (idx + 1.0)[None, :]).astype(dt)
    d_qb = jnp.exp(log_g[:, None] * (CHUNK - idx)[None, :]).astype(dt)
    intra = jnp.einsum('bhnij,bhnje->bhnie',
                       jnp.einsum('bhnid,bhnjd->bhnij', qc, kc) * d_intra[:, None], vc)
    kv_fwd = jnp.einsum('bhncd,bhnce->nbhde', kc * d_tail[:, None, :, None], vc)
    kv_bwd = jnp.einsum('bhncd,bhnce->nbhde', kc * d_head[:, None, :, None], vc)
    d_chunk = jnp.exp(log_g * CHUNK).astype(kv_fwd.dtype)[None, :, None, None]

    def step(carry, kv_c):
        return carry * d_chunk + kv_c, carry

    init = jnp.zeros((b, RET_HEADS, RET_DK, RET_DV), kv_fwd.dtype)
    _, past = lax.scan(step, init, kv_fwd)
    _, future = lax.scan(step, init, kv_bwd, reverse=True)
    cross = (jnp.einsum('bhnid,nbhde->bhnie', qc * d_qf[:, None, :, None], past)
             + jnp.einsum('bhnid,nbhde->bhnie', qc * d_qb[:, None, :, None], future))
    o = (intra + cross).reshape(b, RET_HEADS, s, RET_DV)
    o = _head_layer_norm(o, gn_g).transpose(0, 2, 1, 3).reshape(b, s, RET_HEADS * RET_DV)
    return jax.nn.silu(g) * o


def setup_inputs(seed: int = 0) -> dict:
    key = jax.random.key(seed)
    ks = jax.random.split(key, 34)
    L = DEPTH

    def nrm(k, shape, scale):
        return jax.random.normal(k, shape, jnp.float32) * scale

    def gain(k, shape):
        return 1.0 + 0.1 * jax.random.normal(k, shape, jnp.float32)

    a_c = jax.random.uniform(ks[11], (L, 2, LRU_WIDTH), jnp.float32, minval=0.9, maxval=0.999)
    a_base = a_c ** (1.0 / LRU_C)
    lru_lambda = jnp.log(a_base) - jnp.log1p(-a_base)
    return {
        'x': nrm(ks[0], (BATCH, SEQ, D_MODEL), 1.0),
        'p': nrm(ks[1], (DEPTH, BATCH, SEQ, PLE_DIM), 1.0),
        'norm1_g': gain(ks[2], (L, D_MODEL)),
        'w_in': nrm(ks[3], (L, D_MODEL, N_IN), D_MODEL ** -0.5),
        'gate_b': nrm(ks[4], (L, N_BRANCH, D_MODEL), 0.1),
        'conv_w': nrm(ks[5], (L, CONV_WIDTH, LRU_WIDTH), CONV_WIDTH ** -0.5),
        'conv_b': nrm(ks[6], (L, LRU_WIDTH), 0.01),
        'lru_wa': nrm(ks[7], (L, 2, LRU_BLOCKS, LRU_BLOCK, LRU_BLOCK), LRU_BLOCK ** -0.5),
        'lru_ba': nrm(ks[8], (L, 2, LRU_WIDTH), 0.1),
        'lru_wx': nrm(ks[9], (L, 2, LRU_BLOCKS, LRU_BLOCK, LRU_BLOCK), LRU_BLOCK ** -0.5),
        'lru_bx': nrm(ks[10], (L, 2, LRU_WIDTH), 0.1),
        'lru_lambda': lru_lambda,
        'diff_q_g': gain(ks[12], (L, DIFF_DH)),
        'diff_k_g': gain(ks[13], (L, DIFF_DH)),
        'diff_lam': nrm(ks[14], (L, 4, DIFF_DH), 0.1),
        'diff_sub_g': gain(ks[15], (L, 2 * DIFF_DH)),
        'mla_qa_g': gain(ks[16], (L, MLA_Q_LORA)),
        'mla_wuq': nrm(ks[17], (L, MLA_Q_LORA, MLA_HEADS * (MLA_NOPE + MLA_ROPE)), MLA_Q_LORA ** -0.5),
        'mla_kva_g': gain(ks[18], (L, MLA_KV_LORA)),
        'mla_wukv': nrm(ks[19], (L, MLA_KV_LORA, MLA_HEADS * (MLA_NOPE + MLA_V)), MLA_KV_LORA ** -0.5),
        'mla_q_g': gain(ks[20], (L, MLA_NOPE + MLA_ROPE)),
        'mla_k_g': gain(ks[21], (L, MLA_NOPE + MLA_ROPE)),
        'ret_gn_g': gain(ks[22], (L, RET_DV)),
        'w_br_a': nrm(ks[23], (L, LRU_WIDTH, D_MODEL), LRU_WIDTH ** -0.5),
        'w_br_b': nrm(ks[24], (L, DIFF_HEADS * 2 * DIFF_DH, D_MODEL), (DIFF_HEADS * 2 * DIFF_DH) ** -0.5),
        'w_br_c': nrm(ks[25], (L, MLA_HEADS * MLA_V, D_MODEL), (MLA_HEADS * MLA_V) ** -0.5),
        'w_br_d': nrm(ks[26], (L, RET_HEADS * RET_DV, D_MODEL), (RET_HEADS * RET_DV) ** -0.5),
        'w_out': nrm(ks[27], (L, D_MODEL, D_MODEL), D_MODEL ** -0.5),
        'norm2_g': gain(ks[28], (L, D_MODEL)),
        'w_ff1': nrm(ks[29], (L, D_MODEL, D_FF), D_MODEL ** -0.5),
        'w_ff2': nrm(ks[30], (L, D_FF, D_MODEL), D_FF ** -0.5),
        'norm3_g': gain(ks[31], (L, D_MODEL)),
        'w_ple_gate': nrm(ks[32], (L, D_MODEL, D_MODEL), D_MODEL ** -0.5),
        'w_ple_proj': nrm(ks[33], (L, PLE_DIM, D_MODEL), PLE_DIM ** -0.5),
    }


def reference(x, p, norm1_g, w_in, gate_b, conv_w, conv_b, lru_wa, lru_ba, lru_wx, lru_bx, lru_lambda,
              diff_q_g, diff_k_g, diff_lam, diff_sub_g, mla_qa_g, mla_wuq, mla_kva_g, mla_wukv,
              mla_q_g, mla_k_g, ret_gn_g, w_br_a, w_br_b, w_br_c, w_br_d, w_out, norm2_g,
              w_ff1, w_ff2, norm3_g, w_ple_gate, w_ple_proj):
    b, s, _ = x.shape
    cos_b, sin_b = _rope_tables(s, DIFF_ROT, ROPE_THETA, x.dtype)
    cos_c, sin_c = _rope_tables(s, MLA_ROPE, ROPE_THETA, x.dtype)
    cos_d, sin_d = _rope_tables(s, RET_DK, RET_THETA, x.dtype)
    for i in range(DEPTH):
        xn = _rms_norm(x, norm1_g[i])
        (a_x, a_g, b_q, b_k, b_v, c_q, c_kv, c_kpe,
         d_q, d_k, d_v, d_g, gate_logits) = jnp.split(xn @ w_in[i], IN_OFFSETS, axis=-1)
        y_a = _rglru_mixer(a_x, a_g, conv_w[i], conv_b[i], lru_wa[i], lru_ba[i], lru_wx[i], lru_bx[i],
                           lru_lambda[i])
        lam_init = 0.8 - 0.6 * math.exp(-0.3 * i)
        y_b = _diff_attention(b_q, b_k, b_v, diff_q_g[i], diff_k_g[i], diff_lam[i], diff_sub_g[i],
                              lam_init, cos_b, sin_b)
        y_c = _mla(c_q, c_kv, c_kpe, mla_qa_g[i], mla_wuq[i], mla_kva_g[i], mla_wukv[i],
                   mla_q_g[i], mla_k_g[i], cos_c, sin_c)
        y_d = _retention(d_q, d_k, d_v, d_g, ret_gn_g[i], cos_d, sin_d)
        gates = jax.nn.sigmoid(gate_logits.reshape(b, s, N_BRANCH, D_MODEL) + gate_b[i])
        merged = (gates[:, :, 0] * (y_a @ w_br_a[i]) + gates[:, :, 1] * (y_b @ w_br_b[i])
                  + gates[:, :, 2] * (y_c @ w_br_c[i]) + gates[:, :, 3] * (y_d @ w_br_d[i]))
        x = x + merged @ w_out[i]
        h = _rms_norm(x, norm2_g[i])
        x = x + jnp.square(jax.nn.relu(h @ w_ff1[i])) @ w_ff2[i]
        ple_gate = jax.nn.sigmoid(_rms_norm(x, norm3_g[i]) @ w_ple_gate[i])
        x = x + ple_gate * (p[i] @ w_ple_proj[i])
    return x
```

```python
import math
import numpy as np
import concourse.bass as bass
import concourse.mybir as mybir
from concourse.bass_utils import run_bass_kernel_spmd
from contextlib import ExitStack

F32 = mybir.dt.float32
BF16 = mybir.dt.bfloat16
I32 = mybir.dt.int32
AF = mybir.ActivationFunctionType
ALU = mybir.AluOpType
AX = mybir.AxisListType

DM = 1024
NIN = 12992
DFF = 4096
PLED = 256
EPS = 1e-6
OFF = dict(ax=0, ag=1024, bq=2048, bk=3072, bv=4096, cq=5120, ckv=5504, ckpe=5760,
           dq=5824, dk=6336, dv=6848, dg=7872, gl=8896)
SAME_SYNC = True
import os
PH = os.environ.get("KPH", "B3,C3,D3").split(",")
KDEPTH = int(os.environ.get("KDEPTH", "3"))
KCP = os.environ.get("KCP", "scalar")
LENG0 = "vector"
LENG1 = os.environ.get("KLENG1", "vector")
KDEFER = int(os.environ.get("KDEFER", "2"))
KGI = int(os.environ.get("KGI", "2"))
KDACT = int(os.environ.get("KDACT", "2"))
KDD = int(os.environ.get("KDD", "5"))
KE1 = int(os.environ.get("KE1", "3"))
KE2 = int(os.environ.get("KE2", "8"))


class Dom:
    def __init__(self, name, sem, mult):
        self.name, self.sem, self.mult, self.count = name, sem, mult, 0


class Buf:
    __slots__ = ("name", "w", "r")

    def __init__(self, name=""):
        self.name, self.w, self.r = name, {}, {}


class Eng:
    def __init__(self, name, eng, dom):
        self.name, self.eng, self.dom = name, eng, dom
        self.seen = {}
        self.slots = []
        self.si = 0


class KB:
    def __init__(self, S, NSEQ, L, debug=()):
        self.S, self.NSEQ, self.L = S, NSEQ, L
        self.NT, self.NG = S // 128, S // 512
        self.debug = set(debug)
        self.nc = bass.Bass("TRN2", target_bir_lowering=False)
        self.st = ExitStack()
        self.E = {}
        self.doms = []
        self.dbufs = {}
        self.cnt = 0
        self.wi = 0

    def setup_sync(self):
        nc = self.nc
        for name in ["tensor", "vector", "scalar", "gpsimd", "sync"]:
            sem = self.st.enter_context(nc.semaphore("s_" + name))
            dom = Dom(name, sem, 1)
            self.doms.append(dom)
            self.E[name] = Eng(name, getattr(nc, name), dom)
        for q, n in (("sync", 24), ("gpsimd", 16)):
            for i in range(n):
                sem = self.st.enter_context(nc.semaphore("d_%s%d" % (q, i)))
                dom = Dom("d_%s%d" % (q, i), sem, 16)
                self.doms.append(dom)
                self.E[q].slots.append(dom)

    def _deps(self, reads, writes):
        deps = {}
        for b in reads:
            for d, i in b.w.items():
                if deps.get(d, 0) < i:
                    deps[d] = i
        for b in writes:
            for d, i in b.w.items():
                if deps.get(d, 0) < i:
                    deps[d] = i
            for d, i in b.r.items():
                if deps.get(d, 0) < i:
                    deps[d] = i
        return deps

    def _wait(self, E, deps):
        for dom, idx in deps.items():
            if idx <= 0:
                continue
            if dom is E.dom and (E.name == "tensor" or not SAME_SYNC):
                continue
            if E.seen.get(dom, 0) >= idx:
                continue
            E.eng.wait_ge(dom.sem, idx * dom.mult)
            E.seen[dom] = idx

    def op(self, en, fn, reads=(), writes=()):
        E = self.E[en]
        self._wait(E, self._deps(reads, writes))
        ins = fn(E.eng)
        E.dom.count += 1
        ins.then_inc(E.dom.sem, 1)
        c = E.dom.count
        for b in reads:
            b.r[E.dom] = c
        for b in writes:
            b.w[E.dom] = c
        self.cnt += 1

    def dma(self, q, out, in_, reads=(), writes=(), slow=False):
        E = self.E[q]
        slot = E.slots[E.si % len(E.slots)]
        E.si += 1
        deps = self._deps(reads, writes)
        if slot.count > 0 and deps.get(slot, 0) < slot.count:
            deps[slot] = slot.count
        self._wait(E, deps)
        if slow:
            ins = E.eng.dma_start(out=out, in_=in_, allow_slow_non_contiguous=True)
        else:
            ins = E.eng.dma_start(out=out, in_=in_)
        ins.then_inc(slot.sem, 16)
        slot.count += 1
        for b in reads:
            b.r[slot] = slot.count
        for b in writes:
            b.w[slot] = slot.count
        self.cnt += 1

    def barrier(self):
        for E in self.E.values():
            for d in self.doms:
                if d.count > 0 and E.seen.get(d, 0) < d.count:
                    E.eng.wait_ge(d.sem, d.count * d.mult)
                    E.seen[d] = d.count

    def dbuf(self, *key):
        b = self.dbufs.get(key)
        if b is None:
            b = Buf(str(key))
            self.dbufs[key] = b
        return b

    def sb(self, stack, name, shape, dt):
        self.uid = getattr(self, "uid", 0) + 1
        t = stack.enter_context(self.nc.sbuf_tensor("%s_%d" % (name, self.uid), list(shape), dt))
        return t, Buf(name)

    def sbpool(self, stack, name, shape, dt, n):
        return [self.sb(stack, "%s%d" % (name, i), shape, dt) for i in range(n)]

    def act(self, out, in_, func, reads, writes, bias=None, scale=None, accum=None):
        kw = {}
        if bias is not None:
            kw["bias"] = bias
        if scale is not None:
            kw["scale"] = scale
        if accum is not None:
            kw["accum_out"] = accum
        self.op("scalar", lambda e: e.activation(out=out, in_=in_, func=func, **kw), reads, writes)

    def tt(self, out, in0, in1, op, reads, writes, en="vector"):
        self.op(en, lambda e: e.tensor_tensor(out=out, in0=in0, in1=in1, op=op), reads, writes)

    def ts(self, out, in0, s1, s2, op0, op1, reads, writes, en="vector"):
        if op1 is None:
            self.op(en, lambda e: e.tensor_scalar(out=out, in0=in0, scalar1=s1, scalar2=None, op0=op0), reads, writes)
        else:
            self.op(en, lambda e: e.tensor_scalar(out=out, in0=in0, scalar1=s1, scalar2=s2, op0=op0, op1=op1),
                    reads, writes)

    def stt(self, out, in0, scalar, in1, op0, op1, reads, writes):
        self.op("vector", lambda e: e.scalar_tensor_tensor(out=out, in0=in0, scalar=scalar, in1=in1, op0=op0, op1=op1),
                reads, writes)

    def cp(self, en, out, in_, reads, writes):
        if en == "scalar":
            self.op("scalar", lambda e: e.activation(out=out, in_=in_, func=AF.Copy), reads, writes)
        else:
            self.op(en, lambda e: e.tensor_copy(out=out, in_=in_), reads, writes)

    def mm(self, out, lhsT, rhs, start, stop, reads, writes):
        self.op("tensor", lambda e: e.matmul(out, lhsT=lhsT, rhs=rhs, start=start, stop=stop), reads, writes)

    def tp(self, out, in_, ident, reads, writes):
        self.op("tensor", lambda e: e.transpose(out=out, in_=in_, identity=ident), reads, writes)

    def rsqrt(self, out, in_, scale, reads, writes, eps=EPS):
        self.act(out, in_, AF.Ln, reads, writes, bias=self.epscol[0:out.shape[0], :] if eps == EPS else eps,
                 scale=scale)
        self.act(out, out, AF.Exp, list(writes), writes, scale=-0.5)

    _alt = 0

    def alt(self):
        self._alt ^= 1
        return "scalar" if self._alt else "vector"

    def declare_io(self):
        nc, S, NSEQ, L = self.nc, self.S, self.NSEQ, self.L
        T = NSEQ * S
        self.T = T
        d = lambda n, sh, dt=F32: nc.dram_tensor(n, list(sh), dt, kind="ExternalInput")
        self.x = d("x", [T, DM])
        self.p = d("p", [L, T, PLED])
        self.norm1_g = d("norm1_g", [L, DM])
        self.w_in = d("w_in", [L, DM, NIN])
        self.gate_b = d("gate_b", [L, 4, DM])
        self.conv_w = d("conv_w", [L, 4, DM])
        self.conv_b = d("conv_b", [L, DM])
        self.lru_wa = d("lru_wa", [L, 2, 16, 64, 64])
        self.lru_ba = d("lru_ba", [L, 2, DM])
        self.lru_wx = d("lru_wx", [L, 2, 16, 64, 64])
        self.lru_bx = d("lru_bx", [L, 2, DM])
        self.lru_lambda = d("lru_lambda", [L, 2, DM])
        self.diff_q_g = d("diff_q_g", [L, 64])
        self.diff_k_g = d("diff_k_g", [L, 64])
        self.diff_lam = d("diff_lam", [L, 4, 64])
        self.diff_sub_g = d("diff_sub_g", [L, 128])
        self.mla_qa_g = d("mla_qa_g", [L, 384])
        self.mla_wuq = d("mla_wuq", [L, 384, 1536])
        self.mla_kva_g = d("mla_kva_g", [L, 256])
        self.mla_wukv = d("mla_wukv", [L, 256, 2048])
        self.mla_q_g = d("mla_q_g", [L, 192])
        self.mla_k_g = d("mla_k_g", [L, 192])
        self.ret_gn_g = d("ret_gn_g", [L, 128])
        self.w_br = [d("w_br_" + c, [L, DM, DM]) for c in "abcd"]
        self.w_out = d("w_out", [L, DM, DM])
        self.norm2_g = d("norm2_g", [L, DM])
        self.w_ff1 = d("w_ff1", [L, DM, DFF])
        self.w_ff2 = d("w_ff2", [L, DFF, DM])
        self.norm3_g = d("norm3_g", [L, DM])
        self.w_ple_gate = d("w_ple_gate", [L, DM, DM])
        self.w_ple_proj = d("w_ple_proj", [L, PLED, DM])
        self.out = nc.dram_tensor("out", [T, DM], F32, kind="ExternalOutput")

        def scr(n, sh, dt):
            kind = "ExternalOutput" if n in self.debug else "Internal"
            return nc.dram_tensor(n, list(sh), dt, kind=kind)

        self.scr = scr
        self.wb = {}
        for l in range(L):
            self.wb["w_in", l] = scr("wb_in%d" % l, [DM, NIN], BF16)
            self.wb["wuq", l] = scr("wb_wuq%d" % l, [384, 1536], BF16)
            self.wb["wukv", l] = scr("wb_wukv%d" % l, [256, 2048], BF16)
            for i in range(4):
                self.wb["br%d" % i, l] = scr("wb_br%d_%d" % (i, l), [DM, DM], BF16)
            self.wb["out", l] = scr("wb_out%d" % l, [DM, DM], BF16)
            self.wb["ff1", l] = scr("wb_ff1%d" % l, [DM, DFF], BF16)
            self.wb["ff2", l] = scr("wb_ff2%d" % l, [DFF, DM], BF16)
            self.wb["pg", l] = scr("wb_pg%d" % l, [DM, DM], BF16)
            self.wb["pp", l] = scr("wb_pp%d" % l, [PLED, DM], BF16)
        self.xT = scr("xT", [DM, T], F32)
        self.QTb = scr("QTb", [1024, S], BF16)
        self.KTb = scr("KTb", [1024, S], BF16)
        self.Vb = scr("Vb", [S, 1024], BF16)
        self.QTc = scr("QTc", [8 * 192, S], BF16)
        self.KTc = scr("KTc", [8 * 192, S], BF16)
        self.Vc = scr("Vc", [S, 1024], BF16)
        self.QTd = scr("QTd", [512, S], BF16)
        self.KTd = scr("KTd", [512, S], BF16)
        self.Vd = scr("Vd", [S, 1024], BF16)
        self.GdT = scr("GdT", [1024, S], BF16)
        self.YT = scr("YT", [4 * 1024, S], BF16)
        self.GT = scr("GT", [4 * 1024, S], BF16)
        self.ropeD = [scr("rope%d" % i, [S, 2, r2], F32) for i, r2 in enumerate((8, 32, 32))]

    def build_consts(self):
        nc = self.nc
        st = self.st
        self.identf, self.identf_b = self.sb(st, "identf", [128, 128], F32)
        self.identb, self.identb_b = self.sb(st, "identb", [128, 128], BF16)
        self.onesf, self.onesf_b = self.sb(st, "onesf", [128, 128], F32)
        self.onesb, self.onesb_b = self.sb(st, "onesb", [128, 128], BF16)
        self.epscol, self.epscol_b = self.sb(st, "epscol", [128, 1], F32)
        self.onecol, self.onecol_b = self.sb(st, "onecol", [128, 1], F32)
        self.op("vector", lambda e: e.memset(self.onecol[:], 1.0), [], [self.onecol_b])
        self.CB = [self.identf_b, self.identb_b, self.onesf_b, self.onesb_b, self.epscol_b]
        with ExitStack() as s2:
            it, itb = self.sb(s2, "c_it", [128, 128], I32)
            tf, tfb = self.sb(s2, "c_tf", [128, 128], F32)
            self.op("gpsimd", lambda e: e.iota(it[:], pattern=[[1, 128]], base=0, channel_multiplier=-1), [], [itb])
            self.cp("vector", tf[:], it[:], [itb], [tfb])
            self.ts(self.identf[:], tf[:], 0.0, None, ALU.is_equal, None, [tfb], [self.identf_b])
            self.cp("vector", self.identb[:], self.identf[:], [self.identf_b], [self.identb_b])
            self.op("vector", lambda e: e.memset(self.onesf[:], 1.0), [], [self.onesf_b])
            self.op("vector", lambda e: e.memset(self.onesb[:], 1.0), [], [self.onesb_b])
            self.op("vector", lambda e: e.memset(self.epscol[:], EPS), [], [self.epscol_b])
            self.barrier()
        self.psbig = st.enter_context(nc.psum_tensor("psbig", [128, 8, 512], F32))
        self.ps = [(self.psbig[:, i, :], Buf("ps%d" % i)) for i in range(8)]
        self.psi = 0

    def psn(self, lo=0, hi=8):
        i = lo + self.psi % (hi - lo)
        self.psi += 1
        return self.ps[i]

    def precast(self):
        L = self.L
        jobs = []
        for l in range(L):
            jobs.append((self.w_in[l], self.wb["w_in", l], DM, NIN))
            jobs.append((self.mla_wuq[l], self.wb["wuq", l], 384, 1536))
            jobs.append((self.mla_wukv[l], self.wb["wukv", l], 256, 2048))
            for i in range(4):
                jobs.append((self.w_br[i][l], self.wb["br%d" % i, l], DM, DM))
            jobs.append((self.w_out[l], self.wb["out", l], DM, DM))
            jobs.append((self.w_ff1[l], self.wb["ff1", l], DM, DFF))
            jobs.append((self.w_ff2[l], self.wb["ff2", l], DFF, DM))
            jobs.append((self.w_ple_gate[l], self.wb["pg", l], DM, DM))
            jobs.append((self.w_ple_proj[l], self.wb["pp", l], PLED, DM))
        with ExitStack() as s2:
            CW = 2048
            stg = self.sbpool(s2, "pc_f", [128, CW], F32, 3)
            stb = self.sbpool(s2, "pc_b", [128, CW], BF16, 3)
            i = 0
            for src, dst, K, N in jobs:
                for r0 in range(0, K, 128):
                    for c0 in range(0, N, CW):
                        cw = min(CW, N - c0)
                        f, fb = stg[i % 3]
                        b, bb = stb[i % 3]
                        self.dma("sync", f[:, 0:cw], src[r0:r0 + 128, c0:c0 + cw], [], [fb])
                        self.cp(self.alt(), b[:, 0:cw], f[:, 0:cw], [fb], [bb])
                        self.dma("gpsimd", dst[r0:r0 + 128, c0:c0 + cw], b[:, 0:cw], [bb], [])
                        i += 1
            self.barrier()

    def build_rope(self):
        NT = self.NT
        cfgs = ((8, 16, 500000.0), (32, 64, 500000.0), (32, 64, 10000.0))
        C1 = 6.28125
        C2 = 2.0 * math.pi - C1
        with ExitStack() as s2:
            pi_, pib = self.sb(s2, "r_pi", [128, NT], I32)
            pos, posb = self.sb(s2, "r_pos", [128, NT], F32)
            self.op("gpsimd", lambda e: e.iota(pi_[:], pattern=[[128, NT]], base=0, channel_multiplier=1), [], [pib])
            self.cp("vector", pos[:], pi_[:], [pib], [posb])
            for ci, (r2, rot, theta) in enumerate(cfgs):
                with ExitStack() as s3:
                    invf, invfb = self.sb(s3, "r_invf", [128, r2], F32)
                    ang, angb = self.sb(s3, "r_ang", [128, NT, r2], F32)
                    a, ab = self.sb(s3, "r_a", [128, NT, r2], F32)
                    kf, kfb = self.sb(s3, "r_kf", [128, NT, r2], F32)
                    ki, kib = self.sb(s3, "r_ki", [128, NT, r2], I32)
                    mk, mkb = self.sb(s3, "r_mk", [128, NT, r2], F32)
                    tab, tabb = self.sb(s3, "r_tab", [128, NT, 2, r2], F32)
                    for j in range(r2):
                        v = float(np.float32(theta) ** np.float32(-(2.0 * j) / rot))
                        self.op("vector", lambda e, j=j, v=v: e.memset(invf[:, j:j + 1], v), [], [invfb])
                    pos_b = bass.AP(pos[:].tensor, pos[:].offset, [list(pos[:].ap[0]), [1, NT], [0, r2]])
                    inv_b = bass.AP(invf[:].tensor, invf[:].offset, [list(invf[:].ap[0]), [0, NT], [1, r2]])
                    self.tt(ang[:], pos_b, inv_b, ALU.mult, [posb, invfb], [angb])
                    for which, shift in ((1, 0.0), (0, math.pi / 2)):
                        self.ts(a[:], ang[:], shift, None, ALU.add, None, [angb], [ab])
                        self.ts(kf[:], a[:], 1.0 / (2 * math.pi), None, ALU.mult, None, [ab], [kfb])
                        self.cp("vector", ki[:], kf[:], [kfb], [kib])
                        self.cp("vector", kf[:], ki[:], [kib], [kfb])
                        self.stt(a[:], kf[:], -C1, a[:], ALU.mult, ALU.add, [kfb, ab], [ab])
                        self.stt(a[:], kf[:], -C2, a[:], ALU.mult, ALU.add, [kfb, ab], [ab])
                        self.ts(mk[:], a[:], math.pi, None, ALU.is_gt, None, [ab], [mkb])
                        self.stt(a[:], mk[:], -2 * math.pi, a[:], ALU.mult, ALU.add, [mkb, ab], [ab])
                        self.ts(mk[:], a[:], -math.pi, None, ALU.is_lt, None, [ab], [mkb])
                        self.stt(a[:], mk[:], 2 * math.pi, a[:], ALU.mult, ALU.add, [mkb, ab], [ab])
                        self.ts(a[:], a[:], 3.1415925, -3.1415925, ALU.min, ALU.max, [ab], [ab])
                        self.act(tab[:, :, which, :], a[:], AF.Sin, [ab], [tabb])
                    self.dma("gpsimd", self.ropeD[ci].ap().rearrange("(t p) c j -> p t c j", p=128), tab[:],
                             [tabb], [self.dbuf("rope", ci)], slow=True)
                    self.barrier()

    def transpose_in(self):
        T = self.T
        with ExitStack() as s2:
            xt = self.sbpool(s2, "ti_x", [128, DM], F32, 3)
            stg = self.sbpool(s2, "ti_s", [128, 8, 512], F32, 2)
            n = 0
            for g in range(T // 512):
                sg, sgb = stg[g % 2]
                for ti in range(4):
                    t0 = g * 512 + ti * 128
                    x_, xb = xt[n % 3]
                    n += 1
                    self.dma("sync", x_[:], self.x[t0:t0 + 128, :], [], [xb])
                    for half in range(2):
                        pt, pb = self.psn()
                        for q in range(4):
                            kc = half * 4 + q
                            self.tp(pt[:, q * 128:(q + 1) * 128], x_[:, kc * 128:(kc + 1) * 128], self.identf[:],
                                    [xb, self.identf_b], [pb])
                        self.cp(self.alt(), sg[:, half * 4:half * 4 + 4, ti * 128:(ti + 1) * 128],
                                pt[:].rearrange("p (q t) -> p q t", q=4), [pb], [sgb])
                self.dma("gpsimd", self.xT[:, g * 512:(g + 1) * 512].rearrange("(kc p) t -> p kc t", p=128), sg[:],
                         [sgb], [self.dbuf("xT", g)])
            self.barrier()

    def transpose_out(self):
        T = self.T
        with ExitStack() as s2:
            xg = self.sbpool(s2, "to_x", [128, 8, 512], F32, 2)
            ot = self.sbpool(s2, "to_o", [128, DM], F32, 3)
            n = 0
            for g in range(T // 512):
                x_, xb = xg[g % 2]
                self.dma("sync", x_[:], self.xT[:, g * 512:(g + 1) * 512].rearrange("(kc p) t -> p kc t", p=128),
                         [self.dbuf("xT", g)], [xb])
                for ti in range(4):
                    o_, ob = ot[n % 3]
                    n += 1
                    for half in range(2):
                        pt, pb = self.psn()
                        for q in range(4):
                            kc = half * 4 + q
                            self.tp(pt[:, q * 128:(q + 1) * 128], x_[:, kc, ti * 128:(ti + 1) * 128], self.identf[:],
                                    [xb, self.identf_b], [pb])
                        self.cp(self.alt(), o_[:, half * 512:(half + 1) * 512], pt[:], [pb], [ob])
                    t0 = g * 512 + ti * 128
                    self.dma("gpsimd", self.out[t0:t0 + 128, :], o_[:], [ob], [self.dbuf("out", t0)])
            self.barrier()


    @staticmethod
    def bc_last(a, n):
        return bass.AP(a.tensor, a.offset, [list(x) for x in a.ap] + [[0, n]])

    @staticmethod
    def bc_col(a, n):
        return bass.AP(a.tensor, a.offset, [list(a.ap[0]), [0, n]])

    @staticmethod
    def bc_mid(a, n):
        ap = [list(x) for x in a.ap]
        return bass.AP(a.tensor, a.offset, [ap[0], [0, n]] + ap[1:])

    def load_cols(self, dst, dstb, src_ap, pattern, **kw):
        self.dma("sync", dst, src_ap.rearrange(pattern, **kw), [], [dstb], slow=True)

    def load_bc(self, dst, dstb, src_row_ap):
        self.dma("sync", dst, src_row_ap.broadcast_to([128, src_row_ap.shape[-1]]), [], [dstb])

    def psb(self, i):
        return self.psbig[:, i, :].bitcast(BF16)

    def norm_fm(self, stack_unused, col0, ntok, gcols, gcolsb, outT, outTb, xkeep=None, loaded=False):
        with ExitStack() as s2:
            if xkeep is None:
                xg_pool = self.sbpool(s2, "nf_x", [128, 8, 512], F32, 2)
            sq_pool = self.sbpool(s2, "nf_sq", [128, 512], BF16, 3)
            rs_pool = self.sbpool(s2, "nf_rs", [128, 512], F32, 2)
            n = 0
            for g in range(ntok // 512):
                c0 = col0 + g * 512
                if xkeep is None:
                    xg, xgb = xg_pool[g % 2]
                    xv = xg
                    self.dma("sync", xg[:], self.xT[:, c0:c0 + 512].rearrange("(kc p) t -> p kc t", p=128),
                             [self.dbuf("xT", c0 // 512)], [xgb])
                    xs = lambda kc: xg[:, kc, :]
                else:
                    xk, xgb = xkeep
                    if not loaded:
                        self.dma("sync", xk[:, :, g * 512:(g + 1) * 512],
                                 self.xT[:, c0:c0 + 512].rearrange("(kc p) t -> p kc t", p=128),
                                 [self.dbuf("xT", c0 // 512)], [xgb])
                    xs = lambda kc, g=g: xk[:, kc, g * 512:(g + 1) * 512]
                pt, pb = self.psn()
                for kc in range(8):
                    sq, sqb = sq_pool[n % 3]
                    n += 1
                    self.act(sq[:], xs(kc), AF.Square, [xgb], [sqb])
                    self.mm(pt[:], self.onesb[:], sq[:], kc == 0, kc == 7, [sqb, self.onesb_b], [pb])
                rs, rsb = rs_pool[g % 2]
                self.rsqrt(rs[:], pt[:], 1.0 / DM, [pb], [rsb])
                for kc in range(8):
                    self.stt(outT[:, kc, g * 512:(g + 1) * 512], xs(kc), gcols[:, kc:kc + 1], rs[:],
                             ALU.mult, ALU.mult, [xgb, rsb, gcolsb], [outTb])

    def gemm(self, mode, AT, ATb, KC, ntok, W, col0, ncols, epi, wpool, k0=0, ps_lo=0, ps_hi=8):
        wt, wtb = wpool[self.wi % len(wpool)]
        self.wi += 1
        ATl = ATb if isinstance(ATb, list) else [ATb]
        wv = wt[:, 0:KC * ncols].rearrange("p (k n) -> p k n", k=KC)
        self.dma("sync", wv, W[:, col0:col0 + ncols].rearrange("(kc p) n -> p kc n", p=128), [], [wtb])
        if mode == "tm":
            pend = []
            nt_ = ntok // 128
            for t in range(nt_):
                pt, pb = self.psn(ps_lo, ps_hi)
                for kc in range(KC):
                    self.mm(pt[:, 0:ncols], AT[:, k0 + kc, t * 128:(t + 1) * 128], wv[:, kc, :], kc == 0, kc == KC - 1,
                            ATl + [wtb], [pb])
                r = epi(pt, pb, t)
                if r is not None:
                    pend.append(r)
                if pend and (len(pend) == KGI or t == nt_ - 1):
                    while pend:
                        for g_ in list(pend):
                            try:
                                next(g_)
                            except StopIteration:
                                pend.remove(g_)
        else:
            for m in range(ncols // 128):
                for g in range(ntok // 512):
                    pt, pb = self.psn(ps_lo, ps_hi)
                    for kc in range(KC):
                        self.mm(pt[:], wv[:, kc, m * 128:(m + 1) * 128], AT[:, k0 + kc, g * 512:(g + 1) * 512],
                                kc == 0, kc == KC - 1, ATl + [wtb], [pb])
                    epi(pt, pb, m, g)

    def norm_rope_g(self, v, vb, G, Dg, t, tmp, gain=None, gainb=None, normdim=None, ss_extra=None, rope=None):
        v3 = v.rearrange("p (g d) -> p g d", g=G)
        sq, sqb, ss, ssb, rt, rtb = tmp
        if gain is not None:
            W = G * Dg
            self.tt(sq[:, 0:W], v, v, ALU.mult, [vb], [sqb])
            yield
            self.op("vector", lambda e: e.tensor_reduce(out=ss[:, 0:G], in_=sq[:, 0:W].rearrange("p (g d) -> p g d", g=G),
                                                        axis=AX.X, op=ALU.add), [sqb], [ssb])
            yield
            if ss_extra is not None:
                ex, exb = ss_extra
                self.tt(ss[:, 0:G], ss[:, 0:G], ex, ALU.add, [ssb, exb], [ssb])
                yield
            self.rsqrt(ss[:, 0:G], ss[:, 0:G], 1.0 / normdim, [ssb], [ssb])
            yield
            self.tt(v3, v3, self.bc_last(ss[:, 0:G], Dg), ALU.mult, [vb, ssb], [vb])
            yield
            self.tt(v3, v3, self.bc_mid(gain, G), ALU.mult, [vb, gainb], [vb])
            yield
        if rope is not None:
            off, r2, tab, tabb = rope
            x1 = v3[:, :, off:off + r2]
            x2 = v3[:, :, off + r2:off + 2 * r2]
            cos = self.bc_mid(tab[:, t, 0, :], G)
            sin = self.bc_mid(tab[:, t, 1, :], G)
            n = G * r2
            tv = [rt[:, i * n:(i + 1) * n].rearrange("p (g r) -> p g r", g=G) for i in range(4)]
            self.tt(tv[0], x1, cos, ALU.mult, [vb, tabb], [rtb])
            yield
            self.tt(tv[1], x2, sin, ALU.mult, [vb, tabb], [rtb])
            yield
            self.tt(tv[2], x2, cos, ALU.mult, [vb, tabb], [rtb])
            yield
            self.tt(tv[3], x1, sin, ALU.mult, [vb, tabb], [rtb])
            yield
            self.tt(x1, tv[0], tv[1], ALU.subtract, [rtb], [vb])
            yield
            self.tt(x2, tv[2], tv[3], ALU.add, [rtb], [vb])
            yield

    def norm_rope(self, *a, **kw):
        for _ in self.norm_rope_g(*a, **kw):
            pass

    def tr_stage(self, src, srcb, blocks, tq, stage, stageb):
        i = self.psi % 8
        self.psi += 1
        pb = self.ps[i][1]
        pv = self.psb(i)
        for bi, (lo, w) in enumerate(blocks):
            self.tp(pv[0:w, bi * 128:(bi + 1) * 128], src[:, lo:lo + w], self.identb[:], [srcb, self.identb_b], [pb])
        nb = len(blocks)
        if all(w == 128 for _, w in blocks):
            self.cp(self.alt(), stage[:, 0:nb, tq * 128:(tq + 1) * 128],
                    pv[:, 0:nb * 128].rearrange("p (b t) -> p b t", b=nb), [pb], [stageb])
        else:
            for bi, (lo, w) in enumerate(blocks):
                self.cp(self.alt(), stage[0:w, bi, tq * 128:(tq + 1) * 128], pv[0:w, bi * 128:(bi + 1) * 128],
                        [pb], [stageb])

    def phase1(self, l, s):
        S, NT, NG = self.S, self.NT, self.NG
        tok0 = s * S
        Wd = self.wb["w_in", l]
        with ExitStack() as P:
            xnT, xnTb = self.sb(P, "xnT", [128, 8, S], BF16)
            g1, g1b = self.sb(P, "g1", [128, 8], F32)
            self.load_cols(g1[:], g1b, self.norm1_g[l], "(kc p) -> p kc", p=128)
            self.norm_fm(None, tok0, S, g1, g1b, xnT, xnTb)
            self.barrier()
            self.mark(" P1lru")
            wpool = self.sbpool(P, "w", [128, 4096], BF16, 2)
            o16 = self.sbpool(P, "o16", [128, 512], BF16, 3)
            self.oi = 0
            with ExitStack() as PA:
                self.lru(PA, l, s, xnT, xnTb, Wd, wpool)
                self.barrier()
                self.mark(" P1qkv")
            with ExitStack() as PQ:
                self.prep_qkv(PQ, l, s, xnT, xnTb, Wd, wpool, o16)
                self.barrier()

    def lru(self, PA, l, s, xnT, xnTb, Wd, wpool):
        S, NT, NG = self.S, self.NT, self.NG
        TC = min(1024, S)
        nch = S // TC
        cw, cwb = self.sb(PA, "cw", [128, 4, 8], F32)
        cb, cbb = self.sb(PA, "cb", [128, 8], F32)
        ba, bab = self.sb(PA, "ba", [128, 2, 8], F32)
        bx, bxb = self.sb(PA, "bx", [128, 2, 8], F32)
        lam, lamb = self.sb(PA, "lam", [128, 16], F32)
        hh, hhb = self.sb(PA, "hh", [128, 16], F32)
        cd, cdb = self.sb(PA, "cd", [128, 16], F32)
        cd2, cd2b = self.sb(PA, "cd2", [128, 16], F32)
        for t_ in range(4):
            self.load_cols(cw[:, t_, :], cwb, self.conv_w[l, t_], "(c p) -> p c", p=128)
        self.load_cols(cb[:], cbb, self.conv_b[l], "(c p) -> p c", p=128)
        for d_ in range(2):
            self.load_cols(ba[:, d_, :], bab, self.lru_ba[l, d_], "(c p) -> p c", p=128)
            self.load_cols(bx[:, d_, :], bxb, self.lru_bx[l, d_], "(c p) -> p c", p=128)
            self.load_cols(lam[:, d_ * 8:(d_ + 1) * 8], lamb, self.lru_lambda[l, d_], "(c p) -> p c", p=128)
        self.act(lam[:], lam[:], AF.Exp, [lamb], [lamb], scale=-1.0)
        self.ts(hh[:], lam[:], -1.0 / 6, 1.0 / 5, ALU.mult, ALU.add, [lamb], [hhb])
        for cst in (-1.0 / 4, 1.0 / 3, -1.0 / 2, 1.0):
            self.tt(hh[:], hh[:], lam[:], ALU.mult, [hhb, lamb], [hhb])
            self.ts(hh[:], hh[:], cst, None, ALU.add, None, [hhb], [hhb])
        self.tt(hh[:], hh[:], lam[:], ALU.mult, [hhb, lamb], [hhb])
        self.ts(cd[:], hh[:], -8.0, None, ALU.mult, None, [hhb], [cdb])
        self.ts(cd2[:], hh[:], -16.0, None, ALU.mult, None, [hhb], [cd2b])
        wst, wstb = self.sb(PA, "wst", [128, 4, 128], F32)
        wbf, wbfb = self.sb(PA, "wbf", [128, 4, 128], BF16)
        self.op("vector", lambda e: e.memset(wst[:], 0.0), [], [wstb])
        Pb, Pbb = self.sb(PA, "Pb", [128, S + 4], F32)
        gg, ggb = self.sb(PA, "gg", [128, S], BF16)
        xc, xcb_ = self.sb(PA, "xc", [128, S], F32)
        x16, x16b = self.sb(PA, "x16", [128, S], BF16)
        Bsets = []
        for d_ in range(2):
            B1, B1b = self.sb(PA, "B1%d" % d_, [128, TC], F32)
            B2, B2b = self.sb(PA, "B2%d" % d_, [128, TC], F32)
            B3, B3b = self.sb(PA, "B3%d" % d_, [128, TC], F32)
            Bsets.append((B1, B1b, B2, B2b, B3, B3b))
        hb, hbb = self.sb(PA, "hb", [128, S], F32)
        tA = self.sbpool(PA, "tA", [128, 512], F32, 2)
        self.op("vector", lambda e: e.memset(Pb[:, 0:2], 0.0), [], [Pbb])
        self.op("vector", lambda e: e.memset(Pb[:, S + 2:S + 4], 0.0), [], [Pbb])
        wsrc = (self.lru_wa, self.lru_wx)
        for c in range(8):
            for wi in range(2):
                for d in range(2):
                    for half in range(2):
                        self.dma("sync", wst[half * 64:(half + 1) * 64, wi * 2 + d, half * 64:(half + 1) * 64],
                                 wsrc[wi][l, d, 2 * c + half], [], [wstb])
            self.cp("vector", wbf[:], wst[:], [wstb], [wbfb])

            def epi_x(pt, pb, m, g):
                self.cp(self.alt(), Pb[:, 2 + g * 512:2 + (g + 1) * 512], pt[:], [pb], [Pbb])

            def epi_g(pt, pb, m, g):
                t1, t1b = tA[g % 2]
                self.act(t1[:], pt[:], AF.Square, [pb], [t1b])
                self.ts(t1[:], t1[:], 0.044715, 1.0, ALU.mult, ALU.add, [t1b], [t1b])
                self.tt(t1[:], t1[:], pt[:], ALU.mult, [t1b, pb], [t1b])
                self.act(t1[:], t1[:], AF.Sigmoid, [t1b], [t1b], scale=1.5957691216057308)
                self.tt(gg[:, g * 512:(g + 1) * 512], t1[:], pt[:], ALU.mult, [t1b, pb], [ggb])

            self.gemm("fm", xnT, xnTb, 8, S, Wd, OFF["ax"] + c * 128, 128, epi_x, wpool)
            self.gemm("fm", xnT, xnTb, 8, S, Wd, OFF["ag"] + c * 128, 128, epi_g, wpool)
            self.ts(xc[:], Pb[:, 0:S], cw[:, 0, c:c + 1], cb[:, c:c + 1], ALU.mult, ALU.add, [Pbb, cwb, cbb], [xcb_])
            for j in range(1, 4):
                self.stt(xc[:], Pb[:, j:j + S], cw[:, j, c:c + 1], xc[:], ALU.mult, ALU.add, [Pbb, cwb, xcb_], [xcb_])
            self.cp("scalar", x16[:], xc[:], [xcb_], [x16b])
            hs = Pb

            def dgen(d, c=c):
                D1, D1b, D2, D2b, D3, D3b = Bsets[d]
                order = range(nch) if d == 0 else range(nch - 1, -1, -1)
                first = True
                for ch in order:
                    t0 = ch * TC
                    for sub in range(TC // 512):
                        cs = slice(t0 + sub * 512, t0 + (sub + 1) * 512)
                        bs = slice(sub * 512, (sub + 1) * 512)
                        pt, pb = self.psn()
                        self.mm(pt[:], wbf[:, 0 + d, :], x16[:, cs], True, True, [wbfb, x16b], [pb])
                        self.act(D1[:, bs], pt[:], AF.Sigmoid, [pb, bab], [D1b], bias=ba[:, d, c:c + 1])
                        yield
                        pt, pb = self.psn()
                        self.mm(pt[:], wbf[:, 2 + d, :], x16[:, cs], True, True, [wbfb, x16b], [pb])
                        self.act(D3[:, bs], pt[:], AF.Sigmoid, [pb, bxb], [D3b], bias=bx[:, d, c:c + 1])
                        yield
                    k = d * 8 + c
                    self.act(D2[:], D1[:], AF.Exp, [D1b, cd2b], [D2b], scale=cd2[:, k:k + 1])
                    yield
                    self.act(D2[:], D2[:], AF.Relu, [D2b], [D2b], scale=-1.0, bias=self.onecol[:, :])
                    yield
                    self.act(D2[:], D2[:], AF.Sqrt, [D2b], [D2b])
                    yield
                    self.act(D1[:], D1[:], AF.Exp, [D1b, cdb], [D1b], scale=cd[:, k:k + 1])
                    yield
                    self.tt(D3[:], D3[:], xc[:, t0:t0 + TC], ALU.mult, [D3b, xcb_], [D3b])
                    yield
                    self.tt(D3[:], D3[:], D2[:], ALU.mult, [D3b, D2b], [D3b])
                    yield
                    if d == 0:
                        init = 0.0 if first else hs[:, 2 + t0 - 1:2 + t0]
                        self.op("vector", lambda e, t0=t0, init=init: e.tensor_tensor_scan(
                            out=hs[:, 2 + t0:2 + t0 + TC], data0=D1[:], data1=D3[:], initial=init,
                            op0=ALU.mult, op1=ALU.add), [D1b, D3b, Pbb], [Pbb])
                    else:
                        init = 0.0 if first else hb[:, t0 + TC:t0 + TC + 1]
                        ov = hb[:, t0:t0 + TC]
                        orev = bass.AP(ov.tensor, ov.offset + (TC - 1), [list(ov.ap[0]), [-1, TC]])
                        self.op("vector", lambda e, init=init, orev=orev: e.tensor_tensor_scan(
                            out=orev, data0=D1[:, ::-1], data1=D3[:, ::-1], initial=init,
                            op0=ALU.mult, op1=ALU.add), [D1b, D3b, hbb], [hbb])
                    yield
                    first = False

            gens = [dgen(0), dgen(1)]
            while gens:
                for g_ in list(gens):
                    try:
                        next(g_)
                    except StopIteration:
                        gens.remove(g_)
            self.tt(hs[:, 2:S + 2], hs[:, 2:S + 2], hb[:], ALU.add, [Pbb, hbb], [Pbb])
            self.tt(x16[:], hs[:, 2:S + 2], gg[:], ALU.mult, [Pbb, ggb, x16b], [x16b])
            self.dma("gpsimd", self.YT[c * 128:(c + 1) * 128, :], x16[:], [x16b], [self.dbuf("YT", 0, c)])


    def prep_qkv(self, PQ, l, s, xnT, xnTb, Wd, wpool, o16):
        S, NT, NG = self.S, self.NT, self.NG
        def bct(name, src, n):
            t, b = self.sb(PQ, name, [128, n], F32)
            self.load_bc(t[:], b, src)
            return t, b
        qg_b, qg_bb = bct("dqg", self.diff_q_g[l:l + 1, :], 64)
        kg_b, kg_bb = bct("dkg", self.diff_k_g[l:l + 1, :], 64)
        qa_g, qa_gb = bct("mqa", self.mla_qa_g[l:l + 1, :], 384)
        kva_g, kva_gb = bct("mkva", self.mla_kva_g[l:l + 1, :], 256)
        mq_g, mq_gb = bct("mqg", self.mla_q_g[l:l + 1, :], 192)
        mk_g, mk_gb = bct("mkg", self.mla_k_g[l:l + 1, :], 192)
        gb, gbb = self.sb(PQ, "gateb", [128, 4, 8], F32)
        for b_ in range(4):
            self.load_cols(gb[:, b_, :], gbb, self.gate_b[l, b_], "(m p) -> p m", p=128)
        ropes = []
        for ci, r2 in enumerate((8, 32, 32)):
            t, b = self.sb(PQ, "rope%d" % ci, [128, NT, 2, r2], F32)
            self.dma("sync", t[:], self.ropeD[ci].ap().rearrange("(t p) c j -> p t c j", p=128),
                     [self.dbuf("rope", ci)], [b], slow=True)
            ropes.append((t, b))
        cqnT, cqnTb = self.sb(PQ, "cqnT", [128, 3, S], BF16)
        ckvnT, ckvnTb = self.sb(PQ, "ckvnT", [128, 2, S], BF16)
        kper, kperb = self.sb(PQ, "kper", [128, NT, 64], F32)
        sspe, sspeb = self.sb(PQ, "sspe", [128, NT], F32)
        NV = KDEFER + 3
        vpool = self.sbpool(PQ, "v", [128, 512], F32, NV)
        v16pool = self.sbpool(PQ, "v16", [128, 512], BF16, NV)
        dq = []

        def defer(fn):
            dq.append(fn)
            while len(dq) > KDEFER:
                dq.pop(0)()

        def flush():
            while dq:
                dq.pop(0)()
        tmps = []
        for i_ in range(2):
            sq_, sqb_ = self.sb(PQ, "sq%d" % i_, [128, 512], F32)
            ss_, ssb_ = self.sb(PQ, "ss%d" % i_, [128, 8], F32)
            rt_, rtb_ = self.sb(PQ, "rt%d" % i_, [128, 1024], F32)
            tmps.append((sq_, sqb_, ss_, ssb_, rt_, rtb_))
        tmp = tmps[0]
        sq, sqb, ss, ssb, rt, rtb = tmp
        stages = self.sbpool(PQ, "stg", [128, 4, 512], BF16, 2)
        st_i = [0]
        vi = [0]

        def nextv():
            r = vpool[vi[0] % NV] + v16pool[vi[0] % NV]
            vi[0] += 1
            return r

        def store_stage(stage, stageb, dests, g):
            for bi, (dt_, r0, nr) in enumerate(dests):
                self.dma("gpsimd", dt_[r0:r0 + nr, g * 512:(g + 1) * 512], stage[0:nr, bi, :], [stageb],
                         [self.dbuf(dt_.name, r0, g)])

        def seg_qk(AT, ATb, KC, W, col0, ncols, G, Dg, gain, gainb, normdim, rope, blocks, dests_fn, scale=None):
            state = {}

            def epi(pt, pb, t):
                v, vb, v16, v16b = nextv()
                self.cp("scalar", v[:, 0:ncols], pt[:, 0:ncols], [pb], [vb])
                yield
                for _ in self.norm_rope_g(v[:, 0:ncols], vb, G, Dg, t, tmps[t % 2], gain=gain, gainb=gainb,
                                          normdim=normdim, rope=rope):
                    yield
                if scale is None:
                    self.cp("scalar", v16[:, 0:ncols], v[:, 0:ncols], [vb], [v16b])
                else:
                    self.act(v16[:, 0:ncols], v[:, 0:ncols], AF.Copy, [vb], [v16b], scale=scale)
                def later(t=t, v16=v16, v16b=v16b):
                    if t % 4 == 0:
                        state["st"] = stages[st_i[0] % 2]
                        st_i[0] += 1
                    stage, stageb = state["st"]
                    self.tr_stage(v16, v16b, blocks, t % 4, stage, stageb)
                    if t % 4 == 3:
                        store_stage(stage, stageb, dests_fn(), t // 4)
                defer(later)
            self.gemm("tm", AT, ATb, KC, S, W, col0, ncols, epi, wpool)

        def seg_v(col0, dst, dcol0):
            def epi(pt, pb, t):
                o, ob = o16[self.oi % 3]
                self.oi += 1
                self.cp(self.alt(), o[:], pt[:], [pb], [ob])
                self.dma("gpsimd", dst[t * 128:(t + 1) * 128, dcol0:dcol0 + 512], o[:], [ob],
                         [self.dbuf(dst.name, t, dcol0)])
            self.gemm("tm", xnT, xnTb, 8, S, Wd, col0, 512, epi, wpool)

        b4 = [(i * 128, 128) for i in range(4)]
        for nt in range(2):
            seg_qk(xnT, xnTb, 8, Wd, OFF["bq"] + nt * 512, 512, 8, 64, qg_b[:], qg_bb, 64,
                   (0, 8, ropes[0][0], ropes[0][1]), b4,
                   lambda nt=nt: [(self.QTb, nt * 512 + i * 128, 128) for i in range(4)])
            seg_qk(xnT, xnTb, 8, Wd, OFF["bk"] + nt * 512, 512, 8, 64, kg_b[:], kg_bb, 64,
                   (0, 8, ropes[0][0], ropes[0][1]), b4,
                   lambda nt=nt: [(self.KTb, nt * 512 + i * 128, 128) for i in range(4)])
            seg_v(OFF["bv"] + nt * 512, self.Vb, nt * 512)
            seg_v(OFF["dv"] + nt * 512, self.Vd, nt * 512)
        self.mark("  q:dqdk")
        seg_qk(xnT, xnTb, 8, Wd, OFF["dq"], 512, 8, 64, None, None, None, (0, 32, ropes[2][0], ropes[2][1]), b4,
               lambda: [(self.QTd, i * 128, 128) for i in range(4)])
        seg_qk(xnT, xnTb, 8, Wd, OFF["dk"], 512, 8, 64, None, None, None, (0, 32, ropes[2][0], ropes[2][1]), b4,
               lambda: [(self.KTd, i * 128, 128) for i in range(4)], scale=0.125)

        self.mark("  q:cq")
        def epi_cq(pt, pb, t):
            v, vb, v16, v16b = nextv()
            self.cp("scalar", v[:, 0:384], pt[:, 0:384], [pb], [vb])
            yield
            for _ in self.norm_rope_g(v[:, 0:384], vb, 1, 384, t, tmps[t % 2], gain=qa_g[:], gainb=qa_gb, normdim=384):
                yield
            self.cp("scalar", v16[:, 0:384], v[:, 0:384], [vb], [v16b])

            def later(t=t, v16=v16, v16b=v16b):
                i = self.psi % 8
                self.psi += 1
                pv = self.psb(i)
                for bi in range(3):
                    self.tp(pv[:, bi * 128:(bi + 1) * 128], v16[:, bi * 128:(bi + 1) * 128], self.identb[:],
                            [v16b, self.identb_b], [self.ps[i][1]])
                self.cp(self.alt(), cqnT[:, :, t * 128:(t + 1) * 128], pv[:, 0:384].rearrange("p (b t) -> p b t", b=3),
                        [self.ps[i][1]], [cqnTb])
            defer(later)
        self.gemm("tm", xnT, xnTb, 8, S, Wd, OFF["cq"], 384, epi_cq, wpool)

        def epi_ckv(pt, pb, t):
            v, vb, v16, v16b = nextv()
            sq, sqb, ss, ssb, rt, rtb = tmps[t % 2]
            self.cp("scalar", v[:, 0:320], pt[:, 0:320], [pb], [vb])
            yield
            self.tt(sq[:, 0:64], v[:, 256:320], v[:, 256:320], ALU.mult, [vb], [sqb])
            yield
            self.op("vector", lambda e: e.tensor_reduce(out=sspe[:, t:t + 1], in_=sq[:, 0:64], axis=AX.X, op=ALU.add),
                    [sqb], [sspeb])
            yield
            self.tt(v[:, 256:320], v[:, 256:320], mk_g[:, 128:192], ALU.mult, [vb, mk_gb], [vb])
            yield
            for _ in self.norm_rope_g(v[:, 256:320], vb, 1, 64, t, tmps[t % 2], rope=(0, 32, ropes[1][0], ropes[1][1])):
                yield
            self.cp("vector", kper[:, t, :], v[:, 256:320], [vb], [kperb])
            yield
            for _ in self.norm_rope_g(v[:, 0:256], vb, 1, 256, t, tmps[t % 2], gain=kva_g[:], gainb=kva_gb, normdim=256):
                yield
            self.cp("scalar", v16[:, 0:256], v[:, 0:256], [vb], [v16b])

            def later(t=t, v16=v16, v16b=v16b):
                i = self.psi % 8
                self.psi += 1
                pv = self.psb(i)
                for bi in range(2):
                    self.tp(pv[:, bi * 128:(bi + 1) * 128], v16[:, bi * 128:(bi + 1) * 128], self.identb[:],
                            [v16b, self.identb_b], [self.ps[i][1]])
                self.cp(self.alt(), ckvnT[:, :, t * 128:(t + 1) * 128], pv[:, 0:256].rearrange("p (b t) -> p b t", b=2),
                        [self.ps[i][1]], [ckvnTb])
            defer(later)
        self.gemm("tm", xnT, xnTb, 8, S, Wd, OFF["ckv"], 320, epi_ckv, wpool)

        self.mark("  q:fm")
        def seg_fm(col0, func, bias_fn, dst, row0):
            def epi(pt, pb, m, g):
                o, ob = o16[self.oi % 3]
                self.oi += 1
                b = bias_fn(m)
                if b is None:
                    self.act(o[:], pt[:], func, [pb], [ob])
                else:
                    self.act(o[:], pt[:], func, [pb, gbb], [ob], bias=b)
                r0 = row0 + m * 128
                self.dma("gpsimd", dst[r0:r0 + 128, g * 512:(g + 1) * 512], o[:], [ob], [self.dbuf(dst.name, r0, g)])
            self.gemm("fm", xnT, xnTb, 8, S, Wd, col0, 512, epi, wpool)
        for nt in range(2):
            seg_fm(OFF["dg"] + nt * 512, AF.Silu, lambda m: None, self.GdT, nt * 512)
        for b_ in range(4):
            for nt in range(2):
                seg_fm(OFF["gl"] + b_ * 1024 + nt * 512, AF.Sigmoid,
                       lambda m, b_=b_, nt=nt: gb[:, b_, nt * 4 + m:nt * 4 + m + 1], self.GT, b_ * 1024 + nt * 512)

        self.mark("  q:qup")
        flush()
        Wq = self.wb["wuq", l]
        Wkv = self.wb["wukv", l]
        bq = [(0, 128), (128, 64), (192, 128), (320, 64)]
        for hp in range(4):
            seg_qk(cqnT, cqnTb, 3, Wq, hp * 384, 384, 2, 192, mq_g[:], mq_gb, 192,
                   (128, 32, ropes[1][0], ropes[1][1]), bq,
                   lambda hp=hp: [(self.QTc, (2 * hp) * 192, 128), (self.QTc, (2 * hp) * 192 + 128, 64),
                                  (self.QTc, (2 * hp + 1) * 192, 128), (self.QTc, (2 * hp + 1) * 192 + 128, 64)])
        self.mark("  q:kvup")
        for hp in range(4):
            state = {}

            def epi_kv(pt, pb, t, hp=hp, state=state):
                v, vb, v16, v16b = nextv()
                sq, sqb, ss, ssb, rt, rtb = tmps[t % 2]
                self.cp("scalar", v[:], pt[:], [pb], [vb])
                yield
                v4 = v[:].rearrange("p (h c d) -> p h c d", h=2, c=2)
                kn = v4[:, :, 0, :]
                s4 = sq[:].rearrange("p (h c d) -> p h c d", h=2, c=2)
                self.tt(s4[:, :, 0, :], kn, kn, ALU.mult, [vb], [sqb])
                yield
                self.op("vector", lambda e: e.tensor_reduce(out=ss[:, 0:2], in_=s4[:, :, 0, :], axis=AX.X, op=ALU.add),
                        [sqb], [ssb])
                yield
                self.tt(ss[:, 0:2], ss[:, 0:2], self.bc_col(sspe[:, t:t + 1], 2),
                        ALU.add, [ssb, sspeb], [ssb])
                yield
                self.rsqrt(ss[:, 0:2], ss[:, 0:2], 1.0 / 192, [ssb], [ssb])
                yield
                self.tt(kn, kn, self.bc_last(ss[:, 0:2], 128), ALU.mult, [vb, ssb], [vb])
                yield
                self.tt(kn, kn, self.bc_mid(mk_g[:, 0:128], 2), ALU.mult, [vb, mk_gb], [vb])
                yield
                v16v = v16[:, 0:256].rearrange("p (h d) -> p h d", h=2)
                self.cp("scalar", v16v, kn, [vb], [v16b])
                yield
                pe = v16[:, 256:384].rearrange("p (h d) -> p h d", h=2)
                self.tt(pe, self.bc_mid(kper[:, t, :], 2), self.bc_last(ss[:, 0:2], 64), ALU.mult,
                        [kperb, ssb], [v16b])
                yield
                o, ob = o16[self.oi % 3]
                self.oi += 1
                ov = o[:, 0:256].rearrange("p (h d) -> p h d", h=2)
                self.cp("vector", ov, v4[:, :, 1, :], [vb], [ob])
                self.dma("gpsimd", self.Vc[t * 128:(t + 1) * 128, hp * 256:(hp + 1) * 256], o[:, 0:256], [ob],
                         [self.dbuf("Vc", t, hp)])
                def later(t=t, v16=v16, v16b=v16b):
                    if t % 4 == 0:
                        state["st"] = stages[st_i[0] % 2]
                        st_i[0] += 1
                    stage, stageb = state["st"]
                    self.tr_stage(v16, v16b, [(0, 128), (128, 128), (256, 64), (320, 64)], t % 4, stage, stageb)
                    if t % 4 == 3:
                        store_stage(stage, stageb, [(self.KTc, (2 * hp) * 192, 128), (self.KTc, (2 * hp + 1) * 192, 128),
                                                    (self.KTc, (2 * hp) * 192 + 128, 64),
                                                    (self.KTc, (2 * hp + 1) * 192 + 128, 64)], t // 4)
                defer(later)
            self.gemm("tm", ckvnT, ckvnTb, 2, S, Wkv, hp * 512, 512, epi_kv, wpool)
        flush()


    def load_rows(self, dst, dstb, src, r0, nr):
        self.dma("sync", dst[0:nr, :], src[r0:r0 + nr, :], [], [dstb])

    def load_v(self, dst, dstb, src, c0):
        self.dma("sync", dst[:], src[:, c0:c0 + 128].rearrange("(t p) e -> p t e", p=128), [], [dstb])

    def phaseB(self, l, s):
        S, NT, NG = self.S, self.NT, self.NG
        lam_init = 0.8 - 0.6 * math.exp(-0.3 * l)
        with ExitStack() as P:
            lp, lpb = self.sb(P, "lp", [128, 256], F32)
            self.load_bc(lp[:], lpb, self.diff_lam[l:l + 1].rearrange("a b c -> a (b c)"))
            pr, prb = self.sb(P, "pr", [128, 128], F32)
            e2, e2b = self.sb(P, "e2", [128, 2], F32)
            neglam, neglamb = self.sb(P, "neglam", [128, 1], F32)
            gcol, gcolb = self.sb(P, "gcol", [128, 1], F32)
            lp4 = lp[:].rearrange("p (a b c) -> p a b c", a=2, b=2)
            self.tt(pr[:].rearrange("p (a c) -> p a c", a=2), lp4[:, :, 0, :], lp4[:, :, 1, :], ALU.mult, [lpb], [prb])
            self.op("vector", lambda e: e.tensor_reduce(out=e2[:], in_=pr[:].rearrange("p (a c) -> p a c", a=2),
                                                        axis=AX.X, op=ALU.add), [prb], [e2b])
            self.act(e2[:], e2[:], AF.Exp, [e2b], [e2b])
            self.tt(neglam[:], e2[:, 1:2], e2[:, 0:1], ALU.subtract, [e2b], [neglamb])
            self.ts(neglam[:], neglam[:], -lam_init, None, ALU.add, None, [neglamb], [neglamb])
            self.load_cols(gcol[:], gcolb, self.diff_sub_g[l], "(p o) -> p o", o=1)
            self.ts(gcol[:], gcol[:], 1.0 - lam_init, None, ALU.mult, None, [gcolb], [gcolb])
            QT = self.sbpool(P, "QT", [128, S], BF16, 2)
            KT = self.sbpool(P, "KT", [128, S], BF16, 2)
            V = self.sbpool(P, "V", [128, NT, 128], BF16, 2)
            Pt = self.sbpool(P, "Pt", [128, 512], BF16, 4)
            tf = self.sbpool(P, "tf", [128, 512], F32, 6)
            y16 = self.sbpool(P, "y16", [128, 512], BF16, 2)
            pi = 0
            for h in range(8):
                q, qb = QT[h % 2]
                k, kb = KT[h % 2]
                v, vb = V[h % 2]
                self.load_rows(q, qb, self.QTb, h * 128, 128)
                self.load_rows(k, kb, self.KTb, h * 128, 128)
                self.load_v(v, vb, self.Vb, h * 128)
                for qg in range(NG):
                    qs = slice(qg * 512, (qg + 1) * 512)
                    O = (self.ps[0], self.ps[1])
                    Lp = (self.ps[2], self.ps[3])
                    for kt in range(NT):
                        ks = slice(kt * 128, (kt + 1) * 128)
                        for si in range(2):
                            lo, hi = si * 64, si * 64 + 64
                            sp, spb = self.psn(4, 8)
                            self.mm(sp[:], k[lo:hi, ks], q[lo:hi, qs], True, True, [kb, qb], [spb])
                            p_, p_b = Pt[pi % 4]
                            pi += 1
                            self.act(p_[:], sp[:], AF.Exp, [spb], [p_b], scale=0.125)
                            self.mm(O[si][0][:], v[:, kt, :], p_[:], kt == 0, kt == NT - 1, [vb, p_b], [O[si][1]])
                            self.mm(Lp[si][0][:], self.onesb[:], p_[:], kt == 0, kt == NT - 1, [self.onesb_b, p_b],
                                    [Lp[si][1]])
                    o1, o1b = tf[0]
                    o2, o2b = tf[1]
                    r1, r1b = tf[2]
                    r2, r2b = tf[3]
                    oo, oob = tf[4]
                    rs, rsb = tf[5]
                    self.cp("scalar", o1[:], O[0][0][:], [O[0][1]], [o1b])
                    self.cp("vector", o2[:], O[1][0][:], [O[1][1]], [o2b])
                    self.act(r1[:], Lp[0][0][:], AF.Ln, [Lp[0][1]], [r1b])
                    self.act(r2[:], Lp[1][0][:], AF.Ln, [Lp[1][1]], [r2b])
                    self.act(r1[:], r1[:], AF.Exp, [r1b], [r1b], scale=-1.0)
                    self.act(r2[:], r2[:], AF.Exp, [r2b], [r2b], scale=-1.0)
                    self.tt(o1[:], o1[:], r1[:], ALU.mult, [o1b, r1b], [o1b])
                    self.tt(o2[:], o2[:], r2[:], ALU.mult, [o2b, r2b], [o2b])
                    self.stt(oo[:], o2[:], neglam[:, 0:1], o1[:], ALU.mult, ALU.add, [o2b, o1b, neglamb], [oob])
                    sq, sqb = Pt[pi % 4]
                    pi += 1
                    self.act(sq[:], oo[:], AF.Square, [oob], [sqb])
                    sp, spb = self.psn(4, 8)
                    self.mm(sp[:], self.onesb[:], sq[:], True, True, [self.onesb_b, sqb], [spb])
                    self.rsqrt(rs[:], sp[:], 1.0 / 128, [spb], [rsb])
                    y, yb = y16[(h * NG + qg) % 2]
                    self.stt(y[:], oo[:], gcol[:, 0:1], rs[:], ALU.mult, ALU.mult, [oob, rsb, gcolb], [yb])
                    self.dma("gpsimd", self.YT[1024 + h * 128:1024 + (h + 1) * 128, qs], y[:], [yb],
                             [self.dbuf("YT", 1, h, qg)])
            self.barrier()

    def phaseC(self, l, s):
        S, NT, NG = self.S, self.NT, self.NG
        sc = 192.0 ** -0.5
        with ExitStack() as P:
            Qn = self.sbpool(P, "Qn", [128, S], BF16, 2)
            Qp = self.sbpool(P, "Qp", [64, S], BF16, 2)
            Kn = self.sbpool(P, "Kn", [128, S], BF16, 2)
            Kp = self.sbpool(P, "Kp", [64, S], BF16, 2)
            V = self.sbpool(P, "V", [128, NT, 128], BF16, 2)
            Pt = self.sbpool(P, "Pt", [128, 512], BF16, 4)
            tf = self.sbpool(P, "tf", [128, 512], F32, 4)
            y16 = self.sbpool(P, "y16", [128, 512], BF16, 2)
            pi = 0
            ti = 0
            for h in range(8):
                qn, qnb = Qn[h % 2]
                qp, qpb = Qp[h % 2]
                kn, knb = Kn[h % 2]
                kp, kpb = Kp[h % 2]
                v, vb = V[h % 2]
                self.load_rows(qn, qnb, self.QTc, h * 192, 128)
                self.load_rows(qp, qpb, self.QTc, h * 192 + 128, 64)
                self.load_rows(kn, knb, self.KTc, h * 192, 128)
                self.load_rows(kp, kpb, self.KTc, h * 192 + 128, 64)
                self.load_v(v, vb, self.Vc, h * 128)
                for qg in range(NG):
                    qs = slice(qg * 512, (qg + 1) * 512)
                    O, Ob = self.ps[qg % 2 * 2]
                    Lt, Lb = self.ps[qg % 2 * 2 + 1]
                    for kt in range(NT):
                        ks = slice(kt * 128, (kt + 1) * 128)
                        sp, spb = self.psn(4, 8)
                        self.mm(sp[:], kn[:, ks], qn[:, qs], True, False, [knb, qnb], [spb])
                        self.mm(sp[:], kp[:, ks], qp[:, qs], False, True, [kpb, qpb], [spb])
                        p_, p_b = Pt[pi % 4]
                        pi += 1
                        self.act(p_[:], sp[:], AF.Exp, [spb], [p_b], scale=sc)
                        self.mm(O[:], v[:, kt, :], p_[:], kt == 0, kt == NT - 1, [vb, p_b], [Ob])
                        self.mm(Lt[:], self.onesb[:], p_[:], kt == 0, kt == NT - 1, [self.onesb_b, p_b], [Lb])
                    o1, o1b = tf[ti % 4]
                    r1, r1b = tf[(ti + 1) % 4]
                    ti += 2
                    self.cp("vector", o1[:], O[:], [Ob], [o1b])
                    self.act(r1[:], Lt[:], AF.Ln, [Lb], [r1b])
                    self.act(r1[:], r1[:], AF.Exp, [r1b], [r1b], scale=-1.0)
                    y, yb = y16[(h * NG + qg) % 2]
                    self.tt(y[:], o1[:], r1[:], ALU.mult, [o1b, r1b], [yb])
                    self.dma("gpsimd", self.YT[2048 + h * 128:2048 + (h + 1) * 128, qs], y[:], [yb],
                             [self.dbuf("YT", 2, h, qg)])
            self.barrier()

    def phaseD(self, l, s):
        S, NT, NG = self.S, self.NT, self.NG
        mmax = max(4 * (NG - 1), 1)
        nneg = max(NT - 4, 1)
        with ExitStack() as P:
            gncol, gncolb = self.sb(P, "gncol", [128, 1], F32)
            self.load_cols(gncol[:], gncolb, self.ret_gn_g[l], "(p o) -> p o", o=1)
            ii, iib = self.sb(P, "ii", [128, 512], I32)
            J, Jb = self.sb(P, "J", [128, 512], F32)
            Jr, Jrb = self.sb(P, "Jr", [128, 512], F32)
            A, Ab = self.sb(P, "A", [128, 4, 512], F32)
            pbase, pbaseb = self.sb(P, "pbase", [128, mmax], F32)
            nbase, nbaseb = self.sb(P, "nbase", [128, nneg], F32)
            self.op("gpsimd", lambda e: e.iota(ii[:], pattern=[[1, 512]], base=0, channel_multiplier=0), [], [iib])
            self.cp("vector", J[:], ii[:], [iib], [Jb])
            self.ts(Jr[:], J[:], -1.0, 511.0, ALU.mult, ALU.add, [Jb], [Jrb])
            for m in range(4):
                self.op("gpsimd", lambda e, m=m: e.iota(ii[:], pattern=[[1, 512]], base=-128 * m, channel_multiplier=-1),
                        [iib], [iib])
                self.cp("vector", A[:, m, :], ii[:], [iib], [Ab])
            self.act(A[:], A[:], AF.Abs, [Ab], [Ab])
            self.op("gpsimd", lambda e: e.iota(ii[:, 0:mmax], pattern=[[128, mmax]], base=128, channel_multiplier=-1),
                    [iib], [iib])
            self.cp("vector", pbase[:], ii[:, 0:mmax], [iib], [pbaseb])
            self.op("gpsimd", lambda e: e.iota(ii[:, 0:nneg], pattern=[[128, nneg]], base=1, channel_multiplier=1),
                    [iib], [iib])
            self.cp("vector", nbase[:], ii[:, 0:nneg], [iib], [nbaseb])
            rowp, rowpb = self.sb(P, "rowp", [128, 512], F32)
            rown, rownb = self.sb(P, "rown", [128, 512], F32)
            Dt, Dtb = self.sb(P, "Dt", [128, 4, 512], F32)
            cfp, cfpb = self.sb(P, "cfp", [128, mmax], F32)
            cfn, cfnb = self.sb(P, "cfn", [128, nneg], F32)
            QT = self.sbpool(P, "QT", [64, S], BF16, 2)
            KT = self.sbpool(P, "KT", [64, S], BF16, 2)
            V = self.sbpool(P, "V", [128, NT, 128], BF16, 2)
            Pt = self.sbpool(P, "Pt", [128, 512], BF16, 4)
            tf = self.sbpool(P, "tf", [128, 512], F32, 6)
            sgp = self.sbpool(P, "sg", [128, 512], BF16, 2)
            y16 = self.sbpool(P, "y16", [128, 512], BF16, 2)
            pi = 0
            for h in range(8):
                lg = math.log1p(-2.0 ** (-5.0 - h))
                self.act(rowp[:], J[:], AF.Exp, [Jb], [rowpb], scale=lg)
                self.act(rown[:], Jr[:], AF.Exp, [Jrb], [rownb], scale=lg)
                self.act(Dt[:], A[:], AF.Exp, [Ab], [Dtb], scale=lg)
                self.act(cfp[:], pbase[:], AF.Exp, [pbaseb], [cfpb], scale=lg)
                self.act(cfn[:], nbase[:], AF.Exp, [nbaseb], [cfnb], scale=lg)
                q, qb = QT[h % 2]
                k, kb = KT[h % 2]
                v, vb = V[h % 2]
                self.load_rows(q, qb, self.QTd, h * 64, 64)
                self.load_rows(k, kb, self.KTd, h * 64, 64)
                self.load_v(v, vb, self.Vd, h * 128)
                for qg in range(NG):
                    qs = slice(qg * 512, (qg + 1) * 512)
                    O, Ob = self.ps[qg % 2]
                    sg, sgb = sgp[qg % 2]
                    self.dma("sync", sg[:], self.GdT[h * 128:(h + 1) * 128, qs], [], [sgb])
                    for kt in range(NT):
                        ks = slice(kt * 128, (kt + 1) * 128)
                        sp, spb = self.psn(4, 8)
                        self.mm(sp[:], k[:, ks], q[:, qs], True, True, [kb, qb], [spb])
                        p_, p_b = Pt[pi % 4]
                        pi += 1
                        m = 4 * qg - kt
                        if m >= 1:
                            self.stt(p_[:], sp[:], cfp[:, m - 1:m], rowp[:], ALU.mult, ALU.mult, [spb, cfpb, rowpb], [p_b])
                        elif m <= -4:
                            self.stt(p_[:], sp[:], cfn[:, -m - 4:-m - 3], rown[:], ALU.mult, ALU.mult,
                                     [spb, cfnb, rownb], [p_b])
                        else:
                            self.tt(p_[:], sp[:], Dt[:, -m, :], ALU.mult, [spb, Dtb], [p_b])
                        self.mm(O[:], v[:, kt, :], p_[:], kt == 0, kt == NT - 1, [vb, p_b], [Ob])
                    o1, o1b = tf[0]
                    s1, s1b = tf[1]
                    mn, mnb = tf[2]
                    vr, vrb = tf[3]
                    self.cp("scalar", o1[:], O[:], [Ob], [o1b])
                    self.act(s1[:], O[:], AF.Square, [Ob], [s1b])
                    mp, mpb = self.ps[2]
                    spp, sppb = self.ps[3]
                    self.mm(mp[:], self.onesf[:], o1[:], True, True, [self.onesf_b, o1b], [mpb])
                    self.mm(spp[:], self.onesf[:], s1[:], True, True, [self.onesf_b, s1b], [sppb])
                    self.act(mn[:], mp[:], AF.Copy, [mpb], [mnb], scale=1.0 / 128)
                    self.tt(vr[:], mn[:], mn[:], ALU.mult, [mnb], [vrb])
                    self.stt(vr[:], spp[:], 1.0 / 128, vr[:], ALU.mult, ALU.subtract, [sppb, vrb], [vrb])
                    self.rsqrt(vr[:], vr[:], 1.0, [vrb], [vrb])
                    self.tt(o1[:], o1[:], mn[:], ALU.subtract, [o1b, mnb], [o1b])
                    self.stt(o1[:], o1[:], gncol[:, 0:1], vr[:], ALU.mult, ALU.mult, [o1b, vrb, gncolb], [o1b])
                    y, yb = y16[(h * NG + qg) % 2]
                    self.tt(y[:], o1[:], sg[:], ALU.mult, [o1b, sgb], [yb])
                    self.dma("gpsimd", self.YT[3072 + h * 128:3072 + (h + 1) * 128, qs], y[:], [yb],
                             [self.dbuf("YT", 3, h, qg)])
            self.barrier()


    def pipeline(self, steps, depth, stage1, stage2):
        n = len(steps)
        self.hooks = {}
        self.pit = 0
        for i in range(n + depth):
            self.pit = i
            if i < n:
                stage1(steps[i], i)
            if i >= depth:
                stage2(steps[i - depth], i - depth)
            for fn in self.hooks.pop(i, []):
                fn()
        for k in sorted(self.hooks):
            for fn in self.hooks[k]:
                fn()
        self.hooks = {}

    def later(self, delay, fn):
        self.hooks.setdefault(self.pit + delay, []).append(fn)

    def lsum_mm(self, accs, ps_t, ps_b):
        for j, (a, ab) in enumerate(accs):
            self.mm(ps_t[:], self.onesf[:], a[:], j == 0, j == len(accs) - 1, [self.onesf_b, ab], [ps_b])

    def phaseB2(self, l, s):
        S, NT, NG = self.S, self.NT, self.NG
        lam_init = 0.8 - 0.6 * math.exp(-0.3 * l)
        with ExitStack() as P:
            lp, lpb = self.sb(P, "lp", [128, 256], F32)
            self.load_bc(lp[:], lpb, self.diff_lam[l:l + 1].rearrange("a b c -> a (b c)"))
            pr, prb = self.sb(P, "pr", [128, 128], F32)
            e2, e2b = self.sb(P, "e2", [128, 2], F32)
            neglam, neglamb = self.sb(P, "neglam", [128, 1], F32)
            gcol, gcolb = self.sb(P, "gcol", [128, 1], F32)
            lp4 = lp[:].rearrange("p (a b c) -> p a b c", a=2, b=2)
            self.tt(pr[:].rearrange("p (a c) -> p a c", a=2), lp4[:, :, 0, :], lp4[:, :, 1, :], ALU.mult, [lpb], [prb])
            self.op("vector", lambda e: e.tensor_reduce(out=e2[:], in_=pr[:].rearrange("p (a c) -> p a c", a=2),
                                                        axis=AX.X, op=ALU.add), [prb], [e2b])
            self.act(e2[:], e2[:], AF.Exp, [e2b], [e2b])
            self.tt(neglam[:], e2[:, 1:2], e2[:, 0:1], ALU.subtract, [e2b], [neglamb])
            self.ts(neglam[:], neglam[:], -lam_init, None, ALU.add, None, [neglamb], [neglamb])
            self.load_cols(gcol[:], gcolb, self.diff_sub_g[l], "(p o) -> p o", o=1)
            self.ts(gcol[:], gcol[:], 1.0 - lam_init, None, ALU.mult, None, [gcolb], [gcolb])
            QT = self.sbpool(P, "QT", [128, S], BF16, 2)
            KT = self.sbpool(P, "KT", [128, S], BF16, 2)
            V = self.sbpool(P, "V", [128, NT, 128], BF16, 2)
            NP = 6
            Pt = self.sbpool(P, "Pt", [128, 512], BF16, NP)
            tf = self.sbpool(P, "tf", [128, 512], F32, 6)
            sq16 = self.sbpool(P, "sq16", [128, 512], BF16, 2)
            y16 = self.sbpool(P, "y16", [128, 512], BF16, 2)
            Lacc = [[self.sb(P, "Lacc%d%d" % (a, b), [128, 512], F32) for b in range(2)] for a in range(2)]
            leng = (LENG0, LENG1)

            def loads(h):
                self.load_rows(QT[h % 2][0], QT[h % 2][1], self.QTb, h * 128, 128)
                self.load_rows(KT[h % 2][0], KT[h % 2][1], self.KTb, h * 128, 128)
                self.load_v(V[h % 2][0], V[h % 2][1], self.Vb, h * 128)

            steps = [(h, qg, kt, si) for h in range(8) for qg in range(NG) for kt in range(NT) for si in range(2)]
            sbank = {}
            loads(0)

            def stage1(st, i):
                h, qg, kt, si = st
                q, qb = QT[h % 2]
                k, kb = KT[h % 2]
                par = (h * NG + qg) % 2
                lo, hi = si * 64, si * 64 + 64
                sp, spb = self.psn(4, 8)
                self.mm(sp[:], k[lo:hi, kt * 128:(kt + 1) * 128], q[lo:hi, qg * 512:(qg + 1) * 512], True, True,
                        [kb, qb], [spb])
                p_, p_b = Pt[i % NP]
                self.act(p_[:], sp[:], AF.Exp, [spb], [p_b], scale=0.125)
                a, ab = Lacc[par][si]
                if kt == 0:
                    self.cp(leng[si], a[:], p_[:], [p_b], [ab])
                else:
                    self.tt(a[:], a[:], p_[:], ALU.add, [ab, p_b], [ab], en=leng[si])

            def stage2(st, i):
                h, qg, kt, si = st
                if qg == 0 and kt == 0 and si == 0 and h + 1 < 8:
                    loads(h + 1)
                v, vb = V[h % 2]
                par = (h * NG + qg) % 2
                O, Ob = self.ps[par * 2 + si]
                p_, p_b = Pt[i % NP]
                self.mm(O[:], v[:, kt, :], p_[:], kt == 0, kt == NT - 1, [vb, p_b], [Ob])
                if kt == NT - 1 and si == 1:
                    epilogue(h, qg, par)

            def epilogue(h, qg, par):
                qs = slice(qg * 512, (qg + 1) * 512)
                O0, O0b = self.ps[par * 2]
                O1, O1b = self.ps[par * 2 + 1]
                o1, o1b = tf[0]
                o2, o2b = tf[1]
                r1, r1b = tf[2]
                r2, r2b = tf[3]
                oo, oob = tf[4]
                rs, rsb = tf[5]
                self.cp("scalar", o1[:], O0[:], [O0b], [o1b])
                self.cp("vector", o2[:], O1[:], [O1b], [o2b])
                for si, (r, rb) in enumerate(((r1, r1b), (r2, r2b))):
                    lp_, lpb_ = self.psn(4, 8)
                    self.lsum_mm([Lacc[par][si]], lp_, lpb_)
                    self.act(r[:], lp_[:], AF.Ln, [lpb_], [rb])
                    self.act(r[:], r[:], AF.Exp, [rb], [rb], scale=-1.0)
                self.tt(o1[:], o1[:], r1[:], ALU.mult, [o1b, r1b], [o1b])
                self.tt(o2[:], o2[:], r2[:], ALU.mult, [o2b, r2b], [o2b])
                self.stt(oo[:], o2[:], neglam[:, 0:1], o1[:], ALU.mult, ALU.add, [o2b, o1b, neglamb], [oob])
                sq, sqb = sq16[par]
                self.act(sq[:], oo[:], AF.Square, [oob], [sqb])
                sp, spb = self.psn(4, 8)
                self.mm(sp[:], self.onesb[:], sq[:], True, True, [self.onesb_b, sqb], [spb])
                self.rsqrt(rs[:], sp[:], 1.0 / 128, [spb], [rsb])
                y, yb = y16[par]
                self.stt(y[:], oo[:], gcol[:, 0:1], rs[:], ALU.mult, ALU.mult, [oob, rsb, gcolb], [yb])
                self.dma("gpsimd", self.YT[1024 + h * 128:1024 + (h + 1) * 128, qs], y[:], [yb], [])

            self.pipeline(steps, KDEPTH, stage1, stage2)
            self.barrier()

    def phaseC2(self, l, s):
        S, NT, NG = self.S, self.NT, self.NG
        sc = 192.0 ** -0.5
        with ExitStack() as P:
            Qn = self.sbpool(P, "Qn", [128, S], BF16, 2)
            Qp = self.sbpool(P, "Qp", [64, S], BF16, 2)
            Kn = self.sbpool(P, "Kn", [128, S], BF16, 2)
            Kp = self.sbpool(P, "Kp", [64, S], BF16, 2)
            V = self.sbpool(P, "V", [128, NT, 128], BF16, 2)
            NP = 6
            Pt = self.sbpool(P, "Pt", [128, 512], BF16, NP)
            tf = self.sbpool(P, "tf", [128, 512], F32, 4)
            y16 = self.sbpool(P, "y16", [128, 512], BF16, 2)
            Lacc = [[self.sb(P, "Lacc%d%d" % (a, b), [128, 512], F32) for b in range(2)] for a in range(2)]
            leng = (LENG0, LENG1)

            def loads(h):
                self.load_rows(Qn[h % 2][0], Qn[h % 2][1], self.QTc, h * 192, 128)
                self.load_rows(Qp[h % 2][0], Qp[h % 2][1], self.QTc, h * 192 + 128, 64)
                self.load_rows(Kn[h % 2][0], Kn[h % 2][1], self.KTc, h * 192, 128)
                self.load_rows(Kp[h % 2][0], Kp[h % 2][1], self.KTc, h * 192 + 128, 64)
                self.load_v(V[h % 2][0], V[h % 2][1], self.Vc, h * 128)

            steps = [(h, qg, kt) for h in range(8) for qg in range(NG) for kt in range(NT)]
            loads(0)

            def stage1(st, i):
                h, qg, kt = st
                qn, qnb = Qn[h % 2]
                qp, qpb = Qp[h % 2]
                kn, knb = Kn[h % 2]
                kp, kpb = Kp[h % 2]
                par = (h * NG + qg) % 2
                qs = slice(qg * 512, (qg + 1) * 512)
                ks = slice(kt * 128, (kt + 1) * 128)
                sp, spb = self.psn(2, 8)
                self.mm(sp[:], kn[:, ks], qn[:, qs], True, False, [knb, qnb], [spb])
                self.mm(sp[:], kp[:, ks], qp[:, qs], False, True, [kpb, qpb], [spb])
                p_, p_b = Pt[i % NP]
                self.act(p_[:], sp[:], AF.Exp, [spb], [p_b], scale=sc)
                a, ab = Lacc[par][kt % 2]
                if kt < 2:
                    self.cp(leng[kt % 2], a[:], p_[:], [p_b], [ab])
                else:
                    self.tt(a[:], a[:], p_[:], ALU.add, [ab, p_b], [ab], en=leng[kt % 2])

            def stage2(st, i):
                h, qg, kt = st
                if qg == 0 and kt == 0 and h + 1 < 8:
                    loads(h + 1)
                v, vb = V[h % 2]
                par = (h * NG + qg) % 2
                O, Ob = self.ps[par]
                p_, p_b = Pt[i % NP]
                self.mm(O[:], v[:, kt, :], p_[:], kt == 0, kt == NT - 1, [vb, p_b], [Ob])
                if kt == NT - 1:
                    qs = slice(qg * 512, (qg + 1) * 512)
                    o1, o1b = tf[par * 2]
                    r1, r1b = tf[par * 2 + 1]
                    self.cp("vector", o1[:], O[:], [Ob], [o1b])
                    lp_, lpb_ = self.psn(2, 8)
                    self.lsum_mm(Lacc[par], lp_, lpb_)
                    self.act(r1[:], lp_[:], AF.Ln, [lpb_], [r1b])
                    self.act(r1[:], r1[:], AF.Exp, [r1b], [r1b], scale=-1.0)
                    y, yb = y16[par]
                    self.tt(y[:], o1[:], r1[:], ALU.mult, [o1b, r1b], [yb])
                    self.dma("gpsimd", self.YT[2048 + h * 128:2048 + (h + 1) * 128, qs], y[:], [yb], [])

            self.pipeline(steps, KDEPTH, stage1, stage2)
            self.barrier()

    def phaseD2(self, l, s):
        S, NT, NG = self.S, self.NT, self.NG
        mmax = max(4 * (NG - 1), 1)
        nneg = max(NT - 4, 1)
        with ExitStack() as P:
            gncol, gncolb = self.sb(P, "gncol", [128, 1], F32)
            self.load_cols(gncol[:], gncolb, self.ret_gn_g[l], "(p o) -> p o", o=1)
            ii, iib = self.sb(P, "ii", [128, 512], I32)
            J, Jb = self.sb(P, "J", [128, 512], F32)
            Jr, Jrb = self.sb(P, "Jr", [128, 512], F32)
            A, Ab = self.sb(P, "A", [128, 4, 512], F32)
            pbase, pbaseb = self.sb(P, "pbase", [128, mmax], F32)
            nbase, nbaseb = self.sb(P, "nbase", [128, nneg], F32)
            self.op("gpsimd", lambda e: e.iota(ii[:], pattern=[[1, 512]], base=0, channel_multiplier=0), [], [iib])
            self.cp("vector", J[:], ii[:], [iib], [Jb])
            self.ts(Jr[:], J[:], -1.0, 511.0, ALU.mult, ALU.add, [Jb], [Jrb])
            for m in range(4):
                self.op("gpsimd", lambda e, m=m: e.iota(ii[:], pattern=[[1, 512]], base=-128 * m, channel_multiplier=-1),
                        [iib], [iib])
                self.cp("vector", A[:, m, :], ii[:], [iib], [Ab])
            self.act(A[:], A[:], AF.Abs, [Ab], [Ab])
            self.op("gpsimd", lambda e: e.iota(ii[:, 0:mmax], pattern=[[128, mmax]], base=128, channel_multiplier=-1),
                    [iib], [iib])
            self.cp("vector", pbase[:], ii[:, 0:mmax], [iib], [pbaseb])
            self.op("gpsimd", lambda e: e.iota(ii[:, 0:nneg], pattern=[[128, nneg]], base=1, channel_multiplier=1),
                    [iib], [iib])
            self.cp("vector", nbase[:], ii[:, 0:nneg], [iib], [nbaseb])
            rowp = self.sbpool(P, "rowp", [128, 512], BF16, 2)
            rown = self.sbpool(P, "rown", [128, 512], BF16, 2)
            Dt = self.sbpool(P, "Dt", [128, 4, 512], F32, 2)
            cfp = self.sbpool(P, "cfp", [128, mmax], F32, 2)
            cfn = self.sbpool(P, "cfn", [128, nneg], F32, 2)
            QT = self.sbpool(P, "QT", [64, S], BF16, 2)
            KT = self.sbpool(P, "KT", [64, S], BF16, 2)
            V = self.sbpool(P, "V", [128, NT, 128], BF16, 2)
            NP = 6
            Pt = self.sbpool(P, "Pt", [128, 512], BF16, NP)
            P0 = self.sbpool(P, "P0", [128, 512], BF16, 4)
            tf = self.sbpool(P, "tf", [128, 512], F32, 4)
            sgp = self.sbpool(P, "sg", [128, 512], BF16, 2)
            y16 = self.sbpool(P, "y16", [128, 512], BF16, 2)
            meng = (LENG0, LENG1)

            def loads(h):
                lg = math.log1p(-2.0 ** (-5.0 - h))
                hp = h % 2
                self.act(rowp[hp][0][:], J[:], AF.Exp, [Jb], [rowp[hp][1]], scale=lg)
                self.act(rown[hp][0][:], Jr[:], AF.Exp, [Jrb], [rown[hp][1]], scale=lg)
                self.act(Dt[hp][0][:], A[:], AF.Exp, [Ab], [Dt[hp][1]], scale=lg)
                self.act(cfp[hp][0][:], pbase[:], AF.Exp, [pbaseb], [cfp[hp][1]], scale=lg)
                self.act(cfn[hp][0][:], nbase[:], AF.Exp, [nbaseb], [cfn[hp][1]], scale=lg)
                self.load_rows(QT[hp][0], QT[hp][1], self.QTd, h * 64, 64)
                self.load_rows(KT[hp][0], KT[hp][1], self.KTd, h * 64, 64)
                self.load_v(V[hp][0], V[hp][1], self.Vd, h * 128)

            steps = [(h, qg, kt) for h in range(8) for qg in range(NG) for kt in range(NT)]
            loads(0)

            def stage1(st, i):
                h, qg, kt = st
                hp = h % 2
                q, qb = QT[hp]
                k, kb = KT[hp]
                par = (h * NG + qg) % 2
                if kt == 0:
                    sg, sgb = sgp[par]
                    self.dma("sync", sg[:], self.GdT[h * 128:(h + 1) * 128, qg * 512:(qg + 1) * 512], [], [sgb])
                sp, spb = self.psn(4, 8)
                self.mm(sp[:], k[:, kt * 128:(kt + 1) * 128], q[:, qg * 512:(qg + 1) * 512], True, True, [kb, qb], [spb])
                p_, p_b = Pt[i % NP]
                m = 4 * qg - kt
                if m >= 1 or m <= -4:
                    p0, p0b = P0[i % 4]
                    if m >= 1:
                        cf, cfb = cfp[hp]
                        col = cf[:, m - 1:m]
                        row, rowb = rowp[hp]
                    else:
                        cf, cfb = cfn[hp]
                        col = cf[:, -m - 4:-m - 3]
                        row, rowb = rown[hp]
                    self.act(p0[:], sp[:], AF.Identity, [spb, cfb], [p0b], scale=col)
                    self.tt(p_[:], p0[:], row[:], ALU.mult, [p0b, rowb], [p_b], en=meng[i % 2])
                else:
                    self.tt(p_[:], sp[:], Dt[hp][0][:, -m, :], ALU.mult, [spb, Dt[hp][1]], [p_b])

            def stage2(st, i):
                h, qg, kt = st
                if qg == 0 and kt == 0 and h + 1 < 8:
                    loads(h + 1)
                v, vb = V[h % 2]
                par = (h * NG + qg) % 2
                O, Ob = self.ps[par]
                p_, p_b = Pt[i % NP]
                self.mm(O[:], v[:, kt, :], p_[:], kt == 0, kt == NT - 1, [vb, p_b], [Ob])
                if kt == NT - 1:
                    qs = slice(qg * 512, (qg + 1) * 512)
                    sg, sgb = sgp[par]
                    o1, o1b = tf[0]
                    s1, s1b = tf[1]
                    mn, mnb = tf[2]
                    vr, vrb = tf[3]
                    self.cp(KCP, o1[:], O[:], [Ob], [o1b])
                    self.act(s1[:], O[:], AF.Square, [Ob], [s1b])
                    mp, mpb = self.ps[2]
                    spp, sppb = self.ps[3]
                    self.mm(mp[:], self.onesf[:], o1[:], True, True, [self.onesf_b, o1b], [mpb])
                    self.mm(spp[:], self.onesf[:], s1[:], True, True, [self.onesf_b, s1b], [sppb])
                    self.act(mn[:], mp[:], AF.Copy, [mpb], [mnb], scale=1.0 / 128)
                    self.tt(vr[:], mn[:], mn[:], ALU.mult, [mnb], [vrb])
                    self.stt(vr[:], spp[:], 1.0 / 128, vr[:], ALU.mult, ALU.subtract, [sppb, vrb], [vrb])
                    self.rsqrt(vr[:], vr[:], 1.0, [vrb], [vrb])
                    self.tt(o1[:], o1[:], mn[:], ALU.subtract, [o1b, mnb], [o1b])
                    self.stt(o1[:], o1[:], gncol[:, 0:1], vr[:], ALU.mult, ALU.mult, [o1b, vrb, gncolb], [o1b])
                    y, yb = y16[par]
                    self.tt(y[:], o1[:], sg[:], ALU.mult, [o1b, sgb], [yb])
                    self.dma("gpsimd", self.YT[3072 + h * 128:3072 + (h + 1) * 128, qs], y[:], [yb], [])

            self.pipeline(steps, KDEPTH, stage1, stage2)
            self.barrier()


    def pair(self, j):
        return self.psbig[:, 2 * j:2 * j + 2, :], [self.ps[2 * j][1], self.ps[2 * j + 1][1]]

    def phaseB3(self, l, s):
        S, NT, NG = self.S, self.NT, self.NG
        lam_init = 0.8 - 0.6 * math.exp(-0.3 * l)
        with ExitStack() as P:
            lp, lpb = self.sb(P, "lp", [128, 256], F32)
            self.load_bc(lp[:], lpb, self.diff_lam[l:l + 1].rearrange("a b c -> a (b c)"))
            pr, prb = self.sb(P, "pr", [128, 128], F32)
            e2, e2b = self.sb(P, "e2", [128, 2], F32)
            neglam, neglamb = self.sb(P, "neglam", [128, 1], F32)
            gcol, gcolb = self.sb(P, "gcol", [128, 1], F32)
            lp4 = lp[:].rearrange("p (a b c) -> p a b c", a=2, b=2)
            self.tt(pr[:].rearrange("p (a c) -> p a c", a=2), lp4[:, :, 0, :], lp4[:, :, 1, :], ALU.mult, [lpb], [prb])
            self.op("vector", lambda e: e.tensor_reduce(out=e2[:], in_=pr[:].rearrange("p (a c) -> p a c", a=2),
                                                        axis=AX.X, op=ALU.add), [prb], [e2b])
            self.act(e2[:], e2[:], AF.Exp, [e2b], [e2b])
            self.tt(neglam[:], e2[:, 1:2], e2[:, 0:1], ALU.subtract, [e2b], [neglamb])
            self.ts(neglam[:], neglam[:], -lam_init, None, ALU.add, None, [neglamb], [neglamb])
            self.load_cols(gcol[:], gcolb, self.diff_sub_g[l], "(p o) -> p o", o=1)
            self.ts(gcol[:], gcol[:], 1.0 - lam_init, None, ALU.mult, None, [gcolb], [gcolb])
            QT = self.sbpool(P, "QT", [128, S], BF16, 2)
            KA = self.sbpool(P, "KA", [128, S], BF16, 2)
            KB_ = self.sbpool(P, "KBt", [128, S], BF16, 2)
            for i in range(2):
                self.op("vector", lambda e, i=i: e.memset(KA[i][0][64:128, :], 0.0), [], [KA[i][1]])
                self.op("vector", lambda e, i=i: e.memset(KB_[i][0][0:64, :], 0.0), [], [KB_[i][1]])
            V = self.sbpool(P, "V", [128, NT, 128], BF16, 2)
            NP = 6
            Pt = self.sbpool(P, "Pt", [128, 2, 512], BF16, NP)
            tf = self.sbpool(P, "tf", [128, 512], F32, 6)
            sq16 = self.sbpool(P, "sq16", [128, 512], BF16, 2)
            y16 = self.sbpool(P, "y16", [128, 512], BF16, 2)
            Lacc2 = [self.sb(P, "Lacc%d" % a, [128, 2, 512], F32) for a in range(2)]
            Lacc = [[(Lacc2[a][0][:, b, :], Lacc2[a][1]) for b in range(2)] for a in range(2)]
            T1 = self.sbpool(P, "T1", [128, 2, 512], BF16, 4)
            leng = (LENG0, LENG1)

            def loads(h):
                self.load_rows(QT[h % 2][0], QT[h % 2][1], self.QTb, h * 128, 128)
                self.dma("sync", KA[h % 2][0][0:64, :], self.KTb[h * 128:h * 128 + 64, :], [], [KA[h % 2][1]])
                self.dma("sync", KB_[h % 2][0][64:128, :], self.KTb[h * 128 + 64:h * 128 + 128, :], [], [KB_[h % 2][1]])
                self.load_v(V[h % 2][0], V[h % 2][1], self.Vb, h * 128)

            steps = [(h, qg, kt) for h in range(8) for qg in range(NG) for kt in range(NT)]
            loads(0)

            def stage1(st, i):
                h, qg, kt = st
                q, qb = QT[h % 2]
                par = (h * NG + qg) % 2
                sp, spbs = self.pair(2 + i % 2)
                ks = slice(kt * 128, (kt + 1) * 128)
                qs = slice(qg * 512, (qg + 1) * 512)
                self.mm(sp[:, 0, :], KA[h % 2][0][:, ks], q[:, qs], True, True, [KA[h % 2][1], qb], [spbs[0]])
                self.mm(sp[:, 1, :], KB_[h % 2][0][:, ks], q[:, qs], True, True, [KB_[h % 2][1], qb], [spbs[1]])
                p_, p_b = Pt[i % NP]
                self.act(p_[:], sp, AF.Exp, spbs, [p_b], scale=0.125)
                if kt % 2 == 1:
                    t1, t1b = T1[(i // 2) % 4]
                    pa, pab = Pt[(i - 1) % NP]
                    self.tt(t1[:], pa[:], p_[:], ALU.add, [pab, p_b], [t1b])
                    if kt % 4 == 3:
                        t0_, t0b = T1[((i // 2) - 1) % 4]
                        a, ab = Lacc2[par]
                        if kt == 3:
                            self.tt(a[:], t0_[:], t1[:], ALU.add, [t0b, t1b], [ab])
                        else:
                            self.tt(t1[:], t0_[:], t1[:], ALU.add, [t0b, t1b], [t1b])
                            self.tt(a[:], a[:], t1[:], ALU.add, [ab, t1b], [ab])

            def stage2(st, i):
                h, qg, kt = st
                if qg == 0 and kt == 0 and h + 1 < 8:
                    loads(h + 1)
                v, vb = V[h % 2]
                par = (h * NG + qg) % 2
                p_, p_b = Pt[i % NP]
                for si in range(2):
                    O, Ob = self.ps[par * 2 + si]
                    self.mm(O, v[:, kt, :], p_[:, si, :], kt == 0, kt == NT - 1, [vb, p_b], [Ob])
                if kt == NT - 1:
                    epilogue(h, qg, par, i)

            def epilogue(h, qg, par, i):
                qs = slice(qg * 512, (qg + 1) * 512)
                O0, O0b = self.ps[par * 2]
                O1, O1b = self.ps[par * 2 + 1]
                o1, o1b = tf[0]
                o2, o2b = tf[1]
                r1, r1b = tf[2]
                r2, r2b = tf[3]
                oo, oob = tf[4]
                rs, rsb = tf[5]
                sq, sqb = sq16[par]
                self.cp("scalar", o1[:], O0, [O0b], [o1b])
                self.cp("vector", o2[:], O1, [O1b], [o2b])

                def e1():
                    bnk = 4 + 2 * ((self.pit + 1) % 2)
                    for si, (r, rb) in enumerate(((r1, r1b), (r2, r2b))):
                        lp_, lpb_ = self.ps[bnk + si]
                        self.lsum_mm([Lacc[par][si]], lp_, lpb_)
                        self.act(r[:], lp_, AF.Ln, [lpb_], [rb])
                        self.act(r[:], r[:], AF.Exp, [rb], [rb], scale=-1.0)
                    self.tt(o1[:], o1[:], r1[:], ALU.mult, [o1b, r1b], [o1b])
                    self.tt(o2[:], o2[:], r2[:], ALU.mult, [o2b, r2b], [o2b])
                    self.stt(oo[:], o2[:], neglam[:, 0:1], o1[:], ALU.mult, ALU.add, [o2b, o1b, neglamb], [oob])
                    self.act(sq[:], oo[:], AF.Square, [oob], [sqb])

                def e2():
                    bnk = 4 + 2 * ((self.pit + 1) % 2)
                    sp, spb = self.ps[bnk]
                    self.mm(sp, self.onesb[:], sq[:], True, True, [self.onesb_b, sqb], [spb])
                    self.rsqrt(rs[:], sp, 1.0 / 128, [spb], [rsb])
                    y, yb = y16[par]
                    self.stt(y[:], oo[:], gcol[:, 0:1], rs[:], ALU.mult, ALU.mult, [oob, rsb, gcolb], [yb])
                    self.dma("gpsimd", self.YT[1024 + h * 128:1024 + (h + 1) * 128, qs], y[:], [yb], [])
                self.later(min(KE1, NT - 2), e1)
                self.later(min(KE2, NT - 1), e2)

            self.pipeline(steps, 1, stage1, stage2)
            self.barrier()

    def phaseC3(self, l, s):
        S, NT, NG = self.S, self.NT, self.NG
        sc = 192.0 ** -0.5
        NTP = NT // 2
        with ExitStack() as P:
            Qn = self.sbpool(P, "Qn", [128, S], BF16, 2)
            Qp = self.sbpool(P, "Qp", [128, S], BF16, 2)
            Kn = self.sbpool(P, "Kn", [128, S], BF16, 2)
            Kp = self.sbpool(P, "Kp", [128, S], BF16, 2)
            for i in range(2):
                self.op("vector", lambda e, i=i: e.memset(Qp[i][0][64:128, :], 0.0), [], [Qp[i][1]])
                self.op("vector", lambda e, i=i: e.memset(Kp[i][0][64:128, :], 0.0), [], [Kp[i][1]])
            V = self.sbpool(P, "V", [128, NT, 128], BF16, 2)
            NP = 4
            Pt = self.sbpool(P, "Pt", [128, 2, 512], BF16, NP)
            tf = self.sbpool(P, "tf", [128, 512], F32, 4)
            y16 = self.sbpool(P, "y16", [128, 512], BF16, 2)
            Lacc = [self.sb(P, "Lacc%d" % a, [128, 2, 512], F32) for a in range(2)]

            def loads(h):
                self.load_rows(Qn[h % 2][0], Qn[h % 2][1], self.QTc, h * 192, 128)
                self.load_rows(Qp[h % 2][0], Qp[h % 2][1], self.QTc, h * 192 + 128, 64)
                self.load_rows(Kn[h % 2][0], Kn[h % 2][1], self.KTc, h * 192, 128)
                self.load_rows(Kp[h % 2][0], Kp[h % 2][1], self.KTc, h * 192 + 128, 64)
                self.load_v(V[h % 2][0], V[h % 2][1], self.Vc, h * 128)

            steps = [(h, qg, kp) for h in range(8) for qg in range(NG) for kp in range(NTP)]
            loads(0)

            def stage1(st, i):
                h, qg, kp_ = st
                qn, qnb = Qn[h % 2]
                qp, qpb = Qp[h % 2]
                kn, knb = Kn[h % 2]
                kp, kpb = Kp[h % 2]
                par = (h * NG + qg) % 2
                qs = slice(qg * 512, (qg + 1) * 512)
                sp, spbs = self.pair(1 + i % 3)
                for j in range(2):
                    kt = 2 * kp_ + j
                    ks = slice(kt * 128, (kt + 1) * 128)
                    self.mm(sp[:, j, :], kn[:, ks], qn[:, qs], True, False, [knb, qnb], [spbs[j]])
                    self.mm(sp[:, j, :], kp[:, ks], qp[:, qs], False, True, [kpb, qpb], [spbs[j]])
                p_, p_b = Pt[i % NP]
                self.act(p_[:], sp, AF.Exp, spbs, [p_b], scale=sc)
                a, ab = Lacc[par]
                if kp_ == 0:
                    self.cp("vector", a[:], p_[:], [p_b], [ab])
                else:
                    self.tt(a[:], a[:], p_[:], ALU.add, [ab, p_b], [ab])

            def stage2(st, i):
                h, qg, kp_ = st
                if qg == 0 and kp_ == 0 and h + 1 < 8:
                    loads(h + 1)
                v, vb = V[h % 2]
                par = (h * NG + qg) % 2
                O, Ob = self.ps[par]
                p_, p_b = Pt[i % NP]
                for j in range(2):
                    kt = 2 * kp_ + j
                    self.mm(O, v[:, kt, :], p_[:, j, :], kt == 0, kt == NT - 1, [vb, p_b], [Ob])
                if kp_ == NTP - 1:
                    qs = slice(qg * 512, (qg + 1) * 512)
                    o1, o1b = tf[par * 2]
                    r1, r1b = tf[par * 2 + 1]
                    self.cp("vector", o1[:], O, [Ob], [o1b])
                    lp_, lpb_ = self.ps[2 + 2 * ((i + 1) % 3)]
                    a, ab = Lacc[par]
                    self.mm(lp_, self.onesf[:], a[:, 0, :], True, False, [self.onesf_b, ab], [lpb_])
                    self.mm(lp_, self.onesf[:], a[:, 1, :], False, True, [self.onesf_b, ab], [lpb_])
                    self.act(r1[:], lp_, AF.Ln, [lpb_], [r1b])
                    self.act(r1[:], r1[:], AF.Exp, [r1b], [r1b], scale=-1.0)
                    y, yb = y16[par]
                    self.tt(y[:], o1[:], r1[:], ALU.mult, [o1b, r1b], [yb])
                    self.dma("gpsimd", self.YT[2048 + h * 128:2048 + (h + 1) * 128, qs], y[:], [yb], [])

            self.pipeline(steps, 2, stage1, stage2)
            self.barrier()

    def phaseD3(self, l, s):
        S, NT, NG = self.S, self.NT, self.NG
        mmax = max(4 * (NG - 1), 1)
        nneg = max(NT - 4, 1)
        with ExitStack() as P:
            gncol, gncolb = self.sb(P, "gncol", [128, 1], F32)
            self.load_cols(gncol[:], gncolb, self.ret_gn_g[l], "(p o) -> p o", o=1)
            ii, iib = self.sb(P, "ii", [128, 512], I32)
            J, Jb = self.sb(P, "J", [128, 512], F32)
            Jr, Jrb = self.sb(P, "Jr", [128, 512], F32)
            A, Ab = self.sb(P, "A", [128, 4, 512], F32)
            pbase, pbaseb = self.sb(P, "pbase", [128, mmax], F32)
            nbase, nbaseb = self.sb(P, "nbase", [128, nneg], F32)
            self.op("gpsimd", lambda e: e.iota(ii[:], pattern=[[1, 512]], base=0, channel_multiplier=0), [], [iib])
            self.cp("vector", J[:], ii[:], [iib], [Jb])
            self.ts(Jr[:], J[:], -1.0, 511.0, ALU.mult, ALU.add, [Jb], [Jrb])
            for m in range(4):
                self.op("gpsimd", lambda e, m=m: e.iota(ii[:], pattern=[[1, 512]], base=-128 * m, channel_multiplier=-1),
                        [iib], [iib])
                self.cp("vector", A[:, m, :], ii[:], [iib], [Ab])
            self.act(A[:], A[:], AF.Abs, [Ab], [Ab])
            self.op("gpsimd", lambda e: e.iota(ii[:, 0:mmax], pattern=[[128, mmax]], base=128, channel_multiplier=-1),
                    [iib], [iib])
            self.cp("vector", pbase[:], ii[:, 0:mmax], [iib], [pbaseb])
            self.op("gpsimd", lambda e: e.iota(ii[:, 0:nneg], pattern=[[128, nneg]], base=1, channel_multiplier=1),
                    [iib], [iib])
            self.cp("vector", nbase[:], ii[:, 0:nneg], [iib], [nbaseb])
            rowp = self.sbpool(P, "rowp", [128, 512], BF16, 2)
            rown = self.sbpool(P, "rown", [128, 512], BF16, 2)
            Dt = self.sbpool(P, "Dt", [128, 4, 512], F32, 2)
            cfp = self.sbpool(P, "cfp", [128, mmax], F32, 2)
            cfn = self.sbpool(P, "cfn", [128, nneg], F32, 2)
            QT = self.sbpool(P, "QT", [128, S], BF16, 2)
            KT = self.sbpool(P, "KT", [128, S], BF16, 2)
            for i in range(2):
                self.op("vector", lambda e, i=i: e.memset(QT[i][0][64:128, :], 0.0), [], [QT[i][1]])
                self.op("vector", lambda e, i=i: e.memset(KT[i][0][64:128, :], 0.0), [], [KT[i][1]])
            V = self.sbpool(P, "V", [128, NT, 128], BF16, 2)
            NP = 8
            Pt = self.sbpool(P, "Pt", [128, 512], BF16, NP)
            P0 = self.sbpool(P, "P0", [128, 512], BF16, 6)
            tf = self.sbpool(P, "tf", [128, 512], F32, 4)
            sgp = self.sbpool(P, "sg", [128, 512], BF16, 2)
            y16 = self.sbpool(P, "y16", [128, 512], BF16, 2)
            cnt = [0]

            def loads(h):
                lg = math.log1p(-2.0 ** (-5.0 - h))
                hp = h % 2
                self.act(rowp[hp][0][:], J[:], AF.Exp, [Jb], [rowp[hp][1]], scale=lg)
                self.act(rown[hp][0][:], Jr[:], AF.Exp, [Jrb], [rown[hp][1]], scale=lg)
                self.act(Dt[hp][0][:], A[:], AF.Exp, [Ab], [Dt[hp][1]], scale=lg)
                self.act(cfp[hp][0][:], pbase[:], AF.Exp, [pbaseb], [cfp[hp][1]], scale=lg)
                self.act(cfn[hp][0][:], nbase[:], AF.Exp, [nbaseb], [cfn[hp][1]], scale=lg)
                self.load_rows(QT[hp][0], QT[hp][1], self.QTd, h * 64, 64)
                self.load_rows(KT[hp][0], KT[hp][1], self.KTd, h * 64, 64)
                self.load_v(V[hp][0], V[hp][1], self.Vd, h * 128)

            steps = [(h, qg, kt) for h in range(8) for qg in range(NG) for kt in range(NT)]
            loads(0)

            def stage1(st, i):
                h, qg, kt = st
                hp = h % 2
                q, qb = QT[hp]
                k, kb = KT[hp]
                par = (h * NG + qg) % 2
                if kt == 0:
                    sg, sgb = sgp[par]
                    self.dma("sync", sg[:], self.GdT[h * 128:(h + 1) * 128, qg * 512:(qg + 1) * 512], [], [sgb])
                sp, spb = self.psn(2, 8)
                self.mm(sp, k[:, kt * 128:(kt + 1) * 128], q[:, qg * 512:(qg + 1) * 512], True, True, [kb, qb], [spb])
                p_, p_b = Pt[i % NP]
                m = 4 * qg - kt
                if m >= 1 or m <= -4:
                    if m >= 1:
                        cf, cfb = cfp[hp]
                        col = cf[:, m - 1:m]
                        row, rowb = rowp[hp]
                    else:
                        cf, cfb = cfn[hp]
                        col = cf[:, -m - 4:-m - 3]
                        row, rowb = rown[hp]
                    cnt[0] += 1
                    if cnt[0] % KDACT != 0:
                        p0, p0b = P0[cnt[0] % 6]
                        self.act(p0[:], sp, AF.Identity, [spb, cfb], [p0b], scale=col)
                        self.tt(p_[:], p0[:], row[:], ALU.mult, [p0b, rowb], [p_b], en=LENG1)
                    else:
                        self.stt(p_[:], sp, col, row[:], ALU.mult, ALU.mult, [spb, cfb, rowb], [p_b])
                else:
                    self.tt(p_[:], sp, Dt[hp][0][:, -m, :], ALU.mult, [spb, Dt[hp][1]], [p_b])

            def stage2(st, i):
                h, qg, kt = st
                if qg == 0 and kt == 0 and h + 1 < 8:
                    loads(h + 1)
                v, vb = V[h % 2]
                par = (h * NG + qg) % 2
                O, Ob = self.ps[par]
                p_, p_b = Pt[i % NP]
                self.mm(O, v[:, kt, :], p_[:], kt == 0, kt == NT - 1, [vb, p_b], [Ob])
                if kt == NT - 1:
                    qs = slice(qg * 512, (qg + 1) * 512)
                    sg, sgb = sgp[par]
                    o1, o1b = tf[0]
                    s1, s1b = tf[1]
                    mn, mnb = tf[2]
                    vr, vrb = tf[3]
                    self.cp("scalar", o1[:], O, [Ob], [o1b])
                    self.act(s1[:], O, AF.Square, [Ob], [s1b])
                    mp, mpb = self.psn(2, 8)
                    spp, sppb = self.psn(2, 8)
                    self.mm(mp, self.onesf[:], o1[:], True, True, [self.onesf_b, o1b], [mpb])
                    self.mm(spp, self.onesf[:], s1[:], True, True, [self.onesf_b, s1b], [sppb])
                    self.act(mn[:], mp, AF.Copy, [mpb], [mnb], scale=1.0 / 128)
                    self.tt(vr[:], mn[:], mn[:], ALU.mult, [mnb], [vrb])
                    self.stt(vr[:], spp, 1.0 / 128, vr[:], ALU.mult, ALU.subtract, [sppb, vrb], [vrb])
                    self.rsqrt(vr[:], vr[:], 1.0, [vrb], [vrb])
                    self.tt(o1[:], o1[:], mn[:], ALU.subtract, [o1b, mnb], [o1b])
                    self.stt(o1[:], o1[:], gncol[:, 0:1], vr[:], ALU.mult, ALU.mult, [o1b, vrb, gncolb], [o1b])
                    y, yb = y16[par]
                    self.tt(y[:], o1[:], sg[:], ALU.mult, [o1b, sgb], [yb])
                    self.dma("gpsimd", self.YT[3072 + h * 128:3072 + (h + 1) * 128, qs], y[:], [yb], [])

            self.pipeline(steps, KDD, stage1, stage2)
            self.barrier()

    def phaseE(self, l, s):
        S = self.S
        TB = min(1024, S)
        nblk = S // TB
        NGb = TB // 512
        with ExitStack() as P:
            xk, xkb = self.sb(P, "xk", [128, 8, TB], F32)
            big, bigb = self.sb(P, "big", [128, 32, TB], BF16)
            bigbs = [Buf("big%d" % i_) for i_ in range(4)]
            aT, aTb = self.sb(P, "aT", [128, 8, TB], BF16)
            pT, pTb = self.sb(P, "pT", [128, 2, TB], BF16)
            wpool = self.sbpool(P, "we", [128, 4096], BF16, 3)
            g2, g2b = self.sb(P, "g2", [128, 8], F32)
            g3, g3b = self.sb(P, "g3", [128, 8], F32)
            self.load_cols(g2[:], g2b, self.norm2_g[l], "(kc p) -> p kc", p=128)
            self.load_cols(g3[:], g3b, self.norm3_g[l], "(kc p) -> p kc", p=128)
            acc = self.sbpool(P, "acc", [128, 512], F32, 4 * NGb)
            tmpf = self.sbpool(P, "tmpf", [128, 512], F32, 3)
            gtp = self.sbpool(P, "gt", [128, 512], BF16, 3)
            pl = self.sbpool(P, "pl", [128, 256], F32, 2)
            pl16 = self.sbpool(P, "pl16", [128, 256], BF16, 2)
            cnt = [0]
            for blk in range(nblk):
                c0 = s * S + blk * TB
                lc0 = blk * TB
                self.dma("sync", xk[:], self.xT[:, c0:c0 + TB].rearrange("(kc p) t -> p kc t", p=128), [], [xkb])
                for b_ in range(4):
                    self.dma("sync", big[:, b_ * 8:(b_ + 1) * 8, :],
                             self.YT[b_ * 1024:(b_ + 1) * 1024, lc0:lc0 + TB].rearrange("(kc p) t -> p kc t", p=128),
                             [], [bigbs[b_]])
                for nt in range(2):
                    for b_ in range(4):
                        def epi(pt, pb, m, g, b_=b_, nt=nt):
                            gt, gtb = gtp[cnt[0] % 3]
                            tm, tmb = tmpf[cnt[0] % 3]
                            cnt[0] += 1
                            r0 = b_ * 1024 + (nt * 4 + m) * 128
                            self.dma("sync", gt[:], self.GT[r0:r0 + 128, lc0 + g * 512:lc0 + (g + 1) * 512], [], [gtb])
                            a, ab = acc[m * NGb + g]
                            if b_ == 0:
                                self.tt(a[:], pt[:], gt[:], ALU.mult, [pb, gtb], [ab])
                            elif b_ < 3:
                                self.tt(tm[:], pt[:], gt[:], ALU.mult, [pb, gtb], [tmb])
                                self.tt(a[:], a[:], tm[:], ALU.add, [ab, tmb], [ab], en="gpsimd")
                            else:
                                self.tt(tm[:], pt[:], gt[:], ALU.mult, [pb, gtb], [tmb])
                                self.tt(aT[:, nt * 4 + m, g * 512:(g + 1) * 512], a[:], tm[:], ALU.add, [ab, tmb], [aTb],
                                        en="gpsimd")
                        self.gemm("fm", big, bigbs[b_], 8, TB, self.wb["br%d" % b_, l], nt * 512, 512, epi, wpool, k0=b_ * 8)
                for nt in range(2):
                    def epi(pt, pb, m, g, nt=nt):
                        xs = xk[:, nt * 4 + m, g * 512:(g + 1) * 512]
                        self.tt(xs, xs, pt[:], ALU.add, [xkb, pb], [xkb])
                    self.gemm("fm", aT, aTb, 8, TB, self.wb["out", l], nt * 512, 512, epi, wpool)
                self.norm_fm(None, 0, TB, g2, g2b, aT, aTb, xkeep=(xk, xkb), loaded=True)
                for nt in range(8):
                    def epi(pt, pb, m, g, nt=nt):
                        tm, tmb = tmpf[cnt[0] % 3]
                        cnt[0] += 1
                        self.act(tm[:], pt[:], AF.Relu, [pb], [tmb])
                        self.tt(big[:, nt * 4 + m, g * 512:(g + 1) * 512], tm[:], tm[:], ALU.mult, [tmb],
                                [bigbs[(nt * 4 + m) // 8]])
                    self.gemm("fm", aT, aTb, 8, TB, self.wb["ff1", l], nt * 512, 512, epi, wpool)
                for mt in range(8):
                    def epi(pt, pb, m, g, mt=mt):
                        xs = xk[:, mt, g * 512:(g + 1) * 512]
                        self.tt(xs, xs, pt[:], ALU.add, [xkb, pb], [xkb])
                    self.gemm("fm", big, bigbs, 32, TB, self.wb["ff2", l], mt * 128, 128, epi, wpool)
                self.norm_fm(None, 0, TB, g3, g3b, aT, aTb, xkeep=(xk, xkb), loaded=True)
                for t in range(TB // 128):
                    p_, p_b = pl[t % 2]
                    p16, p16b = pl16[t % 2]
                    self.dma("sync", p_[:], self.p[l, c0 + t * 128:c0 + (t + 1) * 128, :], [], [p_b])
                    self.cp(self.alt(), p16[:], p_[:], [p_b], [p16b])
                    i = self.psi % 8
                    self.psi += 1
                    pv = self.psb(i)
                    for bi in range(2):
                        self.tp(pv[:, bi * 128:(bi + 1) * 128], p16[:, bi * 128:(bi + 1) * 128], self.identb[:],
                                [p16b, self.identb_b], [self.ps[i][1]])
                    self.cp(self.alt(), pT[:, :, t * 128:(t + 1) * 128], pv[:, 0:256].rearrange("p (b t) -> p b t", b=2),
                            [self.ps[i][1]], [pTb])
                for nt in range(2):
                    wg, wgb = wpool[self.wi % 3]
                    self.wi += 1
                    wp, wpb = wpool[self.wi % 3]
                    self.wi += 1
                    wgv = wg[:, 0:4096].rearrange("p (k n) -> p k n", k=8)
                    wpv = wp[:, 0:1024].rearrange("p (k n) -> p k n", k=2)
                    self.dma("sync", wgv, self.wb["pg", l][:, nt * 512:(nt + 1) * 512].rearrange("(kc p) n -> p kc n", p=128),
                             [], [wgb])
                    self.dma("sync", wpv, self.wb["pp", l][:, nt * 512:(nt + 1) * 512].rearrange("(kc p) n -> p kc n", p=128),
                             [], [wpb])
                    for m in range(4):
                        for g in range(NGb):
                            gs = slice(g * 512, (g + 1) * 512)
                            p1, p1b = self.psn()
                            for kc in range(8):
                                self.mm(p1[:], wgv[:, kc, m * 128:(m + 1) * 128], aT[:, kc, gs], kc == 0, kc == 7,
                                        [wgb, aTb], [p1b])
                            p2, p2b = self.psn()
                            for kc in range(2):
                                self.mm(p2[:], wpv[:, kc, m * 128:(m + 1) * 128], pT[:, kc, gs], kc == 0, kc == 1,
                                        [wpb, pTb], [p2b])
                            tm, tmb = tmpf[cnt[0] % 3]
                            cnt[0] += 1
                            self.act(tm[:], p1[:], AF.Sigmoid, [p1b], [tmb])
                            self.tt(tm[:], tm[:], p2[:], ALU.mult, [tmb, p2b], [tmb])
                            xs = xk[:, nt * 4 + m, gs]
                            self.tt(xs, xs, tm[:], ALU.add, [xkb, tmb], [xkb], en="gpsimd")
                self.dma("gpsimd", self.xT[:, c0:c0 + TB].rearrange("(kc p) t -> p kc t", p=128), xk[:], [xkb], [])
            self.barrier()

    def mark(self, name):
        if not hasattr(self, "marks"):
            self.marks = []
        self.marks.append((name, {k: e.dom.count for k, e in self.E.items()}))

    def body(self):
        for l in range(self.L):
            for s in range(self.NSEQ):
                self.mark("P1 %d %d" % (l, s))
                self.phase1(l, s)
                self.mark("PB %d %d" % (l, s))
                (self.phaseB3 if "B3" in PH else self.phaseB2)(l, s)
                self.mark("PC %d %d" % (l, s))
                (self.phaseC3 if "C3" in PH else self.phaseC2)(l, s)
                self.mark("PD %d %d" % (l, s))
                (self.phaseD3 if "D3" in PH else self.phaseD2)(l, s)
                self.mark("PE %d %d" % (l, s))
                self.phaseE(l, s)
        self.mark("END")

    def build(self):
        self.setup_sync()
        self.declare_io()
        self.build_consts()
        self.precast()
        self.build_rope()
        self.transpose_in()
        self.body()
        self.transpose_out()
        self.barrier()
        self.st.close()
        return self.nc


INPUT_NAMES = ["x", "p", "norm1_g", "w_in", "gate_b", "conv_w", "conv_b", "lru_wa", "lru_ba", "lru_wx", "lru_bx",
               "lru_lambda", "diff_q_g", "diff_k_g", "diff_lam", "diff_sub_g", "mla_qa_g", "mla_wuq", "mla_kva_g",
               "mla_wukv", "mla_q_g", "mla_k_g", "ret_gn_g", "w_br_a", "w_br_b", "w_br_c", "w_br_d", "w_out",
               "norm2_g", "w_ff1", "w_ff2", "norm3_g", "w_ple_gate", "w_ple_proj"]


def make_in_maps(inputs, ncores, nseq, S):
    maps = []
    for c in range(ncores):
        m = {}
        for k in INPUT_NAMES:
            v = np.asarray(inputs[k])
            if k == "x":
                v = np.ascontiguousarray(v[c * nseq:(c + 1) * nseq].reshape(nseq * S, DM))
            elif k == "p":
                v = np.ascontiguousarray(v[:, c * nseq:(c + 1) * nseq].reshape(v.shape[0], nseq * S, PLED))
            else:
                v = np.ascontiguousarray(v)
            m[k] = v.astype(np.float32, copy=False)
        maps.append(m)
    return maps


def kernel(**inputs):
    B, S, _ = inputs["x"].shape
    L = inputs["w_in"].shape[0]
    ncores = 8
    nseq = B // ncores
    kb = KB(S, nseq, L)
    nc = kb.build()
    maps = make_in_maps(inputs, ncores, nseq, S)
    res = run_bass_kernel_spmd(nc, maps, core_ids=list(range(ncores)))
    outs = [np.asarray(r["out"]).reshape(nseq, S, DM) for r in res.results]
    return np.concatenate(outs, axis=0).astype(np.float32)
```

```python
import math
import numpy as np
import concourse.bass as bass
import concourse.mybir as mybir
from concourse.bass_utils import run_bass_kernel_spmd
from contextlib import ExitStack

F32 = mybir.dt.float32
BF16 = mybir.dt.bfloat16
I32 = mybir.dt.int32
AF = mybir.ActivationFunctionType
ALU = mybir.AluOpType
AX = mybir.AxisListType

DM = 1024
NIN = 12992
DFF = 4096
PLED = 256
EPS = 1e-6
OFF = dict(ax=0, ag=1024, bq=2048, bk=3072, bv=4096, cq=5120, ckv=5504, ckpe=5760,
           dq=5824, dk=6336, dv=6848, dg=7872, gl=8896)
SAME_SYNC = True
import os
PH = os.environ.get("KPH", "B3,C3,D3").split(",")
KDEPTH = int(os.environ.get("KDEPTH", "3"))
KCP = os.environ.get("KCP", "scalar")
LENG0 = "vector"
LENG1 = os.environ.get("KLENG1", "vector")
KDEFER = int(os.environ.get("KDEFER", "2"))
KGI = int(os.environ.get("KGI", "2"))
KDACT = int(os.environ.get("KDACT", "2"))
KDD = int(os.environ.get("KDD", "5"))
KE1 = int(os.environ.get("KE1", "3"))
KE2 = int(os.environ.get("KE2", "8"))


class Dom:
    def __init__(self, name, sem, mult):
        self.name, self.sem, self.mult, self.count = name, sem, mult, 0


class Buf:
    __slots__ = ("name", "w", "r")

    def __init__(self, name=""):
        self.name, self.w, self.r = name, {}, {}


class Eng:
    def __init__(self, name, eng, dom):
        self.name, self.eng, self.dom = name, eng, dom
        self.seen = {}
        self.slots = []
        self.si = 0


class KB:
    def __init__(self, S, NSEQ, L, debug=()):
        self.S, self.NSEQ, self.L = S, NSEQ, L
        self.NT, self.NG = S // 128, S // 512
        self.debug = set(debug)
        self.nc = bass.Bass("TRN2", target_bir_lowering=False)
        self.st = ExitStack()
        self.E = {}
        self.doms = []
        self.dbufs = {}
        self.cnt = 0
        self.wi = 0

    def setup_sync(self):
        nc = self.nc
        for name in ["tensor", "vector", "scalar", "gpsimd", "sync"]:
            sem = self.st.enter_context(nc.semaphore("s_" + name))
            dom = Dom(name, sem, 1)
            self.doms.append(dom)
            self.E[name] = Eng(name, getattr(nc, name), dom)
        for q, n in (("sync", 24), ("gpsimd", 16)):
            for i in range(n):
                sem = self.st.enter_context(nc.semaphore("d_%s%d" % (q, i)))
                dom = Dom("d_%s%d" % (q, i), sem, 16)
                self.doms.append(dom)
                self.E[q].slots.append(dom)

    def _deps(self, reads, writes):
        deps = {}
        for b in reads:
            for d, i in b.w.items():
                if deps.get(d, 0) < i:
                    deps[d] = i
        for b in writes:
            for d, i in b.w.items():
                if deps.get(d, 0) < i:
                    deps[d] = i
            for d, i in b.r.items():
                if deps.get(d, 0) < i:
                    deps[d] = i
        return deps

    def _wait(self, E, deps):
        for dom, idx in deps.items():
            if idx <= 0:
                continue
            if dom is E.dom and (E.name == "tensor" or not SAME_SYNC):
                continue
            if E.seen.get(dom, 0) >= idx:
                continue
            E.eng.wait_ge(dom.sem, idx * dom.mult)
            E.seen[dom] = idx

    def op(self, en, fn, reads=(), writes=()):
        E = self.E[en]
        self._wait(E, self._deps(reads, writes))
        ins = fn(E.eng)
        E.dom.count += 1
        ins.then_inc(E.dom.sem, 1)
        c = E.dom.count
        for b in reads:
            b.r[E.dom] = c
        for b in writes:
            b.w[E.dom] = c
        self.cnt += 1

    def dma(self, q, out, in_, reads=(), writes=(), slow=False):
        E = self.E[q]
        slot = E.slots[E.si % len(E.slots)]
        E.si += 1
        deps = self._deps(reads, writes)
        if slot.count > 0 and deps.get(slot, 0) < slot.count:
            deps[slot] = slot.count
        self._wait(E, deps)
        if slow:
            ins = E.eng.dma_start(out=out, in_=in_, allow_slow_non_contiguous=True)
        else:
            ins = E.eng.dma_start(out=out, in_=in_)
        ins.then_inc(slot.sem, 16)
        slot.count += 1
        for b in reads:
            b.r[slot] = slot.count
        for b in writes:
            b.w[slot] = slot.count
        self.cnt += 1

    def barrier(self):
        for E in self.E.values():
            for d in self.doms:
                if d.count > 0 and E.seen.get(d, 0) < d.count:
                    E.eng.wait_ge(d.sem, d.count * d.mult)
                    E.seen[d] = d.count

    def dbuf(self, *key):
        b = self.dbufs.get(key)
        if b is None:
            b = Buf(str(key))
            self.dbufs[key] = b
        return b

    def sb(self, stack, name, shape, dt):
        self.uid = getattr(self, "uid", 0) + 1
        t = stack.enter_context(self.nc.sbuf_tensor("%s_%d" % (name, self.uid), list(shape), dt))
        return t, Buf(name)

    def sbpool(self, stack, name, shape, dt, n):
        return [self.sb(stack, "%s%d" % (name, i), shape, dt) for i in range(n)]

    def act(self, out, in_, func, reads, writes, bias=None, scale=None, accum=None):
        kw = {}
        if bias is not None:
            kw["bias"] = bias
        if scale is not None:
            kw["scale"] = scale
        if accum is not None:
            kw["accum_out"] = accum
        self.op("scalar", lambda e: e.activation(out=out, in_=in_, func=func, **kw), reads, writes)

    def tt(self, out, in0, in1, op, reads, writes, en="vector"):
        self.op(en, lambda e: e.tensor_tensor(out=out, in0=in0, in1=in1, op=op), reads, writes)

    def ts(self, out, in0, s1, s2, op0, op1, reads, writes, en="vector"):
        if op1 is None:
            self.op(en, lambda e: e.tensor_scalar(out=out, in0=in0, scalar1=s1, scalar2=None, op0=op0), reads, writes)
        else:
            self.op(en, lambda e: e.tensor_scalar(out=out, in0=in0, scalar1=s1, scalar2=s2, op0=op0, op1=op1),
                    reads, writes)

    def stt(self, out, in0, scalar, in1, op0, op1, reads, writes):
        self.op("vector", lambda e: e.scalar_tensor_tensor(out=out, in0=in0, scalar=scalar, in1=in1, op0=op0, op1=op1),
                reads, writes)

    def cp(self, en, out, in_, reads, writes):
        if en == "scalar":
            self.op("scalar", lambda e: e.activation(out=out, in_=in_, func=AF.Copy), reads, writes)
        else:
            self.op(en, lambda e: e.tensor_copy(out=out, in_=in_), reads, writes)

    def mm(self, out, lhsT, rhs, start, stop, reads, writes):
        self.op("tensor", lambda e: e.matmul(out, lhsT=lhsT, rhs=rhs, start=start, stop=stop), reads, writes)

    def tp(self, out, in_, ident, reads, writes):
        self.op("tensor", lambda e: e.transpose(out=out, in_=in_, identity=ident), reads, writes)

    def rsqrt(self, out, in_, scale, reads, writes, eps=EPS):
        self.act(out, in_, AF.Ln, reads, writes, bias=self.epscol[0:out.shape[0], :] if eps == EPS else eps,
                 scale=scale)
        self.act(out, out, AF.Exp, list(writes), writes, scale=-0.5)

    _alt = 0

    def alt(self):
        self._alt ^= 1
        return "scalar" if self._alt else "vector"

    def declare_io(self):
        nc, S, NSEQ, L = self.nc, self.S, self.NSEQ, self.L
        T = NSEQ * S
        self.T = T
        d = lambda n, sh, dt=F32: nc.dram_tensor(n, list(sh), dt, kind="ExternalInput")
        self.x = d("x", [T, DM])
        self.p = d("p", [L, T, PLED])
        self.norm1_g = d("norm1_g", [L, DM])
        self.w_in = d("w_in", [L, DM, NIN])
        self.gate_b = d("gate_b", [L, 4, DM])
        self.conv_w = d("conv_w", [L, 4, DM])
        self.conv_b = d("conv_b", [L, DM])
        self.lru_wa = d("lru_wa", [L, 2, 16, 64, 64])
        self.lru_ba = d("lru_ba", [L, 2, DM])
        self.lru_wx = d("lru_wx", [L, 2, 16, 64, 64])
        self.lru_bx = d("lru_bx", [L, 2, DM])
        self.lru_lambda = d("lru_lambda", [L, 2, DM])
        self.diff_q_g = d("diff_q_g", [L, 64])
        self.diff_k_g = d("diff_k_g", [L, 64])
        self.diff_lam = d("diff_lam", [L, 4, 64])
        self.diff_sub_g = d("diff_sub_g", [L, 128])
        self.mla_qa_g = d("mla_qa_g", [L, 384])
        self.mla_wuq = d("mla_wuq", [L, 384, 1536])
        self.mla_kva_g = d("mla_kva_g", [L, 256])
        self.mla_wukv = d("mla_wukv", [L, 256, 2048])
        self.mla_q_g = d("mla_q_g", [L, 192])
        self.mla_k_g = d("mla_k_g", [L, 192])
        self.ret_gn_g = d("ret_gn_g", [L, 128])
        self.w_br = [d("w_br_" + c, [L, DM, DM]) for c in "abcd"]
        self.w_out = d("w_out", [L, DM, DM])
        self.norm2_g = d("norm2_g", [L, DM])
        self.w_ff1 = d("w_ff1", [L, DM, DFF])
        self.w_ff2 = d("w_ff2", [L, DFF, DM])
        self.norm3_g = d("norm3_g", [L, DM])
        self.w_ple_gate = d("w_ple_gate", [L, DM, DM])
        self.w_ple_proj = d("w_ple_proj", [L, PLED, DM])
        self.out = nc.dram_tensor("out", [T, DM], F32, kind="ExternalOutput")

        def scr(n, sh, dt):
            kind = "ExternalOutput" if n in self.debug else "Internal"
            return nc.dram_tensor(n, list(sh), dt, kind=kind)

        self.scr = scr
        self.wb = {}
        for l in range(L):
            self.wb["w_in", l] = scr("wb_in%d" % l, [DM, NIN], BF16)
            self.wb["wuq", l] = scr("wb_wuq%d" % l, [384, 1536], BF16)
            self.wb["wukv", l] = scr("wb_wukv%d" % l, [256, 2048], BF16)
            for i in range(4):
                self.wb["br%d" % i, l] = scr("wb_br%d_%d" % (i, l), [DM, DM], BF16)
            self.wb["out", l] = scr("wb_out%d" % l, [DM, DM], BF16)
            self.wb["ff1", l] = scr("wb_ff1%d" % l, [DM, DFF], BF16)
            self.wb["ff2", l] = scr("wb_ff2%d" % l, [DFF, DM], BF16)
            self.wb["pg", l] = scr("wb_pg%d" % l, [DM, DM], BF16)
            self.wb["pp", l] = scr("wb_pp%d" % l, [PLED, DM], BF16)
        self.xT = scr("xT", [DM, T], F32)
        self.QTb = scr("QTb", [1024, S], BF16)
        self.KTb = scr("KTb", [1024, S], BF16)
        self.Vb = scr("Vb", [S, 1024], BF16)
        self.QTc = scr("QTc", [8 * 192, S], BF16)
        self.KTc = scr("KTc", [8 * 192, S], BF16)
        self.Vc = scr("Vc", [S, 1024], BF16)
        self.QTd = scr("QTd", [512, S], BF16)
        self.KTd = scr("KTd", [512, S], BF16)
        self.Vd = scr("Vd", [S, 1024], BF16)
        self.GdT = scr("GdT", [1024, S], BF16)
        self.YT = scr("YT", [4 * 1024, S], BF16)
        self.GT = scr("GT", [4 * 1024, S], BF16)
        self.ropeD = [scr("rope%d" % i, [S, 2, r2], F32) for i, r2 in enumerate((8, 32, 32))]

    def build_consts(self):
        nc = self.nc
        st = self.st
        self.identf, self.identf_b = self.sb(st, "identf", [128, 128], F32)
        self.identb, self.identb_b = self.sb(st, "identb", [128, 128], BF16)
        self.onesf, self.onesf_b = self.sb(st, "onesf", [128, 128], F32)
        self.onesb, self.onesb_b = self.sb(st, "onesb", [128, 128], BF16)
        self.epscol, self.epscol_b = self.sb(st, "epscol", [128, 1], F32)
        self.onecol, self.onecol_b = self.sb(st, "onecol", [128, 1], F32)
        self.op("vector", lambda e: e.memset(self.onecol[:], 1.0), [], [self.onecol_b])
        self.CB = [self.identf_b, self.identb_b, self.onesf_b, self.onesb_b, self.epscol_b]
        with ExitStack() as s2:
            it, itb = self.sb(s2, "c_it", [128, 128], I32)
            tf, tfb = self.sb(s2, "c_tf", [128, 128], F32)
            self.op("gpsimd", lambda e: e.iota(it[:], pattern=[[1, 128]], base=0, channel_multiplier=-1), [], [itb])
            self.cp("vector", tf[:], it[:], [itb], [tfb])
            self.ts(self.identf[:], tf[:], 0.0, None, ALU.is_equal, None, [tfb], [self.identf_b])
            self.cp("vector", self.identb[:], self.identf[:], [self.identf_b], [self.identb_b])
            self.op("vector", lambda e: e.memset(self.onesf[:], 1.0), [], [self.onesf_b])
            self.op("vector", lambda e: e.memset(self.onesb[:], 1.0), [], [self.onesb_b])
            self.op("vector", lambda e: e.memset(self.epscol[:], EPS), [], [self.epscol_b])
            self.barrier()
        self.psbig = st.enter_context(nc.psum_tensor("psbig", [128, 8, 512], F32))
        self.ps = [(self.psbig[:, i, :], Buf("ps%d" % i)) for i in range(8)]
        self.psi = 0

    def psn(self, lo=0, hi=8):
        i = lo + self.psi % (hi - lo)
        self.psi += 1
        return self.ps[i]

    def precast(self):
        L = self.L
        jobs = []
        for l in range(L):
            jobs.append((self.w_in[l], self.wb["w_in", l], DM, NIN))
            jobs.append((self.mla_wuq[l], self.wb["wuq", l], 384, 1536))
            jobs.append((self.mla_wukv[l], self.wb["wukv", l], 256, 2048))
            for i in range(4):
                jobs.append((self.w_br[i][l], self.wb["br%d" % i, l], DM, DM))
            jobs.append((self.w_out[l], self.wb["out", l], DM, DM))
            jobs.append((self.w_ff1[l], self.wb["ff1", l], DM, DFF))
            jobs.append((self.w_ff2[l], self.wb["ff2", l], DFF, DM))
            jobs.append((self.w_ple_gate[l], self.wb["pg", l], DM, DM))
            jobs.append((self.w_ple_proj[l], self.wb["pp", l], PLED, DM))
        with ExitStack() as s2:
            CW = 2048
            stg = self.sbpool(s2, "pc_f", [128, CW], F32, 3)
            stb = self.sbpool(s2, "pc_b", [128, CW], BF16, 3)
            i = 0
            for src, dst, K, N in jobs:
                for r0 in range(0, K, 128):
                    for c0 in range(0, N, CW):
                        cw = min(CW, N - c0)
                        f, fb = stg[i % 3]
                        b, bb = stb[i % 3]
                        self.dma("sync", f[:, 0:cw], src[r0:r0 + 128, c0:c0 + cw], [], [fb])
                        self.cp(self.alt(), b[:, 0:cw], f[:, 0:cw], [fb], [bb])
                        self.dma("gpsimd", dst[r0:r0 + 128, c0:c0 + cw], b[:, 0:cw], [bb], [])
                        i += 1
            self.barrier()

    def build_rope(self):
        NT = self.NT
        cfgs = ((8, 16, 500000.0), (32, 64, 500000.0), (32, 64, 10000.0))
        C1 = 6.28125
        C2 = 2.0 * math.pi - C1
        with ExitStack() as s2:
            pi_, pib = self.sb(s2, "r_pi", [128, NT], I32)
            pos, posb = self.sb(s2, "r_pos", [128, NT], F32)
            self.op("gpsimd", lambda e: e.iota(pi_[:], pattern=[[128, NT]], base=0, channel_multiplier=1), [], [pib])
            self.cp("vector", pos[:], pi_[:], [pib], [posb])
            for ci, (r2, rot, theta) in enumerate(cfgs):
                with ExitStack() as s3:
                    invf, invfb = self.sb(s3, "r_invf", [128, r2], F32)
                    ang, angb = self.sb(s3, "r_ang", [128, NT, r2], F32)
                    a, ab = self.sb(s3, "r_a", [128, NT, r2], F32)
                    kf, kfb = self.sb(s3, "r_kf", [128, NT, r2], F32)
                    ki, kib = self.sb(s3, "r_ki", [128, NT, r2], I32)
                    mk, mkb = self.sb(s3, "r_mk", [128, NT, r2], F32)
                    tab, tabb = self.sb(s3, "r_tab", [128, NT, 2, r2], F32)
                    for j in range(r2):
                        v = float(np.float32(theta) ** np.float32(-(2.0 * j) / rot))
                        self.op("vector", lambda e, j=j, v=v: e.memset(invf[:, j:j + 1], v), [], [invfb])
                    pos_b = bass.AP(pos[:].tensor, pos[:].offset, [list(pos[:].ap[0]), [1, NT], [0, r2]])
                    inv_b = bass.AP(invf[:].tensor, invf[:].offset, [list(invf[:].ap[0]), [0, NT], [1, r2]])
                    self.tt(ang[:], pos_b, inv_b, ALU.mult, [posb, invfb], [angb])
                    for which, shift in ((1, 0.0), (0, math.pi / 2)):
                        self.ts(a[:], ang[:], shift, None, ALU.add, None, [angb], [ab])
                        self.ts(kf[:], a[:], 1.0 / (2 * math.pi), None, ALU.mult, None, [ab], [kfb])
                        self.cp("vector", ki[:], kf[:], [kfb], [kib])
                        self.cp("vector", kf[:], ki[:], [kib], [kfb])
                        self.stt(a[:], kf[:], -C1, a[:], ALU.mult, ALU.add, [kfb, ab], [ab])
                        self.stt(a[:], kf[:], -C2, a[:], ALU.mult, ALU.add, [kfb, ab], [ab])
                        self.ts(mk[:], a[:], math.pi, None, ALU.is_gt, None, [ab], [mkb])
                        self.stt(a[:], mk[:], -2 * math.pi, a[:], ALU.mult, ALU.add, [mkb, ab], [ab])
                        self.ts(mk[:], a[:], -math.pi, None, ALU.is_lt, None, [ab], [mkb])
                        self.stt(a[:], mk[:], 2 * math.pi, a[:], ALU.mult, ALU.add, [mkb, ab], [ab])
                        self.ts(a[:], a[:], 3.1415925, -3.1415925, ALU.min, ALU.max, [ab], [ab])
                        self.act(tab[:, :, which, :], a[:], AF.Sin, [ab], [tabb])
                    self.dma("gpsimd", self.ropeD[ci].ap().rearrange("(t p) c j -> p t c j", p=128), tab[:],
                             [tabb], [self.dbuf("rope", ci)], slow=True)
                    self.barrier()

    def transpose_in(self):
        T = self.T
        with ExitStack() as s2:
            xt = self.sbpool(s2, "ti_x", [128, DM], F32, 3)
            stg = self.sbpool(s2, "ti_s", [128, 8, 512], F32, 2)
            n = 0
            for g in range(T // 512):
                sg, sgb = stg[g % 2]
                for ti in range(4):
                    t0 = g * 512 + ti * 128
                    x_, xb = xt[n % 3]
                    n += 1
                    self.dma("sync", x_[:], self.x[t0:t0 + 128, :], [], [xb])
                    for half in range(2):
                        pt, pb = self.psn()
                        for q in range(4):
                            kc = half * 4 + q
                            self.tp(pt[:, q * 128:(q + 1) * 128], x_[:, kc * 128:(kc + 1) * 128], self.identf[:],
                                    [xb, self.identf_b], [pb])
                        self.cp(self.alt(), sg[:, half * 4:half * 4 + 4, ti * 128:(ti + 1) * 128],
                                pt[:].rearrange("p (q t) -> p q t", q=4), [pb], [sgb])
                self.dma("gpsimd", self.xT[:, g * 512:(g + 1) * 512].rearrange("(kc p) t -> p kc t", p=128), sg[:],
                         [sgb], [self.dbuf("xT", g)])
            self.barrier()

    def transpose_out(self):
        T = self.T
        with ExitStack() as s2:
            xg = self.sbpool(s2, "to_x", [128, 8, 512], F32, 2)
            ot = self.sbpool(s2, "to_o", [128, DM], F32, 3)
            n = 0
            for g in range(T // 512):
                x_, xb = xg[g % 2]
                self.dma("sync", x_[:], self.xT[:, g * 512:(g + 1) * 512].rearrange("(kc p) t -> p kc t", p=128),
                         [self.dbuf("xT", g)], [xb])
                for ti in range(4):
                    o_, ob = ot[n % 3]
                    n += 1
                    for half in range(2):
                        pt, pb = self.psn()
                        for q in range(4):
                            kc = half * 4 + q
                            self.tp(pt[:, q * 128:(q + 1) * 128], x_[:, kc, ti * 128:(ti + 1) * 128], self.identf[:],
                                    [xb, self.identf_b], [pb])
                        self.cp(self.alt(), o_[:, half * 512:(half + 1) * 512], pt[:], [pb], [ob])
                    t0 = g * 512 + ti * 128
                    self.dma("gpsimd", self.out[t0:t0 + 128, :], o_[:], [ob], [self.dbuf("out", t0)])
            self.barrier()


    @staticmethod
    def bc_last(a, n):
        return bass.AP(a.tensor, a.offset, [list(x) for x in a.ap] + [[0, n]])

    @staticmethod
    def bc_col(a, n):
        return bass.AP(a.tensor, a.offset, [list(a.ap[0]), [0, n]])

    @staticmethod
    def bc_mid(a, n):
        ap = [list(x) for x in a.ap]
        return bass.AP(a.tensor, a.offset, [ap[0], [0, n]] + ap[1:])

    def load_cols(self, dst, dstb, src_ap, pattern, **kw):
        self.dma("sync", dst, src_ap.rearrange(pattern, **kw), [], [dstb], slow=True)

    def load_bc(self, dst, dstb, src_row_ap):
        self.dma("sync", dst, src_row_ap.broadcast_to([128, src_row_ap.shape[-1]]), [], [dstb])

    def psb(self, i):
        return self.psbig[:, i, :].bitcast(BF16)

    def norm_fm(self, stack_unused, col0, ntok, gcols, gcolsb, outT, outTb, xkeep=None, loaded=False):
        with ExitStack() as s2:
            if xkeep is None:
                xg_pool = self.sbpool(s2, "nf_x", [128, 8, 512], F32, 2)
            sq_pool = self.sbpool(s2, "nf_sq", [128, 512], BF16, 4)
            rs_pool = self.sbpool(s2, "nf_rs", [128, 512], F32, 2)
            n = 0
            ng = ntok // 512
            bsz = 2 if loaded else 1
            for g0 in range(0, ng, bsz):
                batch = []
                for g in range(g0, min(g0 + bsz, ng)):
                    c0 = col0 + g * 512
                    if xkeep is None:
                        xg, xgb = xg_pool[g % 2]
                        self.dma("sync", xg[:], self.xT[:, c0:c0 + 512].rearrange("(kc p) t -> p kc t", p=128),
                                 [self.dbuf("xT", c0 // 512)], [xgb])
                        xs = lambda kc, xg=xg: xg[:, kc, :]
                    else:
                        xk, xgb = xkeep
                        if not loaded:
                            self.dma("sync", xk[:, :, g * 512:(g + 1) * 512],
                                     self.xT[:, c0:c0 + 512].rearrange("(kc p) t -> p kc t", p=128),
                                     [self.dbuf("xT", c0 // 512)], [xgb])
                        xs = lambda kc, g=g, xk=xk: xk[:, kc, g * 512:(g + 1) * 512]
                    pt, pb = self.psn()
                    for kc in range(8):
                        sq, sqb = sq_pool[n % 4]
                        n += 1
                        self.act(sq[:], xs(kc), AF.Square, [xgb], [sqb])
                        self.mm(pt[:], self.onesb[:], sq[:], kc == 0, kc == 7, [sqb, self.onesb_b], [pb])
                    batch.append((g, xs, xgb, pt, pb))
                for g, xs, xgb, pt, pb in batch:
                    rs, rsb = rs_pool[g % 2]
                    self.rsqrt(rs[:], pt[:], 1.0 / DM, [pb], [rsb])
                for g, xs, xgb, pt, pb in batch:
                    rs, rsb = rs_pool[g % 2]
                    for kc in range(8):
                        self.stt(outT[:, kc, g * 512:(g + 1) * 512], xs(kc), gcols[:, kc:kc + 1], rs[:],
                                 ALU.mult, ALU.mult, [xgb, rsb, gcolsb], [outTb])

    def gemm(self, mode, AT, ATb, KC, ntok, W, col0, ncols, epi, wpool, k0=0, ps_lo=0, ps_hi=8):
        wt, wtb = wpool[self.wi % len(wpool)]
        self.wi += 1
        ATl = ATb if isinstance(ATb, list) else [ATb]
        wv = wt[:, 0:KC * ncols].rearrange("p (k n) -> p k n", k=KC)
        self.dma("sync", wv, W[:, col0:col0 + ncols].rearrange("(kc p) n -> p kc n", p=128), [], [wtb])
        if mode == "tm":
            pend = []
            nt_ = ntok // 128
            for t in range(nt_):
                pt, pb = self.psn(ps_lo, ps_hi)
                for kc in range(KC):
                    self.mm(pt[:, 0:ncols], AT[:, k0 + kc, t * 128:(t + 1) * 128], wv[:, kc, :], kc == 0, kc == KC - 1,
                            ATl + [wtb], [pb])
                r = epi(pt, pb, t)
                if r is not None:
                    pend.append(r)
                if pend and (len(pend) == KGI or t == nt_ - 1):
                    while pend:
                        for g_ in list(pend):
                            try:
                                next(g_)
                            except StopIteration:
                                pend.remove(g_)
        else:
            for m in range(ncols // 128):
                for g in range(ntok // 512):
                    pt, pb = self.psn(ps_lo, ps_hi)
                    for kc in range(KC):
                        self.mm(pt[:], wv[:, kc, m * 128:(m + 1) * 128], AT[:, k0 + kc, g * 512:(g + 1) * 512],
                                kc == 0, kc == KC - 1, ATl + [wtb], [pb])
                    epi(pt, pb, m, g)

    def norm_rope_g(self, v, vb, G, Dg, t, tmp, gain=None, gainb=None, normdim=None, ss_extra=None, rope=None):
        v3 = v.rearrange("p (g d) -> p g d", g=G)
        sq, sqb, ss, ssb, rt, rtb = tmp
        if gain is not None:
            W = G * Dg
            self.tt(sq[:, 0:W], v, v, ALU.mult, [vb], [sqb])
            yield
            self.op("vector", lambda e: e.tensor_reduce(out=ss[:, 0:G], in_=sq[:, 0:W].rearrange("p (g d) -> p g d", g=G),
                                                        axis=AX.X, op=ALU.add), [sqb], [ssb])
            yield
            if ss_extra is not None:
                ex, exb = ss_extra
                self.tt(ss[:, 0:G], ss[:, 0:G], ex, ALU.add, [ssb, exb], [ssb])
                yield
            self.rsqrt(ss[:, 0:G], ss[:, 0:G], 1.0 / normdim, [ssb], [ssb])
            yield
            self.tt(v3, v3, self.bc_last(ss[:, 0:G], Dg), ALU.mult, [vb, ssb], [vb])
            yield
            self.tt(v3, v3, self.bc_mid(gain, G), ALU.mult, [vb, gainb], [vb])
            yield
        if rope is not None:
            off, r2, tab, tabb = rope
            x1 = v3[:, :, off:off + r2]
            x2 = v3[:, :, off + r2:off + 2 * r2]
            cos = self.bc_mid(tab[:, t, 0, :], G)
            sin = self.bc_mid(tab[:, t, 1, :], G)
            n = G * r2
            tv = [rt[:, i * n:(i + 1) * n].rearrange("p (g r) -> p g r", g=G) for i in range(4)]
            self.tt(tv[0], x1, cos, ALU.mult, [vb, tabb], [rtb])
            yield
            self.tt(tv[1], x2, sin, ALU.mult, [vb, tabb], [rtb])
            yield
            self.tt(tv[2], x2, cos, ALU.mult, [vb, tabb], [rtb])
            yield
            self.tt(tv[3], x1, sin, ALU.mult, [vb, tabb], [rtb])
            yield
            self.tt(x1, tv[0], tv[1], ALU.subtract, [rtb], [vb])
            yield
            self.tt(x2, tv[2], tv[3], ALU.add, [rtb], [vb])
            yield

    def norm_rope(self, *a, **kw):
        for _ in self.norm_rope_g(*a, **kw):
            pass

    def tr_stage(self, src, srcb, blocks, tq, stage, stageb):
        i = self.psi % 8
        self.psi += 1
        pb = self.ps[i][1]
        pv = self.psb(i)
        for bi, (lo, w) in enumerate(blocks):
            self.tp(pv[0:w, bi * 128:(bi + 1) * 128], src[:, lo:lo + w], self.identb[:], [srcb, self.identb_b], [pb])
        nb = len(blocks)
        if all(w == 128 for _, w in blocks):
            self.cp(self.alt(), stage[:, 0:nb, tq * 128:(tq + 1) * 128],
                    pv[:, 0:nb * 128].rearrange("p (b t) -> p b t", b=nb), [pb], [stageb])
        else:
            for bi, (lo, w) in enumerate(blocks):
                self.cp(self.alt(), stage[0:w, bi, tq * 128:(tq + 1) * 128], pv[0:w, bi * 128:(bi + 1) * 128],
                        [pb], [stageb])

    def phase1(self, l, s):
        S, NT, NG = self.S, self.NT, self.NG
        tok0 = s * S
        Wd = self.wb["w_in", l]
        with ExitStack() as P:
            xnT, xnTb = self.sb(P, "xnT", [128, 8, S], BF16)
            g1, g1b = self.sb(P, "g1", [128, 8], F32)
            self.load_cols(g1[:], g1b, self.norm1_g[l], "(kc p) -> p kc", p=128)
            self.norm_fm(None, tok0, S, g1, g1b, xnT, xnTb)
            self.barrier()
            self.mark(" P1lru")
            wpool = self.sbpool(P, "w", [128, 4096], BF16, 2)
            o16 = self.sbpool(P, "o16", [128, 512], BF16, 3)
            self.oi = 0
            with ExitStack() as PA:
                self.lru(PA, l, s, xnT, xnTb, Wd, wpool)
                self.barrier()
                self.mark(" P1qkv")
            with ExitStack() as PQ:
                self.prep_qkv(PQ, l, s, xnT, xnTb, Wd, wpool, o16)
                self.barrier()

    def lru(self, PA, l, s, xnT, xnTb, Wd, wpool):
        S, NT, NG = self.S, self.NT, self.NG
        TC = min(1024, S)
        nch = S // TC
        cw, cwb = self.sb(PA, "cw", [128, 4, 8], F32)
        cb, cbb = self.sb(PA, "cb", [128, 8], F32)
        ba, bab = self.sb(PA, "ba", [128, 2, 8], F32)
        bx, bxb = self.sb(PA, "bx", [128, 2, 8], F32)
        lam, lamb = self.sb(PA, "lam", [128, 16], F32)
        hh, hhb = self.sb(PA, "hh", [128, 16], F32)
        cd, cdb = self.sb(PA, "cd", [128, 16], F32)
        cd2, cd2b = self.sb(PA, "cd2", [128, 16], F32)
        for t_ in range(4):
            self.load_cols(cw[:, t_, :], cwb, self.conv_w[l, t_], "(c p) -> p c", p=128)
        self.load_cols(cb[:], cbb, self.conv_b[l], "(c p) -> p c", p=128)
        for d_ in range(2):
            self.load_cols(ba[:, d_, :], bab, self.lru_ba[l, d_], "(c p) -> p c", p=128)
            self.load_cols(bx[:, d_, :], bxb, self.lru_bx[l, d_], "(c p) -> p c", p=128)
            self.load_cols(lam[:, d_ * 8:(d_ + 1) * 8], lamb, self.lru_lambda[l, d_], "(c p) -> p c", p=128)
        self.act(lam[:], lam[:], AF.Exp, [lamb], [lamb], scale=-1.0)
        self.ts(hh[:], lam[:], -1.0 / 6, 1.0 / 5, ALU.mult, ALU.add, [lamb], [hhb])
        for cst in (-1.0 / 4, 1.0 / 3, -1.0 / 2, 1.0):
            self.tt(hh[:], hh[:], lam[:], ALU.mult, [hhb, lamb], [hhb])
            self.ts(hh[:], hh[:], cst, None, ALU.add, None, [hhb], [hhb])
        self.tt(hh[:], hh[:], lam[:], ALU.mult, [hhb, lamb], [hhb])
        self.ts(cd[:], hh[:], -8.0, None, ALU.mult, None, [hhb], [cdb])
        self.ts(cd2[:], hh[:], -16.0, None, ALU.mult, None, [hhb], [cd2b])
        wst, wstb = self.sb(PA, "wst", [128, 4, 128], F32)
        wbf, wbfb = self.sb(PA, "wbf", [128, 4, 128], BF16)
        self.op("vector", lambda e: e.memset(wst[:], 0.0), [], [wstb])
        Pb, Pbb = self.sb(PA, "Pb", [128, S + 4], F32)
        gg, ggb = self.sb(PA, "gg", [128, S], BF16)
        xc, xcb_ = self.sb(PA, "xc", [128, S], F32)
        x16, x16b = self.sb(PA, "x16", [128, S], BF16)
        Bsets = []
        for d_ in range(2):
            B1, B1b = self.sb(PA, "B1%d" % d_, [128, TC], F32)
            B2, B2b = self.sb(PA, "B2%d" % d_, [128, TC], F32)
            B3, B3b = self.sb(PA, "B3%d" % d_, [128, TC], F32)
            Bsets.append((B1, B1b, B2, B2b, B3, B3b))
        hb, hbb = self.sb(PA, "hb", [128, S], F32)
        tA = self.sbpool(PA, "tA", [128, 512], F32, 2)
        self.op("vector", lambda e: e.memset(Pb[:, 0:2], 0.0), [], [Pbb])
        self.op("vector", lambda e: e.memset(Pb[:, S + 2:S + 4], 0.0), [], [Pbb])
        wsrc = (self.lru_wa, self.lru_wx)
        for c in range(8):
            for wi in range(2):
                for d in range(2):
                    for half in range(2):
                        self.dma("sync", wst[half * 64:(half + 1) * 64, wi * 2 + d, half * 64:(half + 1) * 64],
                                 wsrc[wi][l, d, 2 * c + half], [], [wstb])
            self.cp("vector", wbf[:], wst[:], [wstb], [wbfb])

            def epi_x(pt, pb, m, g):
                self.cp(self.alt(), Pb[:, 2 + g * 512:2 + (g + 1) * 512], pt[:], [pb], [Pbb])

            def epi_g(pt, pb, m, g):
                t1, t1b = tA[g % 2]
                self.act(t1[:], pt[:], AF.Square, [pb], [t1b])
                self.ts(t1[:], t1[:], 0.044715, 1.0, ALU.mult, ALU.add, [t1b], [t1b])
                self.tt(t1[:], t1[:], pt[:], ALU.mult, [t1b, pb], [t1b])
                self.act(t1[:], t1[:], AF.Sigmoid, [t1b], [t1b], scale=1.5957691216057308)
                self.tt(gg[:, g * 512:(g + 1) * 512], t1[:], pt[:], ALU.mult, [t1b, pb], [ggb])

            self.gemm("fm", xnT, xnTb, 8, S, Wd, OFF["ax"] + c * 128, 128, epi_x, wpool)
            self.gemm("fm", xnT, xnTb, 8, S, Wd, OFF["ag"] + c * 128, 128, epi_g, wpool)
            self.ts(xc[:], Pb[:, 0:S], cw[:, 0, c:c + 1], cb[:, c:c + 1], ALU.mult, ALU.add, [Pbb, cwb, cbb], [xcb_])
            for j in range(1, 4):
                self.stt(xc[:], Pb[:, j:j + S], cw[:, j, c:c + 1], xc[:], ALU.mult, ALU.add, [Pbb, cwb, xcb_], [xcb_])
            self.cp("scalar", x16[:], xc[:], [xcb_], [x16b])
            hs = Pb

            def dgen(d, c=c):
                D1, D1b, D2, D2b, D3, D3b = Bsets[d]
                order = range(nch) if d == 0 else range(nch - 1, -1, -1)
                first = True
                for ch in order:
                    t0 = ch * TC
                    for sub in range(TC // 512):
                        cs = slice(t0 + sub * 512, t0 + (sub + 1) * 512)
                        bs = slice(sub * 512, (sub + 1) * 512)
                        pt, pb = self.psn()
                        self.mm(pt[:], wbf[:, 0 + d, :], x16[:, cs], True, True, [wbfb, x16b], [pb])
                        self.act(D1[:, bs], pt[:], AF.Sigmoid, [pb, bab], [D1b], bias=ba[:, d, c:c + 1])
                        yield
                        pt, pb = self.psn()
                        self.mm(pt[:], wbf[:, 2 + d, :], x16[:, cs], True, True, [wbfb, x16b], [pb])
                        self.act(D3[:, bs], pt[:], AF.Sigmoid, [pb, bxb], [D3b], bias=bx[:, d, c:c + 1])
                        yield
                    k = d * 8 + c
                    self.act(D2[:], D1[:], AF.Exp, [D1b, cd2b], [D2b], scale=cd2[:, k:k + 1])
                    yield
                    self.act(D2[:], D2[:], AF.Relu, [D2b], [D2b], scale=-1.0, bias=self.onecol[:, :])
                    yield
                    self.act(D2[:], D2[:], AF.Sqrt, [D2b], [D2b])
                    yield
                    self.act(D1[:], D1[:], AF.Exp, [D1b, cdb], [D1b], scale=cd[:, k:k + 1])
                    yield
                    self.tt(D3[:], D3[:], xc[:, t0:t0 + TC], ALU.mult, [D3b, xcb_], [D3b])
                    yield
                    self.tt(D3[:], D3[:], D2[:], ALU.mult, [D3b, D2b], [D3b])
                    yield
                    if d == 0:
                        init = 0.0 if first else hs[:, 2 + t0 - 1:2 + t0]
                        self.op("vector", lambda e, t0=t0, init=init: e.tensor_tensor_scan(
                            out=hs[:, 2 + t0:2 + t0 + TC], data0=D1[:], data1=D3[:], initial=init,
                            op0=ALU.mult, op1=ALU.add), [D1b, D3b, Pbb], [Pbb])
                    else:
                        init = 0.0 if first else hb[:, t0 + TC:t0 + TC + 1]
                        ov = hb[:, t0:t0 + TC]
                        orev = bass.AP(ov.tensor, ov.offset + (TC - 1), [list(ov.ap[0]), [-1, TC]])
                        self.op("vector", lambda e, init=init, orev=orev: e.tensor_tensor_scan(
                            out=orev, data0=D1[:, ::-1], data1=D3[:, ::-1], initial=init,
                            op0=ALU.mult, op1=ALU.add), [D1b, D3b, hbb], [hbb])
                    yield
                    first = False

            gens = [dgen(0), dgen(1)]
            while gens:
                for g_ in list(gens):
                    try:
                        next(g_)
                    except StopIteration:
                        gens.remove(g_)
            self.tt(hs[:, 2:S + 2], hs[:, 2:S + 2], hb[:], ALU.add, [Pbb, hbb], [Pbb])
            self.tt(x16[:], hs[:, 2:S + 2], gg[:], ALU.mult, [Pbb, ggb, x16b], [x16b])
            self.dma("gpsimd", self.YT[c * 128:(c + 1) * 128, :], x16[:], [x16b], [self.dbuf("YT", 0, c)])


    def prep_qkv(self, PQ, l, s, xnT, xnTb, Wd, wpool, o16):
        S, NT, NG = self.S, self.NT, self.NG
        def bct(name, src, n):
            t, b = self.sb(PQ, name, [128, n], F32)
            self.load_bc(t[:], b, src)
            return t, b
        qg_b, qg_bb = bct("dqg", self.diff_q_g[l:l + 1, :], 64)
        kg_b, kg_bb = bct("dkg", self.diff_k_g[l:l + 1, :], 64)
        qa_g, qa_gb = bct("mqa", self.mla_qa_g[l:l + 1, :], 384)
        kva_g, kva_gb = bct("mkva", self.mla_kva_g[l:l + 1, :], 256)
        mq_g, mq_gb = bct("mqg", self.mla_q_g[l:l + 1, :], 192)
        mk_g, mk_gb = bct("mkg", self.mla_k_g[l:l + 1, :], 192)
        gb, gbb = self.sb(PQ, "gateb", [128, 4, 8], F32)
        for b_ in range(4):
            self.load_cols(gb[:, b_, :], gbb, self.gate_b[l, b_], "(m p) -> p m", p=128)
        ropes = []
        for ci, r2 in enumerate((8, 32, 32)):
            t, b = self.sb(PQ, "rope%d" % ci, [128, NT, 2, r2], F32)
            self.dma("sync", t[:], self.ropeD[ci].ap().rearrange("(t p) c j -> p t c j", p=128),
                     [self.dbuf("rope", ci)], [b], slow=True)
            ropes.append((t, b))
        cqnT, cqnTb = self.sb(PQ, "cqnT", [128, 3, S], BF16)
        ckvnT, ckvnTb = self.sb(PQ, "ckvnT", [128, 2, S], BF16)
        kper, kperb = self.sb(PQ, "kper", [128, NT, 64], F32)
        sspe, sspeb = self.sb(PQ, "sspe", [128, NT], F32)
        NV = KDEFER + 3
        vpool = self.sbpool(PQ, "v", [128, 512], F32, NV)
        v16pool = self.sbpool(PQ, "v16", [128, 512], BF16, NV)
        dq = []

        def defer(fn):
            dq.append(fn)
            while len(dq) > KDEFER:
                dq.pop(0)()

        def flush():
            while dq:
                dq.pop(0)()
        tmps = []
        for i_ in range(2):
            sq_, sqb_ = self.sb(PQ, "sq%d" % i_, [128, 512], F32)
            ss_, ssb_ = self.sb(PQ, "ss%d" % i_, [128, 8], F32)
            rt_, rtb_ = self.sb(PQ, "rt%d" % i_, [128, 1024], F32)
            tmps.append((sq_, sqb_, ss_, ssb_, rt_, rtb_))
        tmp = tmps[0]
        sq, sqb, ss, ssb, rt, rtb = tmp
        stages = self.sbpool(PQ, "stg", [128, 4, 512], BF16, 2)
        st_i = [0]
        vi = [0]

        def nextv():
            r = vpool[vi[0] % NV] + v16pool[vi[0] % NV]
            vi[0] += 1
            return r

        def store_stage(stage, stageb, dests, g):
            for bi, (dt_, r0, nr) in enumerate(dests):
                self.dma("gpsimd", dt_[r0:r0 + nr, g * 512:(g + 1) * 512], stage[0:nr, bi, :], [stageb],
                         [self.dbuf(dt_.name, r0, g)])

        def seg_qk(AT, ATb, KC, W, col0, ncols, G, Dg, gain, gainb, normdim, rope, blocks, dests_fn, scale=None):
            state = {}

            def epi(pt, pb, t):
                v, vb, v16, v16b = nextv()
                self.cp("scalar", v[:, 0:ncols], pt[:, 0:ncols], [pb], [vb])
                yield
                for _ in self.norm_rope_g(v[:, 0:ncols], vb, G, Dg, t, tmps[t % 2], gain=gain, gainb=gainb,
                                          normdim=normdim, rope=rope):
                    yield
                if scale is None:
                    self.cp("scalar", v16[:, 0:ncols], v[:, 0:ncols], [vb], [v16b])
                else:
                    self.act(v16[:, 0:ncols], v[:, 0:ncols], AF.Copy, [vb], [v16b], scale=scale)
                def later(t=t, v16=v16, v16b=v16b):
                    if t % 4 == 0:
                        state["st"] = stages[st_i[0] % 2]
                        st_i[0] += 1
                    stage, stageb = state["st"]
                    self.tr_stage(v16, v16b, blocks, t % 4, stage, stageb)
                    if t % 4 == 3:
                        store_stage(stage, stageb, dests_fn(), t // 4)
                defer(later)
            self.gemm("tm", AT, ATb, KC, S, W, col0, ncols, epi, wpool)

        def seg_v(col0, dst, dcol0):
            def epi(pt, pb, t):
                o, ob = o16[self.oi % 3]
                self.oi += 1
                self.cp(self.alt(), o[:], pt[:], [pb], [ob])
                self.dma("gpsimd", dst[t * 128:(t + 1) * 128, dcol0:dcol0 + 512], o[:], [ob],
                         [self.dbuf(dst.name, t, dcol0)])
            self.gemm("tm", xnT, xnTb, 8, S, Wd, col0, 512, epi, wpool)

        b4 = [(i * 128, 128) for i in range(4)]
        for nt in range(2):
            seg_qk(xnT, xnTb, 8, Wd, OFF["bq"] + nt * 512, 512, 8, 64, qg_b[:], qg_bb, 64,
                   (0, 8, ropes[0][0], ropes[0][1]), b4,
                   lambda nt=nt: [(self.QTb, nt * 512 + i * 128, 128) for i in range(4)])
            seg_qk(xnT, xnTb, 8, Wd, OFF["bk"] + nt * 512, 512, 8, 64, kg_b[:], kg_bb, 64,
                   (0, 8, ropes[0][0], ropes[0][1]), b4,
                   lambda nt=nt: [(self.KTb, nt * 512 + i * 128, 128) for i in range(4)])
            seg_v(OFF["bv"] + nt * 512, self.Vb, nt * 512)
            seg_v(OFF["dv"] + nt * 512, self.Vd, nt * 512)
        self.mark("  q:dqdk")
        seg_qk(xnT, xnTb, 8, Wd, OFF["dq"], 512, 8, 64, None, None, None, (0, 32, ropes[2][0], ropes[2][1]), b4,
               lambda: [(self.QTd, i * 128, 128) for i in range(4)])
        seg_qk(xnT, xnTb, 8, Wd, OFF["dk"], 512, 8, 64, None, None, None, (0, 32, ropes[2][0], ropes[2][1]), b4,
               lambda: [(self.KTd, i * 128, 128) for i in range(4)], scale=0.125)

        self.mark("  q:cq")
        def epi_cq(pt, pb, t):
            v, vb, v16, v16b = nextv()
            self.cp("scalar", v[:, 0:384], pt[:, 0:384], [pb], [vb])
            yield
            for _ in self.norm_rope_g(v[:, 0:384], vb, 1, 384, t, tmps[t % 2], gain=qa_g[:], gainb=qa_gb, normdim=384):
                yield
            self.cp("scalar", v16[:, 0:384], v[:, 0:384], [vb], [v16b])

            def later(t=t, v16=v16, v16b=v16b):
                i = self.psi % 8
                self.psi += 1
                pv = self.psb(i)
                for bi in range(3):
                    self.tp(pv[:, bi * 128:(bi + 1) * 128], v16[:, bi * 128:(bi + 1) * 128], self.identb[:],
                            [v16b, self.identb_b], [self.ps[i][1]])
                self.cp(self.alt(), cqnT[:, :, t * 128:(t + 1) * 128], pv[:, 0:384].rearrange("p (b t) -> p b t", b=3),
                        [self.ps[i][1]], [cqnTb])
            defer(later)
        self.gemm("tm", xnT, xnTb, 8, S, Wd, OFF["cq"], 384, epi_cq, wpool)

        def epi_ckv(pt, pb, t):
            v, vb, v16, v16b = nextv()
            sq, sqb, ss, ssb, rt, rtb = tmps[t % 2]
            self.cp("scalar", v[:, 0:320], pt[:, 0:320], [pb], [vb])
            yield
            self.tt(sq[:, 0:64], v[:, 256:320], v[:, 256:320], ALU.mult, [vb], [sqb])
            yield
            self.op("vector", lambda e: e.tensor_reduce(out=sspe[:, t:t + 1], in_=sq[:, 0:64], axis=AX.X, op=ALU.add),
                    [sqb], [sspeb])
            yield
            self.tt(v[:, 256:320], v[:, 256:320], mk_g[:, 128:192], ALU.mult, [vb, mk_gb], [vb])
            yield
            for _ in self.norm_rope_g(v[:, 256:320], vb, 1, 64, t, tmps[t % 2], rope=(0, 32, ropes[1][0], ropes[1][1])):
                yield
            self.cp("vector", kper[:, t, :], v[:, 256:320], [vb], [kperb])
            yield
            for _ in self.norm_rope_g(v[:, 0:256], vb, 1, 256, t, tmps[t % 2], gain=kva_g[:], gainb=kva_gb, normdim=256):
                yield
            self.cp("scalar", v16[:, 0:256], v[:, 0:256], [vb], [v16b])

            def later(t=t, v16=v16, v16b=v16b):
                i = self.psi % 8
                self.psi += 1
                pv = self.psb(i)
                for bi in range(2):
                    self.tp(pv[:, bi * 128:(bi + 1) * 128], v16[:, bi * 128:(bi + 1) * 128], self.identb[:],
                            [v16b, self.identb_b], [self.ps[i][1]])
                self.cp(self.alt(), ckvnT[:, :, t * 128:(t + 1) * 128], pv[:, 0:256].rearrange("p (b t) -> p b t", b=2),
                        [self.ps[i][1]], [ckvnTb])
            defer(later)
        self.gemm("tm", xnT, xnTb, 8, S, Wd, OFF["ckv"], 320, epi_ckv, wpool)

        self.mark("  q:fm")
        def seg_fm(col0, func, bias_fn, dst, row0):
            def epi(pt, pb, m, g):
                o, ob = o16[self.oi % 3]
                self.oi += 1
                b = bias_fn(m)
                if b is None:
                    self.act(o[:], pt[:], func, [pb], [ob])
                else:
                    self.act(o[:], pt[:], func, [pb, gbb], [ob], bias=b)
                r0 = row0 + m * 128
                self.dma("gpsimd", dst[r0:r0 + 128, g * 512:(g + 1) * 512], o[:], [ob], [self.dbuf(dst.name, r0, g)])
            self.gemm("fm", xnT, xnTb, 8, S, Wd, col0, 512, epi, wpool)
        for nt in range(2):
            seg_fm(OFF["dg"] + nt * 512, AF.Silu, lambda m: None, self.GdT, nt * 512)
        for b_ in range(4):
            for nt in range(2):
                seg_fm(OFF["gl"] + b_ * 1024 + nt * 512, AF.Sigmoid,
                       lambda m, b_=b_, nt=nt: gb[:, b_, nt * 4 + m:nt * 4 + m + 1], self.GT, b_ * 1024 + nt * 512)

        self.mark("  q:qup")
        flush()
        Wq = self.wb["wuq", l]
        Wkv = self.wb["wukv", l]
        bq = [(0, 128), (128, 64), (192, 128), (320, 64)]
        for hp in range(4):
            seg_qk(cqnT, cqnTb, 3, Wq, hp * 384, 384, 2, 192, mq_g[:], mq_gb, 192,
                   (128, 32, ropes[1][0], ropes[1][1]), bq,
                   lambda hp=hp: [(self.QTc, (2 * hp) * 192, 128), (self.QTc, (2 * hp) * 192 + 128, 64),
                                  (self.QTc, (2 * hp + 1) * 192, 128), (self.QTc, (2 * hp + 1) * 192 + 128, 64)])
        self.mark("  q:kvup")
        for hp in range(4):
            state = {}

            def epi_kv(pt, pb, t, hp=hp, state=state):
                v, vb, v16, v16b = nextv()
                sq, sqb, ss, ssb, rt, rtb = tmps[t % 2]
                self.cp("scalar", v[:], pt[:], [pb], [vb])
                yield
                v4 = v[:].rearrange("p (h c d) -> p h c d", h=2, c=2)
                kn = v4[:, :, 0, :]
                s4 = sq[:].rearrange("p (h c d) -> p h c d", h=2, c=2)
                self.tt(s4[:, :, 0, :], kn, kn, ALU.mult, [vb], [sqb])
                yield
                self.op("vector", lambda e: e.tensor_reduce(out=ss[:, 0:2], in_=s4[:, :, 0, :], axis=AX.X, op=ALU.add),
                        [sqb], [ssb])
                yield
                self.tt(ss[:, 0:2], ss[:, 0:2], self.bc_col(sspe[:, t:t + 1], 2),
                        ALU.add, [ssb, sspeb], [ssb])
                yield
                self.rsqrt(ss[:, 0:2], ss[:, 0:2], 1.0 / 192, [ssb], [ssb])
                yield
                self.tt(kn, kn, self.bc_last(ss[:, 0:2], 128), ALU.mult, [vb, ssb], [vb])
                yield
                self.tt(kn, kn, self.bc_mid(mk_g[:, 0:128], 2), ALU.mult, [vb, mk_gb], [vb])
                yield
                v16v = v16[:, 0:256].rearrange("p (h d) -> p h d", h=2)
                self.cp("scalar", v16v, kn, [vb], [v16b])
                yield
                pe = v16[:, 256:384].rearrange("p (h d) -> p h d", h=2)
                self.tt(pe, self.bc_mid(kper[:, t, :], 2), self.bc_last(ss[:, 0:2], 64), ALU.mult,
                        [kperb, ssb], [v16b])
                yield
                o, ob = o16[self.oi % 3]
                self.oi += 1
                ov = o[:, 0:256].rearrange("p (h d) -> p h d", h=2)
                self.cp("vector", ov, v4[:, :, 1, :], [vb], [ob])
                self.dma("gpsimd", self.Vc[t * 128:(t + 1) * 128, hp * 256:(hp + 1) * 256], o[:, 0:256], [ob],
                         [self.dbuf("Vc", t, hp)])
                def later(t=t, v16=v16, v16b=v16b):
                    if t % 4 == 0:
                        state["st"] = stages[st_i[0] % 2]
                        st_i[0] += 1
                    stage, stageb = state["st"]
                    self.tr_stage(v16, v16b, [(0, 128), (128, 128), (256, 64), (320, 64)], t % 4, stage, stageb)
                    if t % 4 == 3:
                        store_stage(stage, stageb, [(self.KTc, (2 * hp) * 192, 128), (self.KTc, (2 * hp + 1) * 192, 128),
                                                    (self.KTc, (2 * hp) * 192 + 128, 64),
                                                    (self.KTc, (2 * hp + 1) * 192 + 128, 64)], t // 4)
                defer(later)
            self.gemm("tm", ckvnT, ckvnTb, 2, S, Wkv, hp * 512, 512, epi_kv, wpool)
        flush()


    def load_rows(self, dst, dstb, src, r0, nr):
        self.dma("sync", dst[0:nr, :], src[r0:r0 + nr, :], [], [dstb])

    def load_v(self, dst, dstb, src, c0):
        self.dma("sync", dst[:], src[:, c0:c0 + 128].rearrange("(t p) e -> p t e", p=128), [], [dstb])

    def phaseB(self, l, s):
        S, NT, NG = self.S, self.NT, self.NG
        lam_init = 0.8 - 0.6 * math.exp(-0.3 * l)
        with ExitStack() as P:
            lp, lpb = self.sb(P, "lp", [128, 256], F32)
            self.load_bc(lp[:], lpb, self.diff_lam[l:l + 1].rearrange("a b c -> a (b c)"))
            pr, prb = self.sb(P, "pr", [128, 128], F32)
            e2, e2b = self.sb(P, "e2", [128, 2], F32)
            neglam, neglamb = self.sb(P, "neglam", [128, 1], F32)
            gcol, gcolb = self.sb(P, "gcol", [128, 1], F32)
            lp4 = lp[:].rearrange("p (a b c) -> p a b c", a=2, b=2)
            self.tt(pr[:].rearrange("p (a c) -> p a c", a=2), lp4[:, :, 0, :], lp4[:, :, 1, :], ALU.mult, [lpb], [prb])
            self.op("vector", lambda e: e.tensor_reduce(out=e2[:], in_=pr[:].rearrange("p (a c) -> p a c", a=2),
                                                        axis=AX.X, op=ALU.add), [prb], [e2b])
            self.act(e2[:], e2[:], AF.Exp, [e2b], [e2b])
            self.tt(neglam[:], e2[:, 1:2], e2[:, 0:1], ALU.subtract, [e2b], [neglamb])
            self.ts(neglam[:], neglam[:], -lam_init, None, ALU.add, None, [neglamb], [neglamb])
            self.load_cols(gcol[:], gcolb, self.diff_sub_g[l], "(p o) -> p o", o=1)
            self.ts(gcol[:], gcol[:], 1.0 - lam_init, None, ALU.mult, None, [gcolb], [gcolb])
            QT = self.sbpool(P, "QT", [128, S], BF16, 2)
            KT = self.sbpool(P, "KT", [128, S], BF16, 2)
            V = self.sbpool(P, "V", [128, NT, 128], BF16, 2)
            Pt = self.sbpool(P, "Pt", [128, 512], BF16, 4)
            tf = self.sbpool(P, "tf", [128, 512], F32, 6)
            y16 = self.sbpool(P, "y16", [128, 512], BF16, 2)
            pi = 0
            for h in range(8):
                q, qb = QT[h % 2]
                k, kb = KT[h % 2]
                v, vb = V[h % 2]
                self.load_rows(q, qb, self.QTb, h * 128, 128)
                self.load_rows(k, kb, self.KTb, h * 128, 128)
                self.load_v(v, vb, self.Vb, h * 128)
                for qg in range(NG):
                    qs = slice(qg * 512, (qg + 1) * 512)
                    O = (self.ps[0], self.ps[1])
                    Lp = (self.ps[2], self.ps[3])
                    for kt in range(NT):
                        ks = slice(kt * 128, (kt + 1) * 128)
                        for si in range(2):
                            lo, hi = si * 64, si * 64 + 64
                            sp, spb = self.psn(4, 8)
                            self.mm(sp[:], k[lo:hi, ks], q[lo:hi, qs], True, True, [kb, qb], [spb])
                            p_, p_b = Pt[pi % 4]
                            pi += 1
                            self.act(p_[:], sp[:], AF.Exp, [spb], [p_b], scale=0.125)
                            self.mm(O[si][0][:], v[:, kt, :], p_[:], kt == 0, kt == NT - 1, [vb, p_b], [O[si][1]])
                            self.mm(Lp[si][0][:], self.onesb[:], p_[:], kt == 0, kt == NT - 1, [self.onesb_b, p_b],
                                    [Lp[si][1]])
                    o1, o1b = tf[0]
                    o2, o2b = tf[1]
                    r1, r1b = tf[2]
                    r2, r2b = tf[3]
                    oo, oob = tf[4]
                    rs, rsb = tf[5]
                    self.cp("scalar", o1[:], O[0][0][:], [O[0][1]], [o1b])
                    self.cp("vector", o2[:], O[1][0][:], [O[1][1]], [o2b])
                    self.act(r1[:], Lp[0][0][:], AF.Ln, [Lp[0][1]], [r1b])
                    self.act(r2[:], Lp[1][0][:], AF.Ln, [Lp[1][1]], [r2b])
                    self.act(r1[:], r1[:], AF.Exp, [r1b], [r1b], scale=-1.0)
                    self.act(r2[:], r2[:], AF.Exp, [r2b], [r2b], scale=-1.0)
                    self.tt(o1[:], o1[:], r1[:], ALU.mult, [o1b, r1b], [o1b])
                    self.tt(o2[:], o2[:], r2[:], ALU.mult, [o2b, r2b], [o2b])
                    self.stt(oo[:], o2[:], neglam[:, 0:1], o1[:], ALU.mult, ALU.add, [o2b, o1b, neglamb], [oob])
                    sq, sqb = Pt[pi % 4]
                    pi += 1
                    self.act(sq[:], oo[:], AF.Square, [oob], [sqb])
                    sp, spb = self.psn(4, 8)
                    self.mm(sp[:], self.onesb[:], sq[:], True, True, [self.onesb_b, sqb], [spb])
                    self.rsqrt(rs[:], sp[:], 1.0 / 128, [spb], [rsb])
                    y, yb = y16[(h * NG + qg) % 2]
                    self.stt(y[:], oo[:], gcol[:, 0:1], rs[:], ALU.mult, ALU.mult, [oob, rsb, gcolb], [yb])
                    self.dma("gpsimd", self.YT[1024 + h * 128:1024 + (h + 1) * 128, qs], y[:], [yb],
                             [self.dbuf("YT", 1, h, qg)])
            self.barrier()

    def phaseC(self, l, s):
        S, NT, NG = self.S, self.NT, self.NG
        sc = 192.0 ** -0.5
        with ExitStack() as P:
            Qn = self.sbpool(P, "Qn", [128, S], BF16, 2)
            Qp = self.sbpool(P, "Qp", [64, S], BF16, 2)
            Kn = self.sbpool(P, "Kn", [128, S], BF16, 2)
            Kp = self.sbpool(P, "Kp", [64, S], BF16, 2)
            V = self.sbpool(P, "V", [128, NT, 128], BF16, 2)
            Pt = self.sbpool(P, "Pt", [128, 512], BF16, 4)
            tf = self.sbpool(P, "tf", [128, 512], F32, 4)
            y16 = self.sbpool(P, "y16", [128, 512], BF16, 2)
            pi = 0
            ti = 0
            for h in range(8):
                qn, qnb = Qn[h % 2]
                qp, qpb = Qp[h % 2]
                kn, knb = Kn[h % 2]
                kp, kpb = Kp[h % 2]
                v, vb = V[h % 2]
                self.load_rows(qn, qnb, self.QTc, h * 192, 128)
                self.load_rows(qp, qpb, self.QTc, h * 192 + 128, 64)
                self.load_rows(kn, knb, self.KTc, h * 192, 128)
                self.load_rows(kp, kpb, self.KTc, h * 192 + 128, 64)
                self.load_v(v, vb, self.Vc, h * 128)
                for qg in range(NG):
                    qs = slice(qg * 512, (qg + 1) * 512)
                    O, Ob = self.ps[qg % 2 * 2]
                    Lt, Lb = self.ps[qg % 2 * 2 + 1]
                    for kt in range(NT):
                        ks = slice(kt * 128, (kt + 1) * 128)
                        sp, spb = self.psn(4, 8)
                        self.mm(sp[:], kn[:, ks], qn[:, qs], True, False, [knb, qnb], [spb])
                        self.mm(sp[:], kp[:, ks], qp[:, qs], False, True, [kpb, qpb], [spb])
                        p_, p_b = Pt[pi % 4]
                        pi += 1
                        self.act(p_[:], sp[:], AF.Exp, [spb], [p_b], scale=sc)
                        self.mm(O[:], v[:, kt, :], p_[:], kt == 0, kt == NT - 1, [vb, p_b], [Ob])
                        self.mm(Lt[:], self.onesb[:], p_[:], kt == 0, kt == NT - 1, [self.onesb_b, p_b], [Lb])
                    o1, o1b = tf[ti % 4]
                    r1, r1b = tf[(ti + 1) % 4]
                    ti += 2
                    self.cp("vector", o1[:], O[:], [Ob], [o1b])
                    self.act(r1[:], Lt[:], AF.Ln, [Lb], [r1b])
                    self.act(r1[:], r1[:], AF.Exp, [r1b], [r1b], scale=-1.0)
                    y, yb = y16[(h * NG + qg) % 2]
                    self.tt(y[:], o1[:], r1[:], ALU.mult, [o1b, r1b], [yb])
                    self.dma("gpsimd", self.YT[2048 + h * 128:2048 + (h + 1) * 128, qs], y[:], [yb],
                             [self.dbuf("YT", 2, h, qg)])
            self.barrier()

    def phaseD(self, l, s):
        S, NT, NG = self.S, self.NT, self.NG
        mmax = max(4 * (NG - 1), 1)
        nneg = max(NT - 4, 1)
        with ExitStack() as P:
            gncol, gncolb = self.sb(P, "gncol", [128, 1], F32)
            self.load_cols(gncol[:], gncolb, self.ret_gn_g[l], "(p o) -> p o", o=1)
            ii, iib = self.sb(P, "ii", [128, 512], I32)
            J, Jb = self.sb(P, "J", [128, 512], F32)
            Jr, Jrb = self.sb(P, "Jr", [128, 512], F32)
            A, Ab = self.sb(P, "A", [128, 4, 512], F32)
            pbase, pbaseb = self.sb(P, "pbase", [128, mmax], F32)
            nbase, nbaseb = self.sb(P, "nbase", [128, nneg], F32)
            self.op("gpsimd", lambda e: e.iota(ii[:], pattern=[[1, 512]], base=0, channel_multiplier=0), [], [iib])
            self.cp("vector", J[:], ii[:], [iib], [Jb])
            self.ts(Jr[:], J[:], -1.0, 511.0, ALU.mult, ALU.add, [Jb], [Jrb])
            for m in range(4):
                self.op("gpsimd", lambda e, m=m: e.iota(ii[:], pattern=[[1, 512]], base=-128 * m, channel_multiplier=-1),
                        [iib], [iib])
                self.cp("vector", A[:, m, :], ii[:], [iib], [Ab])
            self.act(A[:], A[:], AF.Abs, [Ab], [Ab])
            self.op("gpsimd", lambda e: e.iota(ii[:, 0:mmax], pattern=[[128, mmax]], base=128, channel_multiplier=-1),
                    [iib], [iib])
            self.cp("vector", pbase[:], ii[:, 0:mmax], [iib], [pbaseb])
            self.op("gpsimd", lambda e: e.iota(ii[:, 0:nneg], pattern=[[128, nneg]], base=1, channel_multiplier=1),
                    [iib], [iib])
            self.cp("vector", nbase[:], ii[:, 0:nneg], [iib], [nbaseb])
            rowp, rowpb = self.sb(P, "rowp", [128, 512], F32)
            rown, rownb = self.sb(P, "rown", [128, 512], F32)
            Dt, Dtb = self.sb(P, "Dt", [128, 4, 512], F32)
            cfp, cfpb = self.sb(P, "cfp", [128, mmax], F32)
            cfn, cfnb = self.sb(P, "cfn", [128, nneg], F32)
            QT = self.sbpool(P, "QT", [64, S], BF16, 2)
            KT = self.sbpool(P, "KT", [64, S], BF16, 2)
            V = self.sbpool(P, "V", [128, NT, 128], BF16, 2)
            Pt = self.sbpool(P, "Pt", [128, 512], BF16, 4)
            tf = self.sbpool(P, "tf", [128, 512], F32, 6)
            sgp = self.sbpool(P, "sg", [128, 512], BF16, 2)
            y16 = self.sbpool(P, "y16", [128, 512], BF16, 2)
            pi = 0
            for h in range(8):
                lg = math.log1p(-2.0 ** (-5.0 - h))
                self.act(rowp[:], J[:], AF.Exp, [Jb], [rowpb], scale=lg)
                self.act(rown[:], Jr[:], AF.Exp, [Jrb], [rownb], scale=lg)
                self.act(Dt[:], A[:], AF.Exp, [Ab], [Dtb], scale=lg)
                self.act(cfp[:], pbase[:], AF.Exp, [pbaseb], [cfpb], scale=lg)
                self.act(cfn[:], nbase[:], AF.Exp, [nbaseb], [cfnb], scale=lg)
                q, qb = QT[h % 2]
                k, kb = KT[h % 2]
                v, vb = V[h % 2]
                self.load_rows(q, qb, self.QTd, h * 64, 64)
                self.load_rows(k, kb, self.KTd, h * 64, 64)
                self.load_v(v, vb, self.Vd, h * 128)
                for qg in range(NG):
                    qs = slice(qg * 512, (qg + 1) * 512)
                    O, Ob = self.ps[qg % 2]
                    sg, sgb = sgp[qg % 2]
                    self.dma("sync", sg[:], self.GdT[h * 128:(h + 1) * 128, qs], [], [sgb])
                    for kt in range(NT):
                        ks = slice(kt * 128, (kt + 1) * 128)
                        sp, spb = self.psn(4, 8)
                        self.mm(sp[:], k[:, ks], q[:, qs], True, True, [kb, qb], [spb])
                        p_, p_b = Pt[pi % 4]
                        pi += 1
                        m = 4 * qg - kt
                        if m >= 1:
                            self.stt(p_[:], sp[:], cfp[:, m - 1:m], rowp[:], ALU.mult, ALU.mult, [spb, cfpb, rowpb], [p_b])
                        elif m <= -4:
                            self.stt(p_[:], sp[:], cfn[:, -m - 4:-m - 3], rown[:], ALU.mult, ALU.mult,
                                     [spb, cfnb, rownb], [p_b])
                        else:
                            self.tt(p_[:], sp[:], Dt[:, -m, :], ALU.mult, [spb, Dtb], [p_b])
                        self.mm(O[:], v[:, kt, :], p_[:], kt == 0, kt == NT - 1, [vb, p_b], [Ob])
                    o1, o1b = tf[0]
                    s1, s1b = tf[1]
                    mn, mnb = tf[2]
                    vr, vrb = tf[3]
                    self.cp("scalar", o1[:], O[:], [Ob], [o1b])
                    self.act(s1[:], O[:], AF.Square, [Ob], [s1b])
                    mp, mpb = self.ps[2]
                    spp, sppb = self.ps[3]
                    self.mm(mp[:], self.onesf[:], o1[:], True, True, [self.onesf_b, o1b], [mpb])
                    self.mm(spp[:], self.onesf[:], s1[:], True, True, [self.onesf_b, s1b], [sppb])
                    self.act(mn[:], mp[:], AF.Copy, [mpb], [mnb], scale=1.0 / 128)
                    self.tt(vr[:], mn[:], mn[:], ALU.mult, [mnb], [vrb])
                    self.stt(vr[:], spp[:], 1.0 / 128, vr[:], ALU.mult, ALU.subtract, [sppb, vrb], [vrb])
                    self.rsqrt(vr[:], vr[:], 1.0, [vrb], [vrb])
                    self.tt(o1[:], o1[:], mn[:], ALU.subtract, [o1b, mnb], [o1b])
                    self.stt(o1[:], o1[:], gncol[:, 0:1], vr[:], ALU.mult, ALU.mult, [o1b, vrb, gncolb], [o1b])
                    y, yb = y16[(h * NG + qg) % 2]
                    self.tt(y[:], o1[:], sg[:], ALU.mult, [o1b, sgb], [yb])
                    self.dma("gpsimd", self.YT[3072 + h * 128:3072 + (h + 1) * 128, qs], y[:], [yb],
                             [self.dbuf("YT", 3, h, qg)])
            self.barrier()


    def pipeline(self, steps, depth, stage1, stage2):
        n = len(steps)
        self.hooks = {}
        self.pit = 0
        for i in range(n + depth):
            self.pit = i
            if i < n:
                stage1(steps[i], i)
            if i >= depth:
                stage2(steps[i - depth], i - depth)
            for fn in self.hooks.pop(i, []):
                fn()
        for k in sorted(self.hooks):
            for fn in self.hooks[k]:
                fn()
        self.hooks = {}

    def later(self, delay, fn):
        self.hooks.setdefault(self.pit + delay, []).append(fn)

    def lsum_mm(self, accs, ps_t, ps_b):
        for j, (a, ab) in enumerate(accs):
            self.mm(ps_t[:], self.onesf[:], a[:], j == 0, j == len(accs) - 1, [self.onesf_b, ab], [ps_b])

    def phaseB2(self, l, s):
        S, NT, NG = self.S, self.NT, self.NG
        lam_init = 0.8 - 0.6 * math.exp(-0.3 * l)
        with ExitStack() as P:
            lp, lpb = self.sb(P, "lp", [128, 256], F32)
            self.load_bc(lp[:], lpb, self.diff_lam[l:l + 1].rearrange("a b c -> a (b c)"))
            pr, prb = self.sb(P, "pr", [128, 128], F32)
            e2, e2b = self.sb(P, "e2", [128, 2], F32)
            neglam, neglamb = self.sb(P, "neglam", [128, 1], F32)
            gcol, gcolb = self.sb(P, "gcol", [128, 1], F32)
            lp4 = lp[:].rearrange("p (a b c) -> p a b c", a=2, b=2)
            self.tt(pr[:].rearrange("p (a c) -> p a c", a=2), lp4[:, :, 0, :], lp4[:, :, 1, :], ALU.mult, [lpb], [prb])
            self.op("vector", lambda e: e.tensor_reduce(out=e2[:], in_=pr[:].rearrange("p (a c) -> p a c", a=2),
                                                        axis=AX.X, op=ALU.add), [prb], [e2b])
            self.act(e2[:], e2[:], AF.Exp, [e2b], [e2b])
            self.tt(neglam[:], e2[:, 1:2], e2[:, 0:1], ALU.subtract, [e2b], [neglamb])
            self.ts(neglam[:], neglam[:], -lam_init, None, ALU.add, None, [neglamb], [neglamb])
            self.load_cols(gcol[:], gcolb, self.diff_sub_g[l], "(p o) -> p o", o=1)
            self.ts(gcol[:], gcol[:], 1.0 - lam_init, None, ALU.mult, None, [gcolb], [gcolb])
            QT = self.sbpool(P, "QT", [128, S], BF16, 2)
            KT = self.sbpool(P, "KT", [128, S], BF16, 2)
            V = self.sbpool(P, "V", [128, NT, 128], BF16, 2)
            NP = 6
            Pt = self.sbpool(P, "Pt", [128, 512], BF16, NP)
            tf = self.sbpool(P, "tf", [128, 512], F32, 6)
            sq16 = self.sbpool(P, "sq16", [128, 512], BF16, 2)
            y16 = self.sbpool(P, "y16", [128, 512], BF16, 2)
            Lacc = [[self.sb(P, "Lacc%d%d" % (a, b), [128, 512], F32) for b in range(2)] for a in range(2)]
            leng = (LENG0, LENG1)

            def loads(h):
                self.load_rows(QT[h % 2][0], QT[h % 2][1], self.QTb, h * 128, 128)
                self.load_rows(KT[h % 2][0], KT[h % 2][1], self.KTb, h * 128, 128)
                self.load_v(V[h % 2][0], V[h % 2][1], self.Vb, h * 128)

            steps = [(h, qg, kt, si) for h in range(8) for qg in range(NG) for kt in range(NT) for si in range(2)]
            sbank = {}
            loads(0)

            def stage1(st, i):
                h, qg, kt, si = st
                q, qb = QT[h % 2]
                k, kb = KT[h % 2]
                par = (h * NG + qg) % 2
                lo, hi = si * 64, si * 64 + 64
                sp, spb = self.psn(4, 8)
                self.mm(sp[:], k[lo:hi, kt * 128:(kt + 1) * 128], q[lo:hi, qg * 512:(qg + 1) * 512], True, True,
                        [kb, qb], [spb])
                p_, p_b = Pt[i % NP]
                self.act(p_[:], sp[:], AF.Exp, [spb], [p_b], scale=0.125)
                a, ab = Lacc[par][si]
                if kt == 0:
                    self.cp(leng[si], a[:], p_[:], [p_b], [ab])
                else:
                    self.tt(a[:], a[:], p_[:], ALU.add, [ab, p_b], [ab], en=leng[si])

            def stage2(st, i):
                h, qg, kt, si = st
                if qg == 0 and kt == 0 and si == 0 and h + 1 < 8:
                    loads(h + 1)
                v, vb = V[h % 2]
                par = (h * NG + qg) % 2
                O, Ob = self.ps[par * 2 + si]
                p_, p_b = Pt[i % NP]
                self.mm(O[:], v[:, kt, :], p_[:], kt == 0, kt == NT - 1, [vb, p_b], [Ob])
                if kt == NT - 1 and si == 1:
                    epilogue(h, qg, par)

            def epilogue(h, qg, par):
                qs = slice(qg * 512, (qg + 1) * 512)
                O0, O0b = self.ps[par * 2]
                O1, O1b = self.ps[par * 2 + 1]
                o1, o1b = tf[0]
                o2, o2b = tf[1]
                r1, r1b = tf[2]
                r2, r2b = tf[3]
                oo, oob = tf[4]
                rs, rsb = tf[5]
                self.cp("scalar", o1[:], O0[:], [O0b], [o1b])
                self.cp("vector", o2[:], O1[:], [O1b], [o2b])
                for si, (r, rb) in enumerate(((r1, r1b), (r2, r2b))):
                    lp_, lpb_ = self.psn(4, 8)
                    self.lsum_mm([Lacc[par][si]], lp_, lpb_)
                    self.act(r[:], lp_[:], AF.Ln, [lpb_], [rb])
                    self.act(r[:], r[:], AF.Exp, [rb], [rb], scale=-1.0)
                self.tt(o1[:], o1[:], r1[:], ALU.mult, [o1b, r1b], [o1b])
                self.tt(o2[:], o2[:], r2[:], ALU.mult, [o2b, r2b], [o2b])
                self.stt(oo[:], o2[:], neglam[:, 0:1], o1[:], ALU.mult, ALU.add, [o2b, o1b, neglamb], [oob])
                sq, sqb = sq16[par]
                self.act(sq[:], oo[:], AF.Square, [oob], [sqb])
                sp, spb = self.psn(4, 8)
                self.mm(sp[:], self.onesb[:], sq[:], True, True, [self.onesb_b, sqb], [spb])
                self.rsqrt(rs[:], sp[:], 1.0 / 128, [spb], [rsb])
                y, yb = y16[par]
                self.stt(y[:], oo[:], gcol[:, 0:1], rs[:], ALU.mult, ALU.mult, [oob, rsb, gcolb], [yb])
                self.dma("gpsimd", self.YT[1024 + h * 128:1024 + (h + 1) * 128, qs], y[:], [yb], [])

            self.pipeline(steps, KDEPTH, stage1, stage2)
            self.barrier()

    def phaseC2(self, l, s):
        S, NT, NG = self.S, self.NT, self.NG
        sc = 192.0 ** -0.5
        with ExitStack() as P:
            Qn = self.sbpool(P, "Qn", [128, S], BF16, 2)
            Qp = self.sbpool(P, "Qp", [64, S], BF16, 2)
            Kn = self.sbpool(P, "Kn", [128, S], BF16, 2)
            Kp = self.sbpool(P, "Kp", [64, S], BF16, 2)
            V = self.sbpool(P, "V", [128, NT, 128], BF16, 2)
            NP = 6
            Pt = self.sbpool(P, "Pt", [128, 512], BF16, NP)
            tf = self.sbpool(P, "tf", [128, 512], F32, 4)
            y16 = self.sbpool(P, "y16", [128, 512], BF16, 2)
            Lacc = [[self.sb(P, "Lacc%d%d" % (a, b), [128, 512], F32) for b in range(2)] for a in range(2)]
            leng = (LENG0, LENG1)

            def loads(h):
                self.load_rows(Qn[h % 2][0], Qn[h % 2][1], self.QTc, h * 192, 128)
                self.load_rows(Qp[h % 2][0], Qp[h % 2][1], self.QTc, h * 192 + 128, 64)
                self.load_rows(Kn[h % 2][0], Kn[h % 2][1], self.KTc, h * 192, 128)
                self.load_rows(Kp[h % 2][0], Kp[h % 2][1], self.KTc, h * 192 + 128, 64)
                self.load_v(V[h % 2][0], V[h % 2][1], self.Vc, h * 128)

            steps = [(h, qg, kt) for h in range(8) for qg in range(NG) for kt in range(NT)]
            loads(0)

            def stage1(st, i):
                h, qg, kt = st
                qn, qnb = Qn[h % 2]
                qp, qpb = Qp[h % 2]
                kn, knb = Kn[h % 2]
                kp, kpb = Kp[h % 2]
                par = (h * NG + qg) % 2
                qs = slice(qg * 512, (qg + 1) * 512)
                ks = slice(kt * 128, (kt + 1) * 128)
                sp, spb = self.psn(2, 8)
                self.mm(sp[:], kn[:, ks], qn[:, qs], True, False, [knb, qnb], [spb])
                self.mm(sp[:], kp[:, ks], qp[:, qs], False, True, [kpb, qpb], [spb])
                p_, p_b = Pt[i % NP]
                self.act(p_[:], sp[:], AF.Exp, [spb], [p_b], scale=sc)
                a, ab = Lacc[par][kt % 2]
                if kt < 2:
                    self.cp(leng[kt % 2], a[:], p_[:], [p_b], [ab])
                else:
                    self.tt(a[:], a[:], p_[:], ALU.add, [ab, p_b], [ab], en=leng[kt % 2])

            def stage2(st, i):
                h, qg, kt = st
                if qg == 0 and kt == 0 and h + 1 < 8:
                    loads(h + 1)
                v, vb = V[h % 2]
                par = (h * NG + qg) % 2
                O, Ob = self.ps[par]
                p_, p_b = Pt[i % NP]
                self.mm(O[:], v[:, kt, :], p_[:], kt == 0, kt == NT - 1, [vb, p_b], [Ob])
                if kt == NT - 1:
                    qs = slice(qg * 512, (qg + 1) * 512)
                    o1, o1b = tf[par * 2]
                    r1, r1b = tf[par * 2 + 1]
                    self.cp("vector", o1[:], O[:], [Ob], [o1b])
                    lp_, lpb_ = self.psn(2, 8)
                    self.lsum_mm(Lacc[par], lp_, lpb_)
                    self.act(r1[:], lp_[:], AF.Ln, [lpb_], [r1b])
                    self.act(r1[:], r1[:], AF.Exp, [r1b], [r1b], scale=-1.0)
                    y, yb = y16[par]
                    self.tt(y[:], o1[:], r1[:], ALU.mult, [o1b, r1b], [yb])
                    self.dma("gpsimd", self.YT[2048 + h * 128:2048 + (h + 1) * 128, qs], y[:], [yb], [])

            self.pipeline(steps, KDEPTH, stage1, stage2)
            self.barrier()

    def phaseD2(self, l, s):
        S, NT, NG = self.S, self.NT, self.NG
        mmax = max(4 * (NG - 1), 1)
        nneg = max(NT - 4, 1)
        with ExitStack() as P:
            gncol, gncolb = self.sb(P, "gncol", [128, 1], F32)
            self.load_cols(gncol[:], gncolb, self.ret_gn_g[l], "(p o) -> p o", o=1)
            ii, iib = self.sb(P, "ii", [128, 512], I32)
            J, Jb = self.sb(P, "J", [128, 512], F32)
            Jr, Jrb = self.sb(P, "Jr", [128, 512], F32)
            A, Ab = self.sb(P, "A", [128, 4, 512], F32)
            pbase, pbaseb = self.sb(P, "pbase", [128, mmax], F32)
            nbase, nbaseb = self.sb(P, "nbase", [128, nneg], F32)
            self.op("gpsimd", lambda e: e.iota(ii[:], pattern=[[1, 512]], base=0, channel_multiplier=0), [], [iib])
            self.cp("vector", J[:], ii[:], [iib], [Jb])
            self.ts(Jr[:], J[:], -1.0, 511.0, ALU.mult, ALU.add, [Jb], [Jrb])
            for m in range(4):
                self.op("gpsimd", lambda e, m=m: e.iota(ii[:], pattern=[[1, 512]], base=-128 * m, channel_multiplier=-1),
                        [iib], [iib])
                self.cp("vector", A[:, m, :], ii[:], [iib], [Ab])
            self.act(A[:], A[:], AF.Abs, [Ab], [Ab])
            self.op("gpsimd", lambda e: e.iota(ii[:, 0:mmax], pattern=[[128, mmax]], base=128, channel_multiplier=-1),
                    [iib], [iib])
            self.cp("vector", pbase[:], ii[:, 0:mmax], [iib], [pbaseb])
            self.op("gpsimd", lambda e: e.iota(ii[:, 0:nneg], pattern=[[128, nneg]], base=1, channel_multiplier=1),
                    [iib], [iib])
            self.cp("vector", nbase[:], ii[:, 0:nneg], [iib], [nbaseb])
            rowp = self.sbpool(P, "rowp", [128, 512], BF16, 2)
            rown = self.sbpool(P, "rown", [128, 512], BF16, 2)
            Dt = self.sbpool(P, "Dt", [128, 4, 512], F32, 2)
            cfp = self.sbpool(P, "cfp", [128, mmax], F32, 2)
            cfn = self.sbpool(P, "cfn", [128, nneg], F32, 2)
            QT = self.sbpool(P, "QT", [64, S], BF16, 2)
            KT = self.sbpool(P, "KT", [64, S], BF16, 2)
            V = self.sbpool(P, "V", [128, NT, 128], BF16, 2)
            NP = 6
            Pt = self.sbpool(P, "Pt", [128, 512], BF16, NP)
            P0 = self.sbpool(P, "P0", [128, 512], BF16, 4)
            tf = self.sbpool(P, "tf", [128, 512], F32, 4)
            sgp = self.sbpool(P, "sg", [128, 512], BF16, 2)
            y16 = self.sbpool(P, "y16", [128, 512], BF16, 2)
            meng = (LENG0, LENG1)

            def loads(h):
                lg = math.log1p(-2.0 ** (-5.0 - h))
                hp = h % 2
                self.act(rowp[hp][0][:], J[:], AF.Exp, [Jb], [rowp[hp][1]], scale=lg)
                self.act(rown[hp][0][:], Jr[:], AF.Exp, [Jrb], [rown[hp][1]], scale=lg)
                self.act(Dt[hp][0][:], A[:], AF.Exp, [Ab], [Dt[hp][1]], scale=lg)
                self.act(cfp[hp][0][:], pbase[:], AF.Exp, [pbaseb], [cfp[hp][1]], scale=lg)
                self.act(cfn[hp][0][:], nbase[:], AF.Exp, [nbaseb], [cfn[hp][1]], scale=lg)
                self.load_rows(QT[hp][0], QT[hp][1], self.QTd, h * 64, 64)
                self.load_rows(KT[hp][0], KT[hp][1], self.KTd, h * 64, 64)
                self.load_v(V[hp][0], V[hp][1], self.Vd, h * 128)

            steps = [(h, qg, kt) for h in range(8) for qg in range(NG) for kt in range(NT)]
            loads(0)

            def stage1(st, i):
                h, qg, kt = st
                hp = h % 2
                q, qb = QT[hp]
                k, kb = KT[hp]
                par = (h * NG + qg) % 2
                if kt == 0:
                    sg, sgb = sgp[par]
                    self.dma("sync", sg[:], self.GdT[h * 128:(h + 1) * 128, qg * 512:(qg + 1) * 512], [], [sgb])
                sp, spb = self.psn(4, 8)
                self.mm(sp[:], k[:, kt * 128:(kt + 1) * 128], q[:, qg * 512:(qg + 1) * 512], True, True, [kb, qb], [spb])
                p_, p_b = Pt[i % NP]
                m = 4 * qg - kt
                if m >= 1 or m <= -4:
                    p0, p0b = P0[i % 4]
                    if m >= 1:
                        cf, cfb = cfp[hp]
                        col = cf[:, m - 1:m]
                        row, rowb = rowp[hp]
                    else:
                        cf, cfb = cfn[hp]
                        col = cf[:, -m - 4:-m - 3]
                        row, rowb = rown[hp]
                    self.act(p0[:], sp[:], AF.Identity, [spb, cfb], [p0b], scale=col)
                    self.tt(p_[:], p0[:], row[:], ALU.mult, [p0b, rowb], [p_b], en=meng[i % 2])
                else:
                    self.tt(p_[:], sp[:], Dt[hp][0][:, -m, :], ALU.mult, [spb, Dt[hp][1]], [p_b])

            def stage2(st, i):
                h, qg, kt = st
                if qg == 0 and kt == 0 and h + 1 < 8:
                    loads(h + 1)
                v, vb = V[h % 2]
                par = (h * NG + qg) % 2
                O, Ob = self.ps[par]
                p_, p_b = Pt[i % NP]
                self.mm(O[:], v[:, kt, :], p_[:], kt == 0, kt == NT - 1, [vb, p_b], [Ob])
                if kt == NT - 1:
                    qs = slice(qg * 512, (qg + 1) * 512)
                    sg, sgb = sgp[par]
                    o1, o1b = tf[0]
                    s1, s1b = tf[1]
                    mn, mnb = tf[2]
                    vr, vrb = tf[3]
                    self.cp(KCP, o1[:], O[:], [Ob], [o1b])
                    self.act(s1[:], O[:], AF.Square, [Ob], [s1b])
                    mp, mpb = self.ps[2]
                    spp, sppb = self.ps[3]
                    self.mm(mp[:], self.onesf[:], o1[:], True, True, [self.onesf_b, o1b], [mpb])
                    self.mm(spp[:], self.onesf[:], s1[:], True, True, [self.onesf_b, s1b], [sppb])
                    self.act(mn[:], mp[:], AF.Copy, [mpb], [mnb], scale=1.0 / 128)
                    self.tt(vr[:], mn[:], mn[:], ALU.mult, [mnb], [vrb])
                    self.stt(vr[:], spp[:], 1.0 / 128, vr[:], ALU.mult, ALU.subtract, [sppb, vrb], [vrb])
                    self.rsqrt(vr[:], vr[:], 1.0, [vrb], [vrb])
                    self.tt(o1[:], o1[:], mn[:], ALU.subtract, [o1b, mnb], [o1b])
                    self.stt(o1[:], o1[:], gncol[:, 0:1], vr[:], ALU.mult, ALU.mult, [o1b, vrb, gncolb], [o1b])
                    y, yb = y16[par]
                    self.tt(y[:], o1[:], sg[:], ALU.mult, [o1b, sgb], [yb])
                    self.dma("gpsimd", self.YT[3072 + h * 128:3072 + (h + 1) * 128, qs], y[:], [yb], [])

            self.pipeline(steps, KDEPTH, stage1, stage2)
            self.barrier()


    def pair(self, j):
        return self.psbig[:, 2 * j:2 * j + 2, :], [self.ps[2 * j][1], self.ps[2 * j + 1][1]]

    def phaseB3(self, l, s):
        S, NT, NG = self.S, self.NT, self.NG
        lam_init = 0.8 - 0.6 * math.exp(-0.3 * l)
        with ExitStack() as P:
            lp, lpb = self.sb(P, "lp", [128, 256], F32)
            self.load_bc(lp[:], lpb, self.diff_lam[l:l + 1].rearrange("a b c -> a (b c)"))
            pr, prb = self.sb(P, "pr", [128, 128], F32)
            e2, e2b = self.sb(P, "e2", [128, 2], F32)
            neglam, neglamb = self.sb(P, "neglam", [128, 1], F32)
            gcol, gcolb = self.sb(P, "gcol", [128, 1], F32)
            lp4 = lp[:].rearrange("p (a b c) -> p a b c", a=2, b=2)
            self.tt(pr[:].rearrange("p (a c) -> p a c", a=2), lp4[:, :, 0, :], lp4[:, :, 1, :], ALU.mult, [lpb], [prb])
            self.op("vector", lambda e: e.tensor_reduce(out=e2[:], in_=pr[:].rearrange("p (a c) -> p a c", a=2),
                                                        axis=AX.X, op=ALU.add), [prb], [e2b])
            self.act(e2[:], e2[:], AF.Exp, [e2b], [e2b])
            self.tt(neglam[:], e2[:, 1:2], e2[:, 0:1], ALU.subtract, [e2b], [neglamb])
            self.ts(neglam[:], neglam[:], -lam_init, None, ALU.add, None, [neglamb], [neglamb])
            self.load_cols(gcol[:], gcolb, self.diff_sub_g[l], "(p o) -> p o", o=1)
            self.ts(gcol[:], gcol[:], 1.0 - lam_init, None, ALU.mult, None, [gcolb], [gcolb])
            QT = self.sbpool(P, "QT", [128, S], BF16, 2)
            KA = self.sbpool(P, "KA", [128, S], BF16, 2)
            KB_ = self.sbpool(P, "KBt", [128, S], BF16, 2)
            for i in range(2):
                self.op("vector", lambda e, i=i: e.memset(KA[i][0][64:128, :], 0.0), [], [KA[i][1]])
                self.op("vector", lambda e, i=i: e.memset(KB_[i][0][0:64, :], 0.0), [], [KB_[i][1]])
            V = self.sbpool(P, "V", [128, NT, 128], BF16, 2)
            NP = 6
            Pt = self.sbpool(P, "Pt", [128, 2, 512], BF16, NP)
            tf = self.sbpool(P, "tf", [128, 512], F32, 6)
            sq16 = self.sbpool(P, "sq16", [128, 512], BF16, 2)
            y16 = self.sbpool(P, "y16", [128, 512], BF16, 2)
            Lacc2 = [self.sb(P, "Lacc%d" % a, [128, 2, 512], F32) for a in range(2)]
            Lacc = [[(Lacc2[a][0][:, b, :], Lacc2[a][1]) for b in range(2)] for a in range(2)]
            T1 = self.sbpool(P, "T1", [128, 2, 512], BF16, 4)
            leng = (LENG0, LENG1)

            def loads(h):
                self.load_rows(QT[h % 2][0], QT[h % 2][1], self.QTb, h * 128, 128)
                self.dma("sync", KA[h % 2][0][0:64, :], self.KTb[h * 128:h * 128 + 64, :], [], [KA[h % 2][1]])
                self.dma("sync", KB_[h % 2][0][64:128, :], self.KTb[h * 128 + 64:h * 128 + 128, :], [], [KB_[h % 2][1]])
                self.load_v(V[h % 2][0], V[h % 2][1], self.Vb, h * 128)

            steps = [(h, qg, kt) for h in range(8) for qg in range(NG) for kt in range(NT)]
            loads(0)

            def stage1(st, i):
                h, qg, kt = st
                q, qb = QT[h % 2]
                par = (h * NG + qg) % 2
                sp, spbs = self.pair(2 + i % 2)
                ks = slice(kt * 128, (kt + 1) * 128)
                qs = slice(qg * 512, (qg + 1) * 512)
                self.mm(sp[:, 0, :], KA[h % 2][0][:, ks], q[:, qs], True, True, [KA[h % 2][1], qb], [spbs[0]])
                self.mm(sp[:, 1, :], KB_[h % 2][0][:, ks], q[:, qs], True, True, [KB_[h % 2][1], qb], [spbs[1]])
                p_, p_b = Pt[i % NP]
                self.act(p_[:], sp, AF.Exp, spbs, [p_b], scale=0.125)
                if kt % 2 == 1:
                    t1, t1b = T1[(i // 2) % 4]
                    pa, pab = Pt[(i - 1) % NP]
                    self.tt(t1[:], pa[:], p_[:], ALU.add, [pab, p_b], [t1b])
                    if kt % 4 == 3:
                        t0_, t0b = T1[((i // 2) - 1) % 4]
                        a, ab = Lacc2[par]
                        if kt == 3:
                            self.tt(a[:], t0_[:], t1[:], ALU.add, [t0b, t1b], [ab])
                        else:
                            self.tt(t1[:], t0_[:], t1[:], ALU.add, [t0b, t1b], [t1b])
                            self.tt(a[:], a[:], t1[:], ALU.add, [ab, t1b], [ab])

            def stage2(st, i):
                h, qg, kt = st
                if qg == 0 and kt == 0 and h + 1 < 8:
                    loads(h + 1)
                v, vb = V[h % 2]
                par = (h * NG + qg) % 2
                p_, p_b = Pt[i % NP]
                for si in range(2):
                    O, Ob = self.ps[par * 2 + si]
                    self.mm(O, v[:, kt, :], p_[:, si, :], kt == 0, kt == NT - 1, [vb, p_b], [Ob])
                if kt == NT - 1:
                    epilogue(h, qg, par, i)

            def epilogue(h, qg, par, i):
                qs = slice(qg * 512, (qg + 1) * 512)
                O0, O0b = self.ps[par * 2]
                O1, O1b = self.ps[par * 2 + 1]
                o1, o1b = tf[0]
                o2, o2b = tf[1]
                r1, r1b = tf[2]
                r2, r2b = tf[3]
                oo, oob = tf[4]
                rs, rsb = tf[5]
                sq, sqb = sq16[par]
                self.cp("scalar", o1[:], O0, [O0b], [o1b])
                self.cp("vector", o2[:], O1, [O1b], [o2b])

                def e1():
                    bnk = 4 + 2 * ((self.pit + 1) % 2)
                    for si, (r, rb) in enumerate(((r1, r1b), (r2, r2b))):
                        lp_, lpb_ = self.ps[bnk + si]
                        self.lsum_mm([Lacc[par][si]], lp_, lpb_)
                        self.act(r[:], lp_, AF.Ln, [lpb_], [rb])
                        self.act(r[:], r[:], AF.Exp, [rb], [rb], scale=-1.0)
                    self.tt(o1[:], o1[:], r1[:], ALU.mult, [o1b, r1b], [o1b])
                    self.tt(o2[:], o2[:], r2[:], ALU.mult, [o2b, r2b], [o2b])
                    self.stt(oo[:], o2[:], neglam[:, 0:1], o1[:], ALU.mult, ALU.add, [o2b, o1b, neglamb], [oob])
                    self.act(sq[:], oo[:], AF.Square, [oob], [sqb])

                def e2():
                    bnk = 4 + 2 * ((self.pit + 1) % 2)
                    sp, spb = self.ps[bnk]
                    self.mm(sp, self.onesb[:], sq[:], True, True, [self.onesb_b, sqb], [spb])
                    self.rsqrt(rs[:], sp, 1.0 / 128, [spb], [rsb])
                    y, yb = y16[par]
                    self.stt(y[:], oo[:], gcol[:, 0:1], rs[:], ALU.mult, ALU.mult, [oob, rsb, gcolb], [yb])
                    self.dma("gpsimd", self.YT[1024 + h * 128:1024 + (h + 1) * 128, qs], y[:], [yb], [])
                self.later(min(KE1, NT - 2), e1)
                self.later(min(KE2, NT - 1), e2)

            self.pipeline(steps, 1, stage1, stage2)
            self.barrier()

    def phaseC3(self, l, s):
        S, NT, NG = self.S, self.NT, self.NG
        sc = 192.0 ** -0.5
        NTP = NT // 2
        with ExitStack() as P:
            Qn = self.sbpool(P, "Qn", [128, S], BF16, 2)
            Qp = self.sbpool(P, "Qp", [128, S], BF16, 2)
            Kn = self.sbpool(P, "Kn", [128, S], BF16, 2)
            Kp = self.sbpool(P, "Kp", [128, S], BF16, 2)
            for i in range(2):
                self.op("vector", lambda e, i=i: e.memset(Qp[i][0][64:128, :], 0.0), [], [Qp[i][1]])
                self.op("vector", lambda e, i=i: e.memset(Kp[i][0][64:128, :], 0.0), [], [Kp[i][1]])
            V = self.sbpool(P, "V", [128, NT, 128], BF16, 2)
            NP = 4
            Pt = self.sbpool(P, "Pt", [128, 2, 512], BF16, NP)
            tf = self.sbpool(P, "tf", [128, 512], F32, 4)
            y16 = self.sbpool(P, "y16", [128, 512], BF16, 2)
            Lacc = [self.sb(P, "Lacc%d" % a, [128, 2, 512], F32) for a in range(2)]

            def loads(h):
                self.load_rows(Qn[h % 2][0], Qn[h % 2][1], self.QTc, h * 192, 128)
                self.load_rows(Qp[h % 2][0], Qp[h % 2][1], self.QTc, h * 192 + 128, 64)
                self.load_rows(Kn[h % 2][0], Kn[h % 2][1], self.KTc, h * 192, 128)
                self.load_rows(Kp[h % 2][0], Kp[h % 2][1], self.KTc, h * 192 + 128, 64)
                self.load_v(V[h % 2][0], V[h % 2][1], self.Vc, h * 128)

            steps = [(h, qg, kp) for h in range(8) for qg in range(NG) for kp in range(NTP)]
            loads(0)

            def stage1(st, i):
                h, qg, kp_ = st
                qn, qnb = Qn[h % 2]
                qp, qpb = Qp[h % 2]
                kn, knb = Kn[h % 2]
                kp, kpb = Kp[h % 2]
                par = (h * NG + qg) % 2
                qs = slice(qg * 512, (qg + 1) * 512)
                sp, spbs = self.pair(1 + i % 3)
                for j in range(2):
                    kt = 2 * kp_ + j
                    ks = slice(kt * 128, (kt + 1) * 128)
                    self.mm(sp[:, j, :], kn[:, ks], qn[:, qs], True, False, [knb, qnb], [spbs[j]])
                    self.mm(sp[:, j, :], kp[:, ks], qp[:, qs], False, True, [kpb, qpb], [spbs[j]])
                p_, p_b = Pt[i % NP]
                self.act(p_[:], sp, AF.Exp, spbs, [p_b], scale=sc)
                a, ab = Lacc[par]
                if kp_ == 0:
                    self.cp("vector", a[:], p_[:], [p_b], [ab])
                else:
                    self.tt(a[:], a[:], p_[:], ALU.add, [ab, p_b], [ab])

            def stage2(st, i):
                h, qg, kp_ = st
                if qg == 0 and kp_ == 0 and h + 1 < 8:
                    loads(h + 1)
                v, vb = V[h % 2]
                par = (h * NG + qg) % 2
                O, Ob = self.ps[par]
                p_, p_b = Pt[i % NP]
                for j in range(2):
                    kt = 2 * kp_ + j
                    self.mm(O, v[:, kt, :], p_[:, j, :], kt == 0, kt == NT - 1, [vb, p_b], [Ob])
                if kp_ == NTP - 1:
                    qs = slice(qg * 512, (qg + 1) * 512)
                    o1, o1b = tf[par * 2]
                    r1, r1b = tf[par * 2 + 1]
                    self.cp("vector", o1[:], O, [Ob], [o1b])
                    lp_, lpb_ = self.ps[2 + 2 * ((i + 1) % 3)]
                    a, ab = Lacc[par]
                    self.mm(lp_, self.onesf[:], a[:, 0, :], True, False, [self.onesf_b, ab], [lpb_])
                    self.mm(lp_, self.onesf[:], a[:, 1, :], False, True, [self.onesf_b, ab], [lpb_])
                    self.act(r1[:], lp_, AF.Ln, [lpb_], [r1b])
                    self.act(r1[:], r1[:], AF.Exp, [r1b], [r1b], scale=-1.0)
                    y, yb = y16[par]
                    self.tt(y[:], o1[:], r1[:], ALU.mult, [o1b, r1b], [yb])
                    self.dma("gpsimd", self.YT[2048 + h * 128:2048 + (h + 1) * 128, qs], y[:], [yb], [])

            self.pipeline(steps, 2, stage1, stage2)
            self.barrier()

    def phaseD3(self, l, s):
        S, NT, NG = self.S, self.NT, self.NG
        mmax = max(4 * (NG - 1), 1)
        nneg = max(NT - 4, 1)
        with ExitStack() as P:
            gncol, gncolb = self.sb(P, "gncol", [128, 1], F32)
            self.load_cols(gncol[:], gncolb, self.ret_gn_g[l], "(p o) -> p o", o=1)
            ii, iib = self.sb(P, "ii", [128, 512], I32)
            J, Jb = self.sb(P, "J", [128, 512], F32)
            Jr, Jrb = self.sb(P, "Jr", [128, 512], F32)
            A, Ab = self.sb(P, "A", [128, 4, 512], F32)
            pbase, pbaseb = self.sb(P, "pbase", [128, mmax], F32)
            nbase, nbaseb = self.sb(P, "nbase", [128, nneg], F32)
            self.op("gpsimd", lambda e: e.iota(ii[:], pattern=[[1, 512]], base=0, channel_multiplier=0), [], [iib])
            self.cp("vector", J[:], ii[:], [iib], [Jb])
            self.ts(Jr[:], J[:], -1.0, 511.0, ALU.mult, ALU.add, [Jb], [Jrb])
            for m in range(4):
                self.op("gpsimd", lambda e, m=m: e.iota(ii[:], pattern=[[1, 512]], base=-128 * m, channel_multiplier=-1),
                        [iib], [iib])
                self.cp("vector", A[:, m, :], ii[:], [iib], [Ab])
            self.act(A[:], A[:], AF.Abs, [Ab], [Ab])
            self.op("gpsimd", lambda e: e.iota(ii[:, 0:mmax], pattern=[[128, mmax]], base=128, channel_multiplier=-1),
                    [iib], [iib])
            self.cp("vector", pbase[:], ii[:, 0:mmax], [iib], [pbaseb])
            self.op("gpsimd", lambda e: e.iota(ii[:, 0:nneg], pattern=[[128, nneg]], base=1, channel_multiplier=1),
                    [iib], [iib])
            self.cp("vector", nbase[:], ii[:, 0:nneg], [iib], [nbaseb])
            rowp = self.sbpool(P, "rowp", [128, 512], BF16, 2)
            rown = self.sbpool(P, "rown", [128, 512], BF16, 2)
            Dt = self.sbpool(P, "Dt", [128, 4, 512], F32, 2)
            cfp = self.sbpool(P, "cfp", [128, mmax], F32, 2)
            cfn = self.sbpool(P, "cfn", [128, nneg], F32, 2)
            QT = self.sbpool(P, "QT", [128, S], BF16, 2)
            KT = self.sbpool(P, "KT", [128, S], BF16, 2)
            for i in range(2):
                self.op("vector", lambda e, i=i: e.memset(QT[i][0][64:128, :], 0.0), [], [QT[i][1]])
                self.op("vector", lambda e, i=i: e.memset(KT[i][0][64:128, :], 0.0), [], [KT[i][1]])
            V = self.sbpool(P, "V", [128, NT, 128], BF16, 2)
            NP = 8
            Pt = self.sbpool(P, "Pt", [128, 512], BF16, NP)
            P0 = self.sbpool(P, "P0", [128, 512], BF16, 6)
            tf = self.sbpool(P, "tf", [128, 512], F32, 4)
            sgp = self.sbpool(P, "sg", [128, 512], BF16, 2)
            y16 = self.sbpool(P, "y16", [128, 512], BF16, 2)
            cnt = [0]

            def loads(h):
                lg = math.log1p(-2.0 ** (-5.0 - h))
                hp = h % 2
                self.act(rowp[hp][0][:], J[:], AF.Exp, [Jb], [rowp[hp][1]], scale=lg)
                self.act(rown[hp][0][:], Jr[:], AF.Exp, [Jrb], [rown[hp][1]], scale=lg)
                self.act(Dt[hp][0][:], A[:], AF.Exp, [Ab], [Dt[hp][1]], scale=lg)
                self.act(cfp[hp][0][:], pbase[:], AF.Exp, [pbaseb], [cfp[hp][1]], scale=lg)
                self.act(cfn[hp][0][:], nbase[:], AF.Exp, [nbaseb], [cfn[hp][1]], scale=lg)
                self.load_rows(QT[hp][0], QT[hp][1], self.QTd, h * 64, 64)
                self.load_rows(KT[hp][0], KT[hp][1], self.KTd, h * 64, 64)
                self.load_v(V[hp][0], V[hp][1], self.Vd, h * 128)

            steps = [(h, qg, kt) for h in range(8) for qg in range(NG) for kt in range(NT)]
            loads(0)

            def stage1(st, i):
                h, qg, kt = st
                hp = h % 2
                q, qb = QT[hp]
                k, kb = KT[hp]
                par = (h * NG + qg) % 2
                if kt == 0:
                    sg, sgb = sgp[par]
                    self.dma("sync", sg[:], self.GdT[h * 128:(h + 1) * 128, qg * 512:(qg + 1) * 512], [], [sgb])
                sp, spb = self.psn(2, 8)
                self.mm(sp, k[:, kt * 128:(kt + 1) * 128], q[:, qg * 512:(qg + 1) * 512], True, True, [kb, qb], [spb])
                p_, p_b = Pt[i % NP]
                m = 4 * qg - kt
                if m >= 1 or m <= -4:
                    if m >= 1:
                        cf, cfb = cfp[hp]
                        col = cf[:, m - 1:m]
                        row, rowb = rowp[hp]
                    else:
                        cf, cfb = cfn[hp]
                        col = cf[:, -m - 4:-m - 3]
                        row, rowb = rown[hp]
                    cnt[0] += 1
                    if cnt[0] % KDACT != 0:
                        p0, p0b = P0[cnt[0] % 6]
                        self.act(p0[:], sp, AF.Identity, [spb, cfb], [p0b], scale=col)
                        self.tt(p_[:], p0[:], row[:], ALU.mult, [p0b, rowb], [p_b], en=LENG1)
                    else:
                        self.stt(p_[:], sp, col, row[:], ALU.mult, ALU.mult, [spb, cfb, rowb], [p_b])
                else:
                    self.tt(p_[:], sp, Dt[hp][0][:, -m, :], ALU.mult, [spb, Dt[hp][1]], [p_b])

            def stage2(st, i):
                h, qg, kt = st
                if qg == 0 and kt == 0 and h + 1 < 8:
                    loads(h + 1)
                v, vb = V[h % 2]
                par = (h * NG + qg) % 2
                O, Ob = self.ps[par]
                p_, p_b = Pt[i % NP]
                self.mm(O, v[:, kt, :], p_[:], kt == 0, kt == NT - 1, [vb, p_b], [Ob])
                if kt == NT - 1:
                    qs = slice(qg * 512, (qg + 1) * 512)
                    sg, sgb = sgp[par]
                    o1, o1b = tf[0]
                    s1, s1b = tf[1]
                    mn, mnb = tf[2]
                    vr, vrb = tf[3]
                    self.cp("scalar", o1[:], O, [Ob], [o1b])
                    self.act(s1[:], O, AF.Square, [Ob], [s1b])
                    mp, mpb = self.psn(2, 8)
                    spp, sppb = self.psn(2, 8)
                    self.mm(mp, self.onesf[:], o1[:], True, True, [self.onesf_b, o1b], [mpb])
                    self.mm(spp, self.onesf[:], s1[:], True, True, [self.onesf_b, s1b], [sppb])
                    self.act(mn[:], mp, AF.Copy, [mpb], [mnb], scale=1.0 / 128)
                    self.tt(vr[:], mn[:], mn[:], ALU.mult, [mnb], [vrb])
                    self.stt(vr[:], spp, 1.0 / 128, vr[:], ALU.mult, ALU.subtract, [sppb, vrb], [vrb])
                    self.rsqrt(vr[:], vr[:], 1.0, [vrb], [vrb])
                    self.tt(o1[:], o1[:], mn[:], ALU.subtract, [o1b, mnb], [o1b])
                    self.stt(o1[:], o1[:], gncol[:, 0:1], vr[:], ALU.mult, ALU.mult, [o1b, vrb, gncolb], [o1b])
                    y, yb = y16[par]
                    self.tt(y[:], o1[:], sg[:], ALU.mult, [o1b, sgb], [yb])
                    self.dma("gpsimd", self.YT[3072 + h * 128:3072 + (h + 1) * 128, qs], y[:], [yb], [])

            self.pipeline(steps, KDD, stage1, stage2)
            self.barrier()

    def phaseE(self, l, s):
        S = self.S
        TB = min(1024, S)
        nblk = S // TB
        NGb = TB // 512
        with ExitStack() as P:
            xk, xkb = self.sb(P, "xk", [128, 8, TB], F32)
            big, bigb = self.sb(P, "big", [128, 32, TB], BF16)
            bigbs = [Buf("big%d" % i_) for i_ in range(4)]
            aT, aTb = self.sb(P, "aT", [128, 8, TB], BF16)
            pT, pTb = self.sb(P, "pT", [128, 2, TB], BF16)
            wpool = self.sbpool(P, "we", [128, 4096], BF16, 3)
            g2, g2b = self.sb(P, "g2", [128, 8], F32)
            g3, g3b = self.sb(P, "g3", [128, 8], F32)
            self.load_cols(g2[:], g2b, self.norm2_g[l], "(kc p) -> p kc", p=128)
            self.load_cols(g3[:], g3b, self.norm3_g[l], "(kc p) -> p kc", p=128)
            acc = self.sbpool(P, "acc", [128, 512], F32, 4 * NGb)
            tmpf = self.sbpool(P, "tmpf", [128, 512], F32, 3)
            gtp = self.sbpool(P, "gt", [128, 512], BF16, 3)
            pl = self.sbpool(P, "pl", [128, 256], F32, 2)
            pl16 = self.sbpool(P, "pl16", [128, 256], BF16, 2)
            cnt = [0]

            def load_y(blk_):
                for b_ in range(4):
                    self.dma("sync", big[:, b_ * 8:(b_ + 1) * 8, :],
                             self.YT[b_ * 1024:(b_ + 1) * 1024, blk_ * TB:(blk_ + 1) * TB].rearrange(
                                 "(kc p) t -> p kc t", p=128), [], [bigbs[b_]])

            for blk in range(nblk):
                c0 = s * S + blk * TB
                lc0 = blk * TB
                if blk == 0:
                    load_y(0)
                for nt in range(2):
                    for b_ in range(4):
                        def epi(pt, pb, m, g, b_=b_, nt=nt):
                            gt, gtb = gtp[cnt[0] % 3]
                            tm, tmb = tmpf[cnt[0] % 3]
                            cnt[0] += 1
                            r0 = b_ * 1024 + (nt * 4 + m) * 128
                            self.dma("sync", gt[:], self.GT[r0:r0 + 128, lc0 + g * 512:lc0 + (g + 1) * 512], [], [gtb])
                            a, ab = acc[m * NGb + g]
                            if b_ == 0:
                                self.tt(a[:], pt[:], gt[:], ALU.mult, [pb, gtb], [ab])
                            elif b_ < 3:
                                self.tt(tm[:], pt[:], gt[:], ALU.mult, [pb, gtb], [tmb])
                                self.tt(a[:], a[:], tm[:], ALU.add, [ab, tmb], [ab], en="gpsimd")
                            else:
                                self.tt(tm[:], pt[:], gt[:], ALU.mult, [pb, gtb], [tmb])
                                self.tt(aT[:, nt * 4 + m, g * 512:(g + 1) * 512], a[:], tm[:], ALU.add, [ab, tmb], [aTb],
                                        en="gpsimd")
                        self.gemm("fm", big, bigbs[b_], 8, TB, self.wb["br%d" % b_, l], nt * 512, 512, epi, wpool, k0=b_ * 8)
                self.dma("sync", xk[:], self.xT[:, c0:c0 + TB].rearrange("(kc p) t -> p kc t", p=128), [], [xkb])
                for nt in range(2):
                    def epi(pt, pb, m, g, nt=nt):
                        xs = xk[:, nt * 4 + m, g * 512:(g + 1) * 512]
                        self.tt(xs, xs, pt[:], ALU.add, [xkb, pb], [xkb])
                    self.gemm("fm", aT, aTb, 8, TB, self.wb["out", l], nt * 512, 512, epi, wpool)
                self.norm_fm(None, 0, TB, g2, g2b, aT, aTb, xkeep=(xk, xkb), loaded=True)
                for nt in range(8):
                    def epi(pt, pb, m, g, nt=nt):
                        tm, tmb = tmpf[cnt[0] % 3]
                        cnt[0] += 1
                        self.act(tm[:], pt[:], AF.Relu, [pb], [tmb])
                        self.tt(big[:, nt * 4 + m, g * 512:(g + 1) * 512], tm[:], tm[:], ALU.mult, [tmb],
                                [bigbs[(nt * 4 + m) // 8]])
                    self.gemm("fm", aT, aTb, 8, TB, self.wb["ff1", l], nt * 512, 512, epi, wpool)
                for mt in range(8):
                    def epi(pt, pb, m, g, mt=mt):
                        xs = xk[:, mt, g * 512:(g + 1) * 512]
                        self.tt(xs, xs, pt[:], ALU.add, [xkb, pb], [xkb])
                    self.gemm("fm", big, bigbs, 32, TB, self.wb["ff2", l], mt * 128, 128, epi, wpool)
                if blk + 1 < nblk:
                    load_y(blk + 1)
                self.norm_fm(None, 0, TB, g3, g3b, aT, aTb, xkeep=(xk, xkb), loaded=True)
                for t in range(TB // 128):
                    p_, p_b = pl[t % 2]
                    p16, p16b = pl16[t % 2]
                    self.dma("sync", p_[:], self.p[l, c0 + t * 128:c0 + (t + 1) * 128, :], [], [p_b])
                    self.cp(self.alt(), p16[:], p_[:], [p_b], [p16b])
                    i = self.psi % 8
                    self.psi += 1
                    pv = self.psb(i)
                    for bi in range(2):
                        self.tp(pv[:, bi * 128:(bi + 1) * 128], p16[:, bi * 128:(bi + 1) * 128], self.identb[:],
                                [p16b, self.identb_b], [self.ps[i][1]])
                    self.cp(self.alt(), pT[:, :, t * 128:(t + 1) * 128], pv[:, 0:256].rearrange("p (b t) -> p b t", b=2),
                            [self.ps[i][1]], [pTb])
                for nt in range(2):
                    wg, wgb = wpool[self.wi % 3]
                    self.wi += 1
                    wp, wpb = wpool[self.wi % 3]
                    self.wi += 1
                    wgv = wg[:, 0:4096].rearrange("p (k n) -> p k n", k=8)
                    wpv = wp[:, 0:1024].rearrange("p (k n) -> p k n", k=2)
                    self.dma("sync", wgv, self.wb["pg", l][:, nt * 512:(nt + 1) * 512].rearrange("(kc p) n -> p kc n", p=128),
                             [], [wgb])
                    self.dma("sync", wpv, self.wb["pp", l][:, nt * 512:(nt + 1) * 512].rearrange("(kc p) n -> p kc n", p=128),
                             [], [wpb])
                    for m in range(4):
                        for g in range(NGb):
                            gs = slice(g * 512, (g + 1) * 512)
                            p1, p1b = self.psn()
                            for kc in range(8):
                                self.mm(p1[:], wgv[:, kc, m * 128:(m + 1) * 128], aT[:, kc, gs], kc == 0, kc == 7,
                                        [wgb, aTb], [p1b])
                            p2, p2b = self.psn()
                            for kc in range(2):
                                self.mm(p2[:], wpv[:, kc, m * 128:(m + 1) * 128], pT[:, kc, gs], kc == 0, kc == 1,
                                        [wpb, pTb], [p2b])
                            tm, tmb = tmpf[cnt[0] % 3]
                            cnt[0] += 1
                            self.act(tm[:], p1[:], AF.Sigmoid, [p1b], [tmb])
                            self.tt(tm[:], tm[:], p2[:], ALU.mult, [tmb, p2b], [tmb])
                            xs = xk[:, nt * 4 + m, gs]
                            self.tt(xs, xs, tm[:], ALU.add, [xkb, tmb], [xkb], en="gpsimd")
                self.dma("gpsimd", self.xT[:, c0:c0 + TB].rearrange("(kc p) t -> p kc t", p=128), xk[:], [xkb], [])
            self.barrier()

    def mark(self, name):
        if not hasattr(self, "marks"):
            self.marks = []
        self.marks.append((name, {k: e.dom.count for k, e in self.E.items()}))

    def body(self):
        for l in range(self.L):
            for s in range(self.NSEQ):
                self.mark("P1 %d %d" % (l, s))
                self.phase1(l, s)
                self.mark("PB %d %d" % (l, s))
                (self.phaseB3 if "B3" in PH else self.phaseB2)(l, s)
                self.mark("PC %d %d" % (l, s))
                (self.phaseC3 if "C3" in PH else self.phaseC2)(l, s)
                self.mark("PD %d %d" % (l, s))
                (self.phaseD3 if "D3" in PH else self.phaseD2)(l, s)
                self.mark("PE %d %d" % (l, s))
                self.phaseE(l, s)
        self.mark("END")

    def build(self):
        self.setup_sync()
        self.declare_io()
        self.build_consts()
        self.precast()
        self.build_rope()
        self.transpose_in()
        self.body()
        self.transpose_out()
        self.barrier()
        self.st.close()
        return self.nc


INPUT_NAMES = ["x", "p", "norm1_g", "w_in", "gate_b", "conv_w", "conv_b", "lru_wa", "lru_ba", "lru_wx", "lru_bx",
               "lru_lambda", "diff_q_g", "diff_k_g", "diff_lam", "diff_sub_g", "mla_qa_g", "mla_wuq", "mla_kva_g",
               "mla_wukv", "mla_q_g", "mla_k_g", "ret_gn_g", "w_br_a", "w_br_b", "w_br_c", "w_br_d", "w_out",
               "norm2_g", "w_ff1", "w_ff2", "norm3_g", "w_ple_gate", "w_ple_proj"]


def make_in_maps(inputs, ncores, nseq, S):
    maps = []
    for c in range(ncores):
        m = {}
        for k in INPUT_NAMES:
            v = np.asarray(inputs[k])
            if k == "x":
                v = np.ascontiguousarray(v[c * nseq:(c + 1) * nseq].reshape(nseq * S, DM))
            elif k == "p":
                v = np.ascontiguousarray(v[:, c * nseq:(c + 1) * nseq].reshape(v.shape[0], nseq * S, PLED))
            else:
                v = np.ascontiguousarray(v)
            m[k] = v.astype(np.float32, copy=False)
        maps.append(m)
    return maps


def kernel(**inputs):
    B, S, _ = inputs["x"].shape
    L = inputs["w_in"].shape[0]
    ncores = 8
    nseq = B // ncores
    kb = KB(S, nseq, L)
    nc = kb.build()
    maps = make_in_maps(inputs, ncores, nseq, S)
    res = run_bass_kernel_spmd(nc, maps, core_ids=list(range(ncores)))
    outs = [np.asarray(r["out"]).reshape(nseq, S, DM) for r in res.results]
    return np.concatenate(outs, axis=0).astype(np.float32)
```

```python
import math
import numpy as np
import concourse.bass as bass
import concourse.mybir as mybir
from concourse.bass_utils import run_bass_kernel_spmd
from contextlib import ExitStack

F32 = mybir.dt.float32
BF16 = mybir.dt.bfloat16
I32 = mybir.dt.int32
AF = mybir.ActivationFunctionType
ALU = mybir.AluOpType
AX = mybir.AxisListType

DM = 1024
NIN = 12992
DFF = 4096
PLED = 256
EPS = 1e-6
OFF = dict(ax=0, ag=1024, bq=2048, bk=3072, bv=4096, cq=5120, ckv=5504, ckpe=5760,
           dq=5824, dk=6336, dv=6848, dg=7872, gl=8896)
SAME_SYNC = True
import os
PH = os.environ.get("KPH", "B3,C3,D3").split(",")
KDEPTH = int(os.environ.get("KDEPTH", "3"))
KCP = os.environ.get("KCP", "scalar")
LENG0 = "vector"
LENG1 = os.environ.get("KLENG1", "vector")
KDEFER = int(os.environ.get("KDEFER", "2"))
KGI = int(os.environ.get("KGI", "2"))
KDACT = int(os.environ.get("KDACT", "2"))
KDD = int(os.environ.get("KDD", "5"))
KE1 = int(os.environ.get("KE1", "3"))
KE2 = int(os.environ.get("KE2", "8"))


class Dom:
    def __init__(self, name, sem, mult):
        self.name, self.sem, self.mult, self.count = name, sem, mult, 0


class Buf:
    __slots__ = ("name", "w", "r")

    def __init__(self, name=""):
        self.name, self.w, self.r = name, {}, {}


class Eng:
    def __init__(self, name, eng, dom):
        self.name, self.eng, self.dom = name, eng, dom
        self.seen = {}
        self.slots = []
        self.si = 0


class KB:
    def __init__(self, S, NSEQ, L, debug=()):
        self.S, self.NSEQ, self.L = S, NSEQ, L
        self.NT, self.NG = S // 128, S // 512
        self.debug = set(debug)
        self.nc = bass.Bass("TRN2", target_bir_lowering=False)
        self.st = ExitStack()
        self.E = {}
        self.doms = []
        self.dbufs = {}
        self.cnt = 0
        self.wi = 0

    def setup_sync(self):
        nc = self.nc
        for name in ["tensor", "vector", "scalar", "gpsimd", "sync"]:
            sem = self.st.enter_context(nc.semaphore("s_" + name))
            dom = Dom(name, sem, 1)
            self.doms.append(dom)
            self.E[name] = Eng(name, getattr(nc, name), dom)
        for q, n in (("sync", 24), ("gpsimd", 16)):
            for i in range(n):
                sem = self.st.enter_context(nc.semaphore("d_%s%d" % (q, i)))
                dom = Dom("d_%s%d" % (q, i), sem, 16)
                self.doms.append(dom)
                self.E[q].slots.append(dom)

    def _deps(self, reads, writes):
        deps = {}
        for b in reads:
            for d, i in b.w.items():
                if deps.get(d, 0) < i:
                    deps[d] = i
        for b in writes:
            for d, i in b.w.items():
                if deps.get(d, 0) < i:
                    deps[d] = i
            for d, i in b.r.items():
                if deps.get(d, 0) < i:
                    deps[d] = i
        return deps

    def _wait(self, E, deps):
        for dom, idx in deps.items():
            if idx <= 0:
                continue
            if dom is E.dom and (E.name == "tensor" or not SAME_SYNC):
                continue
            if E.seen.get(dom, 0) >= idx:
                continue
            E.eng.wait_ge(dom.sem, idx * dom.mult)
            E.seen[dom] = idx

    def op(self, en, fn, reads=(), writes=()):
        E = self.E[en]
        self._wait(E, self._deps(reads, writes))
        ins = fn(E.eng)
        E.dom.count += 1
        ins.then_inc(E.dom.sem, 1)
        c = E.dom.count
        for b in reads:
            b.r[E.dom] = c
        for b in writes:
            b.w[E.dom] = c
        self.cnt += 1

    def dma(self, q, out, in_, reads=(), writes=(), slow=False):
        E = self.E[q]
        slot = E.slots[E.si % len(E.slots)]
        E.si += 1
        deps = self._deps(reads, writes)
        if slot.count > 0 and deps.get(slot, 0) < slot.count:
            deps[slot] = slot.count
        self._wait(E, deps)
        if slow:
            ins = E.eng.dma_start(out=out, in_=in_, allow_slow_non_contiguous=True)
        else:
            ins = E.eng.dma_start(out=out, in_=in_)
        ins.then_inc(slot.sem, 16)
        slot.count += 1
        for b in reads:
            b.r[slot] = slot.count
        for b in writes:
            b.w[slot] = slot.count
        self.cnt += 1

    def barrier(self):
        for E in self.E.values():
            for d in self.doms:
                if d.count > 0 and E.seen.get(d, 0) < d.count:
                    E.eng.wait_ge(d.sem, d.count * d.mult)
                    E.seen[d] = d.count

    def dbuf(self, *key):
        b = self.dbufs.get(key)
        if b is None:
            b = Buf(str(key))
            self.dbufs[key] = b
        return b

    def sb(self, stack, name, shape, dt):
        self.uid = getattr(self, "uid", 0) + 1
        t = stack.enter_context(self.nc.sbuf_tensor("%s_%d" % (name, self.uid), list(shape), dt))
        return t, Buf(name)

    def sbpool(self, stack, name, shape, dt, n):
        return [self.sb(stack, "%s%d" % (name, i), shape, dt) for i in range(n)]

    def act(self, out, in_, func, reads, writes, bias=None, scale=None, accum=None):
        kw = {}
        if bias is not None:
            kw["bias"] = bias
        if scale is not None:
            kw["scale"] = scale
        if accum is not None:
            kw["accum_out"] = accum
        self.op("scalar", lambda e: e.activation(out=out, in_=in_, func=func, **kw), reads, writes)

    def tt(self, out, in0, in1, op, reads, writes, en="vector"):
        self.op(en, lambda e: e.tensor_tensor(out=out, in0=in0, in1=in1, op=op), reads, writes)

    def ts(self, out, in0, s1, s2, op0, op1, reads, writes, en="vector"):
        if op1 is None:
            self.op(en, lambda e: e.tensor_scalar(out=out, in0=in0, scalar1=s1, scalar2=None, op0=op0), reads, writes)
        else:
            self.op(en, lambda e: e.tensor_scalar(out=out, in0=in0, scalar1=s1, scalar2=s2, op0=op0, op1=op1),
                    reads, writes)

    def stt(self, out, in0, scalar, in1, op0, op1, reads, writes):
        self.op("vector", lambda e: e.scalar_tensor_tensor(out=out, in0=in0, scalar=scalar, in1=in1, op0=op0, op1=op1),
                reads, writes)

    def cp(self, en, out, in_, reads, writes):
        if en == "scalar":
            self.op("scalar", lambda e: e.activation(out=out, in_=in_, func=AF.Copy), reads, writes)
        else:
            self.op(en, lambda e: e.tensor_copy(out=out, in_=in_), reads, writes)

    def mm(self, out, lhsT, rhs, start, stop, reads, writes):
        self.op("tensor", lambda e: e.matmul(out, lhsT=lhsT, rhs=rhs, start=start, stop=stop), reads, writes)

    def tp(self, out, in_, ident, reads, writes):
        self.op("tensor", lambda e: e.transpose(out=out, in_=in_, identity=ident), reads, writes)

    def rsqrt(self, out, in_, scale, reads, writes, eps=EPS):
        self.act(out, in_, AF.Ln, reads, writes, bias=self.epscol[0:out.shape[0], :] if eps == EPS else eps,
                 scale=scale)
        self.act(out, out, AF.Exp, list(writes), writes, scale=-0.5)

    _alt = 0

    def alt(self):
        self._alt ^= 1
        return "scalar" if self._alt else "vector"

    def declare_io(self):
        nc, S, NSEQ, L = self.nc, self.S, self.NSEQ, self.L
        T = NSEQ * S
        self.T = T
        d = lambda n, sh, dt=F32: nc.dram_tensor(n, list(sh), dt, kind="ExternalInput")
        self.x = d("x", [T, DM])
        self.p = d("p", [L, T, PLED])
        self.norm1_g = d("norm1_g", [L, DM])
        self.w_in = d("w_in", [L, DM, NIN])
        self.gate_b = d("gate_b", [L, 4, DM])
        self.conv_w = d("conv_w", [L, 4, DM])
        self.conv_b = d("conv_b", [L, DM])
        self.lru_wa = d("lru_wa", [L, 2, 16, 64, 64])
        self.lru_ba = d("lru_ba", [L, 2, DM])
        self.lru_wx = d("lru_wx", [L, 2, 16, 64, 64])
        self.lru_bx = d("lru_bx", [L, 2, DM])
        self.lru_lambda = d("lru_lambda", [L, 2, DM])
        self.diff_q_g = d("diff_q_g", [L, 64])
        self.diff_k_g = d("diff_k_g", [L, 64])
        self.diff_lam = d("diff_lam", [L, 4, 64])
        self.diff_sub_g = d("diff_sub_g", [L, 128])
        self.mla_qa_g = d("mla_qa_g", [L, 384])
        self.mla_wuq = d("mla_wuq", [L, 384, 1536])
        self.mla_kva_g = d("mla_kva_g", [L, 256])
        self.mla_wukv = d("mla_wukv", [L, 256, 2048])
        self.mla_q_g = d("mla_q_g", [L, 192])
        self.mla_k_g = d("mla_k_g", [L, 192])
        self.ret_gn_g = d("ret_gn_g", [L, 128])
        self.w_br = [d("w_br_" + c, [L, DM, DM]) for c in "abcd"]
        self.w_out = d("w_out", [L, DM, DM])
        self.norm2_g = d("norm2_g", [L, DM])
        self.w_ff1 = d("w_ff1", [L, DM, DFF])
        self.w_ff2 = d("w_ff2", [L, DFF, DM])
        self.norm3_g = d("norm3_g", [L, DM])
        self.w_ple_gate = d("w_ple_gate", [L, DM, DM])
        self.w_ple_proj = d("w_ple_proj", [L, PLED, DM])
        self.out = nc.dram_tensor("out", [T, DM], F32, kind="ExternalOutput")

        def scr(n, sh, dt):
            kind = "ExternalOutput" if n in self.debug else "Internal"
            return nc.dram_tensor(n, list(sh), dt, kind=kind)

        self.scr = scr
        self.wb = {}
        for l in range(L):
            self.wb["w_in", l] = scr("wb_in%d" % l, [DM, NIN], BF16)
            self.wb["wuq", l] = scr("wb_wuq%d" % l, [384, 1536], BF16)
            self.wb["wukv", l] = scr("wb_wukv%d" % l, [256, 2048], BF16)
            for i in range(4):
                self.wb["br%d" % i, l] = scr("wb_br%d_%d" % (i, l), [DM, DM], BF16)
            self.wb["out", l] = scr("wb_out%d" % l, [DM, DM], BF16)
            self.wb["ff1", l] = scr("wb_ff1%d" % l, [DM, DFF], BF16)
            self.wb["ff2", l] = scr("wb_ff2%d" % l, [DFF, DM], BF16)
            self.wb["pg", l] = scr("wb_pg%d" % l, [DM, DM], BF16)
            self.wb["pp", l] = scr("wb_pp%d" % l, [PLED, DM], BF16)
        self.xT = scr("xT", [DM, T], F32)
        self.QTb = scr("QTb", [1024, S], BF16)
        self.KTb = scr("KTb", [1024, S], BF16)
        self.Vb = scr("Vb", [S, 1024], BF16)
        self.QTc = scr("QTc", [8 * 192, S], BF16)
        self.KTc = scr("KTc", [8 * 192, S], BF16)
        self.Vc = scr("Vc", [S, 1024], BF16)
        self.QTd = scr("QTd", [512, S], BF16)
        self.KTd = scr("KTd", [512, S], BF16)
        self.Vd = scr("Vd", [S, 1024], BF16)
        self.GdT = scr("GdT", [1024, S], BF16)
        self.YT = scr("YT", [4 * 1024, S], BF16)
        self.GT = scr("GT", [4 * 1024, S], BF16)
        self.ropeD = [scr("rope%d" % i, [S, 2, r2], F32) for i, r2 in enumerate((8, 32, 32))]

    def build_consts(self):
        nc = self.nc
        st = self.st
        self.identf, self.identf_b = self.sb(st, "identf", [128, 128], F32)
        self.identb, self.identb_b = self.sb(st, "identb", [128, 128], BF16)
        self.onesf, self.onesf_b = self.sb(st, "onesf", [128, 128], F32)
        self.onesb, self.onesb_b = self.sb(st, "onesb", [128, 128], BF16)
        self.epscol, self.epscol_b = self.sb(st, "epscol", [128, 1], F32)
        self.onecol, self.onecol_b = self.sb(st, "onecol", [128, 1], F32)
        self.op("vector", lambda e: e.memset(self.onecol[:], 1.0), [], [self.onecol_b])
        self.CB = [self.identf_b, self.identb_b, self.onesf_b, self.onesb_b, self.epscol_b]
        with ExitStack() as s2:
            it, itb = self.sb(s2, "c_it", [128, 128], I32)
            tf, tfb = self.sb(s2, "c_tf", [128, 128], F32)
            self.op("gpsimd", lambda e: e.iota(it[:], pattern=[[1, 128]], base=0, channel_multiplier=-1), [], [itb])
            self.cp("vector", tf[:], it[:], [itb], [tfb])
            self.ts(self.identf[:], tf[:], 0.0, None, ALU.is_equal, None, [tfb], [self.identf_b])
            self.cp("vector", self.identb[:], self.identf[:], [self.identf_b], [self.identb_b])
            self.op("vector", lambda e: e.memset(self.onesf[:], 1.0), [], [self.onesf_b])
            self.op("vector", lambda e: e.memset(self.onesb[:], 1.0), [], [self.onesb_b])
            self.op("vector", lambda e: e.memset(self.epscol[:], EPS), [], [self.epscol_b])
            self.barrier()
        self.psbig = st.enter_context(nc.psum_tensor("psbig", [128, 8, 512], F32))
        self.ps = [(self.psbig[:, i, :], Buf("ps%d" % i)) for i in range(8)]
        self.psi = 0

    def psn(self, lo=0, hi=8):
        i = lo + self.psi % (hi - lo)
        self.psi += 1
        return self.ps[i]

    def precast(self):
        L = self.L
        jobs = []
        for l in range(L):
            jobs.append((self.w_in[l], self.wb["w_in", l], DM, NIN))
            jobs.append((self.mla_wuq[l], self.wb["wuq", l], 384, 1536))
            jobs.append((self.mla_wukv[l], self.wb["wukv", l], 256, 2048))
            for i in range(4):
                jobs.append((self.w_br[i][l], self.wb["br%d" % i, l], DM, DM))
            jobs.append((self.w_out[l], self.wb["out", l], DM, DM))
            jobs.append((self.w_ff1[l], self.wb["ff1", l], DM, DFF))
            jobs.append((self.w_ff2[l], self.wb["ff2", l], DFF, DM))
            jobs.append((self.w_ple_gate[l], self.wb["pg", l], DM, DM))
            jobs.append((self.w_ple_proj[l], self.wb["pp", l], PLED, DM))
        with ExitStack() as s2:
            CW = 2048
            stg = self.sbpool(s2, "pc_f", [128, CW], F32, 3)
            stb = self.sbpool(s2, "pc_b", [128, CW], BF16, 3)
            i = 0
            for src, dst, K, N in jobs:
                for r0 in range(0, K, 128):
                    for c0 in range(0, N, CW):
                        cw = min(CW, N - c0)
                        f, fb = stg[i % 3]
                        b, bb = stb[i % 3]
                        self.dma("sync", f[:, 0:cw], src[r0:r0 + 128, c0:c0 + cw], [], [fb])
                        self.cp(self.alt(), b[:, 0:cw], f[:, 0:cw], [fb], [bb])
                        self.dma("gpsimd", dst[r0:r0 + 128, c0:c0 + cw], b[:, 0:cw], [bb], [])
                        i += 1
            self.barrier()

    def build_rope(self):
        NT = self.NT
        cfgs = ((8, 16, 500000.0), (32, 64, 500000.0), (32, 64, 10000.0))
        C1 = 6.28125
        C2 = 2.0 * math.pi - C1
        with ExitStack() as s2:
            pi_, pib = self.sb(s2, "r_pi", [128, NT], I32)
            pos, posb = self.sb(s2, "r_pos", [128, NT], F32)
            self.op("gpsimd", lambda e: e.iota(pi_[:], pattern=[[128, NT]], base=0, channel_multiplier=1), [], [pib])
            self.cp("vector", pos[:], pi_[:], [pib], [posb])
            for ci, (r2, rot, theta) in enumerate(cfgs):
                with ExitStack() as s3:
                    invf, invfb = self.sb(s3, "r_invf", [128, r2], F32)
                    ang, angb = self.sb(s3, "r_ang", [128, NT, r2], F32)
                    a, ab = self.sb(s3, "r_a", [128, NT, r2], F32)
                    kf, kfb = self.sb(s3, "r_kf", [128, NT, r2], F32)
                    ki, kib = self.sb(s3, "r_ki", [128, NT, r2], I32)
                    mk, mkb = self.sb(s3, "r_mk", [128, NT, r2], F32)
                    tab, tabb = self.sb(s3, "r_tab", [128, NT, 2, r2], F32)
                    for j in range(r2):
                        v = float(np.float32(theta) ** np.float32(-(2.0 * j) / rot))
                        self.op("vector", lambda e, j=j, v=v: e.memset(invf[:, j:j + 1], v), [], [invfb])
                    pos_b = bass.AP(pos[:].tensor, pos[:].offset, [list(pos[:].ap[0]), [1, NT], [0, r2]])
                    inv_b = bass.AP(invf[:].tensor, invf[:].offset, [list(invf[:].ap[0]), [0, NT], [1, r2]])
                    self.tt(ang[:], pos_b, inv_b, ALU.mult, [posb, invfb], [angb])
                    for which, shift in ((1, 0.0), (0, math.pi / 2)):
                        self.ts(a[:], ang[:], shift, None, ALU.add, None, [angb], [ab])
                        self.ts(kf[:], a[:], 1.0 / (2 * math.pi), None, ALU.mult, None, [ab], [kfb])
                        self.cp("vector", ki[:], kf[:], [kfb], [kib])
                        self.cp("vector", kf[:], ki[:], [kib], [kfb])
                        self.stt(a[:], kf[:], -C1, a[:], ALU.mult, ALU.add, [kfb, ab], [ab])
                        self.stt(a[:], kf[:], -C2, a[:], ALU.mult, ALU.add, [kfb, ab], [ab])
                        self.ts(mk[:], a[:], math.pi, None, ALU.is_gt, None, [ab], [mkb])
                        self.stt(a[:], mk[:], -2 * math.pi, a[:], ALU.mult, ALU.add, [mkb, ab], [ab])
                        self.ts(mk[:], a[:], -math.pi, None, ALU.is_lt, None, [ab], [mkb])
                        self.stt(a[:], mk[:], 2 * math.pi, a[:], ALU.mult, ALU.add, [mkb, ab], [ab])
                        self.ts(a[:], a[:], 3.1415925, -3.1415925, ALU.min, ALU.max, [ab], [ab])
                        self.act(tab[:, :, which, :], a[:], AF.Sin, [ab], [tabb])
                    self.dma("gpsimd", self.ropeD[ci].ap().rearrange("(t p) c j -> p t c j", p=128), tab[:],
                             [tabb], [self.dbuf("rope", ci)], slow=True)
                    self.barrier()

    def transpose_in(self):
        T = self.T
        with ExitStack() as s2:
            xt = self.sbpool(s2, "ti_x", [128, DM], F32, 3)
            stg = self.sbpool(s2, "ti_s", [128, 8, 512], F32, 2)
            n = 0
            for g in range(T // 512):
                sg, sgb = stg[g % 2]
                for ti in range(4):
                    t0 = g * 512 + ti * 128
                    x_, xb = xt[n % 3]
                    n += 1
                    self.dma("sync", x_[:], self.x[t0:t0 + 128, :], [], [xb])
                    for half in range(2):
                        pt, pb = self.psn()
                        for q in range(4):
                            kc = half * 4 + q
                            self.tp(pt[:, q * 128:(q + 1) * 128], x_[:, kc * 128:(kc + 1) * 128], self.identf[:],
                                    [xb, self.identf_b], [pb])
                        self.cp(self.alt(), sg[:, half * 4:half * 4 + 4, ti * 128:(ti + 1) * 128],
                                pt[:].rearrange("p (q t) -> p q t", q=4), [pb], [sgb])
                self.dma("gpsimd", self.xT[:, g * 512:(g + 1) * 512].rearrange("(kc p) t -> p kc t", p=128), sg[:],
                         [sgb], [self.dbuf("xT", g)])
            self.barrier()

    def transpose_out(self):
        T = self.T
        with ExitStack() as s2:
            xg = self.sbpool(s2, "to_x", [128, 8, 512], F32, 2)
            ot = self.sbpool(s2, "to_o", [128, DM], F32, 3)
            n = 0
            for g in range(T // 512):
                x_, xb = xg[g % 2]
                self.dma("sync", x_[:], self.xT[:, g * 512:(g + 1) * 512].rearrange("(kc p) t -> p kc t", p=128),
                         [self.dbuf("xT", g)], [xb])
                for ti in range(4):
                    o_, ob = ot[n % 3]
                    n += 1
                    for half in range(2):
                        pt, pb = self.psn()
                        for q in range(4):
                            kc = half * 4 + q
                            self.tp(pt[:, q * 128:(q + 1) * 128], x_[:, kc, ti * 128:(ti + 1) * 128], self.identf[:],
                                    [xb, self.identf_b], [pb])
                        self.cp(self.alt(), o_[:, half * 512:(half + 1) * 512], pt[:], [pb], [ob])
                    t0 = g * 512 + ti * 128
                    self.dma("gpsimd", self.out[t0:t0 + 128, :], o_[:], [ob], [self.dbuf("out", t0)])
            self.barrier()


    @staticmethod
    def bc_last(a, n):
        return bass.AP(a.tensor, a.offset, [list(x) for x in a.ap] + [[0, n]])

    @staticmethod
    def bc_col(a, n):
        return bass.AP(a.tensor, a.offset, [list(a.ap[0]), [0, n]])

    @staticmethod
    def bc_mid(a, n):
        ap = [list(x) for x in a.ap]
        return bass.AP(a.tensor, a.offset, [ap[0], [0, n]] + ap[1:])

    def load_cols(self, dst, dstb, src_ap, pattern, **kw):
        self.dma("sync", dst, src_ap.rearrange(pattern, **kw), [], [dstb], slow=True)

    def load_bc(self, dst, dstb, src_row_ap):
        self.dma("sync", dst, src_row_ap.broadcast_to([128, src_row_ap.shape[-1]]), [], [dstb])

    def psb(self, i):
        return self.psbig[:, i, :].bitcast(BF16)

    def norm_fm(self, stack_unused, col0, ntok, gcols, gcolsb, outT, outTb, xkeep=None, loaded=False):
        with ExitStack() as s2:
            if xkeep is None:
                xg_pool = self.sbpool(s2, "nf_x", [128, 8, 512], F32, 2)
            sq_pool = self.sbpool(s2, "nf_sq", [128, 512], BF16, 4)
            rs_pool = self.sbpool(s2, "nf_rs", [128, 512], F32, 2)
            n = 0
            ng = ntok // 512
            bsz = 2 if loaded else 1
            for g0 in range(0, ng, bsz):
                batch = []
                for g in range(g0, min(g0 + bsz, ng)):
                    c0 = col0 + g * 512
                    if xkeep is None:
                        xg, xgb = xg_pool[g % 2]
                        self.dma("sync", xg[:], self.xT[:, c0:c0 + 512].rearrange("(kc p) t -> p kc t", p=128),
                                 [self.dbuf("xT", c0 // 512)], [xgb])
                        xs = lambda kc, xg=xg: xg[:, kc, :]
                    else:
                        xk, xgb = xkeep
                        if not loaded:
                            self.dma("sync", xk[:, :, g * 512:(g + 1) * 512],
                                     self.xT[:, c0:c0 + 512].rearrange("(kc p) t -> p kc t", p=128),
                                     [self.dbuf("xT", c0 // 512)], [xgb])
                        xs = lambda kc, g=g, xk=xk: xk[:, kc, g * 512:(g + 1) * 512]
                    pt, pb = self.psn()
                    for kc in range(8):
                        sq, sqb = sq_pool[n % 4]
                        n += 1
                        self.act(sq[:], xs(kc), AF.Square, [xgb], [sqb])
                        self.mm(pt[:], self.onesb[:], sq[:], kc == 0, kc == 7, [sqb, self.onesb_b], [pb])
                    batch.append((g, xs, xgb, pt, pb))
                for g, xs, xgb, pt, pb in batch:
                    rs, rsb = rs_pool[g % 2]
                    self.rsqrt(rs[:], pt[:], 1.0 / DM, [pb], [rsb])
                for g, xs, xgb, pt, pb in batch:
                    rs, rsb = rs_pool[g % 2]
                    for kc in range(8):
                        self.stt(outT[:, kc, g * 512:(g + 1) * 512], xs(kc), gcols[:, kc:kc + 1], rs[:],
                                 ALU.mult, ALU.mult, [xgb, rsb, gcolsb], [outTb])

    def gemm(self, mode, AT, ATb, KC, ntok, W, col0, ncols, epi, wpool, k0=0, ps_lo=0, ps_hi=8):
        wt, wtb = wpool[self.wi % len(wpool)]
        self.wi += 1
        ATl = ATb if isinstance(ATb, list) else [ATb]
        wv = wt[:, 0:KC * ncols].rearrange("p (k n) -> p k n", k=KC)
        self.dma("sync", wv, W[:, col0:col0 + ncols].rearrange("(kc p) n -> p kc n", p=128), [], [wtb])
        if mode == "tm":
            pend = []
            nt_ = ntok // 128
            for t in range(nt_):
                pt, pb = self.psn(ps_lo, ps_hi)
                for kc in range(KC):
                    self.mm(pt[:, 0:ncols], AT[:, k0 + kc, t * 128:(t + 1) * 128], wv[:, kc, :], kc == 0, kc == KC - 1,
                            ATl + [wtb], [pb])
                r = epi(pt, pb, t)
                if r is not None:
                    pend.append(r)
                if pend and (len(pend) == KGI or t == nt_ - 1):
                    while pend:
                        for g_ in list(pend):
                            try:
                                next(g_)
                            except StopIteration:
                                pend.remove(g_)
        else:
            for m in range(ncols // 128):
                for g in range(ntok // 512):
                    pt, pb = self.psn(ps_lo, ps_hi)
                    for kc in range(KC):
                        self.mm(pt[:], wv[:, kc, m * 128:(m + 1) * 128], AT[:, k0 + kc, g * 512:(g + 1) * 512],
                                kc == 0, kc == KC - 1, ATl + [wtb], [pb])
                    epi(pt, pb, m, g)

    def norm_rope_g(self, v, vb, G, Dg, t, tmp, gain=None, gainb=None, normdim=None, ss_extra=None, rope=None):
        v3 = v.rearrange("p (g d) -> p g d", g=G)
        sq, sqb, ss, ssb, rt, rtb = tmp
        if gain is not None:
            W = G * Dg
            self.tt(sq[:, 0:W], v, v, ALU.mult, [vb], [sqb])
            yield
            self.op("vector", lambda e: e.tensor_reduce(out=ss[:, 0:G], in_=sq[:, 0:W].rearrange("p (g d) -> p g d", g=G),
                                                        axis=AX.X, op=ALU.add), [sqb], [ssb])
            yield
            if ss_extra is not None:
                ex, exb = ss_extra
                self.tt(ss[:, 0:G], ss[:, 0:G], ex, ALU.add, [ssb, exb], [ssb])
                yield
            self.rsqrt(ss[:, 0:G], ss[:, 0:G], 1.0 / normdim, [ssb], [ssb])
            yield
            self.tt(v3, v3, self.bc_last(ss[:, 0:G], Dg), ALU.mult, [vb, ssb], [vb])
            yield
            self.tt(v3, v3, self.bc_mid(gain, G), ALU.mult, [vb, gainb], [vb])
            yield
        if rope is not None:
            off, r2, tab, tabb = rope
            x1 = v3[:, :, off:off + r2]
            x2 = v3[:, :, off + r2:off + 2 * r2]
            cos = self.bc_mid(tab[:, t, 0, :], G)
            sin = self.bc_mid(tab[:, t, 1, :], G)
            n = G * r2
            tv = [rt[:, i * n:(i + 1) * n].rearrange("p (g r) -> p g r", g=G) for i in range(4)]
            self.tt(tv[0], x1, cos, ALU.mult, [vb, tabb], [rtb])
            yield
            self.tt(tv[1], x2, sin, ALU.mult, [vb, tabb], [rtb])
            yield
            self.tt(tv[2], x2, cos, ALU.mult, [vb, tabb], [rtb])
            yield
            self.tt(tv[3], x1, sin, ALU.mult, [vb, tabb], [rtb])
            yield
            self.tt(x1, tv[0], tv[1], ALU.subtract, [rtb], [vb])
            yield
            self.tt(x2, tv[2], tv[3], ALU.add, [rtb], [vb])
            yield

    def norm_rope(self, *a, **kw):
        for _ in self.norm_rope_g(*a, **kw):
            pass

    def tr_stage(self, src, srcb, blocks, tq, stage, stageb):
        i = self.psi % 8
        self.psi += 1
        pb = self.ps[i][1]
        pv = self.psb(i)
        for bi, (lo, w) in enumerate(blocks):
            self.tp(pv[0:w, bi * 128:(bi + 1) * 128], src[:, lo:lo + w], self.identb[:], [srcb, self.identb_b], [pb])
        nb = len(blocks)
        if all(w == 128 for _, w in blocks):
            self.cp(self.alt(), stage[:, 0:nb, tq * 128:(tq + 1) * 128],
                    pv[:, 0:nb * 128].rearrange("p (b t) -> p b t", b=nb), [pb], [stageb])
        else:
            for bi, (lo, w) in enumerate(blocks):
                self.cp(self.alt(), stage[0:w, bi, tq * 128:(tq + 1) * 128], pv[0:w, bi * 128:(bi + 1) * 128],
                        [pb], [stageb])

    def phase1(self, l, s):
        S, NT, NG = self.S, self.NT, self.NG
        tok0 = s * S
        Wd = self.wb["w_in", l]
        with ExitStack() as P:
            xnT, xnTb = self.sb(P, "xnT", [128, 8, S], BF16)
            g1, g1b = self.sb(P, "g1", [128, 8], F32)
            self.load_cols(g1[:], g1b, self.norm1_g[l], "(kc p) -> p kc", p=128)
            self.norm_fm(None, tok0, S, g1, g1b, xnT, xnTb)
            self.barrier()
            self.mark(" P1lru")
            wpool = self.sbpool(P, "w", [128, 4096], BF16, 2)
            o16 = self.sbpool(P, "o16", [128, 512], BF16, 3)
            self.oi = 0
            with ExitStack() as PA:
                self.lru(PA, l, s, xnT, xnTb, Wd, wpool)
                self.barrier()
                self.mark(" P1qkv")
            with ExitStack() as PQ:
                self.prep_qkv(PQ, l, s, xnT, xnTb, Wd, wpool, o16)
                self.barrier()

    def lru(self, PA, l, s, xnT, xnTb, Wd, wpool):
        S, NT, NG = self.S, self.NT, self.NG
        TC = min(1024, S)
        nch = S // TC
        cw, cwb = self.sb(PA, "cw", [128, 4, 8], F32)
        cb, cbb = self.sb(PA, "cb", [128, 8], F32)
        ba, bab = self.sb(PA, "ba", [128, 2, 8], F32)
        bx, bxb = self.sb(PA, "bx", [128, 2, 8], F32)
        lam, lamb = self.sb(PA, "lam", [128, 16], F32)
        hh, hhb = self.sb(PA, "hh", [128, 16], F32)
        cd, cdb = self.sb(PA, "cd", [128, 16], F32)
        cd2, cd2b = self.sb(PA, "cd2", [128, 16], F32)
        for t_ in range(4):
            self.load_cols(cw[:, t_, :], cwb, self.conv_w[l, t_], "(c p) -> p c", p=128)
        self.load_cols(cb[:], cbb, self.conv_b[l], "(c p) -> p c", p=128)
        for d_ in range(2):
            self.load_cols(ba[:, d_, :], bab, self.lru_ba[l, d_], "(c p) -> p c", p=128)
            self.load_cols(bx[:, d_, :], bxb, self.lru_bx[l, d_], "(c p) -> p c", p=128)
            self.load_cols(lam[:, d_ * 8:(d_ + 1) * 8], lamb, self.lru_lambda[l, d_], "(c p) -> p c", p=128)
        self.act(lam[:], lam[:], AF.Exp, [lamb], [lamb], scale=-1.0)
        self.ts(hh[:], lam[:], -1.0 / 6, 1.0 / 5, ALU.mult, ALU.add, [lamb], [hhb])
        for cst in (-1.0 / 4, 1.0 / 3, -1.0 / 2, 1.0):
            self.tt(hh[:], hh[:], lam[:], ALU.mult, [hhb, lamb], [hhb])
            self.ts(hh[:], hh[:], cst, None, ALU.add, None, [hhb], [hhb])
        self.tt(hh[:], hh[:], lam[:], ALU.mult, [hhb, lamb], [hhb])
        self.ts(cd[:], hh[:], -8.0, None, ALU.mult, None, [hhb], [cdb])
        self.ts(cd2[:], hh[:], -16.0, None, ALU.mult, None, [hhb], [cd2b])
        wst, wstb = self.sb(PA, "wst", [128, 4, 128], F32)
        wbf, wbfb = self.sb(PA, "wbf", [128, 4, 128], BF16)
        self.op("vector", lambda e: e.memset(wst[:], 0.0), [], [wstb])
        Pb, Pbb = self.sb(PA, "Pb", [128, S + 4], F32)
        gg, ggb = self.sb(PA, "gg", [128, S], BF16)
        xc, xcb_ = self.sb(PA, "xc", [128, S], F32)
        x16, x16b = self.sb(PA, "x16", [128, S], BF16)
        Bsets = []
        for d_ in range(2):
            B1, B1b = self.sb(PA, "B1%d" % d_, [128, TC], F32)
            B2, B2b = self.sb(PA, "B2%d" % d_, [128, TC], F32)
            B3, B3b = self.sb(PA, "B3%d" % d_, [128, TC], F32)
            Bsets.append((B1, B1b, B2, B2b, B3, B3b))
        hb, hbb = self.sb(PA, "hb", [128, S], F32)
        tA = self.sbpool(PA, "tA", [128, 512], F32, 2)
        self.op("vector", lambda e: e.memset(Pb[:, 0:2], 0.0), [], [Pbb])
        self.op("vector", lambda e: e.memset(Pb[:, S + 2:S + 4], 0.0), [], [Pbb])
        wsrc = (self.lru_wa, self.lru_wx)
        for c in range(8):
            for wi in range(2):
                for d in range(2):
                    for half in range(2):
                        self.dma("sync", wst[half * 64:(half + 1) * 64, wi * 2 + d, half * 64:(half + 1) * 64],
                                 wsrc[wi][l, d, 2 * c + half], [], [wstb])
            self.cp("vector", wbf[:], wst[:], [wstb], [wbfb])

            def epi_x(pt, pb, m, g):
                self.cp(self.alt(), Pb[:, 2 + g * 512:2 + (g + 1) * 512], pt[:], [pb], [Pbb])

            def epi_g(pt, pb, m, g):
                t1, t1b = tA[g % 2]
                self.act(t1[:], pt[:], AF.Square, [pb], [t1b])
                self.ts(t1[:], t1[:], 0.044715, 1.0, ALU.mult, ALU.add, [t1b], [t1b])
                self.tt(t1[:], t1[:], pt[:], ALU.mult, [t1b, pb], [t1b])
                self.act(t1[:], t1[:], AF.Sigmoid, [t1b], [t1b], scale=1.5957691216057308)
                self.tt(gg[:, g * 512:(g + 1) * 512], t1[:], pt[:], ALU.mult, [t1b, pb], [ggb])

            self.gemm("fm", xnT, xnTb, 8, S, Wd, OFF["ax"] + c * 128, 128, epi_x, wpool)
            self.gemm("fm", xnT, xnTb, 8, S, Wd, OFF["ag"] + c * 128, 128, epi_g, wpool)
            self.ts(xc[:], Pb[:, 0:S], cw[:, 0, c:c + 1], cb[:, c:c + 1], ALU.mult, ALU.add, [Pbb, cwb, cbb], [xcb_])
            for j in range(1, 4):
                self.stt(xc[:], Pb[:, j:j + S], cw[:, j, c:c + 1], xc[:], ALU.mult, ALU.add, [Pbb, cwb, xcb_], [xcb_])
            self.cp("scalar", x16[:], xc[:], [xcb_], [x16b])
            hs = Pb

            def dgen(d, c=c):
                D1, D1b, D2, D2b, D3, D3b = Bsets[d]
                order = range(nch) if d == 0 else range(nch - 1, -1, -1)
                first = True
                for ch in order:
                    t0 = ch * TC
                    for sub in range(TC // 512):
                        cs = slice(t0 + sub * 512, t0 + (sub + 1) * 512)
                        bs = slice(sub * 512, (sub + 1) * 512)
                        pt, pb = self.psn()
                        self.mm(pt[:], wbf[:, 0 + d, :], x16[:, cs], True, True, [wbfb, x16b], [pb])
                        self.act(D1[:, bs], pt[:], AF.Sigmoid, [pb, bab], [D1b], bias=ba[:, d, c:c + 1])
                        yield
                        pt, pb = self.psn()
                        self.mm(pt[:], wbf[:, 2 + d, :], x16[:, cs], True, True, [wbfb, x16b], [pb])
                        self.act(D3[:, bs], pt[:], AF.Sigmoid, [pb, bxb], [D3b], bias=bx[:, d, c:c + 1])
                        yield
                    k = d * 8 + c
                    self.act(D2[:], D1[:], AF.Exp, [D1b, cd2b], [D2b], scale=cd2[:, k:k + 1])
                    yield
                    self.act(D2[:], D2[:], AF.Relu, [D2b], [D2b], scale=-1.0, bias=self.onecol[:, :])
                    yield
                    self.act(D2[:], D2[:], AF.Sqrt, [D2b], [D2b])
                    yield
                    self.act(D1[:], D1[:], AF.Exp, [D1b, cdb], [D1b], scale=cd[:, k:k + 1])
                    yield
                    self.tt(D3[:], D3[:], xc[:, t0:t0 + TC], ALU.mult, [D3b, xcb_], [D3b])
                    yield
                    self.tt(D3[:], D3[:], D2[:], ALU.mult, [D3b, D2b], [D3b])
                    yield
                    if d == 0:
                        init = 0.0 if first else hs[:, 2 + t0 - 1:2 + t0]
                        self.op("vector", lambda e, t0=t0, init=init: e.tensor_tensor_scan(
                            out=hs[:, 2 + t0:2 + t0 + TC], data0=D1[:], data1=D3[:], initial=init,
                            op0=ALU.mult, op1=ALU.add), [D1b, D3b, Pbb], [Pbb])
                    else:
                        init = 0.0 if first else hb[:, t0 + TC:t0 + TC + 1]
                        ov = hb[:, t0:t0 + TC]
                        orev = bass.AP(ov.tensor, ov.offset + (TC - 1), [list(ov.ap[0]), [-1, TC]])
                        self.op("vector", lambda e, init=init, orev=orev: e.tensor_tensor_scan(
                            out=orev, data0=D1[:, ::-1], data1=D3[:, ::-1], initial=init,
                            op0=ALU.mult, op1=ALU.add), [D1b, D3b, hbb], [hbb])
                    yield
                    first = False

            gens = [dgen(0), dgen(1)]
            while gens:
                for g_ in list(gens):
                    try:
                        next(g_)
                    except StopIteration:
                        gens.remove(g_)
            self.tt(hs[:, 2:S + 2], hs[:, 2:S + 2], hb[:], ALU.add, [Pbb, hbb], [Pbb])
            self.tt(x16[:], hs[:, 2:S + 2], gg[:], ALU.mult, [Pbb, ggb, x16b], [x16b])
            self.dma("gpsimd", self.YT[c * 128:(c + 1) * 128, :], x16[:], [x16b], [self.dbuf("YT", 0, c)])


    def prep_qkv(self, PQ, l, s, xnT, xnTb, Wd, wpool, o16):
        S, NT, NG = self.S, self.NT, self.NG
        def bct(name, src, n):
            t, b = self.sb(PQ, name, [128, n], F32)
            self.load_bc(t[:], b, src)
            return t, b
        qg_b, qg_bb = bct("dqg", self.diff_q_g[l:l + 1, :], 64)
        kg_b, kg_bb = bct("dkg", self.diff_k_g[l:l + 1, :], 64)
        qa_g, qa_gb = bct("mqa", self.mla_qa_g[l:l + 1, :], 384)
        kva_g, kva_gb = bct("mkva", self.mla_kva_g[l:l + 1, :], 256)
        mq_g, mq_gb = bct("mqg", self.mla_q_g[l:l + 1, :], 192)
        mk_g, mk_gb = bct("mkg", self.mla_k_g[l:l + 1, :], 192)
        gb, gbb = self.sb(PQ, "gateb", [128, 4, 8], F32)
        for b_ in range(4):
            self.load_cols(gb[:, b_, :], gbb, self.gate_b[l, b_], "(m p) -> p m", p=128)
        ropes = []
        for ci, r2 in enumerate((8, 32, 32)):
            t, b = self.sb(PQ, "rope%d" % ci, [128, NT, 2, r2], F32)
            self.dma("sync", t[:], self.ropeD[ci].ap().rearrange("(t p) c j -> p t c j", p=128),
                     [self.dbuf("rope", ci)], [b], slow=True)
            ropes.append((t, b))
        cqnT, cqnTb = self.sb(PQ, "cqnT", [128, 3, S], BF16)
        ckvnT, ckvnTb = self.sb(PQ, "ckvnT", [128, 2, S], BF16)
        kper, kperb = self.sb(PQ, "kper", [128, NT, 64], F32)
        sspe, sspeb = self.sb(PQ, "sspe", [128, NT], F32)
        NV = KDEFER + 3
        vpool = self.sbpool(PQ, "v", [128, 512], F32, NV)
        v16pool = self.sbpool(PQ, "v16", [128, 512], BF16, NV)
        dq = []

        def defer(fn):
            dq.append(fn)
            while len(dq) > KDEFER:
                dq.pop(0)()

        def flush():
            while dq:
                dq.pop(0)()
        tmps = []
        for i_ in range(2):
            sq_, sqb_ = self.sb(PQ, "sq%d" % i_, [128, 512], F32)
            ss_, ssb_ = self.sb(PQ, "ss%d" % i_, [128, 8], F32)
            rt_, rtb_ = self.sb(PQ, "rt%d" % i_, [128, 1024], F32)
            tmps.append((sq_, sqb_, ss_, ssb_, rt_, rtb_))
        tmp = tmps[0]
        sq, sqb, ss, ssb, rt, rtb = tmp
        stages = self.sbpool(PQ, "stg", [128, 4, 512], BF16, 2)
        st_i = [0]
        vi = [0]

        def nextv():
            r = vpool[vi[0] % NV] + v16pool[vi[0] % NV]
            vi[0] += 1
            return r

        def store_stage(stage, stageb, dests, g):
            for bi, (dt_, r0, nr) in enumerate(dests):
                self.dma("gpsimd", dt_[r0:r0 + nr, g * 512:(g + 1) * 512], stage[0:nr, bi, :], [stageb],
                         [self.dbuf(dt_.name, r0, g)])

        def seg_qk(AT, ATb, KC, W, col0, ncols, G, Dg, gain, gainb, normdim, rope, blocks, dests_fn, scale=None):
            state = {}

            def epi(pt, pb, t):
                v, vb, v16, v16b = nextv()
                self.cp("scalar", v[:, 0:ncols], pt[:, 0:ncols], [pb], [vb])
                yield
                for _ in self.norm_rope_g(v[:, 0:ncols], vb, G, Dg, t, tmps[t % 2], gain=gain, gainb=gainb,
                                          normdim=normdim, rope=rope):
                    yield
                if scale is None:
                    self.cp("scalar", v16[:, 0:ncols], v[:, 0:ncols], [vb], [v16b])
                else:
                    self.act(v16[:, 0:ncols], v[:, 0:ncols], AF.Copy, [vb], [v16b], scale=scale)
                def later(t=t, v16=v16, v16b=v16b):
                    if t % 4 == 0:
                        state["st"] = stages[st_i[0] % 2]
                        st_i[0] += 1
                    stage, stageb = state["st"]
                    self.tr_stage(v16, v16b, blocks, t % 4, stage, stageb)
                    if t % 4 == 3:
                        store_stage(stage, stageb, dests_fn(), t // 4)
                defer(later)
            self.gemm("tm", AT, ATb, KC, S, W, col0, ncols, epi, wpool)

        def seg_v(col0, dst, dcol0):
            def epi(pt, pb, t):
                o, ob = o16[self.oi % 3]
                self.oi += 1
                self.cp(self.alt(), o[:], pt[:], [pb], [ob])
                self.dma("gpsimd", dst[t * 128:(t + 1) * 128, dcol0:dcol0 + 512], o[:], [ob],
                         [self.dbuf(dst.name, t, dcol0)])
            self.gemm("tm", xnT, xnTb, 8, S, Wd, col0, 512, epi, wpool)

        b4 = [(i * 128, 128) for i in range(4)]
        for nt in range(2):
            seg_qk(xnT, xnTb, 8, Wd, OFF["bq"] + nt * 512, 512, 8, 64, qg_b[:], qg_bb, 64,
                   (0, 8, ropes[0][0], ropes[0][1]), b4,
                   lambda nt=nt: [(self.QTb, nt * 512 + i * 128, 128) for i in range(4)])
            seg_qk(xnT, xnTb, 8, Wd, OFF["bk"] + nt * 512, 512, 8, 64, kg_b[:], kg_bb, 64,
                   (0, 8, ropes[0][0], ropes[0][1]), b4,
                   lambda nt=nt: [(self.KTb, nt * 512 + i * 128, 128) for i in range(4)])
            seg_v(OFF["bv"] + nt * 512, self.Vb, nt * 512)
            seg_v(OFF["dv"] + nt * 512, self.Vd, nt * 512)
        self.mark("  q:dqdk")
        seg_qk(xnT, xnTb, 8, Wd, OFF["dq"], 512, 8, 64, None, None, None, (0, 32, ropes[2][0], ropes[2][1]), b4,
               lambda: [(self.QTd, i * 128, 128) for i in range(4)])
        seg_qk(xnT, xnTb, 8, Wd, OFF["dk"], 512, 8, 64, None, None, None, (0, 32, ropes[2][0], ropes[2][1]), b4,
               lambda: [(self.KTd, i * 128, 128) for i in range(4)], scale=0.125)

        self.mark("  q:cq")
        def epi_cq(pt, pb, t):
            v, vb, v16, v16b = nextv()
            self.cp("scalar", v[:, 0:384], pt[:, 0:384], [pb], [vb])
            yield
            for _ in self.norm_rope_g(v[:, 0:384], vb, 1, 384, t, tmps[t % 2], gain=qa_g[:], gainb=qa_gb, normdim=384):
                yield
            self.cp("scalar", v16[:, 0:384], v[:, 0:384], [vb], [v16b])

            def later(t=t, v16=v16, v16b=v16b):
                i = self.psi % 8
                self.psi += 1
                pv = self.psb(i)
                for bi in range(3):
                    self.tp(pv[:, bi * 128:(bi + 1) * 128], v16[:, bi * 128:(bi + 1) * 128], self.identb[:],
                            [v16b, self.identb_b], [self.ps[i][1]])
                self.cp(self.alt(), cqnT[:, :, t * 128:(t + 1) * 128], pv[:, 0:384].rearrange("p (b t) -> p b t", b=3),
                        [self.ps[i][1]], [cqnTb])
            defer(later)
        self.gemm("tm", xnT, xnTb, 8, S, Wd, OFF["cq"], 384, epi_cq, wpool)

        def epi_ckv(pt, pb, t):
            v, vb, v16, v16b = nextv()
            sq, sqb, ss, ssb, rt, rtb = tmps[t % 2]
            self.cp("scalar", v[:, 0:320], pt[:, 0:320], [pb], [vb])
            yield
            self.tt(sq[:, 0:64], v[:, 256:320], v[:, 256:320], ALU.mult, [vb], [sqb])
            yield
            self.op("vector", lambda e: e.tensor_reduce(out=sspe[:, t:t + 1], in_=sq[:, 0:64], axis=AX.X, op=ALU.add),
                    [sqb], [sspeb])
            yield
            self.tt(v[:, 256:320], v[:, 256:320], mk_g[:, 128:192], ALU.mult, [vb, mk_gb], [vb])
            yield
            for _ in self.norm_rope_g(v[:, 256:320], vb, 1, 64, t, tmps[t % 2], rope=(0, 32, ropes[1][0], ropes[1][1])):
                yield
            self.cp("vector", kper[:, t, :], v[:, 256:320], [vb], [kperb])
            yield
            for _ in self.norm_rope_g(v[:, 0:256], vb, 1, 256, t, tmps[t % 2], gain=kva_g[:], gainb=kva_gb, normdim=256):
                yield
            self.cp("scalar", v16[:, 0:256], v[:, 0:256], [vb], [v16b])

            def later(t=t, v16=v16, v16b=v16b):
                i = self.psi % 8
                self.psi += 1
                pv = self.psb(i)
                for bi in range(2):
                    self.tp(pv[:, bi * 128:(bi + 1) * 128], v16[:, bi * 128:(bi + 1) * 128], self.identb[:],
                            [v16b, self.identb_b], [self.ps[i][1]])
                self.cp(self.alt(), ckvnT[:, :, t * 128:(t + 1) * 128], pv[:, 0:256].rearrange("p (b t) -> p b t", b=2),
                        [self.ps[i][1]], [ckvnTb])
            defer(later)
        self.gemm("tm", xnT, xnTb, 8, S, Wd, OFF["ckv"], 320, epi_ckv, wpool)

        self.mark("  q:fm")
        def seg_fm(col0, func, bias_fn, dst, row0):
            def epi(pt, pb, m, g):
                o, ob = o16[self.oi % 3]
                self.oi += 1
                b = bias_fn(m)
                if b is None:
                    self.act(o[:], pt[:], func, [pb], [ob])
                else:
                    self.act(o[:], pt[:], func, [pb, gbb], [ob], bias=b)
                r0 = row0 + m * 128
                self.dma("gpsimd", dst[r0:r0 + 128, g * 512:(g + 1) * 512], o[:], [ob], [self.dbuf(dst.name, r0, g)])
            self.gemm("fm", xnT, xnTb, 8, S, Wd, col0, 512, epi, wpool)
        for nt in range(2):
            seg_fm(OFF["dg"] + nt * 512, AF.Silu, lambda m: None, self.GdT, nt * 512)
        for b_ in range(4):
            for nt in range(2):
                seg_fm(OFF["gl"] + b_ * 1024 + nt * 512, AF.Sigmoid,
                       lambda m, b_=b_, nt=nt: gb[:, b_, nt * 4 + m:nt * 4 + m + 1], self.GT, b_ * 1024 + nt * 512)

        self.mark("  q:qup")
        flush()
        Wq = self.wb["wuq", l]
        Wkv = self.wb["wukv", l]
        bq = [(0, 128), (128, 64), (192, 128), (320, 64)]
        for hp in range(4):
            seg_qk(cqnT, cqnTb, 3, Wq, hp * 384, 384, 2, 192, mq_g[:], mq_gb, 192,
                   (128, 32, ropes[1][0], ropes[1][1]), bq,
                   lambda hp=hp: [(self.QTc, (2 * hp) * 192, 128), (self.QTc, (2 * hp) * 192 + 128, 64),
                                  (self.QTc, (2 * hp + 1) * 192, 128), (self.QTc, (2 * hp + 1) * 192 + 128, 64)])
        self.mark("  q:kvup")
        for hp in range(4):
            state = {}

            def epi_kv(pt, pb, t, hp=hp, state=state):
                v, vb, v16, v16b = nextv()
                sq, sqb, ss, ssb, rt, rtb = tmps[t % 2]
                self.cp("scalar", v[:], pt[:], [pb], [vb])
                yield
                v4 = v[:].rearrange("p (h c d) -> p h c d", h=2, c=2)
                kn = v4[:, :, 0, :]
                s4 = sq[:].rearrange("p (h c d) -> p h c d", h=2, c=2)
                self.tt(s4[:, :, 0, :], kn, kn, ALU.mult, [vb], [sqb])
                yield
                self.op("vector", lambda e: e.tensor_reduce(out=ss[:, 0:2], in_=s4[:, :, 0, :], axis=AX.X, op=ALU.add),
                        [sqb], [ssb])
                yield
                self.tt(ss[:, 0:2], ss[:, 0:2], self.bc_col(sspe[:, t:t + 1], 2),
                        ALU.add, [ssb, sspeb], [ssb])
                yield
                self.rsqrt(ss[:, 0:2], ss[:, 0:2], 1.0 / 192, [ssb], [ssb])
                yield
                self.tt(kn, kn, self.bc_last(ss[:, 0:2], 128), ALU.mult, [vb, ssb], [vb])
                yield
                self.tt(kn, kn, self.bc_mid(mk_g[:, 0:128], 2), ALU.mult, [vb, mk_gb], [vb])
                yield
                v16v = v16[:, 0:256].rearrange("p (h d) -> p h d", h=2)
                self.cp("scalar", v16v, kn, [vb], [v16b])
                yield
                pe = v16[:, 256:384].rearrange("p (h d) -> p h d", h=2)
                self.tt(pe, self.bc_mid(kper[:, t, :], 2), self.bc_last(ss[:, 0:2], 64), ALU.mult,
                        [kperb, ssb], [v16b])
                yield
                o, ob = o16[self.oi % 3]
                self.oi += 1
                ov = o[:, 0:256].rearrange("p (h d) -> p h d", h=2)
                self.cp("vector", ov, v4[:, :, 1, :], [vb], [ob])
                self.dma("gpsimd", self.Vc[t * 128:(t + 1) * 128, hp * 256:(hp + 1) * 256], o[:, 0:256], [ob],
                         [self.dbuf("Vc", t, hp)])
                def later(t=t, v16=v16, v16b=v16b):
                    if t % 4 == 0:
                        state["st"] = stages[st_i[0] % 2]
                        st_i[0] += 1
                    stage, stageb = state["st"]
                    self.tr_stage(v16, v16b, [(0, 128), (128, 128), (256, 64), (320, 64)], t % 4, stage, stageb)
                    if t % 4 == 3:
                        store_stage(stage, stageb, [(self.KTc, (2 * hp) * 192, 128), (self.KTc, (2 * hp + 1) * 192, 128),
                                                    (self.KTc, (2 * hp) * 192 + 128, 64),
                                                    (self.KTc, (2 * hp + 1) * 192 + 128, 64)], t // 4)
                defer(later)
            self.gemm("tm", ckvnT, ckvnTb, 2, S, Wkv, hp * 512, 512, epi_kv, wpool)
        flush()


    def load_rows(self, dst, dstb, src, r0, nr):
        self.dma("sync", dst[0:nr, :], src[r0:r0 + nr, :], [], [dstb])

    def load_v(self, dst, dstb, src, c0):
        self.dma("sync", dst[:], src[:, c0:c0 + 128].rearrange("(t p) e -> p t e", p=128), [], [dstb])

    def phaseB(self, l, s):
        S, NT, NG = self.S, self.NT, self.NG
        lam_init = 0.8 - 0.6 * math.exp(-0.3 * l)
        with ExitStack() as P:
            lp, lpb = self.sb(P, "lp", [128, 256], F32)
            self.load_bc(lp[:], lpb, self.diff_lam[l:l + 1].rearrange("a b c -> a (b c)"))
            pr, prb = self.sb(P, "pr", [128, 128], F32)
            e2, e2b = self.sb(P, "e2", [128, 2], F32)
            neglam, neglamb = self.sb(P, "neglam", [128, 1], F32)
            gcol, gcolb = self.sb(P, "gcol", [128, 1], F32)
            lp4 = lp[:].rearrange("p (a b c) -> p a b c", a=2, b=2)
            self.tt(pr[:].rearrange("p (a c) -> p a c", a=2), lp4[:, :, 0, :], lp4[:, :, 1, :], ALU.mult, [lpb], [prb])
            self.op("vector", lambda e: e.tensor_reduce(out=e2[:], in_=pr[:].rearrange("p (a c) -> p a c", a=2),
                                                        axis=AX.X, op=ALU.add), [prb], [e2b])
            self.act(e2[:], e2[:], AF.Exp, [e2b], [e2b])
            self.tt(neglam[:], e2[:, 1:2], e2[:, 0:1], ALU.subtract, [e2b], [neglamb])
            self.ts(neglam[:], neglam[:], -lam_init, None, ALU.add, None, [neglamb], [neglamb])
            self.load_cols(gcol[:], gcolb, self.diff_sub_g[l], "(p o) -> p o", o=1)
            self.ts(gcol[:], gcol[:], 1.0 - lam_init, None, ALU.mult, None, [gcolb], [gcolb])
            QT = self.sbpool(P, "QT", [128, S], BF16, 2)
            KT = self.sbpool(P, "KT", [128, S], BF16, 2)
            V = self.sbpool(P, "V", [128, NT, 128], BF16, 2)
            Pt = self.sbpool(P, "Pt", [128, 512], BF16, 4)
            tf = self.sbpool(P, "tf", [128, 512], F32, 6)
            y16 = self.sbpool(P, "y16", [128, 512], BF16, 2)
            pi = 0
            for h in range(8):
                q, qb = QT[h % 2]
                k, kb = KT[h % 2]
                v, vb = V[h % 2]
                self.load_rows(q, qb, self.QTb, h * 128, 128)
                self.load_rows(k, kb, self.KTb, h * 128, 128)
                self.load_v(v, vb, self.Vb, h * 128)
                for qg in range(NG):
                    qs = slice(qg * 512, (qg + 1) * 512)
                    O = (self.ps[0], self.ps[1])
                    Lp = (self.ps[2], self.ps[3])
                    for kt in range(NT):
                        ks = slice(kt * 128, (kt + 1) * 128)
                        for si in range(2):
                            lo, hi = si * 64, si * 64 + 64
                            sp, spb = self.psn(4, 8)
                            self.mm(sp[:], k[lo:hi, ks], q[lo:hi, qs], True, True, [kb, qb], [spb])
                            p_, p_b = Pt[pi % 4]
                            pi += 1
                            self.act(p_[:], sp[:], AF.Exp, [spb], [p_b], scale=0.125)
                            self.mm(O[si][0][:], v[:, kt, :], p_[:], kt == 0, kt == NT - 1, [vb, p_b], [O[si][1]])
                            self.mm(Lp[si][0][:], self.onesb[:], p_[:], kt == 0, kt == NT - 1, [self.onesb_b, p_b],
                                    [Lp[si][1]])
                    o1, o1b = tf[0]
                    o2, o2b = tf[1]
                    r1, r1b = tf[2]
                    r2, r2b = tf[3]
                    oo, oob = tf[4]
                    rs, rsb = tf[5]
                    self.cp("scalar", o1[:], O[0][0][:], [O[0][1]], [o1b])
                    self.cp("vector", o2[:], O[1][0][:], [O[1][1]], [o2b])
                    self.act(r1[:], Lp[0][0][:], AF.Ln, [Lp[0][1]], [r1b])
                    self.act(r2[:], Lp[1][0][:], AF.Ln, [Lp[1][1]], [r2b])
                    self.act(r1[:], r1[:], AF.Exp, [r1b], [r1b], scale=-1.0)
                    self.act(r2[:], r2[:], AF.Exp, [r2b], [r2b], scale=-1.0)
                    self.tt(o1[:], o1[:], r1[:], ALU.mult, [o1b, r1b], [o1b])
                    self.tt(o2[:], o2[:], r2[:], ALU.mult, [o2b, r2b], [o2b])
                    self.stt(oo[:], o2[:], neglam[:, 0:1], o1[:], ALU.mult, ALU.add, [o2b, o1b, neglamb], [oob])
                    sq, sqb = Pt[pi % 4]
                    pi += 1
                    self.act(sq[:], oo[:], AF.Square, [oob], [sqb])
                    sp, spb = self.psn(4, 8)
                    self.mm(sp[:], self.onesb[:], sq[:], True, True, [self.onesb_b, sqb], [spb])
                    self.rsqrt(rs[:], sp[:], 1.0 / 128, [spb], [rsb])
                    y, yb = y16[(h * NG + qg) % 2]
                    self.stt(y[:], oo[:], gcol[:, 0:1], rs[:], ALU.mult, ALU.mult, [oob, rsb, gcolb], [yb])
                    self.dma("gpsimd", self.YT[1024 + h * 128:1024 + (h + 1) * 128, qs], y[:], [yb],
                             [self.dbuf("YT", 1, h, qg)])
            self.barrier()

    def phaseC(self, l, s):
        S, NT, NG = self.S, self.NT, self.NG
        sc = 192.0 ** -0.5
        with ExitStack() as P:
            Qn = self.sbpool(P, "Qn", [128, S], BF16, 2)
            Qp = self.sbpool(P, "Qp", [64, S], BF16, 2)
            Kn = self.sbpool(P, "Kn", [128, S], BF16, 2)
            Kp = self.sbpool(P, "Kp", [64, S], BF16, 2)
            V = self.sbpool(P, "V", [128, NT, 128], BF16, 2)
            Pt = self.sbpool(P, "Pt", [128, 512], BF16, 4)
            tf = self.sbpool(P, "tf", [128, 512], F32, 4)
            y16 = self.sbpool(P, "y16", [128, 512], BF16, 2)
            pi = 0
            ti = 0
            for h in range(8):
                qn, qnb = Qn[h % 2]
                qp, qpb = Qp[h % 2]
                kn, knb = Kn[h % 2]
                kp, kpb = Kp[h % 2]
                v, vb = V[h % 2]
                self.load_rows(qn, qnb, self.QTc, h * 192, 128)
                self.load_rows(qp, qpb, self.QTc, h * 192 + 128, 64)
                self.load_rows(kn, knb, self.KTc, h * 192, 128)
                self.load_rows(kp, kpb, self.KTc, h * 192 + 128, 64)
                self.load_v(v, vb, self.Vc, h * 128)
                for qg in range(NG):
                    qs = slice(qg * 512, (qg + 1) * 512)
                    O, Ob = self.ps[qg % 2 * 2]
                    Lt, Lb = self.ps[qg % 2 * 2 + 1]
                    for kt in range(NT):
                        ks = slice(kt * 128, (kt + 1) * 128)
                        sp, spb = self.psn(4, 8)
                        self.mm(sp[:], kn[:, ks], qn[:, qs], True, False, [knb, qnb], [spb])
                        self.mm(sp[:], kp[:, ks], qp[:, qs], False, True, [kpb, qpb], [spb])
                        p_, p_b = Pt[pi % 4]
                        pi += 1
                        self.act(p_[:], sp[:], AF.Exp, [spb], [p_b], scale=sc)
                        self.mm(O[:], v[:, kt, :], p_[:], kt == 0, kt == NT - 1, [vb, p_b], [Ob])
                        self.mm(Lt[:], self.onesb[:], p_[:], kt == 0, kt == NT - 1, [self.onesb_b, p_b], [Lb])
                    o1, o1b = tf[ti % 4]
                    r1, r1b = tf[(ti + 1) % 4]
                    ti += 2
                    self.cp("vector", o1[:], O[:], [Ob], [o1b])
                    self.act(r1[:], Lt[:], AF.Ln, [Lb], [r1b])
                    self.act(r1[:], r1[:], AF.Exp, [r1b], [r1b], scale=-1.0)
                    y, yb = y16[(h * NG + qg) % 2]
                    self.tt(y[:], o1[:], r1[:], ALU.mult, [o1b, r1b], [yb])
                    self.dma("gpsimd", self.YT[2048 + h * 128:2048 + (h + 1) * 128, qs], y[:], [yb],
                             [self.dbuf("YT", 2, h, qg)])
            self.barrier()

    def phaseD(self, l, s):
        S, NT, NG = self.S, self.NT, self.NG
        mmax = max(4 * (NG - 1), 1)
        nneg = max(NT - 4, 1)
        with ExitStack() as P:
            gncol, gncolb = self.sb(P, "gncol", [128, 1], F32)
            self.load_cols(gncol[:], gncolb, self.ret_gn_g[l], "(p o) -> p o", o=1)
            ii, iib = self.sb(P, "ii", [128, 512], I32)
            J, Jb = self.sb(P, "J", [128, 512], F32)
            Jr, Jrb = self.sb(P, "Jr", [128, 512], F32)
            A, Ab = self.sb(P, "A", [128, 4, 512], F32)
            pbase, pbaseb = self.sb(P, "pbase", [128, mmax], F32)
            nbase, nbaseb = self.sb(P, "nbase", [128, nneg], F32)
            self.op("gpsimd", lambda e: e.iota(ii[:], pattern=[[1, 512]], base=0, channel_multiplier=0), [], [iib])
            self.cp("vector", J[:], ii[:], [iib], [Jb])
            self.ts(Jr[:], J[:], -1.0, 511.0, ALU.mult, ALU.add, [Jb], [Jrb])
            for m in range(4):
                self.op("gpsimd", lambda e, m=m: e.iota(ii[:], pattern=[[1, 512]], base=-128 * m, channel_multiplier=-1),
                        [iib], [iib])
                self.cp("vector", A[:, m, :], ii[:], [iib], [Ab])
            self.act(A[:], A[:], AF.Abs, [Ab], [Ab])
            self.op("gpsimd", lambda e: e.iota(ii[:, 0:mmax], pattern=[[128, mmax]], base=128, channel_multiplier=-1),
                    [iib], [iib])
            self.cp("vector", pbase[:], ii[:, 0:mmax], [iib], [pbaseb])
            self.op("gpsimd", lambda e: e.iota(ii[:, 0:nneg], pattern=[[128, nneg]], base=1, channel_multiplier=1),
                    [iib], [iib])
            self.cp("vector", nbase[:], ii[:, 0:nneg], [iib], [nbaseb])
            rowp, rowpb = self.sb(P, "rowp", [128, 512], F32)
            rown, rownb = self.sb(P, "rown", [128, 512], F32)
            Dt, Dtb = self.sb(P, "Dt", [128, 4, 512], F32)
            cfp, cfpb = self.sb(P, "cfp", [128, mmax], F32)
            cfn, cfnb = self.sb(P, "cfn", [128, nneg], F32)
            QT = self.sbpool(P, "QT", [64, S], BF16, 2)
            KT = self.sbpool(P, "KT", [64, S], BF16, 2)
            V = self.sbpool(P, "V", [128, NT, 128], BF16, 2)
            Pt = self.sbpool(P, "Pt", [128, 512], BF16, 4)
            tf = self.sbpool(P, "tf", [128, 512], F32, 6)
            sgp = self.sbpool(P, "sg", [128, 512], BF16, 2)
            y16 = self.sbpool(P, "y16", [128, 512], BF16, 2)
            pi = 0
            for h in range(8):
                lg = math.log1p(-2.0 ** (-5.0 - h))
                self.act(rowp[:], J[:], AF.Exp, [Jb], [rowpb], scale=lg)
                self.act(rown[:], Jr[:], AF.Exp, [Jrb], [rownb], scale=lg)
                self.act(Dt[:], A[:], AF.Exp, [Ab], [Dtb], scale=lg)
                self.act(cfp[:], pbase[:], AF.Exp, [pbaseb], [cfpb], scale=lg)
                self.act(cfn[:], nbase[:], AF.Exp, [nbaseb], [cfnb], scale=lg)
                q, qb = QT[h % 2]
                k, kb = KT[h % 2]
                v, vb = V[h % 2]
                self.load_rows(q, qb, self.QTd, h * 64, 64)
                self.load_rows(k, kb, self.KTd, h * 64, 64)
                self.load_v(v, vb, self.Vd, h * 128)
                for qg in range(NG):
                    qs = slice(qg * 512, (qg + 1) * 512)
                    O, Ob = self.ps[qg % 2]
                    sg, sgb = sgp[qg % 2]
                    self.dma("sync", sg[:], self.GdT[h * 128:(h + 1) * 128, qs], [], [sgb])
                    for kt in range(NT):
                        ks = slice(kt * 128, (kt + 1) * 128)
                        sp, spb = self.psn(4, 8)
                        self.mm(sp[:], k[:, ks], q[:, qs], True, True, [kb, qb], [spb])
                        p_, p_b = Pt[pi % 4]
                        pi += 1
                        m = 4 * qg - kt
                        if m >= 1:
                            self.stt(p_[:], sp[:], cfp[:, m - 1:m], rowp[:], ALU.mult, ALU.mult, [spb, cfpb, rowpb], [p_b])
                        elif m <= -4:
                            self.stt(p_[:], sp[:], cfn[:, -m - 4:-m - 3], rown[:], ALU.mult, ALU.mult,
                                     [spb, cfnb, rownb], [p_b])
                        else:
                            self.tt(p_[:], sp[:], Dt[:, -m, :], ALU.mult, [spb, Dtb], [p_b])
                        self.mm(O[:], v[:, kt, :], p_[:], kt == 0, kt == NT - 1, [vb, p_b], [Ob])
                    o1, o1b = tf[0]
                    s1, s1b = tf[1]
                    mn, mnb = tf[2]
                    vr, vrb = tf[3]
                    self.cp("scalar", o1[:], O[:], [Ob], [o1b])
                    self.act(s1[:], O[:], AF.Square, [Ob], [s1b])
                    mp, mpb = self.ps[2]
                    spp, sppb = self.ps[3]
                    self.mm(mp[:], self.onesf[:], o1[:], True, True, [self.onesf_b, o1b], [mpb])
                    self.mm(spp[:], self.onesf[:], s1[:], True, True, [self.onesf_b, s1b], [sppb])
                    self.act(mn[:], mp[:], AF.Copy, [mpb], [mnb], scale=1.0 / 128)
                    self.tt(vr[:], mn[:], mn[:], ALU.mult, [mnb], [vrb])
                    self.stt(vr[:], spp[:], 1.0 / 128, vr[:], ALU.mult, ALU.subtract, [sppb, vrb], [vrb])
                    self.rsqrt(vr[:], vr[:], 1.0, [vrb], [vrb])
                    self.tt(o1[:], o1[:], mn[:], ALU.subtract, [o1b, mnb], [o1b])
                    self.stt(o1[:], o1[:], gncol[:, 0:1], vr[:], ALU.mult, ALU.mult, [o1b, vrb, gncolb], [o1b])
                    y, yb = y16[(h * NG + qg) % 2]
                    self.tt(y[:], o1[:], sg[:], ALU.mult, [o1b, sgb], [yb])
                    self.dma("gpsimd", self.YT[3072 + h * 128:3072 + (h + 1) * 128, qs], y[:], [yb],
                             [self.dbuf("YT", 3, h, qg)])
            self.barrier()


    def pipeline(self, steps, depth, stage1, stage2):
        n = len(steps)
        self.hooks = {}
        self.pit = 0
        for i in range(n + depth):
            self.pit = i
            if i < n:
                stage1(steps[i], i)
            if i >= depth:
                stage2(steps[i - depth], i - depth)
            for fn in self.hooks.pop(i, []):
                fn()
        for k in sorted(self.hooks):
            for fn in self.hooks[k]:
                fn()
        self.hooks = {}

    def later(self, delay, fn):
        self.hooks.setdefault(self.pit + delay, []).append(fn)

    def lsum_mm(self, accs, ps_t, ps_b):
        for j, (a, ab) in enumerate(accs):
            self.mm(ps_t[:], self.onesf[:], a[:], j == 0, j == len(accs) - 1, [self.onesf_b, ab], [ps_b])

    def phaseB2(self, l, s):
        S, NT, NG = self.S, self.NT, self.NG
        lam_init = 0.8 - 0.6 * math.exp(-0.3 * l)
        with ExitStack() as P:
            lp, lpb = self.sb(P, "lp", [128, 256], F32)
            self.load_bc(lp[:], lpb, self.diff_lam[l:l + 1].rearrange("a b c -> a (b c)"))
            pr, prb = self.sb(P, "pr", [128, 128], F32)
            e2, e2b = self.sb(P, "e2", [128, 2], F32)
            neglam, neglamb = self.sb(P, "neglam", [128, 1], F32)
            gcol, gcolb = self.sb(P, "gcol", [128, 1], F32)
            lp4 = lp[:].rearrange("p (a b c) -> p a b c", a=2, b=2)
            self.tt(pr[:].rearrange("p (a c) -> p a c", a=2), lp4[:, :, 0, :], lp4[:, :, 1, :], ALU.mult, [lpb], [prb])
            self.op("vector", lambda e: e.tensor_reduce(out=e2[:], in_=pr[:].rearrange("p (a c) -> p a c", a=2),
                                                        axis=AX.X, op=ALU.add), [prb], [e2b])
            self.act(e2[:], e2[:], AF.Exp, [e2b], [e2b])
            self.tt(neglam[:], e2[:, 1:2], e2[:, 0:1], ALU.subtract, [e2b], [neglamb])
            self.ts(neglam[:], neglam[:], -lam_init, None, ALU.add, None, [neglamb], [neglamb])
            self.load_cols(gcol[:], gcolb, self.diff_sub_g[l], "(p o) -> p o", o=1)
            self.ts(gcol[:], gcol[:], 1.0 - lam_init, None, ALU.mult, None, [gcolb], [gcolb])
            QT = self.sbpool(P, "QT", [128, S], BF16, 2)
            KT = self.sbpool(P, "KT", [128, S], BF16, 2)
            V = self.sbpool(P, "V", [128, NT, 128], BF16, 2)
            NP = 6
            Pt = self.sbpool(P, "Pt", [128, 512], BF16, NP)
            tf = self.sbpool(P, "tf", [128, 512], F32, 6)
            sq16 = self.sbpool(P, "sq16", [128, 512], BF16, 2)
            y16 = self.sbpool(P, "y16", [128, 512], BF16, 2)
            Lacc = [[self.sb(P, "Lacc%d%d" % (a, b), [128, 512], F32) for b in range(2)] for a in range(2)]
            leng = (LENG0, LENG1)

            def loads(h):
                self.load_rows(QT[h % 2][0], QT[h % 2][1], self.QTb, h * 128, 128)
                self.load_rows(KT[h % 2][0], KT[h % 2][1], self.KTb, h * 128, 128)
                self.load_v(V[h % 2][0], V[h % 2][1], self.Vb, h * 128)

            steps = [(h, qg, kt, si) for h in range(8) for qg in range(NG) for kt in range(NT) for si in range(2)]
            sbank = {}
            loads(0)

            def stage1(st, i):
                h, qg, kt, si = st
                q, qb = QT[h % 2]
                k, kb = KT[h % 2]
                par = (h * NG + qg) % 2
                lo, hi = si * 64, si * 64 + 64
                sp, spb = self.psn(4, 8)
                self.mm(sp[:], k[lo:hi, kt * 128:(kt + 1) * 128], q[lo:hi, qg * 512:(qg + 1) * 512], True, True,
                        [kb, qb], [spb])
                p_, p_b = Pt[i % NP]
                self.act(p_[:], sp[:], AF.Exp, [spb], [p_b], scale=0.125)
                a, ab = Lacc[par][si]
                if kt == 0:
                    self.cp(leng[si], a[:], p_[:], [p_b], [ab])
                else:
                    self.tt(a[:], a[:], p_[:], ALU.add, [ab, p_b], [ab], en=leng[si])

            def stage2(st, i):
                h, qg, kt, si = st
                if qg == 0 and kt == 0 and si == 0 and h + 1 < 8:
                    loads(h + 1)
                v, vb = V[h % 2]
                par = (h * NG + qg) % 2
                O, Ob = self.ps[par * 2 + si]
                p_, p_b = Pt[i % NP]
                self.mm(O[:], v[:, kt, :], p_[:], kt == 0, kt == NT - 1, [vb, p_b], [Ob])
                if kt == NT - 1 and si == 1:
                    epilogue(h, qg, par)

            def epilogue(h, qg, par):
                qs = slice(qg * 512, (qg + 1) * 512)
                O0, O0b = self.ps[par * 2]
                O1, O1b = self.ps[par * 2 + 1]
                o1, o1b = tf[0]
                o2, o2b = tf[1]
                r1, r1b = tf[2]
                r2, r2b = tf[3]
                oo, oob = tf[4]
                rs, rsb = tf[5]
                self.cp("scalar", o1[:], O0[:], [O0b], [o1b])
                self.cp("vector", o2[:], O1[:], [O1b], [o2b])
                for si, (r, rb) in enumerate(((r1, r1b), (r2, r2b))):
                    lp_, lpb_ = self.psn(4, 8)
                    self.lsum_mm([Lacc[par][si]], lp_, lpb_)
                    self.act(r[:], lp_[:], AF.Ln, [lpb_], [rb])
                    self.act(r[:], r[:], AF.Exp, [rb], [rb], scale=-1.0)
                self.tt(o1[:], o1[:], r1[:], ALU.mult, [o1b, r1b], [o1b])
                self.tt(o2[:], o2[:], r2[:], ALU.mult, [o2b, r2b], [o2b])
                self.stt(oo[:], o2[:], neglam[:, 0:1], o1[:], ALU.mult, ALU.add, [o2b, o1b, neglamb], [oob])
                sq, sqb = sq16[par]
                self.act(sq[:], oo[:], AF.Square, [oob], [sqb])
                sp, spb = self.psn(4, 8)
                self.mm(sp[:], self.onesb[:], sq[:], True, True, [self.onesb_b, sqb], [spb])
                self.rsqrt(rs[:], sp[:], 1.0 / 128, [spb], [rsb])
                y, yb = y16[par]
                self.stt(y[:], oo[:], gcol[:, 0:1], rs[:], ALU.mult, ALU.mult, [oob, rsb, gcolb], [yb])
                self.dma("gpsimd", self.YT[1024 + h * 128:1024 + (h + 1) * 128, qs], y[:], [yb], [])

            self.pipeline(steps, KDEPTH, stage1, stage2)
            self.barrier()

    def phaseC2(self, l, s):
        S, NT, NG = self.S, self.NT, self.NG
        sc = 192.0 ** -0.5
        with ExitStack() as P:
            Qn = self.sbpool(P, "Qn", [128, S], BF16, 2)
            Qp = self.sbpool(P, "Qp", [64, S], BF16, 2)
            Kn = self.sbpool(P, "Kn", [128, S], BF16, 2)
            Kp = self.sbpool(P, "Kp", [64, S], BF16, 2)
            V = self.sbpool(P, "V", [128, NT, 128], BF16, 2)
            NP = 6
            Pt = self.sbpool(P, "Pt", [128, 512], BF16, NP)
            tf = self.sbpool(P, "tf", [128, 512], F32, 4)
            y16 = self.sbpool(P, "y16", [128, 512], BF16, 2)
            Lacc = [[self.sb(P, "Lacc%d%d" % (a, b), [128, 512], F32) for b in range(2)] for a in range(2)]
            leng = (LENG0, LENG1)

            def loads(h):
                self.load_rows(Qn[h % 2][0], Qn[h % 2][1], self.QTc, h * 192, 128)
                self.load_rows(Qp[h % 2][0], Qp[h % 2][1], self.QTc, h * 192 + 128, 64)
                self.load_rows(Kn[h % 2][0], Kn[h % 2][1], self.KTc, h * 192, 128)
                self.load_rows(Kp[h % 2][0], Kp[h % 2][1], self.KTc, h * 192 + 128, 64)
                self.load_v(V[h % 2][0], V[h % 2][1], self.Vc, h * 128)

            steps = [(h, qg, kt) for h in range(8) for qg in range(NG) for kt in range(NT)]
            loads(0)

            def stage1(st, i):
                h, qg, kt = st
                qn, qnb = Qn[h % 2]
                qp, qpb = Qp[h % 2]
                kn, knb = Kn[h % 2]
                kp, kpb = Kp[h % 2]
                par = (h * NG + qg) % 2
                qs = slice(qg * 512, (qg + 1) * 512)
                ks = slice(kt * 128, (kt + 1) * 128)
                sp, spb = self.psn(2, 8)
                self.mm(sp[:], kn[:, ks], qn[:, qs], True, False, [knb, qnb], [spb])
                self.mm(sp[:], kp[:, ks], qp[:, qs], False, True, [kpb, qpb], [spb])
                p_, p_b = Pt[i % NP]
                self.act(p_[:], sp[:], AF.Exp, [spb], [p_b], scale=sc)
                a, ab = Lacc[par][kt % 2]
                if kt < 2:
                    self.cp(leng[kt % 2], a[:], p_[:], [p_b], [ab])
                else:
                    self.tt(a[:], a[:], p_[:], ALU.add, [ab, p_b], [ab], en=leng[kt % 2])

            def stage2(st, i):
                h, qg, kt = st
                if qg == 0 and kt == 0 and h + 1 < 8:
                    loads(h + 1)
                v, vb = V[h % 2]
                par = (h * NG + qg) % 2
                O, Ob = self.ps[par]
                p_, p_b = Pt[i % NP]
                self.mm(O[:], v[:, kt, :], p_[:], kt == 0, kt == NT - 1, [vb, p_b], [Ob])
                if kt == NT - 1:
                    qs = slice(qg * 512, (qg + 1) * 512)
                    o1, o1b = tf[par * 2]
                    r1, r1b = tf[par * 2 + 1]
                    self.cp("vector", o1[:], O[:], [Ob], [o1b])
                    lp_, lpb_ = self.psn(2, 8)
                    self.lsum_mm(Lacc[par], lp_, lpb_)
                    self.act(r1[:], lp_[:], AF.Ln, [lpb_], [r1b])
                    self.act(r1[:], r1[:], AF.Exp, [r1b], [r1b], scale=-1.0)
                    y, yb = y16[par]
                    self.tt(y[:], o1[:], r1[:], ALU.mult, [o1b, r1b], [yb])
                    self.dma("gpsimd", self.YT[2048 + h * 128:2048 + (h + 1) * 128, qs], y[:], [yb], [])

            self.pipeline(steps, KDEPTH, stage1, stage2)
            self.barrier()

    def phaseD2(self, l, s):
        S, NT, NG = self.S, self.NT, self.NG
        mmax = max(4 * (NG - 1), 1)
        nneg = max(NT - 4, 1)
        with ExitStack() as P:
            gncol, gncolb = self.sb(P, "gncol", [128, 1], F32)
            self.load_cols(gncol[:], gncolb, self.ret_gn_g[l], "(p o) -> p o", o=1)
            ii, iib = self.sb(P, "ii", [128, 512], I32)
            J, Jb = self.sb(P, "J", [128, 512], F32)
            Jr, Jrb = self.sb(P, "Jr", [128, 512], F32)
            A, Ab = self.sb(P, "A", [128, 4, 512], F32)
            pbase, pbaseb = self.sb(P, "pbase", [128, mmax], F32)
            nbase, nbaseb = self.sb(P, "nbase", [128, nneg], F32)
            self.op("gpsimd", lambda e: e.iota(ii[:], pattern=[[1, 512]], base=0, channel_multiplier=0), [], [iib])
            self.cp("vector", J[:], ii[:], [iib], [Jb])
            self.ts(Jr[:], J[:], -1.0, 511.0, ALU.mult, ALU.add, [Jb], [Jrb])
            for m in range(4):
                self.op("gpsimd", lambda e, m=m: e.iota(ii[:], pattern=[[1, 512]], base=-128 * m, channel_multiplier=-1),
                        [iib], [iib])
                self.cp("vector", A[:, m, :], ii[:], [iib], [Ab])
            self.act(A[:], A[:], AF.Abs, [Ab], [Ab])
            self.op("gpsimd", lambda e: e.iota(ii[:, 0:mmax], pattern=[[128, mmax]], base=128, channel_multiplier=-1),
                    [iib], [iib])
            self.cp("vector", pbase[:], ii[:, 0:mmax], [iib], [pbaseb])
            self.op("gpsimd", lambda e: e.iota(ii[:, 0:nneg], pattern=[[128, nneg]], base=1, channel_multiplier=1),
                    [iib], [iib])
            self.cp("vector", nbase[:], ii[:, 0:nneg], [iib], [nbaseb])
            rowp = self.sbpool(P, "rowp", [128, 512], BF16, 2)
            rown = self.sbpool(P, "rown", [128, 512], BF16, 2)
            Dt = self.sbpool(P, "Dt", [128, 4, 512], F32, 2)
            cfp = self.sbpool(P, "cfp", [128, mmax], F32, 2)
            cfn = self.sbpool(P, "cfn", [128, nneg], F32, 2)
            QT = self.sbpool(P, "QT", [64, S], BF16, 2)
            KT = self.sbpool(P, "KT", [64, S], BF16, 2)
            V = self.sbpool(P, "V", [128, NT, 128], BF16, 2)
            NP = 6
            Pt = self.sbpool(P, "Pt", [128, 512], BF16, NP)
            P0 = self.sbpool(P, "P0", [128, 512], BF16, 4)
            tf = self.sbpool(P, "tf", [128, 512], F32, 4)
            sgp = self.sbpool(P, "sg", [128, 512], BF16, 2)
            y16 = self.sbpool(P, "y16", [128, 512], BF16, 2)
            meng = (LENG0, LENG1)

            def loads(h):
                lg = math.log1p(-2.0 ** (-5.0 - h))
                hp = h % 2
                self.act(rowp[hp][0][:], J[:], AF.Exp, [Jb], [rowp[hp][1]], scale=lg)
                self.act(rown[hp][0][:], Jr[:], AF.Exp, [Jrb], [rown[hp][1]], scale=lg)
                self.act(Dt[hp][0][:], A[:], AF.Exp, [Ab], [Dt[hp][1]], scale=lg)
                self.act(cfp[hp][0][:], pbase[:], AF.Exp, [pbaseb], [cfp[hp][1]], scale=lg)
                self.act(cfn[hp][0][:], nbase[:], AF.Exp, [nbaseb], [cfn[hp][1]], scale=lg)
                self.load_rows(QT[hp][0], QT[hp][1], self.QTd, h * 64, 64)
                self.load_rows(KT[hp][0], KT[hp][1], self.KTd, h * 64, 64)
                self.load_v(V[hp][0], V[hp][1], self.Vd, h * 128)

            steps = [(h, qg, kt) for h in range(8) for qg in range(NG) for kt in range(NT)]
            loads(0)

            def stage1(st, i):
                h, qg, kt = st
                hp = h % 2
                q, qb = QT[hp]
                k, kb = KT[hp]
                par = (h * NG + qg) % 2
                if kt == 0:
                    sg, sgb = sgp[par]
                    self.dma("sync", sg[:], self.GdT[h * 128:(h + 1) * 128, qg * 512:(qg + 1) * 512], [], [sgb])
                sp, spb = self.psn(4, 8)
                self.mm(sp[:], k[:, kt * 128:(kt + 1) * 128], q[:, qg * 512:(qg + 1) * 512], True, True, [kb, qb], [spb])
                p_, p_b = Pt[i % NP]
                m = 4 * qg - kt
                if m >= 1 or m <= -4:
                    p0, p0b = P0[i % 4]
                    if m >= 1:
                        cf, cfb = cfp[hp]
                        col = cf[:, m - 1:m]
                        row, rowb = rowp[hp]
                    else:
                        cf, cfb = cfn[hp]
                        col = cf[:, -m - 4:-m - 3]
                        row, rowb = rown[hp]
                    self.act(p0[:], sp[:], AF.Identity, [spb, cfb], [p0b], scale=col)
                    self.tt(p_[:], p0[:], row[:], ALU.mult, [p0b, rowb], [p_b], en=meng[i % 2])
                else:
                    self.tt(p_[:], sp[:], Dt[hp][0][:, -m, :], ALU.mult, [spb, Dt[hp][1]], [p_b])

            def stage2(st, i):
                h, qg, kt = st
                if qg == 0 and kt == 0 and h + 1 < 8:
                    loads(h + 1)
                v, vb = V[h % 2]
                par = (h * NG + qg) % 2
                O, Ob = self.ps[par]
                p_, p_b = Pt[i % NP]
                self.mm(O[:], v[:, kt, :], p_[:], kt == 0, kt == NT - 1, [vb, p_b], [Ob])
                if kt == NT - 1:
                    qs = slice(qg * 512, (qg + 1) * 512)
                    sg, sgb = sgp[par]
                    o1, o1b = tf[0]
                    s1, s1b = tf[1]
                    mn, mnb = tf[2]
                    vr, vrb = tf[3]
                    self.cp(KCP, o1[:], O[:], [Ob], [o1b])
                    self.act(s1[:], O[:], AF.Square, [Ob], [s1b])
                    mp, mpb = self.ps[2]
                    spp, sppb = self.ps[3]
                    self.mm(mp[:], self.onesf[:], o1[:], True, True, [self.onesf_b, o1b], [mpb])
                    self.mm(spp[:], self.onesf[:], s1[:], True, True, [self.onesf_b, s1b], [sppb])
                    self.act(mn[:], mp[:], AF.Copy, [mpb], [mnb], scale=1.0 / 128)
                    self.tt(vr[:], mn[:], mn[:], ALU.mult, [mnb], [vrb])
                    self.stt(vr[:], spp[:], 1.0 / 128, vr[:], ALU.mult, ALU.subtract, [sppb, vrb], [vrb])
                    self.rsqrt(vr[:], vr[:], 1.0, [vrb], [vrb])
                    self.tt(o1[:], o1[:], mn[:], ALU.subtract, [o1b, mnb], [o1b])
                    self.stt(o1[:], o1[:], gncol[:, 0:1], vr[:], ALU.mult, ALU.mult, [o1b, vrb, gncolb], [o1b])
                    y, yb = y16[par]
                    self.tt(y[:], o1[:], sg[:], ALU.mult, [o1b, sgb], [yb])
                    self.dma("gpsimd", self.YT[3072 + h * 128:3072 + (h + 1) * 128, qs], y[:], [yb], [])

            self.pipeline(steps, KDEPTH, stage1, stage2)
            self.barrier()


    def pair(self, j):
        return self.psbig[:, 2 * j:2 * j + 2, :], [self.ps[2 * j][1], self.ps[2 * j + 1][1]]

    def phaseB3(self, l, s):
        S, NT, NG = self.S, self.NT, self.NG
        lam_init = 0.8 - 0.6 * math.exp(-0.3 * l)
        with ExitStack() as P:
            lp, lpb = self.sb(P, "lp", [128, 256], F32)
            self.load_bc(lp[:], lpb, self.diff_lam[l:l + 1].rearrange("a b c -> a (b c)"))
            pr, prb = self.sb(P, "pr", [128, 128], F32)
            e2, e2b = self.sb(P, "e2", [128, 2], F32)
            neglam, neglamb = self.sb(P, "neglam", [128, 1], F32)
            gcol, gcolb = self.sb(P, "gcol", [128, 1], F32)
            lp4 = lp[:].rearrange("p (a b c) -> p a b c", a=2, b=2)
            self.tt(pr[:].rearrange("p (a c) -> p a c", a=2), lp4[:, :, 0, :], lp4[:, :, 1, :], ALU.mult, [lpb], [prb])
            self.op("vector", lambda e: e.tensor_reduce(out=e2[:], in_=pr[:].rearrange("p (a c) -> p a c", a=2),
                                                        axis=AX.X, op=ALU.add), [prb], [e2b])
            self.act(e2[:], e2[:], AF.Exp, [e2b], [e2b])
            self.tt(neglam[:], e2[:, 1:2], e2[:, 0:1], ALU.subtract, [e2b], [neglamb])
            self.ts(neglam[:], neglam[:], -lam_init, None, ALU.add, None, [neglamb], [neglamb])
            self.load_cols(gcol[:], gcolb, self.diff_sub_g[l], "(p o) -> p o", o=1)
            self.ts(gcol[:], gcol[:], 1.0 - lam_init, None, ALU.mult, None, [gcolb], [gcolb])
            QT = self.sbpool(P, "QT", [128, S], BF16, 2)
            KA = self.sbpool(P, "KA", [128, S], BF16, 2)
            KB_ = self.sbpool(P, "KBt", [128, S], BF16, 2)
            for i in range(2):
                self.op("vector", lambda e, i=i: e.memset(KA[i][0][64:128, :], 0.0), [], [KA[i][1]])
                self.op("vector", lambda e, i=i: e.memset(KB_[i][0][0:64, :], 0.0), [], [KB_[i][1]])
            V = self.sbpool(P, "V", [128, NT, 128], BF16, 2)
            NP = 6
            Pt = self.sbpool(P, "Pt", [128, 2, 512], BF16, NP)
            tf = self.sbpool(P, "tf", [128, 512], F32, 6)
            sq16 = self.sbpool(P, "sq16", [128, 512], BF16, 2)
            y16 = self.sbpool(P, "y16", [128, 512], BF16, 2)
            Lacc2 = [self.sb(P, "Lacc%d" % a, [128, 2, 512], F32) for a in range(2)]
            Lacc = [[(Lacc2[a][0][:, b, :], Lacc2[a][1]) for b in range(2)] for a in range(2)]
            T1 = self.sbpool(P, "T1", [128, 2, 512], BF16, 4)
            leng = (LENG0, LENG1)

            def loads(h):
                self.load_rows(QT[h % 2][0], QT[h % 2][1], self.QTb, h * 128, 128)
                self.dma("sync", KA[h % 2][0][0:64, :], self.KTb[h * 128:h * 128 + 64, :], [], [KA[h % 2][1]])
                self.dma("sync", KB_[h % 2][0][64:128, :], self.KTb[h * 128 + 64:h * 128 + 128, :], [], [KB_[h % 2][1]])
                self.load_v(V[h % 2][0], V[h % 2][1], self.Vb, h * 128)

            steps = [(h, qg, kt) for h in range(8) for qg in range(NG) for kt in range(NT)]
            loads(0)

            def stage1(st, i):
                h, qg, kt = st
                q, qb = QT[h % 2]
                par = (h * NG + qg) % 2
                sp, spbs = self.pair(2 + i % 2)
                ks = slice(kt * 128, (kt + 1) * 128)
                qs = slice(qg * 512, (qg + 1) * 512)
                self.mm(sp[:, 0, :], KA[h % 2][0][:, ks], q[:, qs], True, True, [KA[h % 2][1], qb], [spbs[0]])
                self.mm(sp[:, 1, :], KB_[h % 2][0][:, ks], q[:, qs], True, True, [KB_[h % 2][1], qb], [spbs[1]])
                p_, p_b = Pt[i % NP]
                self.act(p_[:], sp, AF.Exp, spbs, [p_b], scale=0.125)
                if kt % 2 == 1:
                    t1, t1b = T1[(i // 2) % 4]
                    pa, pab = Pt[(i - 1) % NP]
                    self.tt(t1[:], pa[:], p_[:], ALU.add, [pab, p_b], [t1b])
                    if kt % 4 == 3:
                        t0_, t0b = T1[((i // 2) - 1) % 4]
                        a, ab = Lacc2[par]
                        if kt == 3:
                            self.tt(a[:], t0_[:], t1[:], ALU.add, [t0b, t1b], [ab])
                        else:
                            self.tt(t1[:], t0_[:], t1[:], ALU.add, [t0b, t1b], [t1b])
                            self.tt(a[:], a[:], t1[:], ALU.add, [ab, t1b], [ab])

            def stage2(st, i):
                h, qg, kt = st
                if qg == 0 and kt == 0 and h + 1 < 8:
                    loads(h + 1)
                v, vb = V[h % 2]
                par = (h * NG + qg) % 2
                p_, p_b = Pt[i % NP]
                for si in range(2):
                    O, Ob = self.ps[par * 2 + si]
                    self.mm(O, v[:, kt, :], p_[:, si, :], kt == 0, kt == NT - 1, [vb, p_b], [Ob])
                if kt == NT - 1:
                    epilogue(h, qg, par, i)

            def epilogue(h, qg, par, i):
                qs = slice(qg * 512, (qg + 1) * 512)
                O0, O0b = self.ps[par * 2]
                O1, O1b = self.ps[par * 2 + 1]
                o1, o1b = tf[0]
                o2, o2b = tf[1]
                r1, r1b = tf[2]
                r2, r2b = tf[3]
                oo, oob = tf[4]
                rs, rsb = tf[5]
                sq, sqb = sq16[par]
                self.cp("vector", o1[:], O0, [O0b], [o1b])
                self.cp("vector", o2[:], O1, [O1b], [o2b])

                def e1():
                    bnk = 4 + 2 * ((self.pit + 1) % 2)
                    for si, (r, rb) in enumerate(((r1, r1b), (r2, r2b))):
                        lp_, lpb_ = self.ps[bnk + si]
                        self.lsum_mm([Lacc[par][si]], lp_, lpb_)
                        self.act(r[:], lp_, AF.Ln, [lpb_], [rb])
                        self.act(r[:], r[:], AF.Exp, [rb], [rb], scale=-1.0)
                    self.tt(o1[:], o1[:], r1[:], ALU.mult, [o1b, r1b], [o1b])
                    self.tt(o2[:], o2[:], r2[:], ALU.mult, [o2b, r2b], [o2b])
                    self.stt(oo[:], o2[:], neglam[:, 0:1], o1[:], ALU.mult, ALU.add, [o2b, o1b, neglamb], [oob])
                    self.tt(sq[:], oo[:], oo[:], ALU.mult, [oob], [sqb])

                def e2():
                    bnk = 4 + 2 * ((self.pit + 1) % 2)
                    sp, spb = self.ps[bnk]
                    self.mm(sp, self.onesb[:], sq[:], True, True, [self.onesb_b, sqb], [spb])
                    self.rsqrt(rs[:], sp, 1.0 / 128, [spb], [rsb])
                    y, yb = y16[par]
                    self.stt(y[:], oo[:], gcol[:, 0:1], rs[:], ALU.mult, ALU.mult, [oob, rsb, gcolb], [yb])
                    self.dma("gpsimd", self.YT[1024 + h * 128:1024 + (h + 1) * 128, qs], y[:], [yb], [])
                self.later(min(KE1, NT - 2), e1)
                self.later(min(KE2, NT - 1), e2)

            self.pipeline(steps, 1, stage1, stage2)
            self.barrier()

    def phaseC3(self, l, s):
        S, NT, NG = self.S, self.NT, self.NG
        sc = 192.0 ** -0.5
        NTP = NT // 2
        with ExitStack() as P:
            Qn = self.sbpool(P, "Qn", [128, S], BF16, 2)
            Qp = self.sbpool(P, "Qp", [128, S], BF16, 2)
            Kn = self.sbpool(P, "Kn", [128, S], BF16, 2)
            Kp = self.sbpool(P, "Kp", [128, S], BF16, 2)
            for i in range(2):
                self.op("vector", lambda e, i=i: e.memset(Qp[i][0][64:128, :], 0.0), [], [Qp[i][1]])
                self.op("vector", lambda e, i=i: e.memset(Kp[i][0][64:128, :], 0.0), [], [Kp[i][1]])
            V = self.sbpool(P, "V", [128, NT, 128], BF16, 2)
            NP = 4
            Pt = self.sbpool(P, "Pt", [128, 2, 512], BF16, NP)
            tf = self.sbpool(P, "tf", [128, 512], F32, 4)
            y16 = self.sbpool(P, "y16", [128, 512], BF16, 2)
            Lacc = [self.sb(P, "Lacc%d" % a, [128, 2, 512], F32) for a in range(2)]

            def loads(h):
                self.load_rows(Qn[h % 2][0], Qn[h % 2][1], self.QTc, h * 192, 128)
                self.load_rows(Qp[h % 2][0], Qp[h % 2][1], self.QTc, h * 192 + 128, 64)
                self.load_rows(Kn[h % 2][0], Kn[h % 2][1], self.KTc, h * 192, 128)
                self.load_rows(Kp[h % 2][0], Kp[h % 2][1], self.KTc, h * 192 + 128, 64)
                self.load_v(V[h % 2][0], V[h % 2][1], self.Vc, h * 128)

            steps = [(h, qg, kp) for h in range(8) for qg in range(NG) for kp in range(NTP)]
            loads(0)

            def stage1(st, i):
                h, qg, kp_ = st
                qn, qnb = Qn[h % 2]
                qp, qpb = Qp[h % 2]
                kn, knb = Kn[h % 2]
                kp, kpb = Kp[h % 2]
                par = (h * NG + qg) % 2
                qs = slice(qg * 512, (qg + 1) * 512)
                sp, spbs = self.pair(1 + i % 3)
                for j in range(2):
                    kt = 2 * kp_ + j
                    ks = slice(kt * 128, (kt + 1) * 128)
                    self.mm(sp[:, j, :], kn[:, ks], qn[:, qs], True, False, [knb, qnb], [spbs[j]])
                    self.mm(sp[:, j, :], kp[:, ks], qp[:, qs], False, True, [kpb, qpb], [spbs[j]])
                p_, p_b = Pt[i % NP]
                self.act(p_[:], sp, AF.Exp, spbs, [p_b], scale=sc)
                a, ab = Lacc[par]
                if kp_ == 0:
                    self.cp("vector", a[:], p_[:], [p_b], [ab])
                else:
                    self.tt(a[:], a[:], p_[:], ALU.add, [ab, p_b], [ab])

            def stage2(st, i):
                h, qg, kp_ = st
                if qg == 0 and kp_ == 0 and h + 1 < 8:
                    loads(h + 1)
                v, vb = V[h % 2]
                par = (h * NG + qg) % 2
                O, Ob = self.ps[par]
                p_, p_b = Pt[i % NP]
                for j in range(2):
                    kt = 2 * kp_ + j
                    self.mm(O, v[:, kt, :], p_[:, j, :], kt == 0, kt == NT - 1, [vb, p_b], [Ob])
                if kp_ == NTP - 1:
                    qs = slice(qg * 512, (qg + 1) * 512)
                    o1, o1b = tf[par * 2]
                    r1, r1b = tf[par * 2 + 1]
                    self.cp("vector", o1[:], O, [Ob], [o1b])

                    def e1(h=h, qs=qs, par=par, o1=o1, o1b=o1b, r1=r1, r1b=r1b):
                        lp_, lpb_ = self.ps[2 + 2 * ((self.pit + 1) % 3)]
                        a, ab = Lacc[par]
                        self.mm(lp_, self.onesf[:], a[:, 0, :], True, False, [self.onesf_b, ab], [lpb_])
                        self.mm(lp_, self.onesf[:], a[:, 1, :], False, True, [self.onesf_b, ab], [lpb_])
                        self.act(r1[:], lp_, AF.Ln, [lpb_], [r1b])
                        self.act(r1[:], r1[:], AF.Exp, [r1b], [r1b], scale=-1.0)
                        y, yb = y16[par]
                        self.tt(y[:], o1[:], r1[:], ALU.mult, [o1b, r1b], [yb])
                        self.dma("gpsimd", self.YT[2048 + h * 128:2048 + (h + 1) * 128, qs], y[:], [yb], [])
                    self.later(min(2, NTP - 1), e1)

            self.pipeline(steps, 2, stage1, stage2)
            self.barrier()

    def phaseD3(self, l, s):
        S, NT, NG = self.S, self.NT, self.NG
        mmax = max(4 * (NG - 1), 1)
        nneg = max(NT - 4, 1)
        with ExitStack() as P:
            gncol, gncolb = self.sb(P, "gncol", [128, 1], F32)
            self.load_cols(gncol[:], gncolb, self.ret_gn_g[l], "(p o) -> p o", o=1)
            ii, iib = self.sb(P, "ii", [128, 512], I32)
            J, Jb = self.sb(P, "J", [128, 512], F32)
            Jr, Jrb = self.sb(P, "Jr", [128, 512], F32)
            A, Ab = self.sb(P, "A", [128, 4, 512], F32)
            pbase, pbaseb = self.sb(P, "pbase", [128, mmax], F32)
            nbase, nbaseb = self.sb(P, "nbase", [128, nneg], F32)
            self.op("gpsimd", lambda e: e.iota(ii[:], pattern=[[1, 512]], base=0, channel_multiplier=0), [], [iib])
            self.cp("vector", J[:], ii[:], [iib], [Jb])
            self.ts(Jr[:], J[:], -1.0, 511.0, ALU.mult, ALU.add, [Jb], [Jrb])
            for m in range(4):
                self.op("gpsimd", lambda e, m=m: e.iota(ii[:], pattern=[[1, 512]], base=-128 * m, channel_multiplier=-1),
                        [iib], [iib])
                self.cp("vector", A[:, m, :], ii[:], [iib], [Ab])
            self.act(A[:], A[:], AF.Abs, [Ab], [Ab])
            self.op("gpsimd", lambda e: e.iota(ii[:, 0:mmax], pattern=[[128, mmax]], base=128, channel_multiplier=-1),
                    [iib], [iib])
            self.cp("vector", pbase[:], ii[:, 0:mmax], [iib], [pbaseb])
            self.op("gpsimd", lambda e: e.iota(ii[:, 0:nneg], pattern=[[128, nneg]], base=1, channel_multiplier=1),
                    [iib], [iib])
            self.cp("vector", nbase[:], ii[:, 0:nneg], [iib], [nbaseb])
            rowp = self.sbpool(P, "rowp", [128, 512], BF16, 2)
            rown = self.sbpool(P, "rown", [128, 512], BF16, 2)
            Dt = self.sbpool(P, "Dt", [128, 4, 512], F32, 2)
            cfp = self.sbpool(P, "cfp", [128, mmax], F32, 2)
            cfn = self.sbpool(P, "cfn", [128, nneg], F32, 2)
            QT = self.sbpool(P, "QT", [128, S], BF16, 2)
            KT = self.sbpool(P, "KT", [128, S], BF16, 2)
            for i in range(2):
                self.op("vector", lambda e, i=i: e.memset(QT[i][0][64:128, :], 0.0), [], [QT[i][1]])
                self.op("vector", lambda e, i=i: e.memset(KT[i][0][64:128, :], 0.0), [], [KT[i][1]])
            V = self.sbpool(P, "V", [128, NT, 128], BF16, 2)
            NP = 8
            Pt = self.sbpool(P, "Pt", [128, 512], BF16, NP)
            P0 = self.sbpool(P, "P0", [128, 512], BF16, 6)
            tf = self.sbpool(P, "tf", [128, 512], F32, 4)
            sgp = self.sbpool(P, "sg", [128, 512], BF16, 2)
            y16 = self.sbpool(P, "y16", [128, 512], BF16, 2)
            cnt = [0]

            def loads(h):
                lg = math.log1p(-2.0 ** (-5.0 - h))
                hp = h % 2
                self.act(rowp[hp][0][:], J[:], AF.Exp, [Jb], [rowp[hp][1]], scale=lg)
                self.act(rown[hp][0][:], Jr[:], AF.Exp, [Jrb], [rown[hp][1]], scale=lg)
                self.act(Dt[hp][0][:], A[:], AF.Exp, [Ab], [Dt[hp][1]], scale=lg)
                self.act(cfp[hp][0][:], pbase[:], AF.Exp, [pbaseb], [cfp[hp][1]], scale=lg)
                self.act(cfn[hp][0][:], nbase[:], AF.Exp, [nbaseb], [cfn[hp][1]], scale=lg)
                self.load_rows(QT[hp][0], QT[hp][1], self.QTd, h * 64, 64)
                self.load_rows(KT[hp][0], KT[hp][1], self.KTd, h * 64, 64)
                self.load_v(V[hp][0], V[hp][1], self.Vd, h * 128)

            steps = [(h, qg, kt) for h in range(8) for qg in range(NG) for kt in range(NT)]
            loads(0)

            def stage1(st, i):
                h, qg, kt = st
                hp = h % 2
                q, qb = QT[hp]
                k, kb = KT[hp]
                par = (h * NG + qg) % 2
                if kt == 0:
                    sg, sgb = sgp[par]
                    self.dma("sync", sg[:], self.GdT[h * 128:(h + 1) * 128, qg * 512:(qg + 1) * 512], [], [sgb])
                sp, spb = self.psn(2, 8)
                self.mm(sp, k[:, kt * 128:(kt + 1) * 128], q[:, qg * 512:(qg + 1) * 512], True, True, [kb, qb], [spb])
                p_, p_b = Pt[i % NP]
                m = 4 * qg - kt
                if m >= 1 or m <= -4:
                    if m >= 1:
                        cf, cfb = cfp[hp]
                        col = cf[:, m - 1:m]
                        row, rowb = rowp[hp]
                    else:
                        cf, cfb = cfn[hp]
                        col = cf[:, -m - 4:-m - 3]
                        row, rowb = rown[hp]
                    cnt[0] += 1
                    if cnt[0] % KDACT != 0:
                        p0, p0b = P0[cnt[0] % 6]
                        self.act(p0[:], sp, AF.Identity, [spb, cfb], [p0b], scale=col)
                        self.tt(p_[:], p0[:], row[:], ALU.mult, [p0b, rowb], [p_b], en=LENG1)
                    else:
                        self.stt(p_[:], sp, col, row[:], ALU.mult, ALU.mult, [spb, cfb, rowb], [p_b])
                else:
                    self.tt(p_[:], sp, Dt[hp][0][:, -m, :], ALU.mult, [spb, Dt[hp][1]], [p_b])

            def stage2(st, i):
                h, qg, kt = st
                if qg == 0 and kt == 0 and h + 1 < 8:
                    loads(h + 1)
                v, vb = V[h % 2]
                par = (h * NG + qg) % 2
                O, Ob = self.ps[par]
                p_, p_b = Pt[i % NP]
                self.mm(O, v[:, kt, :], p_[:], kt == 0, kt == NT - 1, [vb, p_b], [Ob])
                if kt == NT - 1:
                    qs = slice(qg * 512, (qg + 1) * 512)
                    sg, sgb = sgp[par]
                    o1, o1b = tf[0]
                    s1, s1b = tf[1]
                    mn, mnb = tf[2]
                    vr, vrb = tf[3]
                    self.cp("scalar", o1[:], O, [Ob], [o1b])
                    self.act(s1[:], O, AF.Square, [Ob], [s1b])

                    def e1(h=h, qs=qs, par=par, sg=sg, sgb=sgb):
                        mp, mpb = self.psn(2, 8)
                        spp, sppb = self.psn(2, 8)
                        self.mm(mp, self.onesf[:], o1[:], True, True, [self.onesf_b, o1b], [mpb])
                        self.mm(spp, self.onesf[:], s1[:], True, True, [self.onesf_b, s1b], [sppb])
                        self.act(mn[:], mp, AF.Copy, [mpb], [mnb], scale=1.0 / 128)
                        self.tt(vr[:], mn[:], mn[:], ALU.mult, [mnb], [vrb])
                        self.stt(vr[:], spp, 1.0 / 128, vr[:], ALU.mult, ALU.subtract, [sppb, vrb], [vrb])
                        self.rsqrt(vr[:], vr[:], 1.0, [vrb], [vrb])
                        self.tt(o1[:], o1[:], mn[:], ALU.subtract, [o1b, mnb], [o1b])
                        self.stt(o1[:], o1[:], gncol[:, 0:1], vr[:], ALU.mult, ALU.mult, [o1b, vrb, gncolb], [o1b])
                        y, yb = y16[par]
                        self.tt(y[:], o1[:], sg[:], ALU.mult, [o1b, sgb], [yb])
                        self.dma("gpsimd", self.YT[3072 + h * 128:3072 + (h + 1) * 128, qs], y[:], [yb], [])
                    self.later(min(2, NT - 1), e1)

            self.pipeline(steps, KDD, stage1, stage2)
            self.barrier()

    def phaseE(self, l, s):
        S = self.S
        TB = min(1024, S)
        nblk = S // TB
        NGb = TB // 512
        with ExitStack() as P:
            xk, xkb = self.sb(P, "xk", [128, 8, TB], F32)
            big, bigb = self.sb(P, "big", [128, 32, TB], BF16)
            bigbs = [Buf("big%d" % i_) for i_ in range(4)]
            aT, aTb = self.sb(P, "aT", [128, 8, TB], BF16)
            pT, pTb = self.sb(P, "pT", [128, 2, TB], BF16)
            wpool = self.sbpool(P, "we", [128, 4096], BF16, 3)
            g2, g2b = self.sb(P, "g2", [128, 8], F32)
            g3, g3b = self.sb(P, "g3", [128, 8], F32)
            self.load_cols(g2[:], g2b, self.norm2_g[l], "(kc p) -> p kc", p=128)
            self.load_cols(g3[:], g3b, self.norm3_g[l], "(kc p) -> p kc", p=128)
            acc = self.sbpool(P, "acc", [128, 512], F32, 4 * NGb)
            tmpf = self.sbpool(P, "tmpf", [128, 512], F32, 3)
            gtp = self.sbpool(P, "gt", [128, 512], BF16, 3)
            pl = self.sbpool(P, "pl", [128, 256], F32, 2)
            pl16 = self.sbpool(P, "pl16", [128, 256], BF16, 2)
            cnt = [0]

            def load_y(blk_):
                for b_ in range(4):
                    self.dma("sync", big[:, b_ * 8:(b_ + 1) * 8, :],
                             self.YT[b_ * 1024:(b_ + 1) * 1024, blk_ * TB:(blk_ + 1) * TB].rearrange(
                                 "(kc p) t -> p kc t", p=128), [], [bigbs[b_]])

            for blk in range(nblk):
                c0 = s * S + blk * TB
                lc0 = blk * TB
                if blk == 0:
                    load_y(0)
                for nt in range(2):
                    for b_ in range(4):
                        def epi(pt, pb, m, g, b_=b_, nt=nt):
                            gt, gtb = gtp[cnt[0] % 3]
                            tm, tmb = tmpf[cnt[0] % 3]
                            cnt[0] += 1
                            r0 = b_ * 1024 + (nt * 4 + m) * 128
                            self.dma("sync", gt[:], self.GT[r0:r0 + 128, lc0 + g * 512:lc0 + (g + 1) * 512], [], [gtb])
                            a, ab = acc[m * NGb + g]
                            if b_ == 0:
                                self.tt(a[:], pt[:], gt[:], ALU.mult, [pb, gtb], [ab])
                            elif b_ < 3:
                                self.tt(tm[:], pt[:], gt[:], ALU.mult, [pb, gtb], [tmb])
                                self.tt(a[:], a[:], tm[:], ALU.add, [ab, tmb], [ab], en="gpsimd")
                            else:
                                self.tt(tm[:], pt[:], gt[:], ALU.mult, [pb, gtb], [tmb])
                                self.tt(aT[:, nt * 4 + m, g * 512:(g + 1) * 512], a[:], tm[:], ALU.add, [ab, tmb], [aTb],
                                        en="gpsimd")
                        self.gemm("fm", big, bigbs[b_], 8, TB, self.wb["br%d" % b_, l], nt * 512, 512, epi, wpool, k0=b_ * 8)
                self.dma("sync", xk[:], self.xT[:, c0:c0 + TB].rearrange("(kc p) t -> p kc t", p=128), [], [xkb])
                for nt in range(2):
                    def epi(pt, pb, m, g, nt=nt):
                        xs = xk[:, nt * 4 + m, g * 512:(g + 1) * 512]
                        self.tt(xs, xs, pt[:], ALU.add, [xkb, pb], [xkb])
                    self.gemm("fm", aT, aTb, 8, TB, self.wb["out", l], nt * 512, 512, epi, wpool)
                self.norm_fm(None, 0, TB, g2, g2b, aT, aTb, xkeep=(xk, xkb), loaded=True)
                for nt in range(8):
                    def epi(pt, pb, m, g, nt=nt):
                        tm, tmb = tmpf[cnt[0] % 3]
                        cnt[0] += 1
                        self.act(tm[:], pt[:], AF.Relu, [pb], [tmb])
                        self.tt(big[:, nt * 4 + m, g * 512:(g + 1) * 512], tm[:], tm[:], ALU.mult, [tmb],
                                [bigbs[(nt * 4 + m) // 8]])
                    self.gemm("fm", aT, aTb, 8, TB, self.wb["ff1", l], nt * 512, 512, epi, wpool)
                for mt in range(8):
                    def epi(pt, pb, m, g, mt=mt):
                        xs = xk[:, mt, g * 512:(g + 1) * 512]
                        self.tt(xs, xs, pt[:], ALU.add, [xkb, pb], [xkb])
                    self.gemm("fm", big, bigbs, 32, TB, self.wb["ff2", l], mt * 128, 128, epi, wpool)
                if blk + 1 < nblk:
                    load_y(blk + 1)
                self.norm_fm(None, 0, TB, g3, g3b, aT, aTb, xkeep=(xk, xkb), loaded=True)
                for t in range(TB // 128):
                    p_, p_b = pl[t % 2]
                    p16, p16b = pl16[t % 2]
                    self.dma("sync", p_[:], self.p[l, c0 + t * 128:c0 + (t + 1) * 128, :], [], [p_b])
                    self.cp(self.alt(), p16[:], p_[:], [p_b], [p16b])
                    i = self.psi % 8
                    self.psi += 1
                    pv = self.psb(i)
                    for bi in range(2):
                        self.tp(pv[:, bi * 128:(bi + 1) * 128], p16[:, bi * 128:(bi + 1) * 128], self.identb[:],
                                [p16b, self.identb_b], [self.ps[i][1]])
                    self.cp(self.alt(), pT[:, :, t * 128:(t + 1) * 128], pv[:, 0:256].rearrange("p (b t) -> p b t", b=2),
                            [self.ps[i][1]], [pTb])
                for nt in range(2):
                    wg, wgb = wpool[self.wi % 3]
                    self.wi += 1
                    wp, wpb = wpool[self.wi % 3]
                    self.wi += 1
                    wgv = wg[:, 0:4096].rearrange("p (k n) -> p k n", k=8)
                    wpv = wp[:, 0:1024].rearrange("p (k n) -> p k n", k=2)
                    self.dma("sync", wgv, self.wb["pg", l][:, nt * 512:(nt + 1) * 512].rearrange("(kc p) n -> p kc n", p=128),
                             [], [wgb])
                    self.dma("sync", wpv, self.wb["pp", l][:, nt * 512:(nt + 1) * 512].rearrange("(kc p) n -> p kc n", p=128),
                             [], [wpb])
                    for m in range(4):
                        for g in range(NGb):
                            gs = slice(g * 512, (g + 1) * 512)
                            p1, p1b = self.psn()
                            for kc in range(8):
                                self.mm(p1[:], wgv[:, kc, m * 128:(m + 1) * 128], aT[:, kc, gs], kc == 0, kc == 7,
                                        [wgb, aTb], [p1b])
                            p2, p2b = self.psn()
                            for kc in range(2):
                                self.mm(p2[:], wpv[:, kc, m * 128:(m + 1) * 128], pT[:, kc, gs], kc == 0, kc == 1,
                                        [wpb, pTb], [p2b])
                            tm, tmb = tmpf[cnt[0] % 3]
                            cnt[0] += 1
                            self.act(tm[:], p1[:], AF.Sigmoid, [p1b], [tmb])
                            self.tt(tm[:], tm[:], p2[:], ALU.mult, [tmb, p2b], [tmb])
                            xs = xk[:, nt * 4 + m, gs]
                            self.tt(xs, xs, tm[:], ALU.add, [xkb, tmb], [xkb], en="gpsimd")
                self.dma("gpsimd", self.xT[:, c0:c0 + TB].rearrange("(kc p) t -> p kc t", p=128), xk[:], [xkb], [])
            self.barrier()

    def mark(self, name):
        if not hasattr(self, "marks"):
            self.marks = []
        self.marks.append((name, {k: e.dom.count for k, e in self.E.items()}))

    def body(self):
        for l in range(self.L):
            for s in range(self.NSEQ):
                self.mark("P1 %d %d" % (l, s))
                self.phase1(l, s)
                self.mark("PB %d %d" % (l, s))
                (self.phaseB3 if "B3" in PH else self.phaseB2)(l, s)
                self.mark("PC %d %d" % (l, s))
                (self.phaseC3 if "C3" in PH else self.phaseC2)(l, s)
                self.mark("PD %d %d" % (l, s))
                (self.phaseD3 if "D3" in PH else self.phaseD2)(l, s)
                self.mark("PE %d %d" % (l, s))
                self.phaseE(l, s)
        self.mark("END")

    def build(self):
        self.setup_sync()
        self.declare_io()
        self.build_consts()
        self.precast()
        self.build_rope()
        self.transpose_in()
        self.body()
        self.transpose_out()
        self.barrier()
        self.st.close()
        return self.nc


INPUT_NAMES = ["x", "p", "norm1_g", "w_in", "gate_b", "conv_w", "conv_b", "lru_wa", "lru_ba", "lru_wx", "lru_bx",
               "lru_lambda", "diff_q_g", "diff_k_g", "diff_lam", "diff_sub_g", "mla_qa_g", "mla_wuq", "mla_kva_g",
               "mla_wukv", "mla_q_g", "mla_k_g", "ret_gn_g", "w_br_a", "w_br_b", "w_br_c", "w_br_d", "w_out",
               "norm2_g", "w_ff1", "w_ff2", "norm3_g", "w_ple_gate", "w_ple_proj"]


def make_in_maps(inputs, ncores, nseq, S):
    maps = []
    for c in range(ncores):
        m = {}
        for k in INPUT_NAMES:
            v = np.asarray(inputs[k])
            if k == "x":
                v = np.ascontiguousarray(v[c * nseq:(c + 1) * nseq].reshape(nseq * S, DM))
            elif k == "p":
                v = np.ascontiguousarray(v[:, c * nseq:(c + 1) * nseq].reshape(v.shape[0], nseq * S, PLED))
            else:
                v = np.ascontiguousarray(v)
            m[k] = v.astype(np.float32, copy=False)
        maps.append(m)
    return maps


def kernel(**inputs):
    B, S, _ = inputs["x"].shape
    L = inputs["w_in"].shape[0]
    ncores = 8
    nseq = B // ncores
    kb = KB(S, nseq, L)
    nc = kb.build()
    maps = make_in_maps(inputs, ncores, nseq, S)
    res = run_bass_kernel_spmd(nc, maps, core_ids=list(range(ncores)))
    outs = [np.asarray(r["out"]).reshape(nseq, S, DM) for r in res.results]
    return np.concatenate(outs, axis=0).astype(np.float32)
```

```python
import math
import numpy as np
import concourse.bass as bass
import concourse.mybir as mybir
from concourse.bass_utils import run_bass_kernel_spmd
from contextlib import ExitStack

F32 = mybir.dt.float32
BF16 = mybir.dt.bfloat16
I32 = mybir.dt.int32
AF = mybir.ActivationFunctionType
ALU = mybir.AluOpType
AX = mybir.AxisListType

DM = 1024
NIN = 12992
DFF = 4096
PLED = 256
EPS = 1e-6
OFF = dict(ax=0, ag=1024, bq=2048, bk=3072, bv=4096, cq=5120, ckv=5504, ckpe=5760,
           dq=5824, dk=6336, dv=6848, dg=7872, gl=8896)
SAME_SYNC = True
import os
PH = os.environ.get("KPH", "B3,C3,D3").split(",")
KDEPTH = int(os.environ.get("KDEPTH", "3"))
KCP = os.environ.get("KCP", "scalar")
LENG0 = "vector"
LENG1 = os.environ.get("KLENG1", "vector")
KDEFER = int(os.environ.get("KDEFER", "2"))
KGI = int(os.environ.get("KGI", "2"))
KDACT = int(os.environ.get("KDACT", "2"))
KDD = int(os.environ.get("KDD", "5"))
KE1 = int(os.environ.get("KE1", "3"))
KE2 = int(os.environ.get("KE2", "8"))


class Dom:
    def __init__(self, name, sem, mult):
        self.name, self.sem, self.mult, self.count = name, sem, mult, 0


class Buf:
    __slots__ = ("name", "w", "r")

    def __init__(self, name=""):
        self.name, self.w, self.r = name, {}, {}


class Eng:
    def __init__(self, name, eng, dom):
        self.name, self.eng, self.dom = name, eng, dom
        self.seen = {}
        self.slots = []
        self.si = 0


class KB:
    def __init__(self, S, NSEQ, L, debug=()):
        self.S, self.NSEQ, self.L = S, NSEQ, L
        self.NT, self.NG = S // 128, S // 512
        self.debug = set(debug)
        self.nc = bass.Bass("TRN2", target_bir_lowering=False)
        self.st = ExitStack()
        self.E = {}
        self.doms = []
        self.dbufs = {}
        self.cnt = 0
        self.wi = 0

    def setup_sync(self):
        nc = self.nc
        for name in ["tensor", "vector", "scalar", "gpsimd", "sync"]:
            sem = self.st.enter_context(nc.semaphore("s_" + name))
            dom = Dom(name, sem, 1)
            self.doms.append(dom)
            self.E[name] = Eng(name, getattr(nc, name), dom)
        for q, n in (("sync", 24), ("gpsimd", 16)):
            for i in range(n):
                sem = self.st.enter_context(nc.semaphore("d_%s%d" % (q, i)))
                dom = Dom("d_%s%d" % (q, i), sem, 16)
                self.doms.append(dom)
                self.E[q].slots.append(dom)

    def _deps(self, reads, writes):
        deps = {}
        for b in reads:
            for d, i in b.w.items():
                if deps.get(d, 0) < i:
                    deps[d] = i
        for b in writes:
            for d, i in b.w.items():
                if deps.get(d, 0) < i:
                    deps[d] = i
            for d, i in b.r.items():
                if deps.get(d, 0) < i:
                    deps[d] = i
        return deps

    def _wait(self, E, deps):
        for dom, idx in deps.items():
            if idx <= 0:
                continue
            if dom is E.dom and (E.name == "tensor" or not SAME_SYNC):
                continue
            if E.seen.get(dom, 0) >= idx:
                continue
            E.eng.wait_ge(dom.sem, idx * dom.mult)
            E.seen[dom] = idx

    def op(self, en, fn, reads=(), writes=()):
        E = self.E[en]
        self._wait(E, self._deps(reads, writes))
        ins = fn(E.eng)
        E.dom.count += 1
        ins.then_inc(E.dom.sem, 1)
        c = E.dom.count
        for b in reads:
            b.r[E.dom] = c
        for b in writes:
            b.w[E.dom] = c
        self.cnt += 1

    def dma(self, q, out, in_, reads=(), writes=(), slow=False):
        E = self.E[q]
        slot = E.slots[E.si % len(E.slots)]
        E.si += 1
        deps = self._deps(reads, writes)
        if slot.count > 0 and deps.get(slot, 0) < slot.count:
            deps[slot] = slot.count
        self._wait(E, deps)
        if slow:
            ins = E.eng.dma_start(out=out, in_=in_, allow_slow_non_contiguous=True)
        else:
            ins = E.eng.dma_start(out=out, in_=in_)
        ins.then_inc(slot.sem, 16)
        slot.count += 1
        for b in reads:
            b.r[slot] = slot.count
        for b in writes:
            b.w[slot] = slot.count
        self.cnt += 1

    def barrier(self):
        for E in self.E.values():
            for d in self.doms:
                if d.count > 0 and E.seen.get(d, 0) < d.count:
                    E.eng.wait_ge(d.sem, d.count * d.mult)
                    E.seen[d] = d.count

    def dbuf(self, *key):
        b = self.dbufs.get(key)
        if b is None:
            b = Buf(str(key))
            self.dbufs[key] = b
        return b

    def sb(self, stack, name, shape, dt):
        self.uid = getattr(self, "uid", 0) + 1
        t = stack.enter_context(self.nc.sbuf_tensor("%s_%d" % (name, self.uid), list(shape), dt))
        return t, Buf(name)

    def sbpool(self, stack, name, shape, dt, n):
        return [self.sb(stack, "%s%d" % (name, i), shape, dt) for i in range(n)]

    def act(self, out, in_, func, reads, writes, bias=None, scale=None, accum=None):
        kw = {}
        if bias is not None:
            kw["bias"] = bias
        if scale is not None:
            kw["scale"] = scale
        if accum is not None:
            kw["accum_out"] = accum
        self.op("scalar", lambda e: e.activation(out=out, in_=in_, func=func, **kw), reads, writes)

    def tt(self, out, in0, in1, op, reads, writes, en="vector"):
        self.op(en, lambda e: e.tensor_tensor(out=out, in0=in0, in1=in1, op=op), reads, writes)

    def ts(self, out, in0, s1, s2, op0, op1, reads, writes, en="vector"):
        if op1 is None:
            self.op(en, lambda e: e.tensor_scalar(out=out, in0=in0, scalar1=s1, scalar2=None, op0=op0), reads, writes)
        else:
            self.op(en, lambda e: e.tensor_scalar(out=out, in0=in0, scalar1=s1, scalar2=s2, op0=op0, op1=op1),
                    reads, writes)

    def stt(self, out, in0, scalar, in1, op0, op1, reads, writes):
        self.op("vector", lambda e: e.scalar_tensor_tensor(out=out, in0=in0, scalar=scalar, in1=in1, op0=op0, op1=op1),
                reads, writes)

    def cp(self, en, out, in_, reads, writes):
        if en == "scalar":
            self.op("scalar", lambda e: e.activation(out=out, in_=in_, func=AF.Copy), reads, writes)
        else:
            self.op(en, lambda e: e.tensor_copy(out=out, in_=in_), reads, writes)

    def mm(self, out, lhsT, rhs, start, stop, reads, writes):
        self.op("tensor", lambda e: e.matmul(out, lhsT=lhsT, rhs=rhs, start=start, stop=stop), reads, writes)

    def tp(self, out, in_, ident, reads, writes):
        self.op("tensor", lambda e: e.transpose(out=out, in_=in_, identity=ident), reads, writes)

    def rsqrt(self, out, in_, scale, reads, writes, eps=EPS):
        self.act(out, in_, AF.Ln, reads, writes, bias=self.epscol[0:out.shape[0], :] if eps == EPS else eps,
                 scale=scale)
        self.act(out, out, AF.Exp, list(writes), writes, scale=-0.5)

    _alt = 0

    def alt(self):
        self._alt ^= 1
        return "scalar" if self._alt else "vector"

    def declare_io(self):
        nc, S, NSEQ, L = self.nc, self.S, self.NSEQ, self.L
        T = NSEQ * S
        self.T = T
        d = lambda n, sh, dt=F32: nc.dram_tensor(n, list(sh), dt, kind="ExternalInput")
        self.x = d("x", [T, DM])
        self.p = d("p", [L, T, PLED])
        self.norm1_g = d("norm1_g", [L, DM])
        self.w_in = d("w_in", [L, DM, NIN])
        self.gate_b = d("gate_b", [L, 4, DM])
        self.conv_w = d("conv_w", [L, 4, DM])
        self.conv_b = d("conv_b", [L, DM])
        self.lru_wa = d("lru_wa", [L, 2, 16, 64, 64])
        self.lru_ba = d("lru_ba", [L, 2, DM])
        self.lru_wx = d("lru_wx", [L, 2, 16, 64, 64])
        self.lru_bx = d("lru_bx", [L, 2, DM])
        self.lru_lambda = d("lru_lambda", [L, 2, DM])
        self.diff_q_g = d("diff_q_g", [L, 64])
        self.diff_k_g = d("diff_k_g", [L, 64])
        self.diff_lam = d("diff_lam", [L, 4, 64])
        self.diff_sub_g = d("diff_sub_g", [L, 128])
        self.mla_qa_g = d("mla_qa_g", [L, 384])
        self.mla_wuq = d("mla_wuq", [L, 384, 1536])
        self.mla_kva_g = d("mla_kva_g", [L, 256])
        self.mla_wukv = d("mla_wukv", [L, 256, 2048])
        self.mla_q_g = d("mla_q_g", [L, 192])
        self.mla_k_g = d("mla_k_g", [L, 192])
        self.ret_gn_g = d("ret_gn_g", [L, 128])
        self.w_br = [d("w_br_" + c, [L, DM, DM]) for c in "abcd"]
        self.w_out = d("w_out", [L, DM, DM])
        self.norm2_g = d("norm2_g", [L, DM])
        self.w_ff1 = d("w_ff1", [L, DM, DFF])
        self.w_ff2 = d("w_ff2", [L, DFF, DM])
        self.norm3_g = d("norm3_g", [L, DM])
        self.w_ple_gate = d("w_ple_gate", [L, DM, DM])
        self.w_ple_proj = d("w_ple_proj", [L, PLED, DM])
        self.out = nc.dram_tensor("out", [T, DM], F32, kind="ExternalOutput")

        def scr(n, sh, dt):
            kind = "ExternalOutput" if n in self.debug else "Internal"
            return nc.dram_tensor(n, list(sh), dt, kind=kind)

        self.scr = scr
        self.wb = {}
        for l in range(L):
            self.wb["w_in", l] = scr("wb_in%d" % l, [DM, NIN], BF16)
            self.wb["wuq", l] = scr("wb_wuq%d" % l, [384, 1536], BF16)
            self.wb["wukv", l] = scr("wb_wukv%d" % l, [256, 2048], BF16)
            for i in range(4):
                self.wb["br%d" % i, l] = scr("wb_br%d_%d" % (i, l), [DM, DM], BF16)
            self.wb["out", l] = scr("wb_out%d" % l, [DM, DM], BF16)
            self.wb["ff1", l] = scr("wb_ff1%d" % l, [DM, DFF], BF16)
            self.wb["ff2", l] = scr("wb_ff2%d" % l, [DFF, DM], BF16)
            self.wb["pg", l] = scr("wb_pg%d" % l, [DM, DM], BF16)
            self.wb["pp", l] = scr("wb_pp%d" % l, [PLED, DM], BF16)
        self.xT = scr("xT", [DM, T], F32)
        self.QTb = scr("QTb", [1024, S], BF16)
        self.KTb = scr("KTb", [1024, S], BF16)
        self.Vb = scr("Vb", [S, 1024], BF16)
        self.QTc = scr("QTc", [8 * 192, S], BF16)
        self.KTc = scr("KTc", [8 * 192, S], BF16)
        self.Vc = scr("Vc", [S, 1024], BF16)
        self.QTd = scr("QTd", [512, S], BF16)
        self.KTd = scr("KTd", [512, S], BF16)
        self.Vd = scr("Vd", [S, 1024], BF16)
        self.GdT = scr("GdT", [1024, S], BF16)
        self.YT = scr("YT", [4 * 1024, S], BF16)
        self.GT = scr("GT", [4 * 1024, S], BF16)
        self.ropeD = [scr("rope%d" % i, [S, 2, r2], F32) for i, r2 in enumerate((8, 32, 32))]

    def build_consts(self):
        nc = self.nc
        st = self.st
        self.identf, self.identf_b = self.sb(st, "identf", [128, 128], F32)
        self.identb, self.identb_b = self.sb(st, "identb", [128, 128], BF16)
        self.onesf, self.onesf_b = self.sb(st, "onesf", [128, 128], F32)
        self.onesb, self.onesb_b = self.sb(st, "onesb", [128, 128], BF16)
        self.epscol, self.epscol_b = self.sb(st, "epscol", [128, 1], F32)
        self.onecol, self.onecol_b = self.sb(st, "onecol", [128, 1], F32)
        self.op("vector", lambda e: e.memset(self.onecol[:], 1.0), [], [self.onecol_b])
        self.CB = [self.identf_b, self.identb_b, self.onesf_b, self.onesb_b, self.epscol_b]
        with ExitStack() as s2:
            it, itb = self.sb(s2, "c_it", [128, 128], I32)
            tf, tfb = self.sb(s2, "c_tf", [128, 128], F32)
            self.op("gpsimd", lambda e: e.iota(it[:], pattern=[[1, 128]], base=0, channel_multiplier=-1), [], [itb])
            self.cp("vector", tf[:], it[:], [itb], [tfb])
            self.ts(self.identf[:], tf[:], 0.0, None, ALU.is_equal, None, [tfb], [self.identf_b])
            self.cp("vector", self.identb[:], self.identf[:], [self.identf_b], [self.identb_b])
            self.op("vector", lambda e: e.memset(self.onesf[:], 1.0), [], [self.onesf_b])
            self.op("vector", lambda e: e.memset(self.onesb[:], 1.0), [], [self.onesb_b])
            self.op("vector", lambda e: e.memset(self.epscol[:], EPS), [], [self.epscol_b])
            self.barrier()
        self.psbig = st.enter_context(nc.psum_tensor("psbig", [128, 8, 512], F32))
        self.ps = [(self.psbig[:, i, :], Buf("ps%d" % i)) for i in range(8)]
        self.psi = 0

    def psn(self, lo=0, hi=8):
        i = lo + self.psi % (hi - lo)
        self.psi += 1
        return self.ps[i]

    def precast(self):
        L = self.L
        jobs = []
        for l in range(L):
            jobs.append((self.w_in[l], self.wb["w_in", l], DM, NIN))
            jobs.append((self.mla_wuq[l], self.wb["wuq", l], 384, 1536))
            jobs.append((self.mla_wukv[l], self.wb["wukv", l], 256, 2048))
            for i in range(4):
                jobs.append((self.w_br[i][l], self.wb["br%d" % i, l], DM, DM))
            jobs.append((self.w_out[l], self.wb["out", l], DM, DM))
            jobs.append((self.w_ff1[l], self.wb["ff1", l], DM, DFF))
            jobs.append((self.w_ff2[l], self.wb["ff2", l], DFF, DM))
            jobs.append((self.w_ple_gate[l], self.wb["pg", l], DM, DM))
            jobs.append((self.w_ple_proj[l], self.wb["pp", l], PLED, DM))
        with ExitStack() as s2:
            CW = 2048
            stg = self.sbpool(s2, "pc_f", [128, CW], F32, 3)
            stb = self.sbpool(s2, "pc_b", [128, CW], BF16, 3)
            i = 0
            for src, dst, K, N in jobs:
                for r0 in range(0, K, 128):
                    for c0 in range(0, N, CW):
                        cw = min(CW, N - c0)
                        f, fb = stg[i % 3]
                        b, bb = stb[i % 3]
                        self.dma("sync", f[:, 0:cw], src[r0:r0 + 128, c0:c0 + cw], [], [fb])
                        self.cp(self.alt(), b[:, 0:cw], f[:, 0:cw], [fb], [bb])
                        self.dma("gpsimd", dst[r0:r0 + 128, c0:c0 + cw], b[:, 0:cw], [bb], [])
                        i += 1
            self.barrier()

    def build_rope(self):
        NT = self.NT
        cfgs = ((8, 16, 500000.0), (32, 64, 500000.0), (32, 64, 10000.0))
        C1 = 6.28125
        C2 = 2.0 * math.pi - C1
        with ExitStack() as s2:
            pi_, pib = self.sb(s2, "r_pi", [128, NT], I32)
            pos, posb = self.sb(s2, "r_pos", [128, NT], F32)
            self.op("gpsimd", lambda e: e.iota(pi_[:], pattern=[[128, NT]], base=0, channel_multiplier=1), [], [pib])
            self.cp("vector", pos[:], pi_[:], [pib], [posb])
            for ci, (r2, rot, theta) in enumerate(cfgs):
                with ExitStack() as s3:
                    invf, invfb = self.sb(s3, "r_invf", [128, r2], F32)
                    ang, angb = self.sb(s3, "r_ang", [128, NT, r2], F32)
                    a, ab = self.sb(s3, "r_a", [128, NT, r2], F32)
                    kf, kfb = self.sb(s3, "r_kf", [128, NT, r2], F32)
                    ki, kib = self.sb(s3, "r_ki", [128, NT, r2], I32)
                    mk, mkb = self.sb(s3, "r_mk", [128, NT, r2], F32)
                    tab, tabb = self.sb(s3, "r_tab", [128, NT, 2, r2], F32)
                    for j in range(r2):
                        v = float(np.float32(theta) ** np.float32(-(2.0 * j) / rot))
                        self.op("vector", lambda e, j=j, v=v: e.memset(invf[:, j:j + 1], v), [], [invfb])
                    pos_b = bass.AP(pos[:].tensor, pos[:].offset, [list(pos[:].ap[0]), [1, NT], [0, r2]])
                    inv_b = bass.AP(invf[:].tensor, invf[:].offset, [list(invf[:].ap[0]), [0, NT], [1, r2]])
                    self.tt(ang[:], pos_b, inv_b, ALU.mult, [posb, invfb], [angb])
                    for which, shift in ((1, 0.0), (0, math.pi / 2)):
                        self.ts(a[:], ang[:], shift, None, ALU.add, None, [angb], [ab])
                        self.ts(kf[:], a[:], 1.0 / (2 * math.pi), None, ALU.mult, None, [ab], [kfb])
                        self.cp("vector", ki[:], kf[:], [kfb], [kib])
                        self.cp("vector", kf[:], ki[:], [kib], [kfb])
                        self.stt(a[:], kf[:], -C1, a[:], ALU.mult, ALU.add, [kfb, ab], [ab])
                        self.stt(a[:], kf[:], -C2, a[:], ALU.mult, ALU.add, [kfb, ab], [ab])
                        self.ts(mk[:], a[:], math.pi, None, ALU.is_gt, None, [ab], [mkb])
                        self.stt(a[:], mk[:], -2 * math.pi, a[:], ALU.mult, ALU.add, [mkb, ab], [ab])
                        self.ts(mk[:], a[:], -math.pi, None, ALU.is_lt, None, [ab], [mkb])
                        self.stt(a[:], mk[:], 2 * math.pi, a[:], ALU.mult, ALU.add, [mkb, ab], [ab])
                        self.ts(a[:], a[:], 3.1415925, -3.1415925, ALU.min, ALU.max, [ab], [ab])
                        self.act(tab[:, :, which, :], a[:], AF.Sin, [ab], [tabb])
                    self.dma("gpsimd", self.ropeD[ci].ap().rearrange("(t p) c j -> p t c j", p=128), tab[:],
                             [tabb], [self.dbuf("rope", ci)], slow=True)
                    self.barrier()

    def transpose_in(self):
        T = self.T
        with ExitStack() as s2:
            xt = self.sbpool(s2, "ti_x", [128, DM], F32, 3)
            stg = self.sbpool(s2, "ti_s", [128, 8, 512], F32, 2)
            n = 0
            for g in range(T // 512):
                sg, sgb = stg[g % 2]
                for ti in range(4):
                    t0 = g * 512 + ti * 128
                    x_, xb = xt[n % 3]
                    n += 1
                    self.dma("sync", x_[:], self.x[t0:t0 + 128, :], [], [xb])
                    for half in range(2):
                        pt, pb = self.psn()
                        for q in range(4):
                            kc = half * 4 + q
                            self.tp(pt[:, q * 128:(q + 1) * 128], x_[:, kc * 128:(kc + 1) * 128], self.identf[:],
                                    [xb, self.identf_b], [pb])
                        self.cp(self.alt(), sg[:, half * 4:half * 4 + 4, ti * 128:(ti + 1) * 128],
                                pt[:].rearrange("p (q t) -> p q t", q=4), [pb], [sgb])
                self.dma("gpsimd", self.xT[:, g * 512:(g + 1) * 512].rearrange("(kc p) t -> p kc t", p=128), sg[:],
                         [sgb], [self.dbuf("xT", g)])
            self.barrier()

    def transpose_out(self):
        T = self.T
        with ExitStack() as s2:
            xg = self.sbpool(s2, "to_x", [128, 8, 512], F32, 2)
            ot = self.sbpool(s2, "to_o", [128, DM], F32, 3)
            n = 0
            for g in range(T // 512):
                x_, xb = xg[g % 2]
                self.dma("sync", x_[:], self.xT[:, g * 512:(g + 1) * 512].rearrange("(kc p) t -> p kc t", p=128),
                         [self.dbuf("xT", g)], [xb])
                for ti in range(4):
                    o_, ob = ot[n % 3]
                    n += 1
                    for half in range(2):
                        pt, pb = self.psn()
                        for q in range(4):
                            kc = half * 4 + q
                            self.tp(pt[:, q * 128:(q + 1) * 128], x_[:, kc, ti * 128:(ti + 1) * 128], self.identf[:],
                                    [xb, self.identf_b], [pb])
                        self.cp(self.alt(), o_[:, half * 512:(half + 1) * 512], pt[:], [pb], [ob])
                    t0 = g * 512 + ti * 128
                    self.dma("gpsimd", self.out[t0:t0 + 128, :], o_[:], [ob], [self.dbuf("out", t0)])
            self.barrier()


    @staticmethod
    def bc_last(a, n):
        return bass.AP(a.tensor, a.offset, [list(x) for x in a.ap] + [[0, n]])

    @staticmethod
    def bc_col(a, n):
        return bass.AP(a.tensor, a.offset, [list(a.ap[0]), [0, n]])

    @staticmethod
    def bc_mid(a, n):
        ap = [list(x) for x in a.ap]
        return bass.AP(a.tensor, a.offset, [ap[0], [0, n]] + ap[1:])

    def load_cols(self, dst, dstb, src_ap, pattern, **kw):
        self.dma("sync", dst, src_ap.rearrange(pattern, **kw), [], [dstb], slow=True)

    def load_bc(self, dst, dstb, src_row_ap):
        self.dma("sync", dst, src_row_ap.broadcast_to([128, src_row_ap.shape[-1]]), [], [dstb])

    def psb(self, i):
        return self.psbig[:, i, :].bitcast(BF16)

    def norm_fm(self, stack_unused, col0, ntok, gcols, gcolsb, outT, outTb, xkeep=None, loaded=False):
        with ExitStack() as s2:
            if xkeep is None:
                xg_pool = self.sbpool(s2, "nf_x", [128, 8, 512], F32, 2)
            sq_pool = self.sbpool(s2, "nf_sq", [128, 512], BF16, 4)
            rs_pool = self.sbpool(s2, "nf_rs", [128, 512], F32, 2)
            n = 0
            ng = ntok // 512
            bsz = 2 if loaded else 1
            for g0 in range(0, ng, bsz):
                batch = []
                for g in range(g0, min(g0 + bsz, ng)):
                    c0 = col0 + g * 512
                    if xkeep is None:
                        xg, xgb = xg_pool[g % 2]
                        self.dma("sync", xg[:], self.xT[:, c0:c0 + 512].rearrange("(kc p) t -> p kc t", p=128),
                                 [self.dbuf("xT", c0 // 512)], [xgb])
                        xs = lambda kc, xg=xg: xg[:, kc, :]
                    else:
                        xk, xgb = xkeep
                        if not loaded:
                            self.dma("sync", xk[:, :, g * 512:(g + 1) * 512],
                                     self.xT[:, c0:c0 + 512].rearrange("(kc p) t -> p kc t", p=128),
                                     [self.dbuf("xT", c0 // 512)], [xgb])
                        xs = lambda kc, g=g, xk=xk: xk[:, kc, g * 512:(g + 1) * 512]
                    pt, pb = self.psn()
                    for kc in range(8):
                        sq, sqb = sq_pool[n % 4]
                        n += 1
                        self.act(sq[:], xs(kc), AF.Square, [xgb], [sqb])
                        self.mm(pt[:], self.onesb[:], sq[:], kc == 0, kc == 7, [sqb, self.onesb_b], [pb])
                    batch.append((g, xs, xgb, pt, pb))
                for g, xs, xgb, pt, pb in batch:
                    rs, rsb = rs_pool[g % 2]
                    self.rsqrt(rs[:], pt[:], 1.0 / DM, [pb], [rsb])
                for g, xs, xgb, pt, pb in batch:
                    rs, rsb = rs_pool[g % 2]
                    for kc in range(8):
                        self.stt(outT[:, kc, g * 512:(g + 1) * 512], xs(kc), gcols[:, kc:kc + 1], rs[:],
                                 ALU.mult, ALU.mult, [xgb, rsb, gcolsb], [outTb])

    def gemm(self, mode, AT, ATb, KC, ntok, W, col0, ncols, epi, wpool, k0=0, ps_lo=0, ps_hi=8, gi=2):
        wt, wtb = wpool[self.wi % len(wpool)]
        self.wi += 1
        ATl = ATb if isinstance(ATb, list) else [ATb]
        wv = wt[:, 0:KC * ncols].rearrange("p (k n) -> p k n", k=KC)
        self.dma("sync", wv, W[:, col0:col0 + ncols].rearrange("(kc p) n -> p kc n", p=128), [], [wtb])
        if mode == "tm":
            pend = []
            nt_ = ntok // 128
            for t in range(nt_):
                pt, pb = self.psn(ps_lo, ps_hi)
                for kc in range(KC):
                    self.mm(pt[:, 0:ncols], AT[:, k0 + kc, t * 128:(t + 1) * 128], wv[:, kc, :], kc == 0, kc == KC - 1,
                            ATl + [wtb], [pb])
                r = epi(pt, pb, t)
                if r is not None:
                    pend.append(r)
                if pend and (len(pend) == gi or t == nt_ - 1):
                    while pend:
                        for g_ in list(pend):
                            try:
                                next(g_)
                            except StopIteration:
                                pend.remove(g_)
        else:
            for m in range(ncols // 128):
                for g in range(ntok // 512):
                    pt, pb = self.psn(ps_lo, ps_hi)
                    for kc in range(KC):
                        self.mm(pt[:], wv[:, kc, m * 128:(m + 1) * 128], AT[:, k0 + kc, g * 512:(g + 1) * 512],
                                kc == 0, kc == KC - 1, ATl + [wtb], [pb])
                    epi(pt, pb, m, g)

    def norm_rope_g(self, v, vb, G, Dg, t, tmp, gain=None, gainb=None, normdim=None, ss_extra=None, rope=None):
        v3 = v.rearrange("p (g d) -> p g d", g=G)
        sq, sqb, ss, ssb, rt, rtb = tmp
        if gain is not None:
            W = G * Dg
            self.tt(sq[:, 0:W], v, v, ALU.mult, [vb], [sqb])
            yield
            self.op("vector", lambda e: e.tensor_reduce(out=ss[:, 0:G], in_=sq[:, 0:W].rearrange("p (g d) -> p g d", g=G),
                                                        axis=AX.X, op=ALU.add), [sqb], [ssb])
            yield
            if ss_extra is not None:
                ex, exb = ss_extra
                self.tt(ss[:, 0:G], ss[:, 0:G], ex, ALU.add, [ssb, exb], [ssb])
                yield
            self.rsqrt(ss[:, 0:G], ss[:, 0:G], 1.0 / normdim, [ssb], [ssb])
            yield
            self.tt(v3, v3, self.bc_last(ss[:, 0:G], Dg), ALU.mult, [vb, ssb], [vb])
            yield
            self.tt(v3, v3, self.bc_mid(gain, G), ALU.mult, [vb, gainb], [vb])
            yield
        if rope is not None:
            off, r2, tab, tabb = rope
            x1 = v3[:, :, off:off + r2]
            x2 = v3[:, :, off + r2:off + 2 * r2]
            cos = self.bc_mid(tab[:, t, 0, :], G)
            sin = self.bc_mid(tab[:, t, 1, :], G)
            n = G * r2
            tv = [rt[:, i * n:(i + 1) * n].rearrange("p (g r) -> p g r", g=G) for i in range(4)]
            self.tt(tv[0], x1, cos, ALU.mult, [vb, tabb], [rtb])
            yield
            self.tt(tv[1], x2, sin, ALU.mult, [vb, tabb], [rtb])
            yield
            self.tt(tv[2], x2, cos, ALU.mult, [vb, tabb], [rtb])
            yield
            self.tt(tv[3], x1, sin, ALU.mult, [vb, tabb], [rtb])
            yield
            self.tt(x1, tv[0], tv[1], ALU.subtract, [rtb], [vb])
            yield
            self.tt(x2, tv[2], tv[3], ALU.add, [rtb], [vb])
            yield

    def norm_rope(self, *a, **kw):
        for _ in self.norm_rope_g(*a, **kw):
            pass

    def tr_stage(self, src, srcb, blocks, tq, stage, stageb):
        i = self.psi % 8
        self.psi += 1
        pb = self.ps[i][1]
        pv = self.psb(i)
        for bi, (lo, w) in enumerate(blocks):
            self.tp(pv[0:w, bi * 128:(bi + 1) * 128], src[:, lo:lo + w], self.identb[:], [srcb, self.identb_b], [pb])
        nb = len(blocks)
        if all(w == 128 for _, w in blocks):
            self.cp(self.alt(), stage[:, 0:nb, tq * 128:(tq + 1) * 128],
                    pv[:, 0:nb * 128].rearrange("p (b t) -> p b t", b=nb), [pb], [stageb])
        else:
            for bi, (lo, w) in enumerate(blocks):
                self.cp(self.alt(), stage[0:w, bi, tq * 128:(tq + 1) * 128], pv[0:w, bi * 128:(bi + 1) * 128],
                        [pb], [stageb])

    def phase1(self, l, s):
        S, NT, NG = self.S, self.NT, self.NG
        tok0 = s * S
        Wd = self.wb["w_in", l]
        with ExitStack() as P:
            xnT, xnTb = self.sb(P, "xnT", [128, 8, S], BF16)
            g1, g1b = self.sb(P, "g1", [128, 8], F32)
            self.load_cols(g1[:], g1b, self.norm1_g[l], "(kc p) -> p kc", p=128)
            self.norm_fm(None, tok0, S, g1, g1b, xnT, xnTb)
            self.barrier()
            self.mark(" P1lru")
            wpool = self.sbpool(P, "w", [128, 4096], BF16, 2)
            o16 = self.sbpool(P, "o16", [128, 512], BF16, 3)
            self.oi = 0
            with ExitStack() as PA:
                self.lru(PA, l, s, xnT, xnTb, Wd, wpool)
                self.barrier()
                self.mark(" P1qkv")
            with ExitStack() as PQ:
                self.prep_qkv(PQ, l, s, xnT, xnTb, Wd, wpool, o16)
                self.barrier()

    def lru(self, PA, l, s, xnT, xnTb, Wd, wpool):
        S, NT, NG = self.S, self.NT, self.NG
        TC = min(1024, S)
        nch = S // TC
        cw, cwb = self.sb(PA, "cw", [128, 4, 8], F32)
        cb, cbb = self.sb(PA, "cb", [128, 8], F32)
        ba, bab = self.sb(PA, "ba", [128, 2, 8], F32)
        bx, bxb = self.sb(PA, "bx", [128, 2, 8], F32)
        lam, lamb = self.sb(PA, "lam", [128, 16], F32)
        hh, hhb = self.sb(PA, "hh", [128, 16], F32)
        cd, cdb = self.sb(PA, "cd", [128, 16], F32)
        cd2, cd2b = self.sb(PA, "cd2", [128, 16], F32)
        for t_ in range(4):
            self.load_cols(cw[:, t_, :], cwb, self.conv_w[l, t_], "(c p) -> p c", p=128)
        self.load_cols(cb[:], cbb, self.conv_b[l], "(c p) -> p c", p=128)
        for d_ in range(2):
            self.load_cols(ba[:, d_, :], bab, self.lru_ba[l, d_], "(c p) -> p c", p=128)
            self.load_cols(bx[:, d_, :], bxb, self.lru_bx[l, d_], "(c p) -> p c", p=128)
            self.load_cols(lam[:, d_ * 8:(d_ + 1) * 8], lamb, self.lru_lambda[l, d_], "(c p) -> p c", p=128)
        self.act(lam[:], lam[:], AF.Exp, [lamb], [lamb], scale=-1.0)
        self.ts(hh[:], lam[:], -1.0 / 6, 1.0 / 5, ALU.mult, ALU.add, [lamb], [hhb])
        for cst in (-1.0 / 4, 1.0 / 3, -1.0 / 2, 1.0):
            self.tt(hh[:], hh[:], lam[:], ALU.mult, [hhb, lamb], [hhb])
            self.ts(hh[:], hh[:], cst, None, ALU.add, None, [hhb], [hhb])
        self.tt(hh[:], hh[:], lam[:], ALU.mult, [hhb, lamb], [hhb])
        self.ts(cd[:], hh[:], -8.0, None, ALU.mult, None, [hhb], [cdb])
        self.ts(cd2[:], hh[:], -16.0, None, ALU.mult, None, [hhb], [cd2b])
        wst, wstb = self.sb(PA, "wst", [128, 4, 128], F32)
        wbf, wbfb = self.sb(PA, "wbf", [128, 4, 128], BF16)
        self.op("vector", lambda e: e.memset(wst[:], 0.0), [], [wstb])
        Pb, Pbb = self.sb(PA, "Pb", [128, S + 4], F32)
        gg, ggb = self.sb(PA, "gg", [128, S], BF16)
        xc, xcb_ = self.sb(PA, "xc", [128, S], F32)
        x16, x16b = self.sb(PA, "x16", [128, S], BF16)
        Bsets = []
        for d_ in range(2):
            B1, B1b = self.sb(PA, "B1%d" % d_, [128, TC], F32)
            B2, B2b = self.sb(PA, "B2%d" % d_, [128, TC], F32)
            B3, B3b = self.sb(PA, "B3%d" % d_, [128, TC], F32)
            Bsets.append((B1, B1b, B2, B2b, B3, B3b))
        hb, hbb = self.sb(PA, "hb", [128, S], F32)
        tA = self.sbpool(PA, "tA", [128, 512], F32, 2)
        self.op("vector", lambda e: e.memset(Pb[:, 0:2], 0.0), [], [Pbb])
        self.op("vector", lambda e: e.memset(Pb[:, S + 2:S + 4], 0.0), [], [Pbb])
        wsrc = (self.lru_wa, self.lru_wx)
        for c in range(8):
            for wi in range(2):
                for d in range(2):
                    for half in range(2):
                        self.dma("sync", wst[half * 64:(half + 1) * 64, wi * 2 + d, half * 64:(half + 1) * 64],
                                 wsrc[wi][l, d, 2 * c + half], [], [wstb])
            self.cp("vector", wbf[:], wst[:], [wstb], [wbfb])

            def epi_x(pt, pb, m, g):
                self.cp(self.alt(), Pb[:, 2 + g * 512:2 + (g + 1) * 512], pt[:], [pb], [Pbb])

            def epi_g(pt, pb, m, g):
                t1, t1b = tA[g % 2]
                self.act(t1[:], pt[:], AF.Square, [pb], [t1b])
                self.ts(t1[:], t1[:], 0.044715, 1.0, ALU.mult, ALU.add, [t1b], [t1b])
                self.tt(t1[:], t1[:], pt[:], ALU.mult, [t1b, pb], [t1b])
                self.act(t1[:], t1[:], AF.Sigmoid, [t1b], [t1b], scale=1.5957691216057308)
                self.tt(gg[:, g * 512:(g + 1) * 512], t1[:], pt[:], ALU.mult, [t1b, pb], [ggb])

            self.gemm("fm", xnT, xnTb, 8, S, Wd, OFF["ax"] + c * 128, 128, epi_x, wpool)
            self.gemm("fm", xnT, xnTb, 8, S, Wd, OFF["ag"] + c * 128, 128, epi_g, wpool)
            self.ts(xc[:], Pb[:, 0:S], cw[:, 0, c:c + 1], cb[:, c:c + 1], ALU.mult, ALU.add, [Pbb, cwb, cbb], [xcb_])
            for j in range(1, 4):
                self.stt(xc[:], Pb[:, j:j + S], cw[:, j, c:c + 1], xc[:], ALU.mult, ALU.add, [Pbb, cwb, xcb_], [xcb_])
            self.cp("scalar", x16[:], xc[:], [xcb_], [x16b])
            hs = Pb

            def dgen(d, c=c):
                D1, D1b, D2, D2b, D3, D3b = Bsets[d]
                order = range(nch) if d == 0 else range(nch - 1, -1, -1)
                first = True
                for ch in order:
                    t0 = ch * TC
                    for sub in range(TC // 512):
                        cs = slice(t0 + sub * 512, t0 + (sub + 1) * 512)
                        bs = slice(sub * 512, (sub + 1) * 512)
                        pt, pb = self.psn()
                        self.mm(pt[:], wbf[:, 0 + d, :], x16[:, cs], True, True, [wbfb, x16b], [pb])
                        self.act(D1[:, bs], pt[:], AF.Sigmoid, [pb, bab], [D1b], bias=ba[:, d, c:c + 1])
                        yield
                        pt, pb = self.psn()
                        self.mm(pt[:], wbf[:, 2 + d, :], x16[:, cs], True, True, [wbfb, x16b], [pb])
                        self.act(D3[:, bs], pt[:], AF.Sigmoid, [pb, bxb], [D3b], bias=bx[:, d, c:c + 1])
                        yield
                    k = d * 8 + c
                    self.act(D2[:], D1[:], AF.Exp, [D1b, cd2b], [D2b], scale=cd2[:, k:k + 1])
                    yield
                    self.act(D2[:], D2[:], AF.Relu, [D2b], [D2b], scale=-1.0, bias=self.onecol[:, :])
                    yield
                    self.act(D2[:], D2[:], AF.Sqrt, [D2b], [D2b])
                    yield
                    self.act(D1[:], D1[:], AF.Exp, [D1b, cdb], [D1b], scale=cd[:, k:k + 1])
                    yield
                    self.tt(D3[:], D3[:], xc[:, t0:t0 + TC], ALU.mult, [D3b, xcb_], [D3b])
                    yield
                    self.tt(D3[:], D3[:], D2[:], ALU.mult, [D3b, D2b], [D3b])
                    yield
                    if d == 0:
                        init = 0.0 if first else hs[:, 2 + t0 - 1:2 + t0]
                        self.op("vector", lambda e, t0=t0, init=init: e.tensor_tensor_scan(
                            out=hs[:, 2 + t0:2 + t0 + TC], data0=D1[:], data1=D3[:], initial=init,
                            op0=ALU.mult, op1=ALU.add), [D1b, D3b, Pbb], [Pbb])
                    else:
                        init = 0.0 if first else hb[:, t0 + TC:t0 + TC + 1]
                        ov = hb[:, t0:t0 + TC]
                        orev = bass.AP(ov.tensor, ov.offset + (TC - 1), [list(ov.ap[0]), [-1, TC]])
                        self.op("vector", lambda e, init=init, orev=orev: e.tensor_tensor_scan(
                            out=orev, data0=D1[:, ::-1], data1=D3[:, ::-1], initial=init,
                            op0=ALU.mult, op1=ALU.add), [D1b, D3b, hbb], [hbb])
                    yield
                    first = False

            gens = [dgen(0), dgen(1)]
            while gens:
                for g_ in list(gens):
                    try:
                        next(g_)
                    except StopIteration:
                        gens.remove(g_)
            self.tt(hs[:, 2:S + 2], hs[:, 2:S + 2], hb[:], ALU.add, [Pbb, hbb], [Pbb])
            self.tt(x16[:], hs[:, 2:S + 2], gg[:], ALU.mult, [Pbb, ggb, x16b], [x16b])
            self.dma("gpsimd", self.YT[c * 128:(c + 1) * 128, :], x16[:], [x16b], [self.dbuf("YT", 0, c)])


    def prep_qkv(self, PQ, l, s, xnT, xnTb, Wd, wpool, o16):
        S, NT, NG = self.S, self.NT, self.NG
        def bct(name, src, n):
            t, b = self.sb(PQ, name, [128, n], F32)
            self.load_bc(t[:], b, src)
            return t, b
        qg_b, qg_bb = bct("dqg", self.diff_q_g[l:l + 1, :], 64)
        kg_b, kg_bb = bct("dkg", self.diff_k_g[l:l + 1, :], 64)
        qa_g, qa_gb = bct("mqa", self.mla_qa_g[l:l + 1, :], 384)
        kva_g, kva_gb = bct("mkva", self.mla_kva_g[l:l + 1, :], 256)
        mq_g, mq_gb = bct("mqg", self.mla_q_g[l:l + 1, :], 192)
        mk_g, mk_gb = bct("mkg", self.mla_k_g[l:l + 1, :], 192)
        gb, gbb = self.sb(PQ, "gateb", [128, 4, 8], F32)
        for b_ in range(4):
            self.load_cols(gb[:, b_, :], gbb, self.gate_b[l, b_], "(m p) -> p m", p=128)
        ropes = []
        for ci, r2 in enumerate((8, 32, 32)):
            t, b = self.sb(PQ, "rope%d" % ci, [128, NT, 2, r2], F32)
            self.dma("sync", t[:], self.ropeD[ci].ap().rearrange("(t p) c j -> p t c j", p=128),
                     [self.dbuf("rope", ci)], [b], slow=True)
            ropes.append((t, b))
        cqnT, cqnTb = self.sb(PQ, "cqnT", [128, 3, S], BF16)
        ckvnT, ckvnTb = self.sb(PQ, "ckvnT", [128, 2, S], BF16)
        kper, kperb = self.sb(PQ, "kper", [128, NT, 64], F32)
        sspe, sspeb = self.sb(PQ, "sspe", [128, NT], F32)
        NV = KDEFER + 4
        vpool = self.sbpool(PQ, "v", [128, 512], F32, NV)
        v16pool = self.sbpool(PQ, "v16", [128, 512], BF16, NV)
        dq = []

        def defer(fn):
            dq.append(fn)
            while len(dq) > KDEFER:
                dq.pop(0)()

        def flush():
            while dq:
                dq.pop(0)()
        tmps = []
        for i_ in range(3):
            sq_, sqb_ = self.sb(PQ, "sq%d" % i_, [128, 512], F32)
            ss_, ssb_ = self.sb(PQ, "ss%d" % i_, [128, 8], F32)
            rt_, rtb_ = self.sb(PQ, "rt%d" % i_, [128, 1024], F32)
            tmps.append((sq_, sqb_, ss_, ssb_, rt_, rtb_))
        tmp = tmps[0]
        sq, sqb, ss, ssb, rt, rtb = tmp
        stages = self.sbpool(PQ, "stg", [128, 4, 512], BF16, 2)
        st_i = [0]
        vi = [0]

        def nextv():
            r = vpool[vi[0] % NV] + v16pool[vi[0] % NV]
            vi[0] += 1
            return r

        def store_stage(stage, stageb, dests, g):
            for bi, (dt_, r0, nr) in enumerate(dests):
                self.dma("gpsimd", dt_[r0:r0 + nr, g * 512:(g + 1) * 512], stage[0:nr, bi, :], [stageb],
                         [self.dbuf(dt_.name, r0, g)])

        def seg_qk(AT, ATb, KC, W, col0, ncols, G, Dg, gain, gainb, normdim, rope, blocks, dests_fn, scale=None, gi=2):
            state = {}

            def epi(pt, pb, t):
                v, vb, v16, v16b = nextv()
                self.cp("scalar", v[:, 0:ncols], pt[:, 0:ncols], [pb], [vb])
                yield
                for _ in self.norm_rope_g(v[:, 0:ncols], vb, G, Dg, t, tmps[t % gi], gain=gain, gainb=gainb,
                                          normdim=normdim, rope=rope):
                    yield
                if scale is None:
                    self.cp("scalar", v16[:, 0:ncols], v[:, 0:ncols], [vb], [v16b])
                else:
                    self.act(v16[:, 0:ncols], v[:, 0:ncols], AF.Copy, [vb], [v16b], scale=scale)
                def later(t=t, v16=v16, v16b=v16b):
                    if t % 4 == 0:
                        state["st"] = stages[st_i[0] % 2]
                        st_i[0] += 1
                    stage, stageb = state["st"]
                    self.tr_stage(v16, v16b, blocks, t % 4, stage, stageb)
                    if t % 4 == 3:
                        store_stage(stage, stageb, dests_fn(), t // 4)
                defer(later)
            self.gemm("tm", AT, ATb, KC, S, W, col0, ncols, epi, wpool, gi=gi)

        def seg_v(col0, dst, dcol0):
            def epi(pt, pb, t):
                o, ob = o16[self.oi % 3]
                self.oi += 1
                self.cp(self.alt(), o[:], pt[:], [pb], [ob])
                self.dma("gpsimd", dst[t * 128:(t + 1) * 128, dcol0:dcol0 + 512], o[:], [ob],
                         [self.dbuf(dst.name, t, dcol0)])
            self.gemm("tm", xnT, xnTb, 8, S, Wd, col0, 512, epi, wpool)

        b4 = [(i * 128, 128) for i in range(4)]
        for nt in range(2):
            seg_qk(xnT, xnTb, 8, Wd, OFF["bq"] + nt * 512, 512, 8, 64, qg_b[:], qg_bb, 64,
                   (0, 8, ropes[0][0], ropes[0][1]), b4,
                   lambda nt=nt: [(self.QTb, nt * 512 + i * 128, 128) for i in range(4)])
            seg_qk(xnT, xnTb, 8, Wd, OFF["bk"] + nt * 512, 512, 8, 64, kg_b[:], kg_bb, 64,
                   (0, 8, ropes[0][0], ropes[0][1]), b4,
                   lambda nt=nt: [(self.KTb, nt * 512 + i * 128, 128) for i in range(4)])
            seg_v(OFF["bv"] + nt * 512, self.Vb, nt * 512)
            seg_v(OFF["dv"] + nt * 512, self.Vd, nt * 512)
        self.mark("  q:dqdk")
        seg_qk(xnT, xnTb, 8, Wd, OFF["dq"], 512, 8, 64, None, None, None, (0, 32, ropes[2][0], ropes[2][1]), b4,
               lambda: [(self.QTd, i * 128, 128) for i in range(4)])
        seg_qk(xnT, xnTb, 8, Wd, OFF["dk"], 512, 8, 64, None, None, None, (0, 32, ropes[2][0], ropes[2][1]), b4,
               lambda: [(self.KTd, i * 128, 128) for i in range(4)], scale=0.125)

        self.mark("  q:cq")
        def epi_cq(pt, pb, t):
            v, vb, v16, v16b = nextv()
            self.cp("scalar", v[:, 0:384], pt[:, 0:384], [pb], [vb])
            yield
            for _ in self.norm_rope_g(v[:, 0:384], vb, 1, 384, t, tmps[t % 2], gain=qa_g[:], gainb=qa_gb, normdim=384):
                yield
            self.cp("scalar", v16[:, 0:384], v[:, 0:384], [vb], [v16b])

            def later(t=t, v16=v16, v16b=v16b):
                i = self.psi % 8
                self.psi += 1
                pv = self.psb(i)
                for bi in range(3):
                    self.tp(pv[:, bi * 128:(bi + 1) * 128], v16[:, bi * 128:(bi + 1) * 128], self.identb[:],
                            [v16b, self.identb_b], [self.ps[i][1]])
                self.cp(self.alt(), cqnT[:, :, t * 128:(t + 1) * 128], pv[:, 0:384].rearrange("p (b t) -> p b t", b=3),
                        [self.ps[i][1]], [cqnTb])
            defer(later)
        self.gemm("tm", xnT, xnTb, 8, S, Wd, OFF["cq"], 384, epi_cq, wpool)

        def epi_ckv(pt, pb, t):
            v, vb, v16, v16b = nextv()
            sq, sqb, ss, ssb, rt, rtb = tmps[t % 2]
            self.cp("scalar", v[:, 0:320], pt[:, 0:320], [pb], [vb])
            yield
            self.tt(sq[:, 0:64], v[:, 256:320], v[:, 256:320], ALU.mult, [vb], [sqb])
            yield
            self.op("vector", lambda e: e.tensor_reduce(out=sspe[:, t:t + 1], in_=sq[:, 0:64], axis=AX.X, op=ALU.add),
                    [sqb], [sspeb])
            yield
            self.tt(v[:, 256:320], v[:, 256:320], mk_g[:, 128:192], ALU.mult, [vb, mk_gb], [vb])
            yield
            for _ in self.norm_rope_g(v[:, 256:320], vb, 1, 64, t, tmps[t % 2], rope=(0, 32, ropes[1][0], ropes[1][1])):
                yield
            self.cp("vector", kper[:, t, :], v[:, 256:320], [vb], [kperb])
            yield
            for _ in self.norm_rope_g(v[:, 0:256], vb, 1, 256, t, tmps[t % 2], gain=kva_g[:], gainb=kva_gb, normdim=256):
                yield
            self.cp("scalar", v16[:, 0:256], v[:, 0:256], [vb], [v16b])

            def later(t=t, v16=v16, v16b=v16b):
                i = self.psi % 8
                self.psi += 1
                pv = self.psb(i)
                for bi in range(2):
                    self.tp(pv[:, bi * 128:(bi + 1) * 128], v16[:, bi * 128:(bi + 1) * 128], self.identb[:],
                            [v16b, self.identb_b], [self.ps[i][1]])
                self.cp(self.alt(), ckvnT[:, :, t * 128:(t + 1) * 128], pv[:, 0:256].rearrange("p (b t) -> p b t", b=2),
                        [self.ps[i][1]], [ckvnTb])
            defer(later)
        self.gemm("tm", xnT, xnTb, 8, S, Wd, OFF["ckv"], 320, epi_ckv, wpool)

        self.mark("  q:fm")
        def seg_fm(col0, func, bias_fn, dst, row0):
            def epi(pt, pb, m, g):
                o, ob = o16[self.oi % 3]
                self.oi += 1
                b = bias_fn(m)
                if b is None:
                    self.act(o[:], pt[:], func, [pb], [ob])
                else:
                    self.act(o[:], pt[:], func, [pb, gbb], [ob], bias=b)
                r0 = row0 + m * 128
                self.dma("gpsimd", dst[r0:r0 + 128, g * 512:(g + 1) * 512], o[:], [ob], [self.dbuf(dst.name, r0, g)])
            self.gemm("fm", xnT, xnTb, 8, S, Wd, col0, 512, epi, wpool)
        for nt in range(2):
            seg_fm(OFF["dg"] + nt * 512, AF.Silu, lambda m: None, self.GdT, nt * 512)
        for b_ in range(4):
            for nt in range(2):
                seg_fm(OFF["gl"] + b_ * 1024 + nt * 512, AF.Sigmoid,
                       lambda m, b_=b_, nt=nt: gb[:, b_, nt * 4 + m:nt * 4 + m + 1], self.GT, b_ * 1024 + nt * 512)

        self.mark("  q:qup")
        flush()
        Wq = self.wb["wuq", l]
        Wkv = self.wb["wukv", l]
        bq = [(0, 128), (128, 64), (192, 128), (320, 64)]
        for hp in range(4):
            seg_qk(cqnT, cqnTb, 3, Wq, hp * 384, 384, 2, 192, mq_g[:], mq_gb, 192,
                   (128, 32, ropes[1][0], ropes[1][1]), bq,
                   lambda hp=hp: [(self.QTc, (2 * hp) * 192, 128), (self.QTc, (2 * hp) * 192 + 128, 64),
                                  (self.QTc, (2 * hp + 1) * 192, 128), (self.QTc, (2 * hp + 1) * 192 + 128, 64)], gi=3)
        self.mark("  q:kvup")
        for hp in range(4):
            state = {}

            def epi_kv(pt, pb, t, hp=hp, state=state):
                v, vb, v16, v16b = nextv()
                sq, sqb, ss, ssb, rt, rtb = tmps[t % 3]
                self.cp("scalar", v[:], pt[:], [pb], [vb])
                yield
                v4 = v[:].rearrange("p (h c d) -> p h c d", h=2, c=2)
                kn = v4[:, :, 0, :]
                s4 = sq[:].rearrange("p (h c d) -> p h c d", h=2, c=2)
                self.tt(s4[:, :, 0, :], kn, kn, ALU.mult, [vb], [sqb])
                yield
                self.op("vector", lambda e: e.tensor_reduce(out=ss[:, 0:2], in_=s4[:, :, 0, :], axis=AX.X, op=ALU.add),
                        [sqb], [ssb])
                yield
                self.tt(ss[:, 0:2], ss[:, 0:2], self.bc_col(sspe[:, t:t + 1], 2),
                        ALU.add, [ssb, sspeb], [ssb])
                yield
                self.rsqrt(ss[:, 0:2], ss[:, 0:2], 1.0 / 192, [ssb], [ssb])
                yield
                self.tt(kn, kn, self.bc_last(ss[:, 0:2], 128), ALU.mult, [vb, ssb], [vb])
                yield
                self.tt(kn, kn, self.bc_mid(mk_g[:, 0:128], 2), ALU.mult, [vb, mk_gb], [vb])
                yield
                v16v = v16[:, 0:256].rearrange("p (h d) -> p h d", h=2)
                self.cp("scalar", v16v, kn, [vb], [v16b])
                yield
                pe = v16[:, 256:384].rearrange("p (h d) -> p h d", h=2)
                self.tt(pe, self.bc_mid(kper[:, t, :], 2), self.bc_last(ss[:, 0:2], 64), ALU.mult,
                        [kperb, ssb], [v16b])
                yield
                o, ob = o16[self.oi % 3]
                self.oi += 1
                ov = o[:, 0:256].rearrange("p (h d) -> p h d", h=2)
                self.cp("vector", ov, v4[:, :, 1, :], [vb], [ob])
                self.dma("gpsimd", self.Vc[t * 128:(t + 1) * 128, hp * 256:(hp + 1) * 256], o[:, 0:256], [ob],
                         [self.dbuf("Vc", t, hp)])
                def later(t=t, v16=v16, v16b=v16b):
                    if t % 4 == 0:
                        state["st"] = stages[st_i[0] % 2]
                        st_i[0] += 1
                    stage, stageb = state["st"]
                    self.tr_stage(v16, v16b, [(0, 128), (128, 128), (256, 64), (320, 64)], t % 4, stage, stageb)
                    if t % 4 == 3:
                        store_stage(stage, stageb, [(self.KTc, (2 * hp) * 192, 128), (self.KTc, (2 * hp + 1) * 192, 128),
                                                    (self.KTc, (2 * hp) * 192 + 128, 64),
                                                    (self.KTc, (2 * hp + 1) * 192 + 128, 64)], t // 4)
                defer(later)
            self.gemm("tm", ckvnT, ckvnTb, 2, S, Wkv, hp * 512, 512, epi_kv, wpool, gi=3)
        flush()


    def load_rows(self, dst, dstb, src, r0, nr):
        self.dma("sync", dst[0:nr, :], src[r0:r0 + nr, :], [], [dstb])

    def load_v(self, dst, dstb, src, c0):
        self.dma("sync", dst[:], src[:, c0:c0 + 128].rearrange("(t p) e -> p t e", p=128), [], [dstb])

    def phaseB(self, l, s):
        S, NT, NG = self.S, self.NT, self.NG
        lam_init = 0.8 - 0.6 * math.exp(-0.3 * l)
        with ExitStack() as P:
            lp, lpb = self.sb(P, "lp", [128, 256], F32)
            self.load_bc(lp[:], lpb, self.diff_lam[l:l + 1].rearrange("a b c -> a (b c)"))
            pr, prb = self.sb(P, "pr", [128, 128], F32)
            e2, e2b = self.sb(P, "e2", [128, 2], F32)
            neglam, neglamb = self.sb(P, "neglam", [128, 1], F32)
            gcol, gcolb = self.sb(P, "gcol", [128, 1], F32)
            lp4 = lp[:].rearrange("p (a b c) -> p a b c", a=2, b=2)
            self.tt(pr[:].rearrange("p (a c) -> p a c", a=2), lp4[:, :, 0, :], lp4[:, :, 1, :], ALU.mult, [lpb], [prb])
            self.op("vector", lambda e: e.tensor_reduce(out=e2[:], in_=pr[:].rearrange("p (a c) -> p a c", a=2),
                                                        axis=AX.X, op=ALU.add), [prb], [e2b])
            self.act(e2[:], e2[:], AF.Exp, [e2b], [e2b])
            self.tt(neglam[:], e2[:, 1:2], e2[:, 0:1], ALU.subtract, [e2b], [neglamb])
            self.ts(neglam[:], neglam[:], -lam_init, None, ALU.add, None, [neglamb], [neglamb])
            self.load_cols(gcol[:], gcolb, self.diff_sub_g[l], "(p o) -> p o", o=1)
            self.ts(gcol[:], gcol[:], 1.0 - lam_init, None, ALU.mult, None, [gcolb], [gcolb])
            QT = self.sbpool(P, "QT", [128, S], BF16, 2)
            KT = self.sbpool(P, "KT", [128, S], BF16, 2)
            V = self.sbpool(P, "V", [128, NT, 128], BF16, 2)
            Pt = self.sbpool(P, "Pt", [128, 512], BF16, 4)
            tf = self.sbpool(P, "tf", [128, 512], F32, 6)
            y16 = self.sbpool(P, "y16", [128, 512], BF16, 2)
            pi = 0
            for h in range(8):
                q, qb = QT[h % 2]
                k, kb = KT[h % 2]
                v, vb = V[h % 2]
                self.load_rows(q, qb, self.QTb, h * 128, 128)
                self.load_rows(k, kb, self.KTb, h * 128, 128)
                self.load_v(v, vb, self.Vb, h * 128)
                for qg in range(NG):
                    qs = slice(qg * 512, (qg + 1) * 512)
                    O = (self.ps[0], self.ps[1])
                    Lp = (self.ps[2], self.ps[3])
                    for kt in range(NT):
                        ks = slice(kt * 128, (kt + 1) * 128)
                        for si in range(2):
                            lo, hi = si * 64, si * 64 + 64
                            sp, spb = self.psn(4, 8)
                            self.mm(sp[:], k[lo:hi, ks], q[lo:hi, qs], True, True, [kb, qb], [spb])
                            p_, p_b = Pt[pi % 4]
                            pi += 1
                            self.act(p_[:], sp[:], AF.Exp, [spb], [p_b], scale=0.125)
                            self.mm(O[si][0][:], v[:, kt, :], p_[:], kt == 0, kt == NT - 1, [vb, p_b], [O[si][1]])
                            self.mm(Lp[si][0][:], self.onesb[:], p_[:], kt == 0, kt == NT - 1, [self.onesb_b, p_b],
                                    [Lp[si][1]])
                    o1, o1b = tf[0]
                    o2, o2b = tf[1]
                    r1, r1b = tf[2]
                    r2, r2b = tf[3]
                    oo, oob = tf[4]
                    rs, rsb = tf[5]
                    self.cp("scalar", o1[:], O[0][0][:], [O[0][1]], [o1b])
                    self.cp("vector", o2[:], O[1][0][:], [O[1][1]], [o2b])
                    self.act(r1[:], Lp[0][0][:], AF.Ln, [Lp[0][1]], [r1b])
                    self.act(r2[:], Lp[1][0][:], AF.Ln, [Lp[1][1]], [r2b])
                    self.act(r1[:], r1[:], AF.Exp, [r1b], [r1b], scale=-1.0)
                    self.act(r2[:], r2[:], AF.Exp, [r2b], [r2b], scale=-1.0)
                    self.tt(o1[:], o1[:], r1[:], ALU.mult, [o1b, r1b], [o1b])
                    self.tt(o2[:], o2[:], r2[:], ALU.mult, [o2b, r2b], [o2b])
                    self.stt(oo[:], o2[:], neglam[:, 0:1], o1[:], ALU.mult, ALU.add, [o2b, o1b, neglamb], [oob])
                    sq, sqb = Pt[pi % 4]
                    pi += 1
                    self.act(sq[:], oo[:], AF.Square, [oob], [sqb])
                    sp, spb = self.psn(4, 8)
                    self.mm(sp[:], self.onesb[:], sq[:], True, True, [self.onesb_b, sqb], [spb])
                    self.rsqrt(rs[:], sp[:], 1.0 / 128, [spb], [rsb])
                    y, yb = y16[(h * NG + qg) % 2]
                    self.stt(y[:], oo[:], gcol[:, 0:1], rs[:], ALU.mult, ALU.mult, [oob, rsb, gcolb], [yb])
                    self.dma("gpsimd", self.YT[1024 + h * 128:1024 + (h + 1) * 128, qs], y[:], [yb],
                             [self.dbuf("YT", 1, h, qg)])
            self.barrier()

    def phaseC(self, l, s):
        S, NT, NG = self.S, self.NT, self.NG
        sc = 192.0 ** -0.5
        with ExitStack() as P:
            Qn = self.sbpool(P, "Qn", [128, S], BF16, 2)
            Qp = self.sbpool(P, "Qp", [64, S], BF16, 2)
            Kn = self.sbpool(P, "Kn", [128, S], BF16, 2)
            Kp = self.sbpool(P, "Kp", [64, S], BF16, 2)
            V = self.sbpool(P, "V", [128, NT, 128], BF16, 2)
            Pt = self.sbpool(P, "Pt", [128, 512], BF16, 4)
            tf = self.sbpool(P, "tf", [128, 512], F32, 4)
            y16 = self.sbpool(P, "y16", [128, 512], BF16, 2)
            pi = 0
            ti = 0
            for h in range(8):
                qn, qnb = Qn[h % 2]
                qp, qpb = Qp[h % 2]
                kn, knb = Kn[h % 2]
                kp, kpb = Kp[h % 2]
                v, vb = V[h % 2]
                self.load_rows(qn, qnb, self.QTc, h * 192, 128)
                self.load_rows(qp, qpb, self.QTc, h * 192 + 128, 64)
                self.load_rows(kn, knb, self.KTc, h * 192, 128)
                self.load_rows(kp, kpb, self.KTc, h * 192 + 128, 64)
                self.load_v(v, vb, self.Vc, h * 128)
                for qg in range(NG):
                    qs = slice(qg * 512, (qg + 1) * 512)
                    O, Ob = self.ps[qg % 2 * 2]
                    Lt, Lb = self.ps[qg % 2 * 2 + 1]
                    for kt in range(NT):
                        ks = slice(kt * 128, (kt + 1) * 128)
                        sp, spb = self.psn(4, 8)
                        self.mm(sp[:], kn[:, ks], qn[:, qs], True, False, [knb, qnb], [spb])
                        self.mm(sp[:], kp[:, ks], qp[:, qs], False, True, [kpb, qpb], [spb])
                        p_, p_b = Pt[pi % 4]
                        pi += 1
                        self.act(p_[:], sp[:], AF.Exp, [spb], [p_b], scale=sc)
                        self.mm(O[:], v[:, kt, :], p_[:], kt == 0, kt == NT - 1, [vb, p_b], [Ob])
                        self.mm(Lt[:], self.onesb[:], p_[:], kt == 0, kt == NT - 1, [self.onesb_b, p_b], [Lb])
                    o1, o1b = tf[ti % 4]
                    r1, r1b = tf[(ti + 1) % 4]
                    ti += 2
                    self.cp("vector", o1[:], O[:], [Ob], [o1b])
                    self.act(r1[:], Lt[:], AF.Ln, [Lb], [r1b])
                    self.act(r1[:], r1[:], AF.Exp, [r1b], [r1b], scale=-1.0)
                    y, yb = y16[(h * NG + qg) % 2]
                    self.tt(y[:], o1[:], r1[:], ALU.mult, [o1b, r1b], [yb])
                    self.dma("gpsimd", self.YT[2048 + h * 128:2048 + (h + 1) * 128, qs], y[:], [yb],
                             [self.dbuf("YT", 2, h, qg)])
            self.barrier()

    def phaseD(self, l, s):
        S, NT, NG = self.S, self.NT, self.NG
        mmax = max(4 * (NG - 1), 1)
        nneg = max(NT - 4, 1)
        with ExitStack() as P:
            gncol, gncolb = self.sb(P, "gncol", [128, 1], F32)
            self.load_cols(gncol[:], gncolb, self.ret_gn_g[l], "(p o) -> p o", o=1)
            ii, iib = self.sb(P, "ii", [128, 512], I32)
            J, Jb = self.sb(P, "J", [128, 512], F32)
            Jr, Jrb = self.sb(P, "Jr", [128, 512], F32)
            A, Ab = self.sb(P, "A", [128, 4, 512], F32)
            pbase, pbaseb = self.sb(P, "pbase", [128, mmax], F32)
            nbase, nbaseb = self.sb(P, "nbase", [128, nneg], F32)
            self.op("gpsimd", lambda e: e.iota(ii[:], pattern=[[1, 512]], base=0, channel_multiplier=0), [], [iib])
            self.cp("vector", J[:], ii[:], [iib], [Jb])
            self.ts(Jr[:], J[:], -1.0, 511.0, ALU.mult, ALU.add, [Jb], [Jrb])
            for m in range(4):
                self.op("gpsimd", lambda e, m=m: e.iota(ii[:], pattern=[[1, 512]], base=-128 * m, channel_multiplier=-1),
                        [iib], [iib])
                self.cp("vector", A[:, m, :], ii[:], [iib], [Ab])
            self.act(A[:], A[:], AF.Abs, [Ab], [Ab])
            self.op("gpsimd", lambda e: e.iota(ii[:, 0:mmax], pattern=[[128, mmax]], base=128, channel_multiplier=-1),
                    [iib], [iib])
            self.cp("vector", pbase[:], ii[:, 0:mmax], [iib], [pbaseb])
            self.op("gpsimd", lambda e: e.iota(ii[:, 0:nneg], pattern=[[128, nneg]], base=1, channel_multiplier=1),
                    [iib], [iib])
            self.cp("vector", nbase[:], ii[:, 0:nneg], [iib], [nbaseb])
            rowp, rowpb = self.sb(P, "rowp", [128, 512], F32)
            rown, rownb = self.sb(P, "rown", [128, 512], F32)
            Dt, Dtb = self.sb(P, "Dt", [128, 4, 512], F32)
            cfp, cfpb = self.sb(P, "cfp", [128, mmax], F32)
            cfn, cfnb = self.sb(P, "cfn", [128, nneg], F32)
            QT = self.sbpool(P, "QT", [64, S], BF16, 2)
            KT = self.sbpool(P, "KT", [64, S], BF16, 2)
            V = self.sbpool(P, "V", [128, NT, 128], BF16, 2)
            Pt = self.sbpool(P, "Pt", [128, 512], BF16, 4)
            tf = self.sbpool(P, "tf", [128, 512], F32, 6)
            sgp = self.sbpool(P, "sg", [128, 512], BF16, 2)
            y16 = self.sbpool(P, "y16", [128, 512], BF16, 2)
            pi = 0
            for h in range(8):
                lg = math.log1p(-2.0 ** (-5.0 - h))
                self.act(rowp[:], J[:], AF.Exp, [Jb], [rowpb], scale=lg)
                self.act(rown[:], Jr[:], AF.Exp, [Jrb], [rownb], scale=lg)
                self.act(Dt[:], A[:], AF.Exp, [Ab], [Dtb], scale=lg)
                self.act(cfp[:], pbase[:], AF.Exp, [pbaseb], [cfpb], scale=lg)
                self.act(cfn[:], nbase[:], AF.Exp, [nbaseb], [cfnb], scale=lg)
                q, qb = QT[h % 2]
                k, kb = KT[h % 2]
                v, vb = V[h % 2]
                self.load_rows(q, qb, self.QTd, h * 64, 64)
                self.load_rows(k, kb, self.KTd, h * 64, 64)
                self.load_v(v, vb, self.Vd, h * 128)
                for qg in range(NG):
                    qs = slice(qg * 512, (qg + 1) * 512)
                    O, Ob = self.ps[qg % 2]
                    sg, sgb = sgp[qg % 2]
                    self.dma("sync", sg[:], self.GdT[h * 128:(h + 1) * 128, qs], [], [sgb])
                    for kt in range(NT):
                        ks = slice(kt * 128, (kt + 1) * 128)
                        sp, spb = self.psn(4, 8)
                        self.mm(sp[:], k[:, ks], q[:, qs], True, True, [kb, qb], [spb])
                        p_, p_b = Pt[pi % 4]
                        pi += 1
                        m = 4 * qg - kt
                        if m >= 1:
                            self.stt(p_[:], sp[:], cfp[:, m - 1:m], rowp[:], ALU.mult, ALU.mult, [spb, cfpb, rowpb], [p_b])
                        elif m <= -4:
                            self.stt(p_[:], sp[:], cfn[:, -m - 4:-m - 3], rown[:], ALU.mult, ALU.mult,
                                     [spb, cfnb, rownb], [p_b])
                        else:
                            self.tt(p_[:], sp[:], Dt[:, -m, :], ALU.mult, [spb, Dtb], [p_b])
                        self.mm(O[:], v[:, kt, :], p_[:], kt == 0, kt == NT - 1, [vb, p_b], [Ob])
                    o1, o1b = tf[0]
                    s1, s1b = tf[1]
                    mn, mnb = tf[2]
                    vr, vrb = tf[3]
                    self.cp("scalar", o1[:], O[:], [Ob], [o1b])
                    self.act(s1[:], O[:], AF.Square, [Ob], [s1b])
                    mp, mpb = self.ps[2]
                    spp, sppb = self.ps[3]
                    self.mm(mp[:], self.onesf[:], o1[:], True, True, [self.onesf_b, o1b], [mpb])
                    self.mm(spp[:], self.onesf[:], s1[:], True, True, [self.onesf_b, s1b], [sppb])
                    self.act(mn[:], mp[:], AF.Copy, [mpb], [mnb], scale=1.0 / 128)
                    self.tt(vr[:], mn[:], mn[:], ALU.mult, [mnb], [vrb])
                    self.stt(vr[:], spp[:], 1.0 / 128, vr[:], ALU.mult, ALU.subtract, [sppb, vrb], [vrb])
                    self.rsqrt(vr[:], vr[:], 1.0, [vrb], [vrb])
                    self.tt(o1[:], o1[:], mn[:], ALU.subtract, [o1b, mnb], [o1b])
                    self.stt(o1[:], o1[:], gncol[:, 0:1], vr[:], ALU.mult, ALU.mult, [o1b, vrb, gncolb], [o1b])
                    y, yb = y16[(h * NG + qg) % 2]
                    self.tt(y[:], o1[:], sg[:], ALU.mult, [o1b, sgb], [yb])
                    self.dma("gpsimd", self.YT[3072 + h * 128:3072 + (h + 1) * 128, qs], y[:], [yb],
                             [self.dbuf("YT", 3, h, qg)])
            self.barrier()


    def pipeline(self, steps, depth, stage1, stage2):
        n = len(steps)
        self.hooks = {}
        self.pit = 0
        for i in range(n + depth):
            self.pit = i
            if i < n:
                stage1(steps[i], i)
            if i >= depth:
                stage2(steps[i - depth], i - depth)
            for fn in self.hooks.pop(i, []):
                fn()
        for k in sorted(self.hooks):
            for fn in self.hooks[k]:
                fn()
        self.hooks = {}

    def later(self, delay, fn):
        self.hooks.setdefault(self.pit + delay, []).append(fn)

    def lsum_mm(self, accs, ps_t, ps_b):
        for j, (a, ab) in enumerate(accs):
            self.mm(ps_t[:], self.onesf[:], a[:], j == 0, j == len(accs) - 1, [self.onesf_b, ab], [ps_b])

    def phaseB2(self, l, s):
        S, NT, NG = self.S, self.NT, self.NG
        lam_init = 0.8 - 0.6 * math.exp(-0.3 * l)
        with ExitStack() as P:
            lp, lpb = self.sb(P, "lp", [128, 256], F32)
            self.load_bc(lp[:], lpb, self.diff_lam[l:l + 1].rearrange("a b c -> a (b c)"))
            pr, prb = self.sb(P, "pr", [128, 128], F32)
            e2, e2b = self.sb(P, "e2", [128, 2], F32)
            neglam, neglamb = self.sb(P, "neglam", [128, 1], F32)
            gcol, gcolb = self.sb(P, "gcol", [128, 1], F32)
            lp4 = lp[:].rearrange("p (a b c) -> p a b c", a=2, b=2)
            self.tt(pr[:].rearrange("p (a c) -> p a c", a=2), lp4[:, :, 0, :], lp4[:, :, 1, :], ALU.mult, [lpb], [prb])
            self.op("vector", lambda e: e.tensor_reduce(out=e2[:], in_=pr[:].rearrange("p (a c) -> p a c", a=2),
                                                        axis=AX.X, op=ALU.add), [prb], [e2b])
            self.act(e2[:], e2[:], AF.Exp, [e2b], [e2b])
            self.tt(neglam[:], e2[:, 1:2], e2[:, 0:1], ALU.subtract, [e2b], [neglamb])
            self.ts(neglam[:], neglam[:], -lam_init, None, ALU.add, None, [neglamb], [neglamb])
            self.load_cols(gcol[:], gcolb, self.diff_sub_g[l], "(p o) -> p o", o=1)
            self.ts(gcol[:], gcol[:], 1.0 - lam_init, None, ALU.mult, None, [gcolb], [gcolb])
            QT = self.sbpool(P, "QT", [128, S], BF16, 2)
            KT = self.sbpool(P, "KT", [128, S], BF16, 2)
            V = self.sbpool(P, "V", [128, NT, 128], BF16, 2)
            NP = 6
            Pt = self.sbpool(P, "Pt", [128, 512], BF16, NP)
            tf = self.sbpool(P, "tf", [128, 512], F32, 6)
            sq16 = self.sbpool(P, "sq16", [128, 512], BF16, 2)
            y16 = self.sbpool(P, "y16", [128, 512], BF16, 2)
            Lacc = [[self.sb(P, "Lacc%d%d" % (a, b), [128, 512], F32) for b in range(2)] for a in range(2)]
            leng = (LENG0, LENG1)

            def loads(h):
                self.load_rows(QT[h % 2][0], QT[h % 2][1], self.QTb, h * 128, 128)
                self.load_rows(KT[h % 2][0], KT[h % 2][1], self.KTb, h * 128, 128)
                self.load_v(V[h % 2][0], V[h % 2][1], self.Vb, h * 128)

            steps = [(h, qg, kt, si) for h in range(8) for qg in range(NG) for kt in range(NT) for si in range(2)]
            sbank = {}
            loads(0)

            def stage1(st, i):
                h, qg, kt, si = st
                q, qb = QT[h % 2]
                k, kb = KT[h % 2]
                par = (h * NG + qg) % 2
                lo, hi = si * 64, si * 64 + 64
                sp, spb = self.psn(4, 8)
                self.mm(sp[:], k[lo:hi, kt * 128:(kt + 1) * 128], q[lo:hi, qg * 512:(qg + 1) * 512], True, True,
                        [kb, qb], [spb])
                p_, p_b = Pt[i % NP]
                self.act(p_[:], sp[:], AF.Exp, [spb], [p_b], scale=0.125)
                a, ab = Lacc[par][si]
                if kt == 0:
                    self.cp(leng[si], a[:], p_[:], [p_b], [ab])
                else:
                    self.tt(a[:], a[:], p_[:], ALU.add, [ab, p_b], [ab], en=leng[si])

            def stage2(st, i):
                h, qg, kt, si = st
                if qg == 0 and kt == 0 and si == 0 and h + 1 < 8:
                    loads(h + 1)
                v, vb = V[h % 2]
                par = (h * NG + qg) % 2
                O, Ob = self.ps[par * 2 + si]
                p_, p_b = Pt[i % NP]
                self.mm(O[:], v[:, kt, :], p_[:], kt == 0, kt == NT - 1, [vb, p_b], [Ob])
                if kt == NT - 1 and si == 1:
                    epilogue(h, qg, par)

            def epilogue(h, qg, par):
                qs = slice(qg * 512, (qg + 1) * 512)
                O0, O0b = self.ps[par * 2]
                O1, O1b = self.ps[par * 2 + 1]
                o1, o1b = tf[0]
                o2, o2b = tf[1]
                r1, r1b = tf[2]
                r2, r2b = tf[3]
                oo, oob = tf[4]
                rs, rsb = tf[5]
                self.cp("scalar", o1[:], O0[:], [O0b], [o1b])
                self.cp("vector", o2[:], O1[:], [O1b], [o2b])
                for si, (r, rb) in enumerate(((r1, r1b), (r2, r2b))):
                    lp_, lpb_ = self.psn(4, 8)
                    self.lsum_mm([Lacc[par][si]], lp_, lpb_)
                    self.act(r[:], lp_[:], AF.Ln, [lpb_], [rb])
                    self.act(r[:], r[:], AF.Exp, [rb], [rb], scale=-1.0)
                self.tt(o1[:], o1[:], r1[:], ALU.mult, [o1b, r1b], [o1b])
                self.tt(o2[:], o2[:], r2[:], ALU.mult, [o2b, r2b], [o2b])
                self.stt(oo[:], o2[:], neglam[:, 0:1], o1[:], ALU.mult, ALU.add, [o2b, o1b, neglamb], [oob])
                sq, sqb = sq16[par]
                self.act(sq[:], oo[:], AF.Square, [oob], [sqb])
                sp, spb = self.psn(4, 8)
                self.mm(sp[:], self.onesb[:], sq[:], True, True, [self.onesb_b, sqb], [spb])
                self.rsqrt(rs[:], sp[:], 1.0 / 128, [spb], [rsb])
                y, yb = y16[par]
                self.stt(y[:], oo[:], gcol[:, 0:1], rs[:], ALU.mult, ALU.mult, [oob, rsb, gcolb], [yb])
                self.dma("gpsimd", self.YT[1024 + h * 128:1024 + (h + 1) * 128, qs], y[:], [yb], [])

            self.pipeline(steps, KDEPTH, stage1, stage2)
            self.barrier()

    def phaseC2(self, l, s):
        S, NT, NG = self.S, self.NT, self.NG
        sc = 192.0 ** -0.5
        with ExitStack() as P:
            Qn = self.sbpool(P, "Qn", [128, S], BF16, 2)
            Qp = self.sbpool(P, "Qp", [64, S], BF16, 2)
            Kn = self.sbpool(P, "Kn", [128, S], BF16, 2)
            Kp = self.sbpool(P, "Kp", [64, S], BF16, 2)
            V = self.sbpool(P, "V", [128, NT, 128], BF16, 2)
            NP = 6
            Pt = self.sbpool(P, "Pt", [128, 512], BF16, NP)
            tf = self.sbpool(P, "tf", [128, 512], F32, 4)
            y16 = self.sbpool(P, "y16", [128, 512], BF16, 2)
            Lacc = [[self.sb(P, "Lacc%d%d" % (a, b), [128, 512], F32) for b in range(2)] for a in range(2)]
            leng = (LENG0, LENG1)

            def loads(h):
                self.load_rows(Qn[h % 2][0], Qn[h % 2][1], self.QTc, h * 192, 128)
                self.load_rows(Qp[h % 2][0], Qp[h % 2][1], self.QTc, h * 192 + 128, 64)
                self.load_rows(Kn[h % 2][0], Kn[h % 2][1], self.KTc, h * 192, 128)
                self.load_rows(Kp[h % 2][0], Kp[h % 2][1], self.KTc, h * 192 + 128, 64)
                self.load_v(V[h % 2][0], V[h % 2][1], self.Vc, h * 128)

            steps = [(h, qg, kt) for h in range(8) for qg in range(NG) for kt in range(NT)]
            loads(0)

            def stage1(st, i):
                h, qg, kt = st
                qn, qnb = Qn[h % 2]
                qp, qpb = Qp[h % 2]
                kn, knb = Kn[h % 2]
                kp, kpb = Kp[h % 2]
                par = (h * NG + qg) % 2
                qs = slice(qg * 512, (qg + 1) * 512)
                ks = slice(kt * 128, (kt + 1) * 128)
                sp, spb = self.psn(2, 8)
                self.mm(sp[:], kn[:, ks], qn[:, qs], True, False, [knb, qnb], [spb])
                self.mm(sp[:], kp[:, ks], qp[:, qs], False, True, [kpb, qpb], [spb])
                p_, p_b = Pt[i % NP]
                self.act(p_[:], sp[:], AF.Exp, [spb], [p_b], scale=sc)
                a, ab = Lacc[par][kt % 2]
                if kt < 2:
                    self.cp(leng[kt % 2], a[:], p_[:], [p_b], [ab])
                else:
                    self.tt(a[:], a[:], p_[:], ALU.add, [ab, p_b], [ab], en=leng[kt % 2])

            def stage2(st, i):
                h, qg, kt = st
                if qg == 0 and kt == 0 and h + 1 < 8:
                    loads(h + 1)
                v, vb = V[h % 2]
                par = (h * NG + qg) % 2
                O, Ob = self.ps[par]
                p_, p_b = Pt[i % NP]
                self.mm(O[:], v[:, kt, :], p_[:], kt == 0, kt == NT - 1, [vb, p_b], [Ob])
                if kt == NT - 1:
                    qs = slice(qg * 512, (qg + 1) * 512)
                    o1, o1b = tf[par * 2]
                    r1, r1b = tf[par * 2 + 1]
                    self.cp("vector", o1[:], O[:], [Ob], [o1b])
                    lp_, lpb_ = self.psn(2, 8)
                    self.lsum_mm(Lacc[par], lp_, lpb_)
                    self.act(r1[:], lp_[:], AF.Ln, [lpb_], [r1b])
                    self.act(r1[:], r1[:], AF.Exp, [r1b], [r1b], scale=-1.0)
                    y, yb = y16[par]
                    self.tt(y[:], o1[:], r1[:], ALU.mult, [o1b, r1b], [yb])
                    self.dma("gpsimd", self.YT[2048 + h * 128:2048 + (h + 1) * 128, qs], y[:], [yb], [])

            self.pipeline(steps, KDEPTH, stage1, stage2)
            self.barrier()

    def phaseD2(self, l, s):
        S, NT, NG = self.S, self.NT, self.NG
        mmax = max(4 * (NG - 1), 1)
        nneg = max(NT - 4, 1)
        with ExitStack() as P:
            gncol, gncolb = self.sb(P, "gncol", [128, 1], F32)
            self.load_cols(gncol[:], gncolb, self.ret_gn_g[l], "(p o) -> p o", o=1)
            ii, iib = self.sb(P, "ii", [128, 512], I32)
            J, Jb = self.sb(P, "J", [128, 512], F32)
            Jr, Jrb = self.sb(P, "Jr", [128, 512], F32)
            A, Ab = self.sb(P, "A", [128, 4, 512], F32)
            pbase, pbaseb = self.sb(P, "pbase", [128, mmax], F32)
            nbase, nbaseb = self.sb(P, "nbase", [128, nneg], F32)
            self.op("gpsimd", lambda e: e.iota(ii[:], pattern=[[1, 512]], base=0, channel_multiplier=0), [], [iib])
            self.cp("vector", J[:], ii[:], [iib], [Jb])
            self.ts(Jr[:], J[:], -1.0, 511.0, ALU.mult, ALU.add, [Jb], [Jrb])
            for m in range(4):
                self.op("gpsimd", lambda e, m=m: e.iota(ii[:], pattern=[[1, 512]], base=-128 * m, channel_multiplier=-1),
                        [iib], [iib])
                self.cp("vector", A[:, m, :], ii[:], [iib], [Ab])
            self.act(A[:], A[:], AF.Abs, [Ab], [Ab])
            self.op("gpsimd", lambda e: e.iota(ii[:, 0:mmax], pattern=[[128, mmax]], base=128, channel_multiplier=-1),
                    [iib], [iib])
            self.cp("vector", pbase[:], ii[:, 0:mmax], [iib], [pbaseb])
            self.op("gpsimd", lambda e: e.iota(ii[:, 0:nneg], pattern=[[128, nneg]], base=1, channel_multiplier=1),
                    [iib], [iib])
            self.cp("vector", nbase[:], ii[:, 0:nneg], [iib], [nbaseb])
            rowp = self.sbpool(P, "rowp", [128, 512], BF16, 2)
            rown = self.sbpool(P, "rown", [128, 512], BF16, 2)
            Dt = self.sbpool(P, "Dt", [128, 4, 512], F32, 2)
            cfp = self.sbpool(P, "cfp", [128, mmax], F32, 2)
            cfn = self.sbpool(P, "cfn", [128, nneg], F32, 2)
            QT = self.sbpool(P, "QT", [64, S], BF16, 2)
            KT = self.sbpool(P, "KT", [64, S], BF16, 2)
            V = self.sbpool(P, "V", [128, NT, 128], BF16, 2)
            NP = 6
            Pt = self.sbpool(P, "Pt", [128, 512], BF16, NP)
            P0 = self.sbpool(P, "P0", [128, 512], BF16, 4)
            tf = self.sbpool(P, "tf", [128, 512], F32, 4)
            sgp = self.sbpool(P, "sg", [128, 512], BF16, 2)
            y16 = self.sbpool(P, "y16", [128, 512], BF16, 2)
            meng = (LENG0, LENG1)

            def loads(h):
                lg = math.log1p(-2.0 ** (-5.0 - h))
                hp = h % 2
                self.act(rowp[hp][0][:], J[:], AF.Exp, [Jb], [rowp[hp][1]], scale=lg)
                self.act(rown[hp][0][:], Jr[:], AF.Exp, [Jrb], [rown[hp][1]], scale=lg)
                self.act(Dt[hp][0][:], A[:], AF.Exp, [Ab], [Dt[hp][1]], scale=lg)
                self.act(cfp[hp][0][:], pbase[:], AF.Exp, [pbaseb], [cfp[hp][1]], scale=lg)
                self.act(cfn[hp][0][:], nbase[:], AF.Exp, [nbaseb], [cfn[hp][1]], scale=lg)
                self.load_rows(QT[hp][0], QT[hp][1], self.QTd, h * 64, 64)
                self.load_rows(KT[hp][0], KT[hp][1], self.KTd, h * 64, 64)
                self.load_v(V[hp][0], V[hp][1], self.Vd, h * 128)

            steps = [(h, qg, kt) for h in range(8) for qg in range(NG) for kt in range(NT)]
            loads(0)

            def stage1(st, i):
                h, qg, kt = st
                hp = h % 2
                q, qb = QT[hp]
                k, kb = KT[hp]
                par = (h * NG + qg) % 2
                if kt == 0:
                    sg, sgb = sgp[par]
                    self.dma("sync", sg[:], self.GdT[h * 128:(h + 1) * 128, qg * 512:(qg + 1) * 512], [], [sgb])
                sp, spb = self.psn(4, 8)
                self.mm(sp[:], k[:, kt * 128:(kt + 1) * 128], q[:, qg * 512:(qg + 1) * 512], True, True, [kb, qb], [spb])
                p_, p_b = Pt[i % NP]
                m = 4 * qg - kt
                if m >= 1 or m <= -4:
                    p0, p0b = P0[i % 4]
                    if m >= 1:
                        cf, cfb = cfp[hp]
                        col = cf[:, m - 1:m]
                        row, rowb = rowp[hp]
                    else:
                        cf, cfb = cfn[hp]
                        col = cf[:, -m - 4:-m - 3]
                        row, rowb = rown[hp]
                    self.act(p0[:], sp[:], AF.Identity, [spb, cfb], [p0b], scale=col)
                    self.tt(p_[:], p0[:], row[:], ALU.mult, [p0b, rowb], [p_b], en=meng[i % 2])
                else:
                    self.tt(p_[:], sp[:], Dt[hp][0][:, -m, :], ALU.mult, [spb, Dt[hp][1]], [p_b])

            def stage2(st, i):
                h, qg, kt = st
                if qg == 0 and kt == 0 and h + 1 < 8:
                    loads(h + 1)
                v, vb = V[h % 2]
                par = (h * NG + qg) % 2
                O, Ob = self.ps[par]
                p_, p_b = Pt[i % NP]
                self.mm(O[:], v[:, kt, :], p_[:], kt == 0, kt == NT - 1, [vb, p_b], [Ob])
                if kt == NT - 1:
                    qs = slice(qg * 512, (qg + 1) * 512)
                    sg, sgb = sgp[par]
                    o1, o1b = tf[0]
                    s1, s1b = tf[1]
                    mn, mnb = tf[2]
                    vr, vrb = tf[3]
                    self.cp(KCP, o1[:], O[:], [Ob], [o1b])
                    self.act(s1[:], O[:], AF.Square, [Ob], [s1b])
                    mp, mpb = self.ps[2]
                    spp, sppb = self.ps[3]
                    self.mm(mp[:], self.onesf[:], o1[:], True, True, [self.onesf_b, o1b], [mpb])
                    self.mm(spp[:], self.onesf[:], s1[:], True, True, [self.onesf_b, s1b], [sppb])
                    self.act(mn[:], mp[:], AF.Copy, [mpb], [mnb], scale=1.0 / 128)
                    self.tt(vr[:], mn[:], mn[:], ALU.mult, [mnb], [vrb])
                    self.stt(vr[:], spp[:], 1.0 / 128, vr[:], ALU.mult, ALU.subtract, [sppb, vrb], [vrb])
                    self.rsqrt(vr[:], vr[:], 1.0, [vrb], [vrb])
                    self.tt(o1[:], o1[:], mn[:], ALU.subtract, [o1b, mnb], [o1b])
                    self.stt(o1[:], o1[:], gncol[:, 0:1], vr[:], ALU.mult, ALU.mult, [o1b, vrb, gncolb], [o1b])
                    y, yb = y16[par]
                    self.tt(y[:], o1[:], sg[:], ALU.mult, [o1b, sgb], [yb])
                    self.dma("gpsimd", self.YT[3072 + h * 128:3072 + (h + 1) * 128, qs], y[:], [yb], [])

            self.pipeline(steps, KDEPTH, stage1, stage2)
            self.barrier()


    def pair(self, j):
        return self.psbig[:, 2 * j:2 * j + 2, :], [self.ps[2 * j][1], self.ps[2 * j + 1][1]]

    def phaseB3(self, l, s):
        S, NT, NG = self.S, self.NT, self.NG
        lam_init = 0.8 - 0.6 * math.exp(-0.3 * l)
        with ExitStack() as P:
            lp, lpb = self.sb(P, "lp", [128, 256], F32)
            self.load_bc(lp[:], lpb, self.diff_lam[l:l + 1].rearrange("a b c -> a (b c)"))
            pr, prb = self.sb(P, "pr", [128, 128], F32)
            e2, e2b = self.sb(P, "e2", [128, 2], F32)
            neglam, neglamb = self.sb(P, "neglam", [128, 1], F32)
            gcol, gcolb = self.sb(P, "gcol", [128, 1], F32)
            lp4 = lp[:].rearrange("p (a b c) -> p a b c", a=2, b=2)
            self.tt(pr[:].rearrange("p (a c) -> p a c", a=2), lp4[:, :, 0, :], lp4[:, :, 1, :], ALU.mult, [lpb], [prb])
            self.op("vector", lambda e: e.tensor_reduce(out=e2[:], in_=pr[:].rearrange("p (a c) -> p a c", a=2),
                                                        axis=AX.X, op=ALU.add), [prb], [e2b])
            self.act(e2[:], e2[:], AF.Exp, [e2b], [e2b])
            self.tt(neglam[:], e2[:, 1:2], e2[:, 0:1], ALU.subtract, [e2b], [neglamb])
            self.ts(neglam[:], neglam[:], -lam_init, None, ALU.add, None, [neglamb], [neglamb])
            self.load_cols(gcol[:], gcolb, self.diff_sub_g[l], "(p o) -> p o", o=1)
            self.ts(gcol[:], gcol[:], 1.0 - lam_init, None, ALU.mult, None, [gcolb], [gcolb])
            QT = self.sbpool(P, "QT", [128, S], BF16, 2)
            KA = self.sbpool(P, "KA", [128, S], BF16, 2)
            KB_ = self.sbpool(P, "KBt", [128, S], BF16, 2)
            for i in range(2):
                self.op("vector", lambda e, i=i: e.memset(KA[i][0][64:128, :], 0.0), [], [KA[i][1]])
                self.op("vector", lambda e, i=i: e.memset(KB_[i][0][0:64, :], 0.0), [], [KB_[i][1]])
            V = self.sbpool(P, "V", [128, NT, 128], BF16, 2)
            NP = 6
            Pt = self.sbpool(P, "Pt", [128, 2, 512], BF16, NP)
            tf = self.sbpool(P, "tf", [128, 512], F32, 6)
            sq16 = self.sbpool(P, "sq16", [128, 512], BF16, 2)
            y16 = self.sbpool(P, "y16", [128, 512], BF16, 2)
            Lacc2 = [self.sb(P, "Lacc%d" % a, [128, 2, 512], F32) for a in range(2)]
            Lacc = [[(Lacc2[a][0][:, b, :], Lacc2[a][1]) for b in range(2)] for a in range(2)]
            T1 = self.sbpool(P, "T1", [128, 2, 512], BF16, 4)
            leng = (LENG0, LENG1)

            def loads(h):
                self.load_rows(QT[h % 2][0], QT[h % 2][1], self.QTb, h * 128, 128)
                self.dma("sync", KA[h % 2][0][0:64, :], self.KTb[h * 128:h * 128 + 64, :], [], [KA[h % 2][1]])
                self.dma("sync", KB_[h % 2][0][64:128, :], self.KTb[h * 128 + 64:h * 128 + 128, :], [], [KB_[h % 2][1]])
                self.load_v(V[h % 2][0], V[h % 2][1], self.Vb, h * 128)

            steps = [(h, qg, kt) for h in range(8) for qg in range(NG) for kt in range(NT)]
            loads(0)

            def stage1(st, i):
                h, qg, kt = st
                q, qb = QT[h % 2]
                par = (h * NG + qg) % 2
                sp, spbs = self.pair(2 + i % 2)
                ks = slice(kt * 128, (kt + 1) * 128)
                qs = slice(qg * 512, (qg + 1) * 512)
                self.mm(sp[:, 0, :], KA[h % 2][0][:, ks], q[:, qs], True, True, [KA[h % 2][1], qb], [spbs[0]])
                self.mm(sp[:, 1, :], KB_[h % 2][0][:, ks], q[:, qs], True, True, [KB_[h % 2][1], qb], [spbs[1]])
                p_, p_b = Pt[i % NP]
                self.act(p_[:], sp, AF.Exp, spbs, [p_b], scale=0.125)
                if kt % 2 == 1:
                    t1, t1b = T1[(i // 2) % 4]
                    pa, pab = Pt[(i - 1) % NP]
                    self.tt(t1[:], pa[:], p_[:], ALU.add, [pab, p_b], [t1b])
                    if kt % 4 == 3:
                        t0_, t0b = T1[((i // 2) - 1) % 4]
                        a, ab = Lacc2[par]
                        if kt == 3:
                            self.tt(a[:], t0_[:], t1[:], ALU.add, [t0b, t1b], [ab])
                        else:
                            self.tt(t1[:], t0_[:], t1[:], ALU.add, [t0b, t1b], [t1b])
                            self.tt(a[:], a[:], t1[:], ALU.add, [ab, t1b], [ab])

            def stage2(st, i):
                h, qg, kt = st
                if qg == 0 and kt == 0 and h + 1 < 8:
                    loads(h + 1)
                v, vb = V[h % 2]
                par = (h * NG + qg) % 2
                p_, p_b = Pt[i % NP]
                for si in range(2):
                    O, Ob = self.ps[par * 2 + si]
                    self.mm(O, v[:, kt, :], p_[:, si, :], kt == 0, kt == NT - 1, [vb, p_b], [Ob])
                if kt == NT - 1:
                    epilogue(h, qg, par, i)

            def epilogue(h, qg, par, i):
                qs = slice(qg * 512, (qg + 1) * 512)
                O0, O0b = self.ps[par * 2]
                O1, O1b = self.ps[par * 2 + 1]
                o1, o1b = tf[0]
                o2, o2b = tf[1]
                r1, r1b = tf[2]
                r2, r2b = tf[3]
                oo, oob = tf[4]
                rs, rsb = tf[5]
                sq, sqb = sq16[par]
                self.cp("vector", o1[:], O0, [O0b], [o1b])
                self.cp("vector", o2[:], O1, [O1b], [o2b])

                def e1():
                    bnk = 4 + 2 * ((self.pit + 1) % 2)
                    for si, (r, rb) in enumerate(((r1, r1b), (r2, r2b))):
                        lp_, lpb_ = self.ps[bnk + si]
                        self.lsum_mm([Lacc[par][si]], lp_, lpb_)
                        self.act(r[:], lp_, AF.Ln, [lpb_], [rb])
                        self.act(r[:], r[:], AF.Exp, [rb], [rb], scale=-1.0)
                    self.tt(o1[:], o1[:], r1[:], ALU.mult, [o1b, r1b], [o1b])
                    self.tt(o2[:], o2[:], r2[:], ALU.mult, [o2b, r2b], [o2b])
                    self.stt(oo[:], o2[:], neglam[:, 0:1], o1[:], ALU.mult, ALU.add, [o2b, o1b, neglamb], [oob])
                    self.tt(sq[:], oo[:], oo[:], ALU.mult, [oob], [sqb])

                def e2():
                    bnk = 4 + 2 * ((self.pit + 1) % 2)
                    sp, spb = self.ps[bnk]
                    self.mm(sp, self.onesb[:], sq[:], True, True, [self.onesb_b, sqb], [spb])
                    self.rsqrt(rs[:], sp, 1.0 / 128, [spb], [rsb])
                    y, yb = y16[par]
                    self.stt(y[:], oo[:], gcol[:, 0:1], rs[:], ALU.mult, ALU.mult, [oob, rsb, gcolb], [yb])
                    self.dma("gpsimd", self.YT[1024 + h * 128:1024 + (h + 1) * 128, qs], y[:], [yb], [])
                self.later(min(KE1, NT - 2), e1)
                self.later(min(KE2, NT - 1), e2)

            self.pipeline(steps, 1, stage1, stage2)
            self.barrier()

    def phaseC3(self, l, s):
        S, NT, NG = self.S, self.NT, self.NG
        sc = 192.0 ** -0.5
        NTP = NT // 2
        with ExitStack() as P:
            Qn = self.sbpool(P, "Qn", [128, S], BF16, 2)
            Qp = self.sbpool(P, "Qp", [128, S], BF16, 2)
            Kn = self.sbpool(P, "Kn", [128, S], BF16, 2)
            Kp = self.sbpool(P, "Kp", [128, S], BF16, 2)
            for i in range(2):
                self.op("vector", lambda e, i=i: e.memset(Qp[i][0][64:128, :], 0.0), [], [Qp[i][1]])
                self.op("vector", lambda e, i=i: e.memset(Kp[i][0][64:128, :], 0.0), [], [Kp[i][1]])
            V = self.sbpool(P, "V", [128, NT, 128], BF16, 2)
            NP = 4
            Pt = self.sbpool(P, "Pt", [128, 2, 512], BF16, NP)
            tf = self.sbpool(P, "tf", [128, 512], F32, 4)
            y16 = self.sbpool(P, "y16", [128, 512], BF16, 2)
            Lacc = [self.sb(P, "Lacc%d" % a, [128, 2, 512], F32) for a in range(2)]

            def loads(h):
                self.load_rows(Qn[h % 2][0], Qn[h % 2][1], self.QTc, h * 192, 128)
                self.load_rows(Qp[h % 2][0], Qp[h % 2][1], self.QTc, h * 192 + 128, 64)
                self.load_rows(Kn[h % 2][0], Kn[h % 2][1], self.KTc, h * 192, 128)
                self.load_rows(Kp[h % 2][0], Kp[h % 2][1], self.KTc, h * 192 + 128, 64)
                self.load_v(V[h % 2][0], V[h % 2][1], self.Vc, h * 128)

            steps = [(h, qg, kp) for h in range(8) for qg in range(NG) for kp in range(NTP)]
            loads(0)

            def stage1(st, i):
                h, qg, kp_ = st
                qn, qnb = Qn[h % 2]
                qp, qpb = Qp[h % 2]
                kn, knb = Kn[h % 2]
                kp, kpb = Kp[h % 2]
                par = (h * NG + qg) % 2
                qs = slice(qg * 512, (qg + 1) * 512)
                sp, spbs = self.pair(1 + i % 3)
                for j in range(2):
                    kt = 2 * kp_ + j
                    ks = slice(kt * 128, (kt + 1) * 128)
                    self.mm(sp[:, j, :], kn[:, ks], qn[:, qs], True, False, [knb, qnb], [spbs[j]])
                    self.mm(sp[:, j, :], kp[:, ks], qp[:, qs], False, True, [kpb, qpb], [spbs[j]])
                p_, p_b = Pt[i % NP]
                self.act(p_[:], sp, AF.Exp, spbs, [p_b], scale=sc)
                a, ab = Lacc[par]
                if kp_ == 0:
                    self.cp("vector", a[:], p_[:], [p_b], [ab])
                else:
                    self.tt(a[:], a[:], p_[:], ALU.add, [ab, p_b], [ab])

            def stage2(st, i):
                h, qg, kp_ = st
                if qg == 0 and kp_ == 0 and h + 1 < 8:
                    loads(h + 1)
                v, vb = V[h % 2]
                par = (h * NG + qg) % 2
                O, Ob = self.ps[par]
                p_, p_b = Pt[i % NP]
                for j in range(2):
                    kt = 2 * kp_ + j
                    self.mm(O, v[:, kt, :], p_[:, j, :], kt == 0, kt == NT - 1, [vb, p_b], [Ob])
                if kp_ == NTP - 1:
                    qs = slice(qg * 512, (qg + 1) * 512)
                    o1, o1b = tf[par * 2]
                    r1, r1b = tf[par * 2 + 1]
                    self.cp("vector", o1[:], O, [Ob], [o1b])

                    def e1(h=h, qs=qs, par=par, o1=o1, o1b=o1b, r1=r1, r1b=r1b):
                        lp_, lpb_ = self.ps[2 + 2 * ((self.pit + 1) % 3)]
                        a, ab = Lacc[par]
                        self.mm(lp_, self.onesf[:], a[:, 0, :], True, False, [self.onesf_b, ab], [lpb_])
                        self.mm(lp_, self.onesf[:], a[:, 1, :], False, True, [self.onesf_b, ab], [lpb_])
                        self.act(r1[:], lp_, AF.Ln, [lpb_], [r1b])
                        self.act(r1[:], r1[:], AF.Exp, [r1b], [r1b], scale=-1.0)
                        y, yb = y16[par]
                        self.tt(y[:], o1[:], r1[:], ALU.mult, [o1b, r1b], [yb])
                        self.dma("gpsimd", self.YT[2048 + h * 128:2048 + (h + 1) * 128, qs], y[:], [yb], [])
                    self.later(min(2, NTP - 1), e1)

            self.pipeline(steps, 2, stage1, stage2)
            self.barrier()

    def phaseD3(self, l, s):
        S, NT, NG = self.S, self.NT, self.NG
        mmax = max(4 * (NG - 1), 1)
        nneg = max(NT - 4, 1)
        with ExitStack() as P:
            gncol, gncolb = self.sb(P, "gncol", [128, 1], F32)
            self.load_cols(gncol[:], gncolb, self.ret_gn_g[l], "(p o) -> p o", o=1)
            ii, iib = self.sb(P, "ii", [128, 512], I32)
            J, Jb = self.sb(P, "J", [128, 512], F32)
            Jr, Jrb = self.sb(P, "Jr", [128, 512], F32)
            A, Ab = self.sb(P, "A", [128, 4, 512], F32)
            pbase, pbaseb = self.sb(P, "pbase", [128, mmax], F32)
            nbase, nbaseb = self.sb(P, "nbase", [128, nneg], F32)
            self.op("gpsimd", lambda e: e.iota(ii[:], pattern=[[1, 512]], base=0, channel_multiplier=0), [], [iib])
            self.cp("vector", J[:], ii[:], [iib], [Jb])
            self.ts(Jr[:], J[:], -1.0, 511.0, ALU.mult, ALU.add, [Jb], [Jrb])
            for m in range(4):
                self.op("gpsimd", lambda e, m=m: e.iota(ii[:], pattern=[[1, 512]], base=-128 * m, channel_multiplier=-1),
                        [iib], [iib])
                self.cp("vector", A[:, m, :], ii[:], [iib], [Ab])
            self.act(A[:], A[:], AF.Abs, [Ab], [Ab])
            self.op("gpsimd", lambda e: e.iota(ii[:, 0:mmax], pattern=[[128, mmax]], base=128, channel_multiplier=-1),
                    [iib], [iib])
            self.cp("vector", pbase[:], ii[:, 0:mmax], [iib], [pbaseb])
            self.op("gpsimd", lambda e: e.iota(ii[:, 0:nneg], pattern=[[128, nneg]], base=1, channel_multiplier=1),
                    [iib], [iib])
            self.cp("vector", nbase[:], ii[:, 0:nneg], [iib], [nbaseb])
            rowp = self.sbpool(P, "rowp", [128, 512], BF16, 2)
            rown = self.sbpool(P, "rown", [128, 512], BF16, 2)
            Dt = self.sbpool(P, "Dt", [128, 4, 512], F32, 2)
            cfp = self.sbpool(P, "cfp", [128, mmax], F32, 2)
            cfn = self.sbpool(P, "cfn", [128, nneg], F32, 2)
            QT = self.sbpool(P, "QT", [128, S], BF16, 2)
            KT = self.sbpool(P, "KT", [128, S], BF16, 2)
            for i in range(2):
                self.op("vector", lambda e, i=i: e.memset(QT[i][0][64:128, :], 0.0), [], [QT[i][1]])
                self.op("vector", lambda e, i=i: e.memset(KT[i][0][64:128, :], 0.0), [], [KT[i][1]])
            V = self.sbpool(P, "V", [128, NT, 128], BF16, 2)
            NP = 8
            Pt = self.sbpool(P, "Pt", [128, 512], BF16, NP)
            P0 = self.sbpool(P, "P0", [128, 512], BF16, 6)
            tf = self.sbpool(P, "tf", [128, 512], F32, 4)
            sgp = self.sbpool(P, "sg", [128, 512], BF16, 2)
            y16 = self.sbpool(P, "y16", [128, 512], BF16, 2)
            cnt = [0]

            def loads(h):
                lg = math.log1p(-2.0 ** (-5.0 - h))
                hp = h % 2
                self.act(rowp[hp][0][:], J[:], AF.Exp, [Jb], [rowp[hp][1]], scale=lg)
                self.act(rown[hp][0][:], Jr[:], AF.Exp, [Jrb], [rown[hp][1]], scale=lg)
                self.act(Dt[hp][0][:], A[:], AF.Exp, [Ab], [Dt[hp][1]], scale=lg)
                self.act(cfp[hp][0][:], pbase[:], AF.Exp, [pbaseb], [cfp[hp][1]], scale=lg)
                self.act(cfn[hp][0][:], nbase[:], AF.Exp, [nbaseb], [cfn[hp][1]], scale=lg)
                self.load_rows(QT[hp][0], QT[hp][1], self.QTd, h * 64, 64)
                self.load_rows(KT[hp][0], KT[hp][1], self.KTd, h * 64, 64)
                self.load_v(V[hp][0], V[hp][1], self.Vd, h * 128)

            steps = [(h, qg, kt) for h in range(8) for qg in range(NG) for kt in range(NT)]
            loads(0)

            def stage1(st, i):
                h, qg, kt = st
                hp = h % 2
                q, qb = QT[hp]
                k, kb = KT[hp]
                par = (h * NG + qg) % 2
                if kt == 0:
                    sg, sgb = sgp[par]
                    self.dma("sync", sg[:], self.GdT[h * 128:(h + 1) * 128, qg * 512:(qg + 1) * 512], [], [sgb])
                sp, spb = self.psn(2, 8)
                self.mm(sp, k[:, kt * 128:(kt + 1) * 128], q[:, qg * 512:(qg + 1) * 512], True, True, [kb, qb], [spb])
                p_, p_b = Pt[i % NP]
                m = 4 * qg - kt
                if m >= 1 or m <= -4:
                    if m >= 1:
                        cf, cfb = cfp[hp]
                        col = cf[:, m - 1:m]
                        row, rowb = rowp[hp]
                    else:
                        cf, cfb = cfn[hp]
                        col = cf[:, -m - 4:-m - 3]
                        row, rowb = rown[hp]
                    cnt[0] += 1
                    if cnt[0] % KDACT != 0:
                        p0, p0b = P0[cnt[0] % 6]
                        self.act(p0[:], sp, AF.Identity, [spb, cfb], [p0b], scale=col)
                        self.tt(p_[:], p0[:], row[:], ALU.mult, [p0b, rowb], [p_b], en=LENG1)
                    else:
                        self.stt(p_[:], sp, col, row[:], ALU.mult, ALU.mult, [spb, cfb, rowb], [p_b])
                else:
                    self.tt(p_[:], sp, Dt[hp][0][:, -m, :], ALU.mult, [spb, Dt[hp][1]], [p_b])

            def stage2(st, i):
                h, qg, kt = st
                if qg == 0 and kt == 0 and h + 1 < 8:
                    loads(h + 1)
                v, vb = V[h % 2]
                par = (h * NG + qg) % 2
                O, Ob = self.ps[par]
                p_, p_b = Pt[i % NP]
                self.mm(O, v[:, kt, :], p_[:], kt == 0, kt == NT - 1, [vb, p_b], [Ob])
                if kt == NT - 1:
                    qs = slice(qg * 512, (qg + 1) * 512)
                    sg, sgb = sgp[par]
                    o1, o1b = tf[0]
                    s1, s1b = tf[1]
                    mn, mnb = tf[2]
                    vr, vrb = tf[3]
                    self.cp("scalar", o1[:], O, [Ob], [o1b])
                    self.act(s1[:], O, AF.Square, [Ob], [s1b])

                    def e1(h=h, qs=qs, par=par, sg=sg, sgb=sgb):
                        mp, mpb = self.psn(2, 8)
                        spp, sppb = self.psn(2, 8)
                        self.mm(mp, self.onesf[:], o1[:], True, True, [self.onesf_b, o1b], [mpb])
                        self.mm(spp, self.onesf[:], s1[:], True, True, [self.onesf_b, s1b], [sppb])
                        self.act(mn[:], mp, AF.Copy, [mpb], [mnb], scale=1.0 / 128)
                        self.tt(vr[:], mn[:], mn[:], ALU.mult, [mnb], [vrb])
                        self.stt(vr[:], spp, 1.0 / 128, vr[:], ALU.mult, ALU.subtract, [sppb, vrb], [vrb])
                        self.rsqrt(vr[:], vr[:], 1.0, [vrb], [vrb])
                        self.tt(o1[:], o1[:], mn[:], ALU.subtract, [o1b, mnb], [o1b])
                        self.stt(o1[:], o1[:], gncol[:, 0:1], vr[:], ALU.mult, ALU.mult, [o1b, vrb, gncolb], [o1b])
                        y, yb = y16[par]
                        self.tt(y[:], o1[:], sg[:], ALU.mult, [o1b, sgb], [yb])
                        self.dma("gpsimd", self.YT[3072 + h * 128:3072 + (h + 1) * 128, qs], y[:], [yb], [])
                    self.later(min(2, NT - 1), e1)

            self.pipeline(steps, KDD, stage1, stage2)
            self.barrier()

    def phaseE(self, l, s):
        S = self.S
        TB = min(1024, S)
        nblk = S // TB
        NGb = TB // 512
        with ExitStack() as P:
            xk, xkb = self.sb(P, "xk", [128, 8, TB], F32)
            big, bigb = self.sb(P, "big", [128, 32, TB], BF16)
            bigbs = [Buf("big%d" % i_) for i_ in range(4)]
            aT, aTb = self.sb(P, "aT", [128, 8, TB], BF16)
            pT, pTb = self.sb(P, "pT", [128, 2, TB], BF16)
            wpool = self.sbpool(P, "we", [128, 4096], BF16, 3)
            g2, g2b = self.sb(P, "g2", [128, 8], F32)
            g3, g3b = self.sb(P, "g3", [128, 8], F32)
            self.load_cols(g2[:], g2b, self.norm2_g[l], "(kc p) -> p kc", p=128)
            self.load_cols(g3[:], g3b, self.norm3_g[l], "(kc p) -> p kc", p=128)
            acc = self.sbpool(P, "acc", [128, 512], F32, 4 * NGb)
            tmpf = self.sbpool(P, "tmpf", [128, 512], F32, 3)
            gtp = self.sbpool(P, "gt", [128, 512], BF16, 3)
            pl = self.sbpool(P, "pl", [128, 256], F32, 2)
            pl16 = self.sbpool(P, "pl16", [128, 256], BF16, 2)
            cnt = [0]

            def load_y(blk_):
                for b_ in range(4):
                    self.dma("sync", big[:, b_ * 8:(b_ + 1) * 8, :],
                             self.YT[b_ * 1024:(b_ + 1) * 1024, blk_ * TB:(blk_ + 1) * TB].rearrange(
                                 "(kc p) t -> p kc t", p=128), [], [bigbs[b_]])

            for blk in range(nblk):
                c0 = s * S + blk * TB
                lc0 = blk * TB
                if blk == 0:
                    load_y(0)
                for nt in range(2):
                    for b_ in range(4):
                        def epi(pt, pb, m, g, b_=b_, nt=nt):
                            gt, gtb = gtp[cnt[0] % 3]
                            tm, tmb = tmpf[cnt[0] % 3]
                            cnt[0] += 1
                            r0 = b_ * 1024 + (nt * 4 + m) * 128
                            self.dma("sync", gt[:], self.GT[r0:r0 + 128, lc0 + g * 512:lc0 + (g + 1) * 512], [], [gtb])
                            a, ab = acc[m * NGb + g]
                            if b_ == 0:
                                self.tt(a[:], pt[:], gt[:], ALU.mult, [pb, gtb], [ab])
                            elif b_ < 3:
                                self.tt(tm[:], pt[:], gt[:], ALU.mult, [pb, gtb], [tmb])
                                self.tt(a[:], a[:], tm[:], ALU.add, [ab, tmb], [ab], en="gpsimd")
                            else:
                                self.tt(tm[:], pt[:], gt[:], ALU.mult, [pb, gtb], [tmb])
                                self.tt(aT[:, nt * 4 + m, g * 512:(g + 1) * 512], a[:], tm[:], ALU.add, [ab, tmb], [aTb],
                                        en="gpsimd")
                        self.gemm("fm", big, bigbs[b_], 8, TB, self.wb["br%d" % b_, l], nt * 512, 512, epi, wpool, k0=b_ * 8)
                self.dma("sync", xk[:], self.xT[:, c0:c0 + TB].rearrange("(kc p) t -> p kc t", p=128), [], [xkb])
                for nt in range(2):
                    def epi(pt, pb, m, g, nt=nt):
                        xs = xk[:, nt * 4 + m, g * 512:(g + 1) * 512]
                        self.tt(xs, xs, pt[:], ALU.add, [xkb, pb], [xkb])
                    self.gemm("fm", aT, aTb, 8, TB, self.wb["out", l], nt * 512, 512, epi, wpool)
                self.norm_fm(None, 0, TB, g2, g2b, aT, aTb, xkeep=(xk, xkb), loaded=True)
                for nt in range(8):
                    def epi(pt, pb, m, g, nt=nt):
                        tm, tmb = tmpf[cnt[0] % 3]
                        cnt[0] += 1
                        self.act(tm[:], pt[:], AF.Relu, [pb], [tmb])
                        self.tt(big[:, nt * 4 + m, g * 512:(g + 1) * 512], tm[:], tm[:], ALU.mult, [tmb],
                                [bigbs[(nt * 4 + m) // 8]])
                    self.gemm("fm", aT, aTb, 8, TB, self.wb["ff1", l], nt * 512, 512, epi, wpool)
                for mt in range(8):
                    def epi(pt, pb, m, g, mt=mt):
                        xs = xk[:, mt, g * 512:(g + 1) * 512]
                        self.tt(xs, xs, pt[:], ALU.add, [xkb, pb], [xkb])
                    self.gemm("fm", big, bigbs, 32, TB, self.wb["ff2", l], mt * 128, 128, epi, wpool)
                if blk + 1 < nblk:
                    load_y(blk + 1)
                self.norm_fm(None, 0, TB, g3, g3b, aT, aTb, xkeep=(xk, xkb), loaded=True)
                for t in range(TB // 128):
                    p_, p_b = pl[t % 2]
                    p16, p16b = pl16[t % 2]
                    self.dma("sync", p_[:], self.p[l, c0 + t * 128:c0 + (t + 1) * 128, :], [], [p_b])
                    self.cp(self.alt(), p16[:], p_[:], [p_b], [p16b])
                    i = self.psi % 8
                    self.psi += 1
                    pv = self.psb(i)
                    for bi in range(2):
                        self.tp(pv[:, bi * 128:(bi + 1) * 128], p16[:, bi * 128:(bi + 1) * 128], self.identb[:],
                                [p16b, self.identb_b], [self.ps[i][1]])
                    self.cp(self.alt(), pT[:, :, t * 128:(t + 1) * 128], pv[:, 0:256].rearrange("p (b t) -> p b t", b=2),
                            [self.ps[i][1]], [pTb])
                for nt in range(2):
                    wg, wgb = wpool[self.wi % 3]
                    self.wi += 1
                    wp, wpb = wpool[self.wi % 3]
                    self.wi += 1
                    wgv = wg[:, 0:4096].rearrange("p (k n) -> p k n", k=8)
                    wpv = wp[:, 0:1024].rearrange("p (k n) -> p k n", k=2)
                    self.dma("sync", wgv, self.wb["pg", l][:, nt * 512:(nt + 1) * 512].rearrange("(kc p) n -> p kc n", p=128),
                             [], [wgb])
                    self.dma("sync", wpv, self.wb["pp", l][:, nt * 512:(nt + 1) * 512].rearrange("(kc p) n -> p kc n", p=128),
                             [], [wpb])
                    for m in range(4):
                        for g in range(NGb):
                            gs = slice(g * 512, (g + 1) * 512)
                            p1, p1b = self.psn()
                            for kc in range(8):
                                self.mm(p1[:], wgv[:, kc, m * 128:(m + 1) * 128], aT[:, kc, gs], kc == 0, kc == 7,
                                        [wgb, aTb], [p1b])
                            p2, p2b = self.psn()
                            for kc in range(2):
                                self.mm(p2[:], wpv[:, kc, m * 128:(m + 1) * 128], pT[:, kc, gs], kc == 0, kc == 1,
                                        [wpb, pTb], [p2b])
                            tm, tmb = tmpf[cnt[0] % 3]
                            cnt[0] += 1
                            self.act(tm[:], p1[:], AF.Sigmoid, [p1b], [tmb])
                            self.tt(tm[:], tm[:], p2[:], ALU.mult, [tmb, p2b], [tmb])
                            xs = xk[:, nt * 4 + m, gs]
                            self.tt(xs, xs, tm[:], ALU.add, [xkb, tmb], [xkb], en="gpsimd")
                self.dma("gpsimd", self.xT[:, c0:c0 + TB].rearrange("(kc p) t -> p kc t", p=128), xk[:], [xkb], [])
            self.barrier()

    def mark(self, name):
        if not hasattr(self, "marks"):
            self.marks = []
        self.marks.append((name, {k: e.dom.count for k, e in self.E.items()}))

    def body(self):
        for l in range(self.L):
            for s in range(self.NSEQ):
                self.mark("P1 %d %d" % (l, s))
                self.phase1(l, s)
                self.mark("PB %d %d" % (l, s))
                (self.phaseB3 if "B3" in PH else self.phaseB2)(l, s)
                self.mark("PC %d %d" % (l, s))
                (self.phaseC3 if "C3" in PH else self.phaseC2)(l, s)
                self.mark("PD %d %d" % (l, s))
                (self.phaseD3 if "D3" in PH else self.phaseD2)(l, s)
                self.mark("PE %d %d" % (l, s))
                self.phaseE(l, s)
        self.mark("END")

    def build(self):
        self.setup_sync()
        self.declare_io()
        self.build_consts()
        self.precast()
        self.build_rope()
        self.transpose_in()
        self.body()
        self.transpose_out()
        self.barrier()
        self.st.close()
        return self.nc


INPUT_NAMES = ["x", "p", "norm1_g", "w_in", "gate_b", "conv_w", "conv_b", "lru_wa", "lru_ba", "lru_wx", "lru_bx",
               "lru_lambda", "diff_q_g", "diff_k_g", "diff_lam", "diff_sub_g", "mla_qa_g", "mla_wuq", "mla_kva_g",
               "mla_wukv", "mla_q_g", "mla_k_g", "ret_gn_g", "w_br_a", "w_br_b", "w_br_c", "w_br_d", "w_out",
               "norm2_g", "w_ff1", "w_ff2", "norm3_g", "w_ple_gate", "w_ple_proj"]


def make_in_maps(inputs, ncores, nseq, S):
    maps = []
    for c in range(ncores):
        m = {}
        for k in INPUT_NAMES:
            v = np.asarray(inputs[k])
            if k == "x":
                v = np.ascontiguousarray(v[c * nseq:(c + 1) * nseq].reshape(nseq * S, DM))
            elif k == "p":
                v = np.ascontiguousarray(v[:, c * nseq:(c + 1) * nseq].reshape(v.shape[0], nseq * S, PLED))
            else:
                v = np.ascontiguousarray(v)
            m[k] = v.astype(np.float32, copy=False)
        maps.append(m)
    return maps


def kernel(**inputs):
    B, S, _ = inputs["x"].shape
    L = inputs["w_in"].shape[0]
    ncores = 8
    nseq = B // ncores
    kb = KB(S, nseq, L)
    nc = kb.build()
    maps = make_in_maps(inputs, ncores, nseq, S)
    res = run_bass_kernel_spmd(nc, maps, core_ids=list(range(ncores)))
    outs = [np.asarray(r["out"]).reshape(nseq, S, DM) for r in res.results]
    return np.concatenate(outs, axis=0).astype(np.float32)
```

```python
import math
import numpy as np
import concourse.bass as bass
import concourse.mybir as mybir
from concourse.bass_utils import run_bass_kernel_spmd
from contextlib import ExitStack

F32 = mybir.dt.float32
BF16 = mybir.dt.bfloat16
I32 = mybir.dt.int32
AF = mybir.ActivationFunctionType
ALU = mybir.AluOpType
AX = mybir.AxisListType

DM = 1024
NIN = 12992
DFF = 4096
PLED = 256
EPS = 1e-6
OFF = dict(ax=0, ag=1024, bq=2048, bk=3072, bv=4096, cq=5120, ckv=5504, ckpe=5760,
           dq=5824, dk=6336, dv=6848, dg=7872, gl=8896)
SAME_SYNC = True
import os
PH = os.environ.get("KPH", "B3,C3,D3").split(",")
KDEPTH = int(os.environ.get("KDEPTH", "3"))
KCP = os.environ.get("KCP", "scalar")
LENG0 = "vector"
LENG1 = os.environ.get("KLENG1", "vector")
KDEFER = int(os.environ.get("KDEFER", "2"))
KGI = int(os.environ.get("KGI", "2"))
KDACT = int(os.environ.get("KDACT", "3"))
KDD = int(os.environ.get("KDD", "5"))
KE1 = int(os.environ.get("KE1", "3"))
KE2 = int(os.environ.get("KE2", "8"))


class Dom:
    def __init__(self, name, sem, mult):
        self.name, self.sem, self.mult, self.count = name, sem, mult, 0


class Buf:
    __slots__ = ("name", "w", "r")

    def __init__(self, name=""):
        self.name, self.w, self.r = name, {}, {}


class Eng:
    def __init__(self, name, eng, dom):
        self.name, self.eng, self.dom = name, eng, dom
        self.seen = {}
        self.slots = []
        self.si = 0


class KB:
    def __init__(self, S, NSEQ, L, debug=()):
        self.S, self.NSEQ, self.L = S, NSEQ, L
        self.NT, self.NG = S // 128, S // 512
        self.debug = set(debug)
        self.nc = bass.Bass("TRN2", target_bir_lowering=False)
        self.st = ExitStack()
        self.E = {}
        self.doms = []
        self.dbufs = {}
        self.cnt = 0
        self.wi = 0

    def setup_sync(self):
        nc = self.nc
        for name in ["tensor", "vector", "scalar", "gpsimd", "sync"]:
            sem = self.st.enter_context(nc.semaphore("s_" + name))
            dom = Dom(name, sem, 1)
            self.doms.append(dom)
            self.E[name] = Eng(name, getattr(nc, name), dom)
        for q, n in (("sync", 24), ("gpsimd", 16)):
            for i in range(n):
                sem = self.st.enter_context(nc.semaphore("d_%s%d" % (q, i)))
                dom = Dom("d_%s%d" % (q, i), sem, 16)
                self.doms.append(dom)
                self.E[q].slots.append(dom)

    def _deps(self, reads, writes):
        deps = {}
        for b in reads:
            for d, i in b.w.items():
                if deps.get(d, 0) < i:
                    deps[d] = i
        for b in writes:
            for d, i in b.w.items():
                if deps.get(d, 0) < i:
                    deps[d] = i
            for d, i in b.r.items():
                if deps.get(d, 0) < i:
                    deps[d] = i
        return deps

    def _wait(self, E, deps):
        for dom, idx in deps.items():
            if idx <= 0:
                continue
            if dom is E.dom and (E.name == "tensor" or not SAME_SYNC):
                continue
            if E.seen.get(dom, 0) >= idx:
                continue
            E.eng.wait_ge(dom.sem, idx * dom.mult)
            E.seen[dom] = idx

    def op(self, en, fn, reads=(), writes=()):
        E = self.E[en]
        self._wait(E, self._deps(reads, writes))
        ins = fn(E.eng)
        E.dom.count += 1
        ins.then_inc(E.dom.sem, 1)
        c = E.dom.count
        for b in reads:
            b.r[E.dom] = c
        for b in writes:
            b.w[E.dom] = c
        self.cnt += 1

    def dma(self, q, out, in_, reads=(), writes=(), slow=False):
        E = self.E[q]
        slot = E.slots[E.si % len(E.slots)]
        E.si += 1
        deps = self._deps(reads, writes)
        if slot.count > 0 and deps.get(slot, 0) < slot.count:
            deps[slot] = slot.count
        self._wait(E, deps)
        if slow:
            ins = E.eng.dma_start(out=out, in_=in_, allow_slow_non_contiguous=True)
        else:
            ins = E.eng.dma_start(out=out, in_=in_)
        ins.then_inc(slot.sem, 16)
        slot.count += 1
        for b in reads:
            b.r[slot] = slot.count
        for b in writes:
            b.w[slot] = slot.count
        self.cnt += 1

    def barrier(self):
        for E in self.E.values():
            for d in self.doms:
                if d.count > 0 and E.seen.get(d, 0) < d.count:
                    E.eng.wait_ge(d.sem, d.count * d.mult)
                    E.seen[d] = d.count

    def dbuf(self, *key):
        b = self.dbufs.get(key)
        if b is None:
            b = Buf(str(key))
            self.dbufs[key] = b
        return b

    def sb(self, stack, name, shape, dt):
        self.uid = getattr(self, "uid", 0) + 1
        t = stack.enter_context(self.nc.sbuf_tensor("%s_%d" % (name, self.uid), list(shape), dt))
        return t, Buf(name)

    def sbpool(self, stack, name, shape, dt, n):
        return [self.sb(stack, "%s%d" % (name, i), shape, dt) for i in range(n)]

    def act(self, out, in_, func, reads, writes, bias=None, scale=None, accum=None):
        kw = {}
        if bias is not None:
            kw["bias"] = bias
        if scale is not None:
            kw["scale"] = scale
        if accum is not None:
            kw["accum_out"] = accum
        self.op("scalar", lambda e: e.activation(out=out, in_=in_, func=func, **kw), reads, writes)

    def tt(self, out, in0, in1, op, reads, writes, en="vector"):
        self.op(en, lambda e: e.tensor_tensor(out=out, in0=in0, in1=in1, op=op), reads, writes)

    def ts(self, out, in0, s1, s2, op0, op1, reads, writes, en="vector"):
        if op1 is None:
            self.op(en, lambda e: e.tensor_scalar(out=out, in0=in0, scalar1=s1, scalar2=None, op0=op0), reads, writes)
        else:
            self.op(en, lambda e: e.tensor_scalar(out=out, in0=in0, scalar1=s1, scalar2=s2, op0=op0, op1=op1),
                    reads, writes)

    def stt(self, out, in0, scalar, in1, op0, op1, reads, writes):
        self.op("vector", lambda e: e.scalar_tensor_tensor(out=out, in0=in0, scalar=scalar, in1=in1, op0=op0, op1=op1),
                reads, writes)

    def cp(self, en, out, in_, reads, writes):
        if en == "scalar":
            self.op("scalar", lambda e: e.activation(out=out, in_=in_, func=AF.Copy), reads, writes)
        else:
            self.op(en, lambda e: e.tensor_copy(out=out, in_=in_), reads, writes)

    def mm(self, out, lhsT, rhs, start, stop, reads, writes):
        self.op("tensor", lambda e: e.matmul(out, lhsT=lhsT, rhs=rhs, start=start, stop=stop), reads, writes)

    def tp(self, out, in_, ident, reads, writes):
        self.op("tensor", lambda e: e.transpose(out=out, in_=in_, identity=ident), reads, writes)

    def rsqrt(self, out, in_, scale, reads, writes, eps=EPS):
        self.act(out, in_, AF.Ln, reads, writes, bias=self.epscol[0:out.shape[0], :] if eps == EPS else eps,
                 scale=scale)
        self.act(out, out, AF.Exp, list(writes), writes, scale=-0.5)

    _alt = 0

    def alt(self):
        self._alt ^= 1
        return "scalar" if self._alt else "vector"

    def declare_io(self):
        nc, S, NSEQ, L = self.nc, self.S, self.NSEQ, self.L
        T = NSEQ * S
        self.T = T
        d = lambda n, sh, dt=F32: nc.dram_tensor(n, list(sh), dt, kind="ExternalInput")
        self.x = d("x", [T, DM])
        self.p = d("p", [L, T, PLED])
        self.norm1_g = d("norm1_g", [L, DM])
        self.w_in = d("w_in", [L, DM, NIN])
        self.gate_b = d("gate_b", [L, 4, DM])
        self.conv_w = d("conv_w", [L, 4, DM])
        self.conv_b = d("conv_b", [L, DM])
        self.lru_wa = d("lru_wa", [L, 2, 16, 64, 64])
        self.lru_ba = d("lru_ba", [L, 2, DM])
        self.lru_wx = d("lru_wx", [L, 2, 16, 64, 64])
        self.lru_bx = d("lru_bx", [L, 2, DM])
        self.lru_lambda = d("lru_lambda", [L, 2, DM])
        self.diff_q_g = d("diff_q_g", [L, 64])
        self.diff_k_g = d("diff_k_g", [L, 64])
        self.diff_lam = d("diff_lam", [L, 4, 64])
        self.diff_sub_g = d("diff_sub_g", [L, 128])
        self.mla_qa_g = d("mla_qa_g", [L, 384])
        self.mla_wuq = d("mla_wuq", [L, 384, 1536])
        self.mla_kva_g = d("mla_kva_g", [L, 256])
        self.mla_wukv = d("mla_wukv", [L, 256, 2048])
        self.mla_q_g = d("mla_q_g", [L, 192])
        self.mla_k_g = d("mla_k_g", [L, 192])
        self.ret_gn_g = d("ret_gn_g", [L, 128])
        self.w_br = [d("w_br_" + c, [L, DM, DM]) for c in "abcd"]
        self.w_out = d("w_out", [L, DM, DM])
        self.norm2_g = d("norm2_g", [L, DM])
        self.w_ff1 = d("w_ff1", [L, DM, DFF])
        self.w_ff2 = d("w_ff2", [L, DFF, DM])
        self.norm3_g = d("norm3_g", [L, DM])
        self.w_ple_gate = d("w_ple_gate", [L, DM, DM])
        self.w_ple_proj = d("w_ple_proj", [L, PLED, DM])
        self.out = nc.dram_tensor("out", [T, DM], F32, kind="ExternalOutput")

        def scr(n, sh, dt):
            kind = "ExternalOutput" if n in self.debug else "Internal"
            return nc.dram_tensor(n, list(sh), dt, kind=kind)

        self.scr = scr
        self.wb = {}
        for l in range(L):
            self.wb["w_in", l] = scr("wb_in%d" % l, [DM, NIN], BF16)
            self.wb["wuq", l] = scr("wb_wuq%d" % l, [384, 1536], BF16)
            self.wb["wukv", l] = scr("wb_wukv%d" % l, [256, 2048], BF16)
            for i in range(4):
                self.wb["br%d" % i, l] = scr("wb_br%d_%d" % (i, l), [DM, DM], BF16)
            self.wb["out", l] = scr("wb_out%d" % l, [DM, DM], BF16)
            self.wb["ff1", l] = scr("wb_ff1%d" % l, [DM, DFF], BF16)
            self.wb["ff2", l] = scr("wb_ff2%d" % l, [DFF, DM], BF16)
            self.wb["pg", l] = scr("wb_pg%d" % l, [DM, DM], BF16)
            self.wb["pp", l] = scr("wb_pp%d" % l, [PLED, DM], BF16)
        self.xT = scr("xT", [DM, T], F32)
        self.QTb = scr("QTb", [1024, S], BF16)
        self.KTb = scr("KTb", [1024, S], BF16)
        self.Vb = scr("Vb", [S, 1024], BF16)
        self.QTc = scr("QTc", [8 * 192, S], BF16)
        self.KTc = scr("KTc", [8 * 192, S], BF16)
        self.Vc = scr("Vc", [S, 1024], BF16)
        self.QTd = scr("QTd", [512, S], BF16)
        self.KTd = scr("KTd", [512, S], BF16)
        self.Vd = scr("Vd", [S, 1024], BF16)
        self.GdT = scr("GdT", [1024, S], BF16)
        self.YT = scr("YT", [4 * 1024, S], BF16)
        self.GT = scr("GT", [4 * 1024, S], BF16)
        self.ropeD = [scr("rope%d" % i, [S, 2, r2], F32) for i, r2 in enumerate((8, 32, 32))]

    def build_consts(self):
        nc = self.nc
        st = self.st
        self.identf, self.identf_b = self.sb(st, "identf", [128, 128], F32)
        self.identb, self.identb_b = self.sb(st, "identb", [128, 128], BF16)
        self.onesf, self.onesf_b = self.sb(st, "onesf", [128, 128], F32)
        self.onesb, self.onesb_b = self.sb(st, "onesb", [128, 128], BF16)
        self.epscol, self.epscol_b = self.sb(st, "epscol", [128, 1], F32)
        self.onecol, self.onecol_b = self.sb(st, "onecol", [128, 1], F32)
        self.op("vector", lambda e: e.memset(self.onecol[:], 1.0), [], [self.onecol_b])
        self.CB = [self.identf_b, self.identb_b, self.onesf_b, self.onesb_b, self.epscol_b]
        with ExitStack() as s2:
            it, itb = self.sb(s2, "c_it", [128, 128], I32)
            tf, tfb = self.sb(s2, "c_tf", [128, 128], F32)
            self.op("gpsimd", lambda e: e.iota(it[:], pattern=[[1, 128]], base=0, channel_multiplier=-1), [], [itb])
            self.cp("vector", tf[:], it[:], [itb], [tfb])
            self.ts(self.identf[:], tf[:], 0.0, None, ALU.is_equal, None, [tfb], [self.identf_b])
            self.cp("vector", self.identb[:], self.identf[:], [self.identf_b], [self.identb_b])
            self.op("vector", lambda e: e.memset(self.onesf[:], 1.0), [], [self.onesf_b])
            self.op("vector", lambda e: e.memset(self.onesb[:], 1.0), [], [self.onesb_b])
            self.op("vector", lambda e: e.memset(self.epscol[:], EPS), [], [self.epscol_b])
            self.barrier()
        self.psbig = st.enter_context(nc.psum_tensor("psbig", [128, 8, 512], F32))
        self.ps = [(self.psbig[:, i, :], Buf("ps%d" % i)) for i in range(8)]
        self.psi = 0

    def psn(self, lo=0, hi=8):
        i = lo + self.psi % (hi - lo)
        self.psi += 1
        return self.ps[i]

    def precast(self):
        L = self.L
        jobs = []
        for l in range(L):
            jobs.append((self.w_in[l], self.wb["w_in", l], DM, NIN))
            jobs.append((self.mla_wuq[l], self.wb["wuq", l], 384, 1536))
            jobs.append((self.mla_wukv[l], self.wb["wukv", l], 256, 2048))
            for i in range(4):
                jobs.append((self.w_br[i][l], self.wb["br%d" % i, l], DM, DM))
            jobs.append((self.w_out[l], self.wb["out", l], DM, DM))
            jobs.append((self.w_ff1[l], self.wb["ff1", l], DM, DFF))
            jobs.append((self.w_ff2[l], self.wb["ff2", l], DFF, DM))
            jobs.append((self.w_ple_gate[l], self.wb["pg", l], DM, DM))
            jobs.append((self.w_ple_proj[l], self.wb["pp", l], PLED, DM))
        with ExitStack() as s2:
            CW = 2048
            stg = self.sbpool(s2, "pc_f", [128, CW], F32, 3)
            stb = self.sbpool(s2, "pc_b", [128, CW], BF16, 3)
            i = 0
            for src, dst, K, N in jobs:
                for r0 in range(0, K, 128):
                    for c0 in range(0, N, CW):
                        cw = min(CW, N - c0)
                        f, fb = stg[i % 3]
                        b, bb = stb[i % 3]
                        self.dma("sync", f[:, 0:cw], src[r0:r0 + 128, c0:c0 + cw], [], [fb])
                        self.cp(self.alt(), b[:, 0:cw], f[:, 0:cw], [fb], [bb])
                        self.dma("gpsimd", dst[r0:r0 + 128, c0:c0 + cw], b[:, 0:cw], [bb], [])
                        i += 1
            self.barrier()

    def build_rope(self):
        NT = self.NT
        cfgs = ((8, 16, 500000.0), (32, 64, 500000.0), (32, 64, 10000.0))
        C1 = 6.28125
        C2 = 2.0 * math.pi - C1
        with ExitStack() as s2:
            pi_, pib = self.sb(s2, "r_pi", [128, NT], I32)
            pos, posb = self.sb(s2, "r_pos", [128, NT], F32)
            self.op("gpsimd", lambda e: e.iota(pi_[:], pattern=[[128, NT]], base=0, channel_multiplier=1), [], [pib])
            self.cp("vector", pos[:], pi_[:], [pib], [posb])
            for ci, (r2, rot, theta) in enumerate(cfgs):
                with ExitStack() as s3:
                    invf, invfb = self.sb(s3, "r_invf", [128, r2], F32)
                    ang, angb = self.sb(s3, "r_ang", [128, NT, r2], F32)
                    a, ab = self.sb(s3, "r_a", [128, NT, r2], F32)
                    kf, kfb = self.sb(s3, "r_kf", [128, NT, r2], F32)
                    ki, kib = self.sb(s3, "r_ki", [128, NT, r2], I32)
                    mk, mkb = self.sb(s3, "r_mk", [128, NT, r2], F32)
                    tab, tabb = self.sb(s3, "r_tab", [128, NT, 2, r2], F32)
                    for j in range(r2):
                        v = float(np.float32(theta) ** np.float32(-(2.0 * j) / rot))
                        self.op("vector", lambda e, j=j, v=v: e.memset(invf[:, j:j + 1], v), [], [invfb])
                    pos_b = bass.AP(pos[:].tensor, pos[:].offset, [list(pos[:].ap[0]), [1, NT], [0, r2]])
                    inv_b = bass.AP(invf[:].tensor, invf[:].offset, [list(invf[:].ap[0]), [0, NT], [1, r2]])
                    self.tt(ang[:], pos_b, inv_b, ALU.mult, [posb, invfb], [angb])
                    for which, shift in ((1, 0.0), (0, math.pi / 2)):
                        self.ts(a[:], ang[:], shift, None, ALU.add, None, [angb], [ab])
                        self.ts(kf[:], a[:], 1.0 / (2 * math.pi), None, ALU.mult, None, [ab], [kfb])
                        self.cp("vector", ki[:], kf[:], [kfb], [kib])
                        self.cp("vector", kf[:], ki[:], [kib], [kfb])
                        self.stt(a[:], kf[:], -C1, a[:], ALU.mult, ALU.add, [kfb, ab], [ab])
                        self.stt(a[:], kf[:], -C2, a[:], ALU.mult, ALU.add, [kfb, ab], [ab])
                        self.ts(mk[:], a[:], math.pi, None, ALU.is_gt, None, [ab], [mkb])
                        self.stt(a[:], mk[:], -2 * math.pi, a[:], ALU.mult, ALU.add, [mkb, ab], [ab])
                        self.ts(mk[:], a[:], -math.pi, None, ALU.is_lt, None, [ab], [mkb])
                        self.stt(a[:], mk[:], 2 * math.pi, a[:], ALU.mult, ALU.add, [mkb, ab], [ab])
                        self.ts(a[:], a[:], 3.1415925, -3.1415925, ALU.min, ALU.max, [ab], [ab])
                        self.act(tab[:, :, which, :], a[:], AF.Sin, [ab], [tabb])
                    self.dma("gpsimd", self.ropeD[ci].ap().rearrange("(t p) c j -> p t c j", p=128), tab[:],
                             [tabb], [self.dbuf("rope", ci)], slow=True)
                    self.barrier()

    def transpose_in(self):
        T = self.T
        with ExitStack() as s2:
            xt = self.sbpool(s2, "ti_x", [128, DM], F32, 3)
            stg = self.sbpool(s2, "ti_s", [128, 8, 512], F32, 2)
            n = 0
            for g in range(T // 512):
                sg, sgb = stg[g % 2]
                for ti in range(4):
                    t0 = g * 512 + ti * 128
                    x_, xb = xt[n % 3]
                    n += 1
                    self.dma("sync", x_[:], self.x[t0:t0 + 128, :], [], [xb])
                    for half in range(2):
                        pt, pb = self.psn()
                        for q in range(4):
                            kc = half * 4 + q
                            self.tp(pt[:, q * 128:(q + 1) * 128], x_[:, kc * 128:(kc + 1) * 128], self.identf[:],
                                    [xb, self.identf_b], [pb])
                        self.cp(self.alt(), sg[:, half * 4:half * 4 + 4, ti * 128:(ti + 1) * 128],
                                pt[:].rearrange("p (q t) -> p q t", q=4), [pb], [sgb])
                self.dma("gpsimd", self.xT[:, g * 512:(g + 1) * 512].rearrange("(kc p) t -> p kc t", p=128), sg[:],
                         [sgb], [self.dbuf("xT", g)])
            self.barrier()

    def transpose_out(self):
        T = self.T
        with ExitStack() as s2:
            xg = self.sbpool(s2, "to_x", [128, 8, 512], F32, 2)
            ot = self.sbpool(s2, "to_o", [128, DM], F32, 3)
            n = 0
            for g in range(T // 512):
                x_, xb = xg[g % 2]
                self.dma("sync", x_[:], self.xT[:, g * 512:(g + 1) * 512].rearrange("(kc p) t -> p kc t", p=128),
                         [self.dbuf("xT", g)], [xb])
                for ti in range(4):
                    o_, ob = ot[n % 3]
                    n += 1
                    for half in range(2):
                        pt, pb = self.psn()
                        for q in range(4):
                            kc = half * 4 + q
                            self.tp(pt[:, q * 128:(q + 1) * 128], x_[:, kc, ti * 128:(ti + 1) * 128], self.identf[:],
                                    [xb, self.identf_b], [pb])
                        self.cp(self.alt(), o_[:, half * 512:(half + 1) * 512], pt[:], [pb], [ob])
                    t0 = g * 512 + ti * 128
                    self.dma("gpsimd", self.out[t0:t0 + 128, :], o_[:], [ob], [self.dbuf("out", t0)])
            self.barrier()


    @staticmethod
    def bc_last(a, n):
        return bass.AP(a.tensor, a.offset, [list(x) for x in a.ap] + [[0, n]])

    @staticmethod
    def bc_col(a, n):
        return bass.AP(a.tensor, a.offset, [list(a.ap[0]), [0, n]])

    @staticmethod
    def bc_mid(a, n):
        ap = [list(x) for x in a.ap]
        return bass.AP(a.tensor, a.offset, [ap[0], [0, n]] + ap[1:])

    def load_cols(self, dst, dstb, src_ap, pattern, **kw):
        self.dma("sync", dst, src_ap.rearrange(pattern, **kw), [], [dstb], slow=True)

    def load_bc(self, dst, dstb, src_row_ap):
        self.dma("sync", dst, src_row_ap.broadcast_to([128, src_row_ap.shape[-1]]), [], [dstb])

    def psb(self, i):
        return self.psbig[:, i, :].bitcast(BF16)

    def norm_fm(self, stack_unused, col0, ntok, gcols, gcolsb, outT, outTb, xkeep=None, loaded=False):
        with ExitStack() as s2:
            if xkeep is None:
                xg_pool = self.sbpool(s2, "nf_x", [128, 8, 512], F32, 2)
            sq_pool = self.sbpool(s2, "nf_sq", [128, 512], BF16, 4)
            rs_pool = self.sbpool(s2, "nf_rs", [128, 512], F32, 2)
            n = 0
            ng = ntok // 512
            bsz = 2 if loaded else 1
            for g0 in range(0, ng, bsz):
                batch = []
                for g in range(g0, min(g0 + bsz, ng)):
                    c0 = col0 + g * 512
                    if xkeep is None:
                        xg, xgb = xg_pool[g % 2]
                        self.dma("sync", xg[:], self.xT[:, c0:c0 + 512].rearrange("(kc p) t -> p kc t", p=128),
                                 [self.dbuf("xT", c0 // 512)], [xgb])
                        xs = lambda kc, xg=xg: xg[:, kc, :]
                    else:
                        xk, xgb = xkeep
                        if not loaded:
                            self.dma("sync", xk[:, :, g * 512:(g + 1) * 512],
                                     self.xT[:, c0:c0 + 512].rearrange("(kc p) t -> p kc t", p=128),
                                     [self.dbuf("xT", c0 // 512)], [xgb])
                        xs = lambda kc, g=g, xk=xk: xk[:, kc, g * 512:(g + 1) * 512]
                    pt, pb = self.psn()
                    for kc in range(8):
                        sq, sqb = sq_pool[n % 4]
                        n += 1
                        self.act(sq[:], xs(kc), AF.Square, [xgb], [sqb])
                        self.mm(pt[:], self.onesb[:], sq[:], kc == 0, kc == 7, [sqb, self.onesb_b], [pb])
                    batch.append((g, xs, xgb, pt, pb))
                for g, xs, xgb, pt, pb in batch:
                    rs, rsb = rs_pool[g % 2]
                    self.rsqrt(rs[:], pt[:], 1.0 / DM, [pb], [rsb])
                for g, xs, xgb, pt, pb in batch:
                    rs, rsb = rs_pool[g % 2]
                    for kc in range(8):
                        self.stt(outT[:, kc, g * 512:(g + 1) * 512], xs(kc), gcols[:, kc:kc + 1], rs[:],
                                 ALU.mult, ALU.mult, [xgb, rsb, gcolsb], [outTb])

    def gemm(self, mode, AT, ATb, KC, ntok, W, col0, ncols, epi, wpool, k0=0, ps_lo=0, ps_hi=8, gi=2):
        wt, wtb = wpool[self.wi % len(wpool)]
        self.wi += 1
        ATl = ATb if isinstance(ATb, list) else [ATb]
        wv = wt[:, 0:KC * ncols].rearrange("p (k n) -> p k n", k=KC)
        self.dma("sync", wv, W[:, col0:col0 + ncols].rearrange("(kc p) n -> p kc n", p=128), [], [wtb])
        if mode == "tm":
            pend = []
            nt_ = ntok // 128
            for t in range(nt_):
                pt, pb = self.psn(ps_lo, ps_hi)
                for kc in range(KC):
                    self.mm(pt[:, 0:ncols], AT[:, k0 + kc, t * 128:(t + 1) * 128], wv[:, kc, :], kc == 0, kc == KC - 1,
                            ATl + [wtb], [pb])
                r = epi(pt, pb, t)
                if r is not None:
                    pend.append(r)
                if pend and (len(pend) == gi or t == nt_ - 1):
                    while pend:
                        for g_ in list(pend):
                            try:
                                next(g_)
                            except StopIteration:
                                pend.remove(g_)
        else:
            for m in range(ncols // 128):
                for g in range(ntok // 512):
                    pt, pb = self.psn(ps_lo, ps_hi)
                    for kc in range(KC):
                        self.mm(pt[:], wv[:, kc, m * 128:(m + 1) * 128], AT[:, k0 + kc, g * 512:(g + 1) * 512],
                                kc == 0, kc == KC - 1, ATl + [wtb], [pb])
                    epi(pt, pb, m, g)

    def norm_rope_g(self, v, vb, G, Dg, t, tmp, gain=None, gainb=None, normdim=None, ss_extra=None, rope=None):
        v3 = v.rearrange("p (g d) -> p g d", g=G)
        sq, sqb, ss, ssb, rt, rtb = tmp
        if gain is not None:
            W = G * Dg
            self.tt(sq[:, 0:W], v, v, ALU.mult, [vb], [sqb])
            yield
            self.op("vector", lambda e: e.tensor_reduce(out=ss[:, 0:G], in_=sq[:, 0:W].rearrange("p (g d) -> p g d", g=G),
                                                        axis=AX.X, op=ALU.add), [sqb], [ssb])
            yield
            if ss_extra is not None:
                ex, exb = ss_extra
                self.tt(ss[:, 0:G], ss[:, 0:G], ex, ALU.add, [ssb, exb], [ssb])
                yield
            self.rsqrt(ss[:, 0:G], ss[:, 0:G], 1.0 / normdim, [ssb], [ssb])
            yield
            self.tt(v3, v3, self.bc_last(ss[:, 0:G], Dg), ALU.mult, [vb, ssb], [vb])
            yield
            self.tt(v3, v3, self.bc_mid(gain, G), ALU.mult, [vb, gainb], [vb])
            yield
        if rope is not None:
            off, r2, tab, tabb = rope
            x1 = v3[:, :, off:off + r2]
            x2 = v3[:, :, off + r2:off + 2 * r2]
            cos = self.bc_mid(tab[:, t, 0, :], G)
            sin = self.bc_mid(tab[:, t, 1, :], G)
            n = G * r2
            tv = [rt[:, i * n:(i + 1) * n].rearrange("p (g r) -> p g r", g=G) for i in range(4)]
            self.tt(tv[0], x1, cos, ALU.mult, [vb, tabb], [rtb])
            yield
            self.tt(tv[1], x2, sin, ALU.mult, [vb, tabb], [rtb])
            yield
            self.tt(tv[2], x2, cos, ALU.mult, [vb, tabb], [rtb])
            yield
            self.tt(tv[3], x1, sin, ALU.mult, [vb, tabb], [rtb])
            yield
            self.tt(x1, tv[0], tv[1], ALU.subtract, [rtb], [vb])
            yield
            self.tt(x2, tv[2], tv[3], ALU.add, [rtb], [vb])
            yield

    def norm_rope(self, *a, **kw):
        for _ in self.norm_rope_g(*a, **kw):
            pass

    def tr_stage(self, src, srcb, blocks, tq, stage, stageb):
        i = self.psi % 8
        self.psi += 1
        pb = self.ps[i][1]
        pv = self.psb(i)
        for bi, (lo, w) in enumerate(blocks):
            self.tp(pv[0:w, bi * 128:(bi + 1) * 128], src[:, lo:lo + w], self.identb[:], [srcb, self.identb_b], [pb])
        nb = len(blocks)
        if all(w == 128 for _, w in blocks):
            self.cp(self.alt(), stage[:, 0:nb, tq * 128:(tq + 1) * 128],
                    pv[:, 0:nb * 128].rearrange("p (b t) -> p b t", b=nb), [pb], [stageb])
        else:
            for bi, (lo, w) in enumerate(blocks):
                self.cp(self.alt(), stage[0:w, bi, tq * 128:(tq + 1) * 128], pv[0:w, bi * 128:(bi + 1) * 128],
                        [pb], [stageb])

    def phase1(self, l, s):
        S, NT, NG = self.S, self.NT, self.NG
        tok0 = s * S
        Wd = self.wb["w_in", l]
        with ExitStack() as P:
            xnT, xnTb = self.sb(P, "xnT", [128, 8, S], BF16)
            g1, g1b = self.sb(P, "g1", [128, 8], F32)
            self.load_cols(g1[:], g1b, self.norm1_g[l], "(kc p) -> p kc", p=128)
            self.norm_fm(None, tok0, S, g1, g1b, xnT, xnTb)
            self.barrier()
            self.mark(" P1lru")
            wpool = self.sbpool(P, "w", [128, 4096], BF16, 2)
            o16 = self.sbpool(P, "o16", [128, 512], BF16, 3)
            self.oi = 0
            with ExitStack() as PA:
                self.lru(PA, l, s, xnT, xnTb, Wd, wpool)
                self.barrier()
                self.mark(" P1qkv")
            with ExitStack() as PQ:
                self.prep_qkv(PQ, l, s, xnT, xnTb, Wd, wpool, o16)
                self.barrier()

    def lru(self, PA, l, s, xnT, xnTb, Wd, wpool):
        S, NT, NG = self.S, self.NT, self.NG
        TC = min(1024, S)
        nch = S // TC
        cw, cwb = self.sb(PA, "cw", [128, 4, 8], F32)
        cb, cbb = self.sb(PA, "cb", [128, 8], F32)
        ba, bab = self.sb(PA, "ba", [128, 2, 8], F32)
        bx, bxb = self.sb(PA, "bx", [128, 2, 8], F32)
        lam, lamb = self.sb(PA, "lam", [128, 16], F32)
        hh, hhb = self.sb(PA, "hh", [128, 16], F32)
        cd, cdb = self.sb(PA, "cd", [128, 16], F32)
        cd2, cd2b = self.sb(PA, "cd2", [128, 16], F32)
        for t_ in range(4):
            self.load_cols(cw[:, t_, :], cwb, self.conv_w[l, t_], "(c p) -> p c", p=128)
        self.load_cols(cb[:], cbb, self.conv_b[l], "(c p) -> p c", p=128)
        for d_ in range(2):
            self.load_cols(ba[:, d_, :], bab, self.lru_ba[l, d_], "(c p) -> p c", p=128)
            self.load_cols(bx[:, d_, :], bxb, self.lru_bx[l, d_], "(c p) -> p c", p=128)
            self.load_cols(lam[:, d_ * 8:(d_ + 1) * 8], lamb, self.lru_lambda[l, d_], "(c p) -> p c", p=128)
        self.act(lam[:], lam[:], AF.Exp, [lamb], [lamb], scale=-1.0)
        self.ts(hh[:], lam[:], -1.0 / 6, 1.0 / 5, ALU.mult, ALU.add, [lamb], [hhb])
        for cst in (-1.0 / 4, 1.0 / 3, -1.0 / 2, 1.0):
            self.tt(hh[:], hh[:], lam[:], ALU.mult, [hhb, lamb], [hhb])
            self.ts(hh[:], hh[:], cst, None, ALU.add, None, [hhb], [hhb])
        self.tt(hh[:], hh[:], lam[:], ALU.mult, [hhb, lamb], [hhb])
        self.ts(cd[:], hh[:], -8.0, None, ALU.mult, None, [hhb], [cdb])
        self.ts(cd2[:], hh[:], -16.0, None, ALU.mult, None, [hhb], [cd2b])
        wst, wstb = self.sb(PA, "wst", [128, 4, 128], F32)
        wbf, wbfb = self.sb(PA, "wbf", [128, 4, 128], BF16)
        self.op("vector", lambda e: e.memset(wst[:], 0.0), [], [wstb])
        Pb, Pbb = self.sb(PA, "Pb", [128, S + 4], F32)
        gg, ggb = self.sb(PA, "gg", [128, S], BF16)
        xc, xcb_ = self.sb(PA, "xc", [128, S], F32)
        x16, x16b = self.sb(PA, "x16", [128, S], BF16)
        Bsets = []
        for d_ in range(2):
            B1, B1b = self.sb(PA, "B1%d" % d_, [128, TC], F32)
            B2, B2b = self.sb(PA, "B2%d" % d_, [128, TC], F32)
            B3, B3b = self.sb(PA, "B3%d" % d_, [128, TC], F32)
            Bsets.append((B1, B1b, B2, B2b, B3, B3b))
        hb, hbb = self.sb(PA, "hb", [128, S], F32)
        tA = self.sbpool(PA, "tA", [128, 512], F32, 2)
        self.op("vector", lambda e: e.memset(Pb[:, 0:2], 0.0), [], [Pbb])
        self.op("vector", lambda e: e.memset(Pb[:, S + 2:S + 4], 0.0), [], [Pbb])
        wsrc = (self.lru_wa, self.lru_wx)
        for c in range(8):
            for wi in range(2):
                for d in range(2):
                    for half in range(2):
                        self.dma("sync", wst[half * 64:(half + 1) * 64, wi * 2 + d, half * 64:(half + 1) * 64],
                                 wsrc[wi][l, d, 2 * c + half], [], [wstb])
            self.cp("vector", wbf[:], wst[:], [wstb], [wbfb])

            def epi_x(pt, pb, m, g):
                self.cp(self.alt(), Pb[:, 2 + g * 512:2 + (g + 1) * 512], pt[:], [pb], [Pbb])

            def epi_g(pt, pb, m, g):
                t1, t1b = tA[g % 2]
                self.act(t1[:], pt[:], AF.Square, [pb], [t1b])
                self.ts(t1[:], t1[:], 0.044715, 1.0, ALU.mult, ALU.add, [t1b], [t1b])
                self.tt(t1[:], t1[:], pt[:], ALU.mult, [t1b, pb], [t1b])
                self.act(t1[:], t1[:], AF.Sigmoid, [t1b], [t1b], scale=1.5957691216057308)
                self.tt(gg[:, g * 512:(g + 1) * 512], t1[:], pt[:], ALU.mult, [t1b, pb], [ggb])

            self.gemm("fm", xnT, xnTb, 8, S, Wd, OFF["ax"] + c * 128, 128, epi_x, wpool)
            self.gemm("fm", xnT, xnTb, 8, S, Wd, OFF["ag"] + c * 128, 128, epi_g, wpool)
            self.ts(xc[:], Pb[:, 0:S], cw[:, 0, c:c + 1], cb[:, c:c + 1], ALU.mult, ALU.add, [Pbb, cwb, cbb], [xcb_])
            for j in range(1, 4):
                self.stt(xc[:], Pb[:, j:j + S], cw[:, j, c:c + 1], xc[:], ALU.mult, ALU.add, [Pbb, cwb, xcb_], [xcb_])
            self.cp("scalar", x16[:], xc[:], [xcb_], [x16b])
            hs = Pb

            def dgen(d, c=c):
                D1, D1b, D2, D2b, D3, D3b = Bsets[d]
                order = range(nch) if d == 0 else range(nch - 1, -1, -1)
                first = True
                for ch in order:
                    t0 = ch * TC
                    for sub in range(TC // 512):
                        cs = slice(t0 + sub * 512, t0 + (sub + 1) * 512)
                        bs = slice(sub * 512, (sub + 1) * 512)
                        pt, pb = self.psn()
                        self.mm(pt[:], wbf[:, 0 + d, :], x16[:, cs], True, True, [wbfb, x16b], [pb])
                        self.act(D1[:, bs], pt[:], AF.Sigmoid, [pb, bab], [D1b], bias=ba[:, d, c:c + 1])
                        yield
                        pt, pb = self.psn()
                        self.mm(pt[:], wbf[:, 2 + d, :], x16[:, cs], True, True, [wbfb, x16b], [pb])
                        self.act(D3[:, bs], pt[:], AF.Sigmoid, [pb, bxb], [D3b], bias=bx[:, d, c:c + 1])
                        yield
                    k = d * 8 + c
                    self.act(D2[:], D1[:], AF.Exp, [D1b, cd2b], [D2b], scale=cd2[:, k:k + 1])
                    yield
                    self.act(D2[:], D2[:], AF.Relu, [D2b], [D2b], scale=-1.0, bias=self.onecol[:, :])
                    yield
                    self.act(D2[:], D2[:], AF.Sqrt, [D2b], [D2b])
                    yield
                    self.act(D1[:], D1[:], AF.Exp, [D1b, cdb], [D1b], scale=cd[:, k:k + 1])
                    yield
                    self.tt(D3[:], D3[:], xc[:, t0:t0 + TC], ALU.mult, [D3b, xcb_], [D3b])
                    yield
                    self.tt(D3[:], D3[:], D2[:], ALU.mult, [D3b, D2b], [D3b])
                    yield
                    if d == 0:
                        init = 0.0 if first else hs[:, 2 + t0 - 1:2 + t0]
                        self.op("vector", lambda e, t0=t0, init=init: e.tensor_tensor_scan(
                            out=hs[:, 2 + t0:2 + t0 + TC], data0=D1[:], data1=D3[:], initial=init,
                            op0=ALU.mult, op1=ALU.add), [D1b, D3b, Pbb], [Pbb])
                    else:
                        init = 0.0 if first else hb[:, t0 + TC:t0 + TC + 1]
                        ov = hb[:, t0:t0 + TC]
                        orev = bass.AP(ov.tensor, ov.offset + (TC - 1), [list(ov.ap[0]), [-1, TC]])
                        self.op("vector", lambda e, init=init, orev=orev: e.tensor_tensor_scan(
                            out=orev, data0=D1[:, ::-1], data1=D3[:, ::-1], initial=init,
                            op0=ALU.mult, op1=ALU.add), [D1b, D3b, hbb], [hbb])
                    yield
                    first = False

            gens = [dgen(0), dgen(1)]
            while gens:
                for g_ in list(gens):
                    try:
                        next(g_)
                    except StopIteration:
                        gens.remove(g_)
            self.tt(hs[:, 2:S + 2], hs[:, 2:S + 2], hb[:], ALU.add, [Pbb, hbb], [Pbb])
            self.tt(x16[:], hs[:, 2:S + 2], gg[:], ALU.mult, [Pbb, ggb, x16b], [x16b])
            self.dma("gpsimd", self.YT[c * 128:(c + 1) * 128, :], x16[:], [x16b], [self.dbuf("YT", 0, c)])


    def prep_qkv(self, PQ, l, s, xnT, xnTb, Wd, wpool, o16):
        S, NT, NG = self.S, self.NT, self.NG
        def bct(name, src, n):
            t, b = self.sb(PQ, name, [128, n], F32)
            self.load_bc(t[:], b, src)
            return t, b
        qg_b, qg_bb = bct("dqg", self.diff_q_g[l:l + 1, :], 64)
        kg_b, kg_bb = bct("dkg", self.diff_k_g[l:l + 1, :], 64)
        qa_g, qa_gb = bct("mqa", self.mla_qa_g[l:l + 1, :], 384)
        kva_g, kva_gb = bct("mkva", self.mla_kva_g[l:l + 1, :], 256)
        mq_g, mq_gb = bct("mqg", self.mla_q_g[l:l + 1, :], 192)
        mk_g, mk_gb = bct("mkg", self.mla_k_g[l:l + 1, :], 192)
        gb, gbb = self.sb(PQ, "gateb", [128, 4, 8], F32)
        for b_ in range(4):
            self.load_cols(gb[:, b_, :], gbb, self.gate_b[l, b_], "(m p) -> p m", p=128)
        ropes = []
        for ci, r2 in enumerate((8, 32, 32)):
            t, b = self.sb(PQ, "rope%d" % ci, [128, NT, 2, r2], F32)
            self.dma("sync", t[:], self.ropeD[ci].ap().rearrange("(t p) c j -> p t c j", p=128),
                     [self.dbuf("rope", ci)], [b], slow=True)
            ropes.append((t, b))
        cqnT, cqnTb = self.sb(PQ, "cqnT", [128, 3, S], BF16)
        ckvnT, ckvnTb = self.sb(PQ, "ckvnT", [128, 2, S], BF16)
        kper, kperb = self.sb(PQ, "kper", [128, NT, 64], F32)
        sspe, sspeb = self.sb(PQ, "sspe", [128, NT], F32)
        NV = KDEFER + 4
        vpool = self.sbpool(PQ, "v", [128, 512], F32, NV)
        v16pool = self.sbpool(PQ, "v16", [128, 512], BF16, NV)
        dq = []

        def defer(fn):
            dq.append(fn)
            while len(dq) > KDEFER:
                dq.pop(0)()

        def flush():
            while dq:
                dq.pop(0)()
        tmps = []
        for i_ in range(3):
            sq_, sqb_ = self.sb(PQ, "sq%d" % i_, [128, 512], F32)
            ss_, ssb_ = self.sb(PQ, "ss%d" % i_, [128, 8], F32)
            rt_, rtb_ = self.sb(PQ, "rt%d" % i_, [128, 1024], F32)
            tmps.append((sq_, sqb_, ss_, ssb_, rt_, rtb_))
        tmp = tmps[0]
        sq, sqb, ss, ssb, rt, rtb = tmp
        stages = self.sbpool(PQ, "stg", [128, 4, 512], BF16, 2)
        st_i = [0]
        vi = [0]

        def nextv():
            r = vpool[vi[0] % NV] + v16pool[vi[0] % NV]
            vi[0] += 1
            return r

        def store_stage(stage, stageb, dests, g):
            for bi, (dt_, r0, nr) in enumerate(dests):
                self.dma("gpsimd", dt_[r0:r0 + nr, g * 512:(g + 1) * 512], stage[0:nr, bi, :], [stageb],
                         [self.dbuf(dt_.name, r0, g)])

        def seg_qk(AT, ATb, KC, W, col0, ncols, G, Dg, gain, gainb, normdim, rope, blocks, dests_fn, scale=None, gi=2):
            state = {}

            def epi(pt, pb, t):
                v, vb, v16, v16b = nextv()
                self.cp("scalar", v[:, 0:ncols], pt[:, 0:ncols], [pb], [vb])
                yield
                for _ in self.norm_rope_g(v[:, 0:ncols], vb, G, Dg, t, tmps[t % gi], gain=gain, gainb=gainb,
                                          normdim=normdim, rope=rope):
                    yield
                if scale is None:
                    self.cp("scalar", v16[:, 0:ncols], v[:, 0:ncols], [vb], [v16b])
                else:
                    self.act(v16[:, 0:ncols], v[:, 0:ncols], AF.Copy, [vb], [v16b], scale=scale)
                def later(t=t, v16=v16, v16b=v16b):
                    if t % 4 == 0:
                        state["st"] = stages[st_i[0] % 2]
                        st_i[0] += 1
                    stage, stageb = state["st"]
                    self.tr_stage(v16, v16b, blocks, t % 4, stage, stageb)
                    if t % 4 == 3:
                        store_stage(stage, stageb, dests_fn(), t // 4)
                defer(later)
            self.gemm("tm", AT, ATb, KC, S, W, col0, ncols, epi, wpool, gi=gi)

        def seg_v(col0, dst, dcol0):
            def epi(pt, pb, t):
                o, ob = o16[self.oi % 3]
                self.oi += 1
                self.cp(self.alt(), o[:], pt[:], [pb], [ob])
                self.dma("gpsimd", dst[t * 128:(t + 1) * 128, dcol0:dcol0 + 512], o[:], [ob],
                         [self.dbuf(dst.name, t, dcol0)])
            self.gemm("tm", xnT, xnTb, 8, S, Wd, col0, 512, epi, wpool)

        b4 = [(i * 128, 128) for i in range(4)]
        for nt in range(2):
            seg_qk(xnT, xnTb, 8, Wd, OFF["bq"] + nt * 512, 512, 8, 64, qg_b[:], qg_bb, 64,
                   (0, 8, ropes[0][0], ropes[0][1]), b4,
                   lambda nt=nt: [(self.QTb, nt * 512 + i * 128, 128) for i in range(4)])
            seg_qk(xnT, xnTb, 8, Wd, OFF["bk"] + nt * 512, 512, 8, 64, kg_b[:], kg_bb, 64,
                   (0, 8, ropes[0][0], ropes[0][1]), b4,
                   lambda nt=nt: [(self.KTb, nt * 512 + i * 128, 128) for i in range(4)])
            seg_v(OFF["bv"] + nt * 512, self.Vb, nt * 512)
            seg_v(OFF["dv"] + nt * 512, self.Vd, nt * 512)
        self.mark("  q:dqdk")
        seg_qk(xnT, xnTb, 8, Wd, OFF["dq"], 512, 8, 64, None, None, None, (0, 32, ropes[2][0], ropes[2][1]), b4,
               lambda: [(self.QTd, i * 128, 128) for i in range(4)])
        seg_qk(xnT, xnTb, 8, Wd, OFF["dk"], 512, 8, 64, None, None, None, (0, 32, ropes[2][0], ropes[2][1]), b4,
               lambda: [(self.KTd, i * 128, 128) for i in range(4)], scale=0.125)

        self.mark("  q:cq")
        def epi_cq(pt, pb, t):
            v, vb, v16, v16b = nextv()
            self.cp("scalar", v[:, 0:384], pt[:, 0:384], [pb], [vb])
            yield
            for _ in self.norm_rope_g(v[:, 0:384], vb, 1, 384, t, tmps[t % 2], gain=qa_g[:], gainb=qa_gb, normdim=384):
                yield
            self.cp("scalar", v16[:, 0:384], v[:, 0:384], [vb], [v16b])

            def later(t=t, v16=v16, v16b=v16b):
                i = self.psi % 8
                self.psi += 1
                pv = self.psb(i)
                for bi in range(3):
                    self.tp(pv[:, bi * 128:(bi + 1) * 128], v16[:, bi * 128:(bi + 1) * 128], self.identb[:],
                            [v16b, self.identb_b], [self.ps[i][1]])
                self.cp(self.alt(), cqnT[:, :, t * 128:(t + 1) * 128], pv[:, 0:384].rearrange("p (b t) -> p b t", b=3),
                        [self.ps[i][1]], [cqnTb])
            defer(later)
        self.gemm("tm", xnT, xnTb, 8, S, Wd, OFF["cq"], 384, epi_cq, wpool)

        def epi_ckv(pt, pb, t):
            v, vb, v16, v16b = nextv()
            sq, sqb, ss, ssb, rt, rtb = tmps[t % 2]
            self.cp("scalar", v[:, 0:320], pt[:, 0:320], [pb], [vb])
            yield
            self.tt(sq[:, 0:64], v[:, 256:320], v[:, 256:320], ALU.mult, [vb], [sqb])
            yield
            self.op("vector", lambda e: e.tensor_reduce(out=sspe[:, t:t + 1], in_=sq[:, 0:64], axis=AX.X, op=ALU.add),
                    [sqb], [sspeb])
            yield
            self.tt(v[:, 256:320], v[:, 256:320], mk_g[:, 128:192], ALU.mult, [vb, mk_gb], [vb])
            yield
            for _ in self.norm_rope_g(v[:, 256:320], vb, 1, 64, t, tmps[t % 2], rope=(0, 32, ropes[1][0], ropes[1][1])):
                yield
            self.cp("vector", kper[:, t, :], v[:, 256:320], [vb], [kperb])
            yield
            for _ in self.norm_rope_g(v[:, 0:256], vb, 1, 256, t, tmps[t % 2], gain=kva_g[:], gainb=kva_gb, normdim=256):
                yield
            self.cp("scalar", v16[:, 0:256], v[:, 0:256], [vb], [v16b])

            def later(t=t, v16=v16, v16b=v16b):
                i = self.psi % 8
                self.psi += 1
                pv = self.psb(i)
                for bi in range(2):
                    self.tp(pv[:, bi * 128:(bi + 1) * 128], v16[:, bi * 128:(bi + 1) * 128], self.identb[:],
                            [v16b, self.identb_b], [self.ps[i][1]])
                self.cp(self.alt(), ckvnT[:, :, t * 128:(t + 1) * 128], pv[:, 0:256].rearrange("p (b t) -> p b t", b=2),
                        [self.ps[i][1]], [ckvnTb])
            defer(later)
        self.gemm("tm", xnT, xnTb, 8, S, Wd, OFF["ckv"], 320, epi_ckv, wpool)

        self.mark("  q:fm")
        def seg_fm(col0, func, bias_fn, dst, row0):
            def epi(pt, pb, m, g):
                o, ob = o16[self.oi % 3]
                self.oi += 1
                b = bias_fn(m)
                if b is None:
                    self.act(o[:], pt[:], func, [pb], [ob])
                else:
                    self.act(o[:], pt[:], func, [pb, gbb], [ob], bias=b)
                r0 = row0 + m * 128
                self.dma("gpsimd", dst[r0:r0 + 128, g * 512:(g + 1) * 512], o[:], [ob], [self.dbuf(dst.name, r0, g)])
            self.gemm("fm", xnT, xnTb, 8, S, Wd, col0, 512, epi, wpool)
        for nt in range(2):
            seg_fm(OFF["dg"] + nt * 512, AF.Silu, lambda m: None, self.GdT, nt * 512)
        for b_ in range(4):
            for nt in range(2):
                seg_fm(OFF["gl"] + b_ * 1024 + nt * 512, AF.Sigmoid,
                       lambda m, b_=b_, nt=nt: gb[:, b_, nt * 4 + m:nt * 4 + m + 1], self.GT, b_ * 1024 + nt * 512)

        self.mark("  q:qup")
        flush()
        Wq = self.wb["wuq", l]
        Wkv = self.wb["wukv", l]
        bq = [(0, 128), (128, 64), (192, 128), (320, 64)]
        for hp in range(4):
            seg_qk(cqnT, cqnTb, 3, Wq, hp * 384, 384, 2, 192, mq_g[:], mq_gb, 192,
                   (128, 32, ropes[1][0], ropes[1][1]), bq,
                   lambda hp=hp: [(self.QTc, (2 * hp) * 192, 128), (self.QTc, (2 * hp) * 192 + 128, 64),
                                  (self.QTc, (2 * hp + 1) * 192, 128), (self.QTc, (2 * hp + 1) * 192 + 128, 64)], gi=3)
        self.mark("  q:kvup")
        for hp in range(4):
            state = {}

            def epi_kv(pt, pb, t, hp=hp, state=state):
                v, vb, v16, v16b = nextv()
                sq, sqb, ss, ssb, rt, rtb = tmps[t % 3]
                self.cp("scalar", v[:], pt[:], [pb], [vb])
                yield
                v4 = v[:].rearrange("p (h c d) -> p h c d", h=2, c=2)
                kn = v4[:, :, 0, :]
                s4 = sq[:].rearrange("p (h c d) -> p h c d", h=2, c=2)
                self.tt(s4[:, :, 0, :], kn, kn, ALU.mult, [vb], [sqb])
                yield
                self.op("vector", lambda e: e.tensor_reduce(out=ss[:, 0:2], in_=s4[:, :, 0, :], axis=AX.X, op=ALU.add),
                        [sqb], [ssb])
                yield
                self.tt(ss[:, 0:2], ss[:, 0:2], self.bc_col(sspe[:, t:t + 1], 2),
                        ALU.add, [ssb, sspeb], [ssb])
                yield
                self.rsqrt(ss[:, 0:2], ss[:, 0:2], 1.0 / 192, [ssb], [ssb])
                yield
                self.tt(kn, kn, self.bc_last(ss[:, 0:2], 128), ALU.mult, [vb, ssb], [vb])
                yield
                self.tt(kn, kn, self.bc_mid(mk_g[:, 0:128], 2), ALU.mult, [vb, mk_gb], [vb])
                yield
                v16v = v16[:, 0:256].rearrange("p (h d) -> p h d", h=2)
                self.cp("scalar", v16v, kn, [vb], [v16b])
                yield
                pe = v16[:, 256:384].rearrange("p (h d) -> p h d", h=2)
                self.tt(pe, self.bc_mid(kper[:, t, :], 2), self.bc_last(ss[:, 0:2], 64), ALU.mult,
                        [kperb, ssb], [v16b])
                yield
                o, ob = o16[self.oi % 3]
                self.oi += 1
                ov = o[:, 0:256].rearrange("p (h d) -> p h d", h=2)
                self.cp("vector", ov, v4[:, :, 1, :], [vb], [ob])
                self.dma("gpsimd", self.Vc[t * 128:(t + 1) * 128, hp * 256:(hp + 1) * 256], o[:, 0:256], [ob],
                         [self.dbuf("Vc", t, hp)])
                def later(t=t, v16=v16, v16b=v16b):
                    if t % 4 == 0:
                        state["st"] = stages[st_i[0] % 2]
                        st_i[0] += 1
                    stage, stageb = state["st"]
                    self.tr_stage(v16, v16b, [(0, 128), (128, 128), (256, 64), (320, 64)], t % 4, stage, stageb)
                    if t % 4 == 3:
                        store_stage(stage, stageb, [(self.KTc, (2 * hp) * 192, 128), (self.KTc, (2 * hp + 1) * 192, 128),
                                                    (self.KTc, (2 * hp) * 192 + 128, 64),
                                                    (self.KTc, (2 * hp + 1) * 192 + 128, 64)], t // 4)
                defer(later)
            self.gemm("tm", ckvnT, ckvnTb, 2, S, Wkv, hp * 512, 512, epi_kv, wpool, gi=3)
        flush()


    def load_rows(self, dst, dstb, src, r0, nr):
        self.dma("sync", dst[0:nr, :], src[r0:r0 + nr, :], [], [dstb])

    def load_v(self, dst, dstb, src, c0):
        self.dma("sync", dst[:], src[:, c0:c0 + 128].rearrange("(t p) e -> p t e", p=128), [], [dstb])

    def phaseB(self, l, s):
        S, NT, NG = self.S, self.NT, self.NG
        lam_init = 0.8 - 0.6 * math.exp(-0.3 * l)
        with ExitStack() as P:
            lp, lpb = self.sb(P, "lp", [128, 256], F32)
            self.load_bc(lp[:], lpb, self.diff_lam[l:l + 1].rearrange("a b c -> a (b c)"))
            pr, prb = self.sb(P, "pr", [128, 128], F32)
            e2, e2b = self.sb(P, "e2", [128, 2], F32)
            neglam, neglamb = self.sb(P, "neglam", [128, 1], F32)
            gcol, gcolb = self.sb(P, "gcol", [128, 1], F32)
            lp4 = lp[:].rearrange("p (a b c) -> p a b c", a=2, b=2)
            self.tt(pr[:].rearrange("p (a c) -> p a c", a=2), lp4[:, :, 0, :], lp4[:, :, 1, :], ALU.mult, [lpb], [prb])
            self.op("vector", lambda e: e.tensor_reduce(out=e2[:], in_=pr[:].rearrange("p (a c) -> p a c", a=2),
                                                        axis=AX.X, op=ALU.add), [prb], [e2b])
            self.act(e2[:], e2[:], AF.Exp, [e2b], [e2b])
            self.tt(neglam[:], e2[:, 1:2], e2[:, 0:1], ALU.subtract, [e2b], [neglamb])
            self.ts(neglam[:], neglam[:], -lam_init, None, ALU.add, None, [neglamb], [neglamb])
            self.load_cols(gcol[:], gcolb, self.diff_sub_g[l], "(p o) -> p o", o=1)
            self.ts(gcol[:], gcol[:], 1.0 - lam_init, None, ALU.mult, None, [gcolb], [gcolb])
            QT = self.sbpool(P, "QT", [128, S], BF16, 2)
            KT = self.sbpool(P, "KT", [128, S], BF16, 2)
            V = self.sbpool(P, "V", [128, NT, 128], BF16, 2)
            Pt = self.sbpool(P, "Pt", [128, 512], BF16, 4)
            tf = self.sbpool(P, "tf", [128, 512], F32, 6)
            y16 = self.sbpool(P, "y16", [128, 512], BF16, 2)
            pi = 0
            for h in range(8):
                q, qb = QT[h % 2]
                k, kb = KT[h % 2]
                v, vb = V[h % 2]
                self.load_rows(q, qb, self.QTb, h * 128, 128)
                self.load_rows(k, kb, self.KTb, h * 128, 128)
                self.load_v(v, vb, self.Vb, h * 128)
                for qg in range(NG):
                    qs = slice(qg * 512, (qg + 1) * 512)
                    O = (self.ps[0], self.ps[1])
                    Lp = (self.ps[2], self.ps[3])
                    for kt in range(NT):
                        ks = slice(kt * 128, (kt + 1) * 128)
                        for si in range(2):
                            lo, hi = si * 64, si * 64 + 64
                            sp, spb = self.psn(4, 8)
                            self.mm(sp[:], k[lo:hi, ks], q[lo:hi, qs], True, True, [kb, qb], [spb])
                            p_, p_b = Pt[pi % 4]
                            pi += 1
                            self.act(p_[:], sp[:], AF.Exp, [spb], [p_b], scale=0.125)
                            self.mm(O[si][0][:], v[:, kt, :], p_[:], kt == 0, kt == NT - 1, [vb, p_b], [O[si][1]])
                            self.mm(Lp[si][0][:], self.onesb[:], p_[:], kt == 0, kt == NT - 1, [self.onesb_b, p_b],
                                    [Lp[si][1]])
                    o1, o1b = tf[0]
                    o2, o2b = tf[1]
                    r1, r1b = tf[2]
                    r2, r2b = tf[3]
                    oo, oob = tf[4]
                    rs, rsb = tf[5]
                    self.cp("scalar", o1[:], O[0][0][:], [O[0][1]], [o1b])
                    self.cp("vector", o2[:], O[1][0][:], [O[1][1]], [o2b])
                    self.act(r1[:], Lp[0][0][:], AF.Ln, [Lp[0][1]], [r1b])
                    self.act(r2[:], Lp[1][0][:], AF.Ln, [Lp[1][1]], [r2b])
                    self.act(r1[:], r1[:], AF.Exp, [r1b], [r1b], scale=-1.0)
                    self.act(r2[:], r2[:], AF.Exp, [r2b], [r2b], scale=-1.0)
                    self.tt(o1[:], o1[:], r1[:], ALU.mult, [o1b, r1b], [o1b])
                    self.tt(o2[:], o2[:], r2[:], ALU.mult, [o2b, r2b], [o2b])
                    self.stt(oo[:], o2[:], neglam[:, 0:1], o1[:], ALU.mult, ALU.add, [o2b, o1b, neglamb], [oob])
                    sq, sqb = Pt[pi % 4]
                    pi += 1
                    self.act(sq[:], oo[:], AF.Square, [oob], [sqb])
                    sp, spb = self.psn(4, 8)
                    self.mm(sp[:], self.onesb[:], sq[:], True, True, [self.onesb_b, sqb], [spb])
                    self.rsqrt(rs[:], sp[:], 1.0 / 128, [spb], [rsb])
                    y, yb = y16[(h * NG + qg) % 2]
                    self.stt(y[:], oo[:], gcol[:, 0:1], rs[:], ALU.mult, ALU.mult, [oob, rsb, gcolb], [yb])
                    self.dma("gpsimd", self.YT[1024 + h * 128:1024 + (h + 1) * 128, qs], y[:], [yb],
                             [self.dbuf("YT", 1, h, qg)])
            self.barrier()

    def phaseC(self, l, s):
        S, NT, NG = self.S, self.NT, self.NG
        sc = 192.0 ** -0.5
        with ExitStack() as P:
            Qn = self.sbpool(P, "Qn", [128, S], BF16, 2)
            Qp = self.sbpool(P, "Qp", [64, S], BF16, 2)
            Kn = self.sbpool(P, "Kn", [128, S], BF16, 2)
            Kp = self.sbpool(P, "Kp", [64, S], BF16, 2)
            V = self.sbpool(P, "V", [128, NT, 128], BF16, 2)
            Pt = self.sbpool(P, "Pt", [128, 512], BF16, 4)
            tf = self.sbpool(P, "tf", [128, 512], F32, 4)
            y16 = self.sbpool(P, "y16", [128, 512], BF16, 2)
            pi = 0
            ti = 0
            for h in range(8):
                qn, qnb = Qn[h % 2]
                qp, qpb = Qp[h % 2]
                kn, knb = Kn[h % 2]
                kp, kpb = Kp[h % 2]
                v, vb = V[h % 2]
                self.load_rows(qn, qnb, self.QTc, h * 192, 128)
                self.load_rows(qp, qpb, self.QTc, h * 192 + 128, 64)
                self.load_rows(kn, knb, self.KTc, h * 192, 128)
                self.load_rows(kp, kpb, self.KTc, h * 192 + 128, 64)
                self.load_v(v, vb, self.Vc, h * 128)
                for qg in range(NG):
                    qs = slice(qg * 512, (qg + 1) * 512)
                    O, Ob = self.ps[qg % 2 * 2]
                    Lt, Lb = self.ps[qg % 2 * 2 + 1]
                    for kt in range(NT):
                        ks = slice(kt * 128, (kt + 1) * 128)
                        sp, spb = self.psn(4, 8)
                        self.mm(sp[:], kn[:, ks], qn[:, qs], True, False, [knb, qnb], [spb])
                        self.mm(sp[:], kp[:, ks], qp[:, qs], False, True, [kpb, qpb], [spb])
                        p_, p_b = Pt[pi % 4]
                        pi += 1
                        self.act(p_[:], sp[:], AF.Exp, [spb], [p_b], scale=sc)
                        self.mm(O[:], v[:, kt, :], p_[:], kt == 0, kt == NT - 1, [vb, p_b], [Ob])
                        self.mm(Lt[:], self.onesb[:], p_[:], kt == 0, kt == NT - 1, [self.onesb_b, p_b], [Lb])
                    o1, o1b = tf[ti % 4]
                    r1, r1b = tf[(ti + 1) % 4]
                    ti += 2
                    self.cp("vector", o1[:], O[:], [Ob], [o1b])
                    self.act(r1[:], Lt[:], AF.Ln, [Lb], [r1b])
                    self.act(r1[:], r1[:], AF.Exp, [r1b], [r1b], scale=-1.0)
                    y, yb = y16[(h * NG + qg) % 2]
                    self.tt(y[:], o1[:], r1[:], ALU.mult, [o1b, r1b], [yb])
                    self.dma("gpsimd", self.YT[2048 + h * 128:2048 + (h + 1) * 128, qs], y[:], [yb],
                             [self.dbuf("YT", 2, h, qg)])
            self.barrier()

    def phaseD(self, l, s):
        S, NT, NG = self.S, self.NT, self.NG
        mmax = max(4 * (NG - 1), 1)
        nneg = max(NT - 4, 1)
        with ExitStack() as P:
            gncol, gncolb = self.sb(P, "gncol", [128, 1], F32)
            self.load_cols(gncol[:], gncolb, self.ret_gn_g[l], "(p o) -> p o", o=1)
            ii, iib = self.sb(P, "ii", [128, 512], I32)
            J, Jb = self.sb(P, "J", [128, 512], F32)
            Jr, Jrb = self.sb(P, "Jr", [128, 512], F32)
            A, Ab = self.sb(P, "A", [128, 4, 512], F32)
            pbase, pbaseb = self.sb(P, "pbase", [128, mmax], F32)
            nbase, nbaseb = self.sb(P, "nbase", [128, nneg], F32)
            self.op("gpsimd", lambda e: e.iota(ii[:], pattern=[[1, 512]], base=0, channel_multiplier=0), [], [iib])
            self.cp("vector", J[:], ii[:], [iib], [Jb])
            self.ts(Jr[:], J[:], -1.0, 511.0, ALU.mult, ALU.add, [Jb], [Jrb])
            for m in range(4):
                self.op("gpsimd", lambda e, m=m: e.iota(ii[:], pattern=[[1, 512]], base=-128 * m, channel_multiplier=-1),
                        [iib], [iib])
                self.cp("vector", A[:, m, :], ii[:], [iib], [Ab])
            self.act(A[:], A[:], AF.Abs, [Ab], [Ab])
            self.op("gpsimd", lambda e: e.iota(ii[:, 0:mmax], pattern=[[128, mmax]], base=128, channel_multiplier=-1),
                    [iib], [iib])
            self.cp("vector", pbase[:], ii[:, 0:mmax], [iib], [pbaseb])
            self.op("gpsimd", lambda e: e.iota(ii[:, 0:nneg], pattern=[[128, nneg]], base=1, channel_multiplier=1),
                    [iib], [iib])
            self.cp("vector", nbase[:], ii[:, 0:nneg], [iib], [nbaseb])
            rowp, rowpb = self.sb(P, "rowp", [128, 512], F32)
            rown, rownb = self.sb(P, "rown", [128, 512], F32)
            Dt, Dtb = self.sb(P, "Dt", [128, 4, 512], F32)
            cfp, cfpb = self.sb(P, "cfp", [128, mmax], F32)
            cfn, cfnb = self.sb(P, "cfn", [128, nneg], F32)
            QT = self.sbpool(P, "QT", [64, S], BF16, 2)
            KT = self.sbpool(P, "KT", [64, S], BF16, 2)
            V = self.sbpool(P, "V", [128, NT, 128], BF16, 2)
            Pt = self.sbpool(P, "Pt", [128, 512], BF16, 4)
            tf = self.sbpool(P, "tf", [128, 512], F32, 6)
            sgp = self.sbpool(P, "sg", [128, 512], BF16, 2)
            y16 = self.sbpool(P, "y16", [128, 512], BF16, 2)
            pi = 0
            for h in range(8):
                lg = math.log1p(-2.0 ** (-5.0 - h))
                self.act(rowp[:], J[:], AF.Exp, [Jb], [rowpb], scale=lg)
                self.act(rown[:], Jr[:], AF.Exp, [Jrb], [rownb], scale=lg)
                self.act(Dt[:], A[:], AF.Exp, [Ab], [Dtb], scale=lg)
                self.act(cfp[:], pbase[:], AF.Exp, [pbaseb], [cfpb], scale=lg)
                self.act(cfn[:], nbase[:], AF.Exp, [nbaseb], [cfnb], scale=lg)
                q, qb = QT[h % 2]
                k, kb = KT[h % 2]
                v, vb = V[h % 2]
                self.load_rows(q, qb, self.QTd, h * 64, 64)
                self.load_rows(k, kb, self.KTd, h * 64, 64)
                self.load_v(v, vb, self.Vd, h * 128)
                for qg in range(NG):
                    qs = slice(qg * 512, (qg + 1) * 512)
                    O, Ob = self.ps[qg % 2]
                    sg, sgb = sgp[qg % 2]
                    self.dma("sync", sg[:], self.GdT[h * 128:(h + 1) * 128, qs], [], [sgb])
                    for kt in range(NT):
                        ks = slice(kt * 128, (kt + 1) * 128)
                        sp, spb = self.psn(4, 8)
                        self.mm(sp[:], k[:, ks], q[:, qs], True, True, [kb, qb], [spb])
                        p_, p_b = Pt[pi % 4]
                        pi += 1
                        m = 4 * qg - kt
                        if m >= 1:
                            self.stt(p_[:], sp[:], cfp[:, m - 1:m], rowp[:], ALU.mult, ALU.mult, [spb, cfpb, rowpb], [p_b])
                        elif m <= -4:
                            self.stt(p_[:], sp[:], cfn[:, -m - 4:-m - 3], rown[:], ALU.mult, ALU.mult,
                                     [spb, cfnb, rownb], [p_b])
                        else:
                            self.tt(p_[:], sp[:], Dt[:, -m, :], ALU.mult, [spb, Dtb], [p_b])
                        self.mm(O[:], v[:, kt, :], p_[:], kt == 0, kt == NT - 1, [vb, p_b], [Ob])
                    o1, o1b = tf[0]
                    s1, s1b = tf[1]
                    mn, mnb = tf[2]
                    vr, vrb = tf[3]
                    self.cp("scalar", o1[:], O[:], [Ob], [o1b])
                    self.act(s1[:], O[:], AF.Square, [Ob], [s1b])
                    mp, mpb = self.ps[2]
                    spp, sppb = self.ps[3]
                    self.mm(mp[:], self.onesf[:], o1[:], True, True, [self.onesf_b, o1b], [mpb])
                    self.mm(spp[:], self.onesf[:], s1[:], True, True, [self.onesf_b, s1b], [sppb])
                    self.act(mn[:], mp[:], AF.Copy, [mpb], [mnb], scale=1.0 / 128)
                    self.tt(vr[:], mn[:], mn[:], ALU.mult, [mnb], [vrb])
                    self.stt(vr[:], spp[:], 1.0 / 128, vr[:], ALU.mult, ALU.subtract, [sppb, vrb], [vrb])
                    self.rsqrt(vr[:], vr[:], 1.0, [vrb], [vrb])
                    self.tt(o1[:], o1[:], mn[:], ALU.subtract, [o1b, mnb], [o1b])
                    self.stt(o1[:], o1[:], gncol[:, 0:1], vr[:], ALU.mult, ALU.mult, [o1b, vrb, gncolb], [o1b])
                    y, yb = y16[(h * NG + qg) % 2]
                    self.tt(y[:], o1[:], sg[:], ALU.mult, [o1b, sgb], [yb])
                    self.dma("gpsimd", self.YT[3072 + h * 128:3072 + (h + 1) * 128, qs], y[:], [yb],
                             [self.dbuf("YT", 3, h, qg)])
            self.barrier()


    def pipeline(self, steps, depth, stage1, stage2):
        n = len(steps)
        self.hooks = {}
        self.pit = 0
        for i in range(n + depth):
            self.pit = i
            if i < n:
                stage1(steps[i], i)
            if i >= depth:
                stage2(steps[i - depth], i - depth)
            for fn in self.hooks.pop(i, []):
                fn()
        for k in sorted(self.hooks):
            for fn in self.hooks[k]:
                fn()
        self.hooks = {}

    def later(self, delay, fn):
        self.hooks.setdefault(self.pit + delay, []).append(fn)

    def lsum_mm(self, accs, ps_t, ps_b):
        for j, (a, ab) in enumerate(accs):
            self.mm(ps_t[:], self.onesf[:], a[:], j == 0, j == len(accs) - 1, [self.onesf_b, ab], [ps_b])

    def phaseB2(self, l, s):
        S, NT, NG = self.S, self.NT, self.NG
        lam_init = 0.8 - 0.6 * math.exp(-0.3 * l)
        with ExitStack() as P:
            lp, lpb = self.sb(P, "lp", [128, 256], F32)
            self.load_bc(lp[:], lpb, self.diff_lam[l:l + 1].rearrange("a b c -> a (b c)"))
            pr, prb = self.sb(P, "pr", [128, 128], F32)
            e2, e2b = self.sb(P, "e2", [128, 2], F32)
            neglam, neglamb = self.sb(P, "neglam", [128, 1], F32)
            gcol, gcolb = self.sb(P, "gcol", [128, 1], F32)
            lp4 = lp[:].rearrange("p (a b c) -> p a b c", a=2, b=2)
            self.tt(pr[:].rearrange("p (a c) -> p a c", a=2), lp4[:, :, 0, :], lp4[:, :, 1, :], ALU.mult, [lpb], [prb])
            self.op("vector", lambda e: e.tensor_reduce(out=e2[:], in_=pr[:].rearrange("p (a c) -> p a c", a=2),
                                                        axis=AX.X, op=ALU.add), [prb], [e2b])
            self.act(e2[:], e2[:], AF.Exp, [e2b], [e2b])
            self.tt(neglam[:], e2[:, 1:2], e2[:, 0:1], ALU.subtract, [e2b], [neglamb])
            self.ts(neglam[:], neglam[:], -lam_init, None, ALU.add, None, [neglamb], [neglamb])
            self.load_cols(gcol[:], gcolb, self.diff_sub_g[l], "(p o) -> p o", o=1)
            self.ts(gcol[:], gcol[:], 1.0 - lam_init, None, ALU.mult, None, [gcolb], [gcolb])
            QT = self.sbpool(P, "QT", [128, S], BF16, 2)
            KT = self.sbpool(P, "KT", [128, S], BF16, 2)
            V = self.sbpool(P, "V", [128, NT, 128], BF16, 2)
            NP = 6
            Pt = self.sbpool(P, "Pt", [128, 512], BF16, NP)
            tf = self.sbpool(P, "tf", [128, 512], F32, 6)
            sq16 = self.sbpool(P, "sq16", [128, 512], BF16, 2)
            y16 = self.sbpool(P, "y16", [128, 512], BF16, 2)
            Lacc = [[self.sb(P, "Lacc%d%d" % (a, b), [128, 512], F32) for b in range(2)] for a in range(2)]
            leng = (LENG0, LENG1)

            def loads(h):
                self.load_rows(QT[h % 2][0], QT[h % 2][1], self.QTb, h * 128, 128)
                self.load_rows(KT[h % 2][0], KT[h % 2][1], self.KTb, h * 128, 128)
                self.load_v(V[h % 2][0], V[h % 2][1], self.Vb, h * 128)

            steps = [(h, qg, kt, si) for h in range(8) for qg in range(NG) for kt in range(NT) for si in range(2)]
            sbank = {}
            loads(0)

            def stage1(st, i):
                h, qg, kt, si = st
                q, qb = QT[h % 2]
                k, kb = KT[h % 2]
                par = (h * NG + qg) % 2
                lo, hi = si * 64, si * 64 + 64
                sp, spb = self.psn(4, 8)
                self.mm(sp[:], k[lo:hi, kt * 128:(kt + 1) * 128], q[lo:hi, qg * 512:(qg + 1) * 512], True, True,
                        [kb, qb], [spb])
                p_, p_b = Pt[i % NP]
                self.act(p_[:], sp[:], AF.Exp, [spb], [p_b], scale=0.125)
                a, ab = Lacc[par][si]
                if kt == 0:
                    self.cp(leng[si], a[:], p_[:], [p_b], [ab])
                else:
                    self.tt(a[:], a[:], p_[:], ALU.add, [ab, p_b], [ab], en=leng[si])

            def stage2(st, i):
                h, qg, kt, si = st
                if qg == 0 and kt == 0 and si == 0 and h + 1 < 8:
                    loads(h + 1)
                v, vb = V[h % 2]
                par = (h * NG + qg) % 2
                O, Ob = self.ps[par * 2 + si]
                p_, p_b = Pt[i % NP]
                self.mm(O[:], v[:, kt, :], p_[:], kt == 0, kt == NT - 1, [vb, p_b], [Ob])
                if kt == NT - 1 and si == 1:
                    epilogue(h, qg, par)

            def epilogue(h, qg, par):
                qs = slice(qg * 512, (qg + 1) * 512)
                O0, O0b = self.ps[par * 2]
                O1, O1b = self.ps[par * 2 + 1]
                o1, o1b = tf[0]
                o2, o2b = tf[1]
                r1, r1b = tf[2]
                r2, r2b = tf[3]
                oo, oob = tf[4]
                rs, rsb = tf[5]
                self.cp("scalar", o1[:], O0[:], [O0b], [o1b])
                self.cp("vector", o2[:], O1[:], [O1b], [o2b])
                for si, (r, rb) in enumerate(((r1, r1b), (r2, r2b))):
                    lp_, lpb_ = self.psn(4, 8)
                    self.lsum_mm([Lacc[par][si]], lp_, lpb_)
                    self.act(r[:], lp_[:], AF.Ln, [lpb_], [rb])
                    self.act(r[:], r[:], AF.Exp, [rb], [rb], scale=-1.0)
                self.tt(o1[:], o1[:], r1[:], ALU.mult, [o1b, r1b], [o1b])
                self.tt(o2[:], o2[:], r2[:], ALU.mult, [o2b, r2b], [o2b])
                self.stt(oo[:], o2[:], neglam[:, 0:1], o1[:], ALU.mult, ALU.add, [o2b, o1b, neglamb], [oob])
                sq, sqb = sq16[par]
                self.act(sq[:], oo[:], AF.Square, [oob], [sqb])
                sp, spb = self.psn(4, 8)
                self.mm(sp[:], self.onesb[:], sq[:], True, True, [self.onesb_b, sqb], [spb])
                self.rsqrt(rs[:], sp[:], 1.0 / 128, [spb], [rsb])
                y, yb = y16[par]
                self.stt(y[:], oo[:], gcol[:, 0:1], rs[:], ALU.mult, ALU.mult, [oob, rsb, gcolb], [yb])
                self.dma("gpsimd", self.YT[1024 + h * 128:1024 + (h + 1) * 128, qs], y[:], [yb], [])

            self.pipeline(steps, KDEPTH, stage1, stage2)
            self.barrier()

    def phaseC2(self, l, s):
        S, NT, NG = self.S, self.NT, self.NG
        sc = 192.0 ** -0.5
        with ExitStack() as P:
            Qn = self.sbpool(P, "Qn", [128, S], BF16, 2)
            Qp = self.sbpool(P, "Qp", [64, S], BF16, 2)
            Kn = self.sbpool(P, "Kn", [128, S], BF16, 2)
            Kp = self.sbpool(P, "Kp", [64, S], BF16, 2)
            V = self.sbpool(P, "V", [128, NT, 128], BF16, 2)
            NP = 6
            Pt = self.sbpool(P, "Pt", [128, 512], BF16, NP)
            tf = self.sbpool(P, "tf", [128, 512], F32, 4)
            y16 = self.sbpool(P, "y16", [128, 512], BF16, 2)
            Lacc = [[self.sb(P, "Lacc%d%d" % (a, b), [128, 512], F32) for b in range(2)] for a in range(2)]
            leng = (LENG0, LENG1)

            def loads(h):
                self.load_rows(Qn[h % 2][0], Qn[h % 2][1], self.QTc, h * 192, 128)
                self.load_rows(Qp[h % 2][0], Qp[h % 2][1], self.QTc, h * 192 + 128, 64)
                self.load_rows(Kn[h % 2][0], Kn[h % 2][1], self.KTc, h * 192, 128)
                self.load_rows(Kp[h % 2][0], Kp[h % 2][1], self.KTc, h * 192 + 128, 64)
                self.load_v(V[h % 2][0], V[h % 2][1], self.Vc, h * 128)

            steps = [(h, qg, kt) for h in range(8) for qg in range(NG) for kt in range(NT)]
            loads(0)

            def stage1(st, i):
                h, qg, kt = st
                qn, qnb = Qn[h % 2]
                qp, qpb = Qp[h % 2]
                kn, knb = Kn[h % 2]
                kp, kpb = Kp[h % 2]
                par = (h * NG + qg) % 2
                qs = slice(qg * 512, (qg + 1) * 512)
                ks = slice(kt * 128, (kt + 1) * 128)
                sp, spb = self.psn(2, 8)
                self.mm(sp[:], kn[:, ks], qn[:, qs], True, False, [knb, qnb], [spb])
                self.mm(sp[:], kp[:, ks], qp[:, qs], False, True, [kpb, qpb], [spb])
                p_, p_b = Pt[i % NP]
                self.act(p_[:], sp[:], AF.Exp, [spb], [p_b], scale=sc)
                a, ab = Lacc[par][kt % 2]
                if kt < 2:
                    self.cp(leng[kt % 2], a[:], p_[:], [p_b], [ab])
                else:
                    self.tt(a[:], a[:], p_[:], ALU.add, [ab, p_b], [ab], en=leng[kt % 2])

            def stage2(st, i):
                h, qg, kt = st
                if qg == 0 and kt == 0 and h + 1 < 8:
                    loads(h + 1)
                v, vb = V[h % 2]
                par = (h * NG + qg) % 2
                O, Ob = self.ps[par]
                p_, p_b = Pt[i % NP]
                self.mm(O[:], v[:, kt, :], p_[:], kt == 0, kt == NT - 1, [vb, p_b], [Ob])
                if kt == NT - 1:
                    qs = slice(qg * 512, (qg + 1) * 512)
                    o1, o1b = tf[par * 2]
                    r1, r1b = tf[par * 2 + 1]
                    self.cp("vector", o1[:], O[:], [Ob], [o1b])
                    lp_, lpb_ = self.psn(2, 8)
                    self.lsum_mm(Lacc[par], lp_, lpb_)
                    self.act(r1[:], lp_[:], AF.Ln, [lpb_], [r1b])
                    self.act(r1[:], r1[:], AF.Exp, [r1b], [r1b], scale=-1.0)
                    y, yb = y16[par]
                    self.tt(y[:], o1[:], r1[:], ALU.mult, [o1b, r1b], [yb])
                    self.dma("gpsimd", self.YT[2048 + h * 128:2048 + (h + 1) * 128, qs], y[:], [yb], [])

            self.pipeline(steps, KDEPTH, stage1, stage2)
            self.barrier()

    def phaseD2(self, l, s):
        S, NT, NG = self.S, self.NT, self.NG
        mmax = max(4 * (NG - 1), 1)
        nneg = max(NT - 4, 1)
        with ExitStack() as P:
            gncol, gncolb = self.sb(P, "gncol", [128, 1], F32)
            self.load_cols(gncol[:], gncolb, self.ret_gn_g[l], "(p o) -> p o", o=1)
            ii, iib = self.sb(P, "ii", [128, 512], I32)
            J, Jb = self.sb(P, "J", [128, 512], F32)
            Jr, Jrb = self.sb(P, "Jr", [128, 512], F32)
            A, Ab = self.sb(P, "A", [128, 4, 512], F32)
            pbase, pbaseb = self.sb(P, "pbase", [128, mmax], F32)
            nbase, nbaseb = self.sb(P, "nbase", [128, nneg], F32)
            self.op("gpsimd", lambda e: e.iota(ii[:], pattern=[[1, 512]], base=0, channel_multiplier=0), [], [iib])
            self.cp("vector", J[:], ii[:], [iib], [Jb])
            self.ts(Jr[:], J[:], -1.0, 511.0, ALU.mult, ALU.add, [Jb], [Jrb])
            for m in range(4):
                self.op("gpsimd", lambda e, m=m: e.iota(ii[:], pattern=[[1, 512]], base=-128 * m, channel_multiplier=-1),
                        [iib], [iib])
                self.cp("vector", A[:, m, :], ii[:], [iib], [Ab])
            self.act(A[:], A[:], AF.Abs, [Ab], [Ab])
            self.op("gpsimd", lambda e: e.iota(ii[:, 0:mmax], pattern=[[128, mmax]], base=128, channel_multiplier=-1),
                    [iib], [iib])
            self.cp("vector", pbase[:], ii[:, 0:mmax], [iib], [pbaseb])
            self.op("gpsimd", lambda e: e.iota(ii[:, 0:nneg], pattern=[[128, nneg]], base=1, channel_multiplier=1),
                    [iib], [iib])
            self.cp("vector", nbase[:], ii[:, 0:nneg], [iib], [nbaseb])
            rowp = self.sbpool(P, "rowp", [128, 512], BF16, 2)
            rown = self.sbpool(P, "rown", [128, 512], BF16, 2)
            Dt = self.sbpool(P, "Dt", [128, 4, 512], F32, 2)
            cfp = self.sbpool(P, "cfp", [128, mmax], F32, 2)
            cfn = self.sbpool(P, "cfn", [128, nneg], F32, 2)
            QT = self.sbpool(P, "QT", [64, S], BF16, 2)
            KT = self.sbpool(P, "KT", [64, S], BF16, 2)
            V = self.sbpool(P, "V", [128, NT, 128], BF16, 2)
            NP = 6
            Pt = self.sbpool(P, "Pt", [128, 512], BF16, NP)
            P0 = self.sbpool(P, "P0", [128, 512], BF16, 4)
            tf = self.sbpool(P, "tf", [128, 512], F32, 4)
            sgp = self.sbpool(P, "sg", [128, 512], BF16, 2)
            y16 = self.sbpool(P, "y16", [128, 512], BF16, 2)
            meng = (LENG0, LENG1)

            def loads(h):
                lg = math.log1p(-2.0 ** (-5.0 - h))
                hp = h % 2
                self.act(rowp[hp][0][:], J[:], AF.Exp, [Jb], [rowp[hp][1]], scale=lg)
                self.act(rown[hp][0][:], Jr[:], AF.Exp, [Jrb], [rown[hp][1]], scale=lg)
                self.act(Dt[hp][0][:], A[:], AF.Exp, [Ab], [Dt[hp][1]], scale=lg)
                self.act(cfp[hp][0][:], pbase[:], AF.Exp, [pbaseb], [cfp[hp][1]], scale=lg)
                self.act(cfn[hp][0][:], nbase[:], AF.Exp, [nbaseb], [cfn[hp][1]], scale=lg)
                self.load_rows(QT[hp][0], QT[hp][1], self.QTd, h * 64, 64)
                self.load_rows(KT[hp][0], KT[hp][1], self.KTd, h * 64, 64)
                self.load_v(V[hp][0], V[hp][1], self.Vd, h * 128)

            steps = [(h, qg, kt) for h in range(8) for qg in range(NG) for kt in range(NT)]
            loads(0)

            def stage1(st, i):
                h, qg, kt = st
                hp = h % 2
                q, qb = QT[hp]
                k, kb = KT[hp]
                par = (h * NG + qg) % 2
                if kt == 0:
                    sg, sgb = sgp[par]
                    self.dma("sync", sg[:], self.GdT[h * 128:(h + 1) * 128, qg * 512:(qg + 1) * 512], [], [sgb])
                sp, spb = self.psn(4, 8)
                self.mm(sp[:], k[:, kt * 128:(kt + 1) * 128], q[:, qg * 512:(qg + 1) * 512], True, True, [kb, qb], [spb])
                p_, p_b = Pt[i % NP]
                m = 4 * qg - kt
                if m >= 1 or m <= -4:
                    p0, p0b = P0[i % 4]
                    if m >= 1:
                        cf, cfb = cfp[hp]
                        col = cf[:, m - 1:m]
                        row, rowb = rowp[hp]
                    else:
                        cf, cfb = cfn[hp]
                        col = cf[:, -m - 4:-m - 3]
                        row, rowb = rown[hp]
                    self.act(p0[:], sp[:], AF.Identity, [spb, cfb], [p0b], scale=col)
                    self.tt(p_[:], p0[:], row[:], ALU.mult, [p0b, rowb], [p_b], en=meng[i % 2])
                else:
                    self.tt(p_[:], sp[:], Dt[hp][0][:, -m, :], ALU.mult, [spb, Dt[hp][1]], [p_b])

            def stage2(st, i):
                h, qg, kt = st
                if qg == 0 and kt == 0 and h + 1 < 8:
                    loads(h + 1)
                v, vb = V[h % 2]
                par = (h * NG + qg) % 2
                O, Ob = self.ps[par]
                p_, p_b = Pt[i % NP]
                self.mm(O[:], v[:, kt, :], p_[:], kt == 0, kt == NT - 1, [vb, p_b], [Ob])
                if kt == NT - 1:
                    qs = slice(qg * 512, (qg + 1) * 512)
                    sg, sgb = sgp[par]
                    o1, o1b = tf[0]
                    s1, s1b = tf[1]
                    mn, mnb = tf[2]
                    vr, vrb = tf[3]
                    self.cp(KCP, o1[:], O[:], [Ob], [o1b])
                    self.act(s1[:], O[:], AF.Square, [Ob], [s1b])
                    mp, mpb = self.ps[2]
                    spp, sppb = self.ps[3]
                    self.mm(mp[:], self.onesf[:], o1[:], True, True, [self.onesf_b, o1b], [mpb])
                    self.mm(spp[:], self.onesf[:], s1[:], True, True, [self.onesf_b, s1b], [sppb])
                    self.act(mn[:], mp[:], AF.Copy, [mpb], [mnb], scale=1.0 / 128)
                    self.tt(vr[:], mn[:], mn[:], ALU.mult, [mnb], [vrb])
                    self.stt(vr[:], spp[:], 1.0 / 128, vr[:], ALU.mult, ALU.subtract, [sppb, vrb], [vrb])
                    self.rsqrt(vr[:], vr[:], 1.0, [vrb], [vrb])
                    self.tt(o1[:], o1[:], mn[:], ALU.subtract, [o1b, mnb], [o1b])
                    self.stt(o1[:], o1[:], gncol[:, 0:1], vr[:], ALU.mult, ALU.mult, [o1b, vrb, gncolb], [o1b])
                    y, yb = y16[par]
                    self.tt(y[:], o1[:], sg[:], ALU.mult, [o1b, sgb], [yb])
                    self.dma("gpsimd", self.YT[3072 + h * 128:3072 + (h + 1) * 128, qs], y[:], [yb], [])

            self.pipeline(steps, KDEPTH, stage1, stage2)
            self.barrier()


    def pair(self, j):
        return self.psbig[:, 2 * j:2 * j + 2, :], [self.ps[2 * j][1], self.ps[2 * j + 1][1]]

    def phaseB3(self, l, s):
        S, NT, NG = self.S, self.NT, self.NG
        lam_init = 0.8 - 0.6 * math.exp(-0.3 * l)
        with ExitStack() as P:
            lp, lpb = self.sb(P, "lp", [128, 256], F32)
            self.load_bc(lp[:], lpb, self.diff_lam[l:l + 1].rearrange("a b c -> a (b c)"))
            pr, prb = self.sb(P, "pr", [128, 128], F32)
            e2, e2b = self.sb(P, "e2", [128, 2], F32)
            neglam, neglamb = self.sb(P, "neglam", [128, 1], F32)
            gcol, gcolb = self.sb(P, "gcol", [128, 1], F32)
            lp4 = lp[:].rearrange("p (a b c) -> p a b c", a=2, b=2)
            self.tt(pr[:].rearrange("p (a c) -> p a c", a=2), lp4[:, :, 0, :], lp4[:, :, 1, :], ALU.mult, [lpb], [prb])
            self.op("vector", lambda e: e.tensor_reduce(out=e2[:], in_=pr[:].rearrange("p (a c) -> p a c", a=2),
                                                        axis=AX.X, op=ALU.add), [prb], [e2b])
            self.act(e2[:], e2[:], AF.Exp, [e2b], [e2b])
            self.tt(neglam[:], e2[:, 1:2], e2[:, 0:1], ALU.subtract, [e2b], [neglamb])
            self.ts(neglam[:], neglam[:], -lam_init, None, ALU.add, None, [neglamb], [neglamb])
            self.load_cols(gcol[:], gcolb, self.diff_sub_g[l], "(p o) -> p o", o=1)
            self.ts(gcol[:], gcol[:], 1.0 - lam_init, None, ALU.mult, None, [gcolb], [gcolb])
            QT = self.sbpool(P, "QT", [128, S], BF16, 2)
            KA = self.sbpool(P, "KA", [128, S], BF16, 2)
            KB_ = self.sbpool(P, "KBt", [128, S], BF16, 2)
            for i in range(2):
                self.op("vector", lambda e, i=i: e.memset(KA[i][0][64:128, :], 0.0), [], [KA[i][1]])
                self.op("vector", lambda e, i=i: e.memset(KB_[i][0][0:64, :], 0.0), [], [KB_[i][1]])
            V = self.sbpool(P, "V", [128, NT, 128], BF16, 2)
            NP = 6
            Pt = self.sbpool(P, "Pt", [128, 2, 512], BF16, NP)
            tf = self.sbpool(P, "tf", [128, 512], F32, 6)
            sq16 = self.sbpool(P, "sq16", [128, 512], BF16, 2)
            y16 = self.sbpool(P, "y16", [128, 512], BF16, 2)
            Lacc2 = [self.sb(P, "Lacc%d" % a, [128, 2, 512], F32) for a in range(2)]
            Lacc = [[(Lacc2[a][0][:, b, :], Lacc2[a][1]) for b in range(2)] for a in range(2)]
            T1 = self.sbpool(P, "T1", [128, 2, 512], BF16, 4)
            leng = (LENG0, LENG1)

            def loads(h):
                self.load_rows(QT[h % 2][0], QT[h % 2][1], self.QTb, h * 128, 128)
                self.dma("sync", KA[h % 2][0][0:64, :], self.KTb[h * 128:h * 128 + 64, :], [], [KA[h % 2][1]])
                self.dma("sync", KB_[h % 2][0][64:128, :], self.KTb[h * 128 + 64:h * 128 + 128, :], [], [KB_[h % 2][1]])
                self.load_v(V[h % 2][0], V[h % 2][1], self.Vb, h * 128)

            steps = [(h, qg, kt) for h in range(8) for qg in range(NG) for kt in range(NT)]
            loads(0)

            def stage1(st, i):
                h, qg, kt = st
                q, qb = QT[h % 2]
                par = (h * NG + qg) % 2
                sp, spbs = self.pair(2 + i % 2)
                ks = slice(kt * 128, (kt + 1) * 128)
                qs = slice(qg * 512, (qg + 1) * 512)
                self.mm(sp[:, 0, :], KA[h % 2][0][:, ks], q[:, qs], True, True, [KA[h % 2][1], qb], [spbs[0]])
                self.mm(sp[:, 1, :], KB_[h % 2][0][:, ks], q[:, qs], True, True, [KB_[h % 2][1], qb], [spbs[1]])
                p_, p_b = Pt[i % NP]
                self.act(p_[:], sp, AF.Exp, spbs, [p_b], scale=0.125)
                if kt % 2 == 1:
                    t1, t1b = T1[(i // 2) % 4]
                    pa, pab = Pt[(i - 1) % NP]
                    self.tt(t1[:], pa[:], p_[:], ALU.add, [pab, p_b], [t1b])
                    if kt % 4 == 3:
                        t0_, t0b = T1[((i // 2) - 1) % 4]
                        a, ab = Lacc2[par]
                        if kt == 3:
                            self.tt(a[:], t0_[:], t1[:], ALU.add, [t0b, t1b], [ab])
                        else:
                            self.tt(t1[:], t0_[:], t1[:], ALU.add, [t0b, t1b], [t1b])
                            self.tt(a[:], a[:], t1[:], ALU.add, [ab, t1b], [ab])

            def stage2(st, i):
                h, qg, kt = st
                if qg == 0 and kt == 0 and h + 1 < 8:
                    loads(h + 1)
                v, vb = V[h % 2]
                par = (h * NG + qg) % 2
                p_, p_b = Pt[i % NP]
                for si in range(2):
                    O, Ob = self.ps[par * 2 + si]
                    self.mm(O, v[:, kt, :], p_[:, si, :], kt == 0, kt == NT - 1, [vb, p_b], [Ob])
                if kt == NT - 1:
                    epilogue(h, qg, par, i)

            def epilogue(h, qg, par, i):
                qs = slice(qg * 512, (qg + 1) * 512)
                O0, O0b = self.ps[par * 2]
                O1, O1b = self.ps[par * 2 + 1]
                o1, o1b = tf[0]
                o2, o2b = tf[1]
                r1, r1b = tf[2]
                r2, r2b = tf[3]
                oo, oob = tf[4]
                rs, rsb = tf[5]
                sq, sqb = sq16[par]
                self.cp("vector", o1[:], O0, [O0b], [o1b])
                self.cp("vector", o2[:], O1, [O1b], [o2b])

                def e1():
                    bnk = 4 + 2 * ((self.pit + 1) % 2)
                    for si, (r, rb) in enumerate(((r1, r1b), (r2, r2b))):
                        lp_, lpb_ = self.ps[bnk + si]
                        self.lsum_mm([Lacc[par][si]], lp_, lpb_)
                        self.act(r[:], lp_, AF.Ln, [lpb_], [rb])
                        self.act(r[:], r[:], AF.Exp, [rb], [rb], scale=-1.0)
                    self.tt(o1[:], o1[:], r1[:], ALU.mult, [o1b, r1b], [o1b])
                    self.tt(o2[:], o2[:], r2[:], ALU.mult, [o2b, r2b], [o2b])
                    self.stt(oo[:], o2[:], neglam[:, 0:1], o1[:], ALU.mult, ALU.add, [o2b, o1b, neglamb], [oob])
                    self.tt(sq[:], oo[:], oo[:], ALU.mult, [oob], [sqb])

                def e2():
                    bnk = 4 + 2 * ((self.pit + 1) % 2)
                    sp, spb = self.ps[bnk]
                    self.mm(sp, self.onesb[:], sq[:], True, True, [self.onesb_b, sqb], [spb])
                    self.rsqrt(rs[:], sp, 1.0 / 128, [spb], [rsb])
                    y, yb = y16[par]
                    self.stt(y[:], oo[:], gcol[:, 0:1], rs[:], ALU.mult, ALU.mult, [oob, rsb, gcolb], [yb])
                    self.dma("gpsimd", self.YT[1024 + h * 128:1024 + (h + 1) * 128, qs], y[:], [yb], [])
                self.later(min(KE1, NT - 2), e1)
                self.later(min(KE2, NT - 1), e2)

            self.pipeline(steps, 1, stage1, stage2)
            self.barrier()

    def phaseC3(self, l, s):
        S, NT, NG = self.S, self.NT, self.NG
        sc = 192.0 ** -0.5
        NTP = NT // 2
        with ExitStack() as P:
            Qn = self.sbpool(P, "Qn", [128, S], BF16, 2)
            Qp = self.sbpool(P, "Qp", [128, S], BF16, 2)
            Kn = self.sbpool(P, "Kn", [128, S], BF16, 2)
            Kp = self.sbpool(P, "Kp", [128, S], BF16, 2)
            for i in range(2):
                self.op("vector", lambda e, i=i: e.memset(Qp[i][0][64:128, :], 0.0), [], [Qp[i][1]])
                self.op("vector", lambda e, i=i: e.memset(Kp[i][0][64:128, :], 0.0), [], [Kp[i][1]])
            V = self.sbpool(P, "V", [128, NT, 128], BF16, 2)
            NP = 4
            Pt = self.sbpool(P, "Pt", [128, 2, 512], BF16, NP)
            tf = self.sbpool(P, "tf", [128, 512], F32, 4)
            y16 = self.sbpool(P, "y16", [128, 512], BF16, 2)
            Lacc = [self.sb(P, "Lacc%d" % a, [128, 2, 512], F32) for a in range(2)]

            def loads(h):
                self.load_rows(Qn[h % 2][0], Qn[h % 2][1], self.QTc, h * 192, 128)
                self.load_rows(Qp[h % 2][0], Qp[h % 2][1], self.QTc, h * 192 + 128, 64)
                self.load_rows(Kn[h % 2][0], Kn[h % 2][1], self.KTc, h * 192, 128)
                self.load_rows(Kp[h % 2][0], Kp[h % 2][1], self.KTc, h * 192 + 128, 64)
                self.load_v(V[h % 2][0], V[h % 2][1], self.Vc, h * 128)

            steps = [(h, qg, kp) for h in range(8) for qg in range(NG) for kp in range(NTP)]
            loads(0)

            def stage1(st, i):
                h, qg, kp_ = st
                qn, qnb = Qn[h % 2]
                qp, qpb = Qp[h % 2]
                kn, knb = Kn[h % 2]
                kp, kpb = Kp[h % 2]
                par = (h * NG + qg) % 2
                qs = slice(qg * 512, (qg + 1) * 512)
                sp, spbs = self.pair(1 + i % 3)
                for j in range(2):
                    kt = 2 * kp_ + j
                    ks = slice(kt * 128, (kt + 1) * 128)
                    self.mm(sp[:, j, :], kn[:, ks], qn[:, qs], True, False, [knb, qnb], [spbs[j]])
                    self.mm(sp[:, j, :], kp[:, ks], qp[:, qs], False, True, [kpb, qpb], [spbs[j]])
                p_, p_b = Pt[i % NP]
                self.act(p_[:], sp, AF.Exp, spbs, [p_b], scale=sc)
                a, ab = Lacc[par]
                if kp_ == 0:
                    self.cp("vector", a[:], p_[:], [p_b], [ab])
                else:
                    self.tt(a[:], a[:], p_[:], ALU.add, [ab, p_b], [ab])

            def stage2(st, i):
                h, qg, kp_ = st
                if qg == 0 and kp_ == 0 and h + 1 < 8:
                    loads(h + 1)
                v, vb = V[h % 2]
                par = (h * NG + qg) % 2
                O, Ob = self.ps[par]
                p_, p_b = Pt[i % NP]
                for j in range(2):
                    kt = 2 * kp_ + j
                    self.mm(O, v[:, kt, :], p_[:, j, :], kt == 0, kt == NT - 1, [vb, p_b], [Ob])
                if kp_ == NTP - 1:
                    qs = slice(qg * 512, (qg + 1) * 512)
                    o1, o1b = tf[par * 2]
                    r1, r1b = tf[par * 2 + 1]
                    self.cp("vector", o1[:], O, [Ob], [o1b])

                    def e1(h=h, qs=qs, par=par, o1=o1, o1b=o1b, r1=r1, r1b=r1b):
                        lp_, lpb_ = self.ps[2 + 2 * ((self.pit + 1) % 3)]
                        a, ab = Lacc[par]
                        self.mm(lp_, self.onesf[:], a[:, 0, :], True, False, [self.onesf_b, ab], [lpb_])
                        self.mm(lp_, self.onesf[:], a[:, 1, :], False, True, [self.onesf_b, ab], [lpb_])
                        self.act(r1[:], lp_, AF.Ln, [lpb_], [r1b])
                        self.act(r1[:], r1[:], AF.Exp, [r1b], [r1b], scale=-1.0)
                        y, yb = y16[par]
                        self.tt(y[:], o1[:], r1[:], ALU.mult, [o1b, r1b], [yb])
                        self.dma("gpsimd", self.YT[2048 + h * 128:2048 + (h + 1) * 128, qs], y[:], [yb], [])
                    self.later(min(2, NTP - 1), e1)

            self.pipeline(steps, 2, stage1, stage2)
            self.barrier()

    def phaseD3(self, l, s):
        S, NT, NG = self.S, self.NT, self.NG
        mmax = max(4 * (NG - 1), 1)
        nneg = max(NT - 4, 1)
        with ExitStack() as P:
            gncol, gncolb = self.sb(P, "gncol", [128, 1], F32)
            self.load_cols(gncol[:], gncolb, self.ret_gn_g[l], "(p o) -> p o", o=1)
            ii, iib = self.sb(P, "ii", [128, 512], I32)
            J, Jb = self.sb(P, "J", [128, 512], F32)
            Jr, Jrb = self.sb(P, "Jr", [128, 512], F32)
            A, Ab = self.sb(P, "A", [128, 4, 512], F32)
            pbase, pbaseb = self.sb(P, "pbase", [128, mmax], F32)
            nbase, nbaseb = self.sb(P, "nbase", [128, nneg], F32)
            self.op("gpsimd", lambda e: e.iota(ii[:], pattern=[[1, 512]], base=0, channel_multiplier=0), [], [iib])
            self.cp("vector", J[:], ii[:], [iib], [Jb])
            self.ts(Jr[:], J[:], -1.0, 511.0, ALU.mult, ALU.add, [Jb], [Jrb])
            for m in range(4):
                self.op("gpsimd", lambda e, m=m: e.iota(ii[:], pattern=[[1, 512]], base=-128 * m, channel_multiplier=-1),
                        [iib], [iib])
                self.cp("vector", A[:, m, :], ii[:], [iib], [Ab])
            self.act(A[:], A[:], AF.Abs, [Ab], [Ab])
            self.op("gpsimd", lambda e: e.iota(ii[:, 0:mmax], pattern=[[128, mmax]], base=128, channel_multiplier=-1),
                    [iib], [iib])
            self.cp("vector", pbase[:], ii[:, 0:mmax], [iib], [pbaseb])
            self.op("gpsimd", lambda e: e.iota(ii[:, 0:nneg], pattern=[[128, nneg]], base=1, channel_multiplier=1),
                    [iib], [iib])
            self.cp("vector", nbase[:], ii[:, 0:nneg], [iib], [nbaseb])
            rowp = self.sbpool(P, "rowp", [128, 512], BF16, 2)
            rown = self.sbpool(P, "rown", [128, 512], BF16, 2)
            Dt = self.sbpool(P, "Dt", [128, 4, 512], F32, 2)
            cfp = self.sbpool(P, "cfp", [128, mmax], F32, 2)
            cfn = self.sbpool(P, "cfn", [128, nneg], F32, 2)
            QT = self.sbpool(P, "QT", [128, S], BF16, 2)
            KT = self.sbpool(P, "KT", [128, S], BF16, 2)
            for i in range(2):
                self.op("vector", lambda e, i=i: e.memset(QT[i][0][64:128, :], 0.0), [], [QT[i][1]])
                self.op("vector", lambda e, i=i: e.memset(KT[i][0][64:128, :], 0.0), [], [KT[i][1]])
            V = self.sbpool(P, "V", [128, NT, 128], BF16, 2)
            NP = 8
            Pt = self.sbpool(P, "Pt", [128, 512], BF16, NP)
            P0 = self.sbpool(P, "P0", [128, 512], BF16, 6)
            tf = self.sbpool(P, "tf", [128, 512], F32, 4)
            sgp = self.sbpool(P, "sg", [128, 512], BF16, 2)
            y16 = self.sbpool(P, "y16", [128, 512], BF16, 2)
            cnt = [0]

            def loads(h):
                lg = math.log1p(-2.0 ** (-5.0 - h))
                hp = h % 2
                self.act(rowp[hp][0][:], J[:], AF.Exp, [Jb], [rowp[hp][1]], scale=lg)
                self.act(rown[hp][0][:], Jr[:], AF.Exp, [Jrb], [rown[hp][1]], scale=lg)
                self.act(Dt[hp][0][:], A[:], AF.Exp, [Ab], [Dt[hp][1]], scale=lg)
                self.act(cfp[hp][0][:], pbase[:], AF.Exp, [pbaseb], [cfp[hp][1]], scale=lg)
                self.act(cfn[hp][0][:], nbase[:], AF.Exp, [nbaseb], [cfn[hp][1]], scale=lg)
                self.load_rows(QT[hp][0], QT[hp][1], self.QTd, h * 64, 64)
                self.load_rows(KT[hp][0], KT[hp][1], self.KTd, h * 64, 64)
                self.load_v(V[hp][0], V[hp][1], self.Vd, h * 128)

            steps = [(h, qg, kt) for h in range(8) for qg in range(NG) for kt in range(NT)]
            loads(0)

            def stage1(st, i):
                h, qg, kt = st
                hp = h % 2
                q, qb = QT[hp]
                k, kb = KT[hp]
                par = (h * NG + qg) % 2
                if kt == 0:
                    sg, sgb = sgp[par]
                    self.dma("sync", sg[:], self.GdT[h * 128:(h + 1) * 128, qg * 512:(qg + 1) * 512], [], [sgb])
                sp, spb = self.psn(2, 8)
                self.mm(sp, k[:, kt * 128:(kt + 1) * 128], q[:, qg * 512:(qg + 1) * 512], True, True, [kb, qb], [spb])
                p_, p_b = Pt[i % NP]
                m = 4 * qg - kt
                if m >= 1 or m <= -4:
                    if m >= 1:
                        cf, cfb = cfp[hp]
                        col = cf[:, m - 1:m]
                        row, rowb = rowp[hp]
                    else:
                        cf, cfb = cfn[hp]
                        col = cf[:, -m - 4:-m - 3]
                        row, rowb = rown[hp]
                    cnt[0] += 1
                    if cnt[0] % KDACT != 0:
                        p0, p0b = P0[cnt[0] % 6]
                        self.act(p0[:], sp, AF.Identity, [spb, cfb], [p0b], scale=col)
                        self.tt(p_[:], p0[:], row[:], ALU.mult, [p0b, rowb], [p_b], en=LENG1)
                    else:
                        self.stt(p_[:], sp, col, row[:], ALU.mult, ALU.mult, [spb, cfb, rowb], [p_b])
                else:
                    self.tt(p_[:], sp, Dt[hp][0][:, -m, :], ALU.mult, [spb, Dt[hp][1]], [p_b])

            def stage2(st, i):
                h, qg, kt = st
                if qg == 0 and kt == 0 and h + 1 < 8:
                    loads(h + 1)
                v, vb = V[h % 2]
                par = (h * NG + qg) % 2
                O, Ob = self.ps[par]
                p_, p_b = Pt[i % NP]
                self.mm(O, v[:, kt, :], p_[:], kt == 0, kt == NT - 1, [vb, p_b], [Ob])
                if kt == NT - 1:
                    qs = slice(qg * 512, (qg + 1) * 512)
                    sg, sgb = sgp[par]
                    o1, o1b = tf[0]
                    s1, s1b = tf[1]
                    mn, mnb = tf[2]
                    vr, vrb = tf[3]
                    self.cp("scalar", o1[:], O, [Ob], [o1b])
                    self.act(s1[:], O, AF.Square, [Ob], [s1b])

                    def e1(h=h, qs=qs, par=par, sg=sg, sgb=sgb):
                        mp, mpb = self.psn(2, 8)
                        spp, sppb = self.psn(2, 8)
                        self.mm(mp, self.onesf[:], o1[:], True, True, [self.onesf_b, o1b], [mpb])
                        self.mm(spp, self.onesf[:], s1[:], True, True, [self.onesf_b, s1b], [sppb])
                        self.act(mn[:], mp, AF.Copy, [mpb], [mnb], scale=1.0 / 128)
                        self.tt(vr[:], mn[:], mn[:], ALU.mult, [mnb], [vrb])
                        self.stt(vr[:], spp, 1.0 / 128, vr[:], ALU.mult, ALU.subtract, [sppb, vrb], [vrb])
                        self.rsqrt(vr[:], vr[:], 1.0, [vrb], [vrb])
                        self.tt(o1[:], o1[:], mn[:], ALU.subtract, [o1b, mnb], [o1b])
                        self.stt(o1[:], o1[:], gncol[:, 0:1], vr[:], ALU.mult, ALU.mult, [o1b, vrb, gncolb], [o1b])
                        y, yb = y16[par]
                        self.tt(y[:], o1[:], sg[:], ALU.mult, [o1b, sgb], [yb])
                        self.dma("gpsimd", self.YT[3072 + h * 128:3072 + (h + 1) * 128, qs], y[:], [yb], [])
                    self.later(min(2, NT - 1), e1)

            self.pipeline(steps, KDD, stage1, stage2)
            self.barrier()

    def phaseE(self, l, s):
        S = self.S
        TB = min(1024, S)
        nblk = S // TB
        NGb = TB // 512
        with ExitStack() as P:
            xk, xkb = self.sb(P, "xk", [128, 8, TB], F32)
            big, bigb = self.sb(P, "big", [128, 32, TB], BF16)
            bigbs = [Buf("big%d" % i_) for i_ in range(4)]
            aT, aTb = self.sb(P, "aT", [128, 8, TB], BF16)
            pT, pTb = self.sb(P, "pT", [128, 2, TB], BF16)
            wpool = self.sbpool(P, "we", [128, 4096], BF16, 3)
            g2, g2b = self.sb(P, "g2", [128, 8], F32)
            g3, g3b = self.sb(P, "g3", [128, 8], F32)
            self.load_cols(g2[:], g2b, self.norm2_g[l], "(kc p) -> p kc", p=128)
            self.load_cols(g3[:], g3b, self.norm3_g[l], "(kc p) -> p kc", p=128)
            acc = self.sbpool(P, "acc", [128, 512], F32, 4 * NGb)
            tmpf = self.sbpool(P, "tmpf", [128, 512], F32, 3)
            gtp = self.sbpool(P, "gt", [128, 512], BF16, 3)
            pl = self.sbpool(P, "pl", [128, 256], F32, 2)
            pl16 = self.sbpool(P, "pl16", [128, 256], BF16, 2)
            cnt = [0]

            def load_y(blk_):
                for b_ in range(4):
                    self.dma("sync", big[:, b_ * 8:(b_ + 1) * 8, :],
                             self.YT[b_ * 1024:(b_ + 1) * 1024, blk_ * TB:(blk_ + 1) * TB].rearrange(
                                 "(kc p) t -> p kc t", p=128), [], [bigbs[b_]])

            for blk in range(nblk):
                c0 = s * S + blk * TB
                lc0 = blk * TB
                if blk == 0:
                    load_y(0)
                for nt in range(2):
                    for b_ in range(4):
                        def epi(pt, pb, m, g, b_=b_, nt=nt):
                            gt, gtb = gtp[cnt[0] % 3]
                            tm, tmb = tmpf[cnt[0] % 3]
                            cnt[0] += 1
                            r0 = b_ * 1024 + (nt * 4 + m) * 128
                            self.dma("sync", gt[:], self.GT[r0:r0 + 128, lc0 + g * 512:lc0 + (g + 1) * 512], [], [gtb])
                            a, ab = acc[m * NGb + g]
                            if b_ == 0:
                                self.tt(a[:], pt[:], gt[:], ALU.mult, [pb, gtb], [ab])
                            elif b_ < 3:
                                self.tt(tm[:], pt[:], gt[:], ALU.mult, [pb, gtb], [tmb])
                                self.tt(a[:], a[:], tm[:], ALU.add, [ab, tmb], [ab], en="gpsimd")
                            else:
                                self.tt(tm[:], pt[:], gt[:], ALU.mult, [pb, gtb], [tmb])
                                self.tt(aT[:, nt * 4 + m, g * 512:(g + 1) * 512], a[:], tm[:], ALU.add, [ab, tmb], [aTb],
                                        en="gpsimd")
                        self.gemm("fm", big, bigbs[b_], 8, TB, self.wb["br%d" % b_, l], nt * 512, 512, epi, wpool, k0=b_ * 8)
                self.dma("sync", xk[:], self.xT[:, c0:c0 + TB].rearrange("(kc p) t -> p kc t", p=128), [], [xkb])
                for nt in range(2):
                    def epi(pt, pb, m, g, nt=nt):
                        xs = xk[:, nt * 4 + m, g * 512:(g + 1) * 512]
                        self.tt(xs, xs, pt[:], ALU.add, [xkb, pb], [xkb])
                    self.gemm("fm", aT, aTb, 8, TB, self.wb["out", l], nt * 512, 512, epi, wpool)
                self.norm_fm(None, 0, TB, g2, g2b, aT, aTb, xkeep=(xk, xkb), loaded=True)
                for nt in range(8):
                    def epi(pt, pb, m, g, nt=nt):
                        tm, tmb = tmpf[cnt[0] % 3]
                        cnt[0] += 1
                        self.act(tm[:], pt[:], AF.Relu, [pb], [tmb])
                        self.tt(big[:, nt * 4 + m, g * 512:(g + 1) * 512], tm[:], tm[:], ALU.mult, [tmb],
                                [bigbs[(nt * 4 + m) // 8]])
                    self.gemm("fm", aT, aTb, 8, TB, self.wb["ff1", l], nt * 512, 512, epi, wpool)
                for mt in range(8):
                    def epi(pt, pb, m, g, mt=mt):
                        xs = xk[:, mt, g * 512:(g + 1) * 512]
                        self.tt(xs, xs, pt[:], ALU.add, [xkb, pb], [xkb])
                    self.gemm("fm", big, bigbs, 32, TB, self.wb["ff2", l], mt * 128, 128, epi, wpool)
                if blk + 1 < nblk:
                    load_y(blk + 1)
                self.norm_fm(None, 0, TB, g3, g3b, aT, aTb, xkeep=(xk, xkb), loaded=True)
                for t in range(TB // 128):
                    p_, p_b = pl[t % 2]
                    p16, p16b = pl16[t % 2]
                    self.dma("sync", p_[:], self.p[l, c0 + t * 128:c0 + (t + 1) * 128, :], [], [p_b])
                    self.cp(self.alt(), p16[:], p_[:], [p_b], [p16b])
                    i = self.psi % 8
                    self.psi += 1
                    pv = self.psb(i)
                    for bi in range(2):
                        self.tp(pv[:, bi * 128:(bi + 1) * 128], p16[:, bi * 128:(bi + 1) * 128], self.identb[:],
                                [p16b, self.identb_b], [self.ps[i][1]])
                    self.cp(self.alt(), pT[:, :, t * 128:(t + 1) * 128], pv[:, 0:256].rearrange("p (b t) -> p b t", b=2),
                            [self.ps[i][1]], [pTb])
                for nt in range(2):
                    wg, wgb = wpool[self.wi % 3]
                    self.wi += 1
                    wp, wpb = wpool[self.wi % 3]
                    self.wi += 1
                    wgv = wg[:, 0:4096].rearrange("p (k n) -> p k n", k=8)
                    wpv = wp[:, 0:1024].rearrange("p (k n) -> p k n", k=2)
                    self.dma("sync", wgv, self.wb["pg", l][:, nt * 512:(nt + 1) * 512].rearrange("(kc p) n -> p kc n", p=128),
                             [], [wgb])
                    self.dma("sync", wpv, self.wb["pp", l][:, nt * 512:(nt + 1) * 512].rearrange("(kc p) n -> p kc n", p=128),
                             [], [wpb])
                    for m in range(4):
                        for g in range(NGb):
                            gs = slice(g * 512, (g + 1) * 512)
                            p1, p1b = self.psn()
                            for kc in range(8):
                                self.mm(p1[:], wgv[:, kc, m * 128:(m + 1) * 128], aT[:, kc, gs], kc == 0, kc == 7,
                                        [wgb, aTb], [p1b])
                            p2, p2b = self.psn()
                            for kc in range(2):
                                self.mm(p2[:], wpv[:, kc, m * 128:(m + 1) * 128], pT[:, kc, gs], kc == 0, kc == 1,
                                        [wpb, pTb], [p2b])
                            tm, tmb = tmpf[cnt[0] % 3]
                            cnt[0] += 1
                            self.act(tm[:], p1[:], AF.Sigmoid, [p1b], [tmb])
                            self.tt(tm[:], tm[:], p2[:], ALU.mult, [tmb, p2b], [tmb])
                            xs = xk[:, nt * 4 + m, gs]
                            self.tt(xs, xs, tm[:], ALU.add, [xkb, tmb], [xkb], en="gpsimd")
                self.dma("gpsimd", self.xT[:, c0:c0 + TB].rearrange("(kc p) t -> p kc t", p=128), xk[:], [xkb], [])
            self.barrier()

    def mark(self, name):
        if not hasattr(self, "marks"):
            self.marks = []
        self.marks.append((name, {k: e.dom.count for k, e in self.E.items()}))

    def body(self):
        for l in range(self.L):
            for s in range(self.NSEQ):
                self.mark("P1 %d %d" % (l, s))
                self.phase1(l, s)
                self.mark("PB %d %d" % (l, s))
                (self.phaseB3 if "B3" in PH else self.phaseB2)(l, s)
                self.mark("PC %d %d" % (l, s))
                (self.phaseC3 if "C3" in PH else self.phaseC2)(l, s)
                self.mark("PD %d %d" % (l, s))
                (self.phaseD3 if "D3" in PH else self.phaseD2)(l, s)
                self.mark("PE %d %d" % (l, s))
                self.phaseE(l, s)
        self.mark("END")

    def build(self):
        self.setup_sync()
        self.declare_io()
        self.build_consts()
        self.precast()
        self.build_rope()
        self.transpose_in()
        self.body()
        self.transpose_out()
        self.barrier()
        self.st.close()
        return self.nc


INPUT_NAMES = ["x", "p", "norm1_g", "w_in", "gate_b", "conv_w", "conv_b", "lru_wa", "lru_ba", "lru_wx", "lru_bx",
               "lru_lambda", "diff_q_g", "diff_k_g", "diff_lam", "diff_sub_g", "mla_qa_g", "mla_wuq", "mla_kva_g",
               "mla_wukv", "mla_q_g", "mla_k_g", "ret_gn_g", "w_br_a", "w_br_b", "w_br_c", "w_br_d", "w_out",
               "norm2_g", "w_ff1", "w_ff2", "norm3_g", "w_ple_gate", "w_ple_proj"]


def make_in_maps(inputs, ncores, nseq, S):
    maps = []
    for c in range(ncores):
        m = {}
        for k in INPUT_NAMES:
            v = np.asarray(inputs[k])
            if k == "x":
                v = np.ascontiguousarray(v[c * nseq:(c + 1) * nseq].reshape(nseq * S, DM))
            elif k == "p":
                v = np.ascontiguousarray(v[:, c * nseq:(c + 1) * nseq].reshape(v.shape[0], nseq * S, PLED))
            else:
                v = np.ascontiguousarray(v)
            m[k] = v.astype(np.float32, copy=False)
        maps.append(m)
    return maps


def kernel(**inputs):
    B, S, _ = inputs["x"].shape
    L = inputs["w_in"].shape[0]
    ncores = 8
    nseq = B // ncores
    kb = KB(S, nseq, L)
    nc = kb.build()
    maps = make_in_maps(inputs, ncores, nseq, S)
    res = run_bass_kernel_spmd(nc, maps, core_ids=list(range(ncores)))
    outs = [np.asarray(r["out"]).reshape(nseq, S, DM) for r in res.results]
    return np.concatenate(outs, axis=0).astype(np.float32)
```
